# Optimizing a Trainium2 kernel written in Bass

```python
import math
import jax, jax.numpy as jnp
from jax import lax
import numpy as np

D_MODEL = 1024
BATCH = 16
SEQ = 2048
DEPTH = 2
DEC_BATCH = 32
DEC_SEQ = 4
PAST_LEN = 16384
PAGE_SIZE = 128

N_MIXERS = 2
N_CONV_LAYERS = (DEPTH + 1) // 2
N_NSA_LAYERS = DEPTH // 2
D_FF = 2816
CONV_W = 3
N_HEADS = 16
HEAD_DIM = D_MODEL // N_HEADS
N_KV_HEADS = 4
GROUP = N_HEADS // N_KV_HEADS
KV_W = N_KV_HEADS * HEAD_DIM
NSA_IN = N_HEADS * HEAD_DIM + 6 * KV_W + 3 * N_HEADS
L_CMP = 32
L_SEL = 64
TOP_N = 16
WINDOW = 512
Q_CHUNK = 16
NORM_EPS = 1e-6
FORCE_BONUS = 1e4
NEG = -1e30

kernel_name = "conv_nsa_macaron_hybrid_step"


def rmsnorm(x, g):
    xf = x.astype(jnp.float32)
    y = xf * lax.rsqrt(jnp.mean(xf * xf, axis=-1, keepdims=True) + NORM_EPS)
    return (y * g.astype(jnp.float32)).astype(x.dtype)


def swiglu(x, w_gu, w_down):
    g, u = jnp.split(x @ w_gu, 2, axis=-1)
    return (jax.nn.silu(g) * u) @ w_down


def short_conv_mixer(h, conv_state, w_in, w_conv, w_out):
    bg, cg, v = jnp.split(h @ w_in, 3, axis=-1)
    u = cg * v
    u_ext = jnp.concatenate([conv_state.astype(u.dtype), u], axis=1)
    T = h.shape[1]
    conv = w_conv[0] * u_ext[:, 0:T]
    for i in range(1, CONV_W):
        conv = conv + w_conv[i] * u_ext[:, i:i + T]
    y = (bg * conv) @ w_out
    return y, u_ext[:, -(CONV_W - 1):]


def nsa_project(h, w_in):
    B, T, _ = h.shape
    splits = np.cumsum([N_HEADS * HEAD_DIM] + [KV_W] * 6).tolist()
    p = jnp.split(h @ w_in, splits, axis=-1)
    q = p[0].reshape(B, T, N_KV_HEADS, GROUP, HEAD_DIM)

    def kv(a, b):
        return jnp.stack([a, b], axis=2).reshape(B, T, 2, N_KV_HEADS, HEAD_DIM)

    gates = jax.nn.sigmoid(p[7]).reshape(B, T, 3, N_KV_HEADS, GROUP)
    return q, kv(p[1], p[2]), kv(p[3], p[4]), kv(p[5], p[6]), gates


def compress(rows, pe, w1k, w2k, w1v, w2v):
    B, T = rows.shape[:2]
    blk = rows.reshape(B, T // L_CMP, L_CMP, 2, N_KV_HEADS, HEAD_DIM) + pe[:, None, None, :]
    k = jax.nn.gelu(jnp.einsum('bnlkd,lde->bnke', blk[:, :, :, 0], w1k)) @ w2k
    v = jax.nn.gelu(jnp.einsum('bnlkd,lde->bnke', blk[:, :, :, 1], w1v)) @ w2v
    return k, v


def nsa_attend(q, t_pos, gates, k_cmp, v_cmp, gather_sel, kv_win, s_win):
    B, Tq = q.shape[:2]
    scale = HEAD_DIM ** -0.5
    nb_c = k_cmp.shape[1]
    blk_end = (jnp.arange(nb_c) + 1) * L_CMP - 1
    m_c = (blk_end[None, :] <= t_pos[:, None])[None, :, None, None, :]
    s_c = jnp.einsum('bqkgd,bnkd->bqkgn', q, k_cmp).astype(jnp.float32) * scale
    p_c = jax.nn.softmax(jnp.where(m_c, s_c, NEG), axis=-1) * m_c
    o_c = jnp.einsum('bqkgn,bnkd->bqkgd', p_c.astype(q.dtype), v_cmp)
    ratio = L_SEL // L_CMP
    nb_s = nb_c // ratio
    imp = p_c.sum(axis=3).reshape(B, Tq, N_KV_HEADS, nb_s, ratio).sum(-1)
    blk = jnp.arange(nb_s)[None, :]
    cur = (t_pos // L_SEL)[:, None]
    forced = ((blk == 0) | (blk == cur) | (blk == cur - 1))[None, :, None, :]
    valid = (blk * L_SEL <= t_pos[:, None])[None, :, None, :]
    score = jnp.where(forced, FORCE_BONUS, jnp.where(valid, imp, -FORCE_BONUS))
    _, idx = lax.top_k(score, min(TOP_N, nb_s))
    k_sel, v_sel = gather_sel(idx)
    key_pos = idx[..., None] * L_SEL + jnp.arange(L_SEL)
    m_s = (key_pos <= t_pos[None, :, None, None, None])[:, :, :, None]
    s_s = jnp.einsum('bqkgd,bqkjld->bqkgjl', q, k_sel).astype(jnp.float32) * scale
    s_s = jnp.where(m_s, s_s, NEG)
    n_sel = s_s.shape[4]
    p_s = jax.nn.softmax(s_s.reshape(B, Tq, N_KV_HEADS, GROUP, n_sel * L_SEL), axis=-1)
    p_s = p_s.reshape(B, Tq, N_KV_HEADS, GROUP, n_sel, L_SEL)
    o_s = jnp.einsum('bqkgjl,bqkjld->bqkgd', p_s.astype(q.dtype), v_sel)
    d = t_pos[:, None] - s_win[None, :]
    m_w = ((d >= 0) & (d < WINDOW) & (s_win[None, :] >= 0))[None, :, None, None, :]
    s_w = jnp.einsum('bqkgd,bskd->bqkgs', q, kv_win[:, :, 0]).astype(jnp.float32) * scale
    p_w = jax.nn.softmax(jnp.where(m_w, s_w, NEG), axis=-1)
    o_w = jnp.einsum('bqkgs,bskd->bqkgd', p_w.astype(q.dtype), kv_win[:, :, 1])
    o = gates[:, :, 0, ..., None] * o_c + gates[:, :, 1, ..., None] * o_s + gates[:, :, 2, ..., None] * o_w
    return o.reshape(B, Tq, N_HEADS * HEAD_DIM)


def nsa_prompt(h, w_in, pe, w1k, w2k, w1v, w2v, w_out):
    B, T = h.shape[:2]
    q, kv_c, kv_s, kv_w, gates = nsa_project(h, w_in)
    k_c, v_c = compress(kv_c, pe, w1k, w2k, w1v, w2v)
    sel_blocks = kv_s.reshape(B, T // L_SEL, L_SEL, 2, N_KV_HEADS, HEAD_DIM)
    win_pad = jnp.pad(kv_w, ((0, 0), (WINDOW, 0), (0, 0), (0, 0), (0, 0)))
    b_idx = jnp.arange(B)[:, None, None, None]
    kv_idx = jnp.arange(N_KV_HEADS)[None, None, :, None]

    def gather_sel(idx):
        blk = sel_blocks[b_idx, idx, :, :, kv_idx, :]
        return blk[..., 0, :], blk[..., 1, :]

    def chunk(c):
        start = c * Q_CHUNK
        qc = lax.dynamic_slice_in_dim(q, start, Q_CHUNK, axis=1)
        gc = lax.dynamic_slice_in_dim(gates, start, Q_CHUNK, axis=1)
        t_pos = start + jnp.arange(Q_CHUNK)
        kvw = lax.dynamic_slice_in_dim(win_pad, start, WINDOW + Q_CHUNK, axis=1)
        s_win = start - WINDOW + jnp.arange(WINDOW + Q_CHUNK)
        return nsa_attend(qc, t_pos, gc, k_c, v_c, gather_sel, kvw, s_win)

    o = lax.map(chunk, jnp.arange(T // Q_CHUNK))
    o = o.transpose(1, 0, 2, 3).reshape(B, T, N_HEADS * HEAD_DIM)
    return o @ w_out, kv_c, kv_s, kv_w[:, -min(WINDOW, T):]


def nsa_sample(h, cache_cmp, cache_sel, cache_win, page_table, w_in, pe, w1k, w2k, w1v, w2v, w_out):
    B, T = h.shape[:2]
    q, kv_c, kv_s, kv_w, gates = nsa_project(h, w_in)
    n_pages = page_table.shape[1]
    past = n_pages * PAGE_SIZE
    t_pos = past + jnp.arange(T)
    n_new_blk = -(-T // L_SEL)
    pad = ((0, 0), (0, n_new_blk * L_SEL - T), (0, 0), (0, 0), (0, 0))
    past_c = cache_cmp[page_table].reshape(B, past, 2, N_KV_HEADS, HEAD_DIM)
    rows_c = jnp.concatenate([past_c, jnp.pad(kv_c, pad).astype(past_c.dtype)], axis=1)
    k_c, v_c = compress(rows_c, pe, w1k, w2k, w1v, w2v)
    spp = PAGE_SIZE // L_SEL
    pool_blocks = cache_sel.reshape(-1, L_SEL, 2, N_KV_HEADS, HEAD_DIM)
    new_blocks = jnp.pad(kv_s, pad).reshape(B, n_new_blk, L_SEL, 2, N_KV_HEADS, HEAD_DIM)
    n_past_blk = past // L_SEL
    b_idx = jnp.arange(B)[:, None, None, None]
    kv_idx = jnp.arange(N_KV_HEADS)[None, None, :, None]

    def gather_sel(idx):
        from_past = (idx < n_past_blk)[..., None, None, None]
        ip = jnp.minimum(idx, n_past_blk - 1)
        phys = page_table[b_idx, ip // spp] * spp + ip % spp
        blk_p = pool_blocks[phys, :, :, kv_idx, :]
        jn = jnp.clip(idx - n_past_blk, 0, n_new_blk - 1)
        blk_n = new_blocks[b_idx, jn, :, :, kv_idx, :]
        blk = jnp.where(from_past, blk_p, blk_n.astype(blk_p.dtype))
        return blk[..., 0, :], blk[..., 1, :]

    wb = cache_win.shape[1]
    kvw = jnp.concatenate([cache_win.astype(kv_w.dtype), kv_w], axis=1)
    s_win = past - wb + jnp.arange(wb + T)
    o = nsa_attend(q, t_pos, gates, k_c, v_c, gather_sel, kvw, s_win)
    return o @ w_out, kv_c, kv_s, kvw[:, -wb:]


def setup_inputs(seed: int = 0) -> dict:
    key = jax.random.key(seed)
    ks = iter(jax.random.split(key, 32))
    n_pages = PAST_LEN // PAGE_SIZE
    n_used = DEC_BATCH * n_pages
    n_phys = (n_used * 5) // 4
    wb = min(WINDOW, PAST_LEN)

    def nrm(shape, scale):
        return jax.random.normal(next(ks), shape, jnp.float32) * scale

    def gain(shape):
        return 1.0 + nrm(shape, 0.02)

    page_table = jax.random.permutation(next(ks), n_phys)[:n_used].astype(jnp.int32).reshape(DEC_BATCH, n_pages)
    return {
        "x_prompt": nrm((BATCH, SEQ, D_MODEL), 1.0),
        "x_sample": nrm((DEC_BATCH, DEC_SEQ, D_MODEL), 1.0),
        "state_conv": nrm((N_CONV_LAYERS, DEC_BATCH, CONV_W - 1, D_MODEL), 1.0),
        "cache_cmp_kv": nrm((N_NSA_LAYERS, n_phys, PAGE_SIZE, 2, N_KV_HEADS, HEAD_DIM), 1.0),
        "cache_sel_kv": nrm((N_NSA_LAYERS, n_phys, PAGE_SIZE, 2, N_KV_HEADS, HEAD_DIM), 1.0),
        "cache_win_kv": nrm((N_NSA_LAYERS, DEC_BATCH, wb, 2, N_KV_HEADS, HEAD_DIM), 1.0),
        "page_table": page_table,
        "norm_ffa": gain((DEPTH, D_MODEL)),
        "w_ffa_gu": nrm((DEPTH, D_MODEL, 2 * D_FF), D_MODEL ** -0.5),
        "w_ffa_down": nrm((DEPTH, D_FF, D_MODEL), D_FF ** -0.5),
        "norm_mix": gain((DEPTH, D_MODEL)),
        "norm_ffb": gain((DEPTH, D_MODEL)),
        "w_ffb_gu": nrm((DEPTH, D_MODEL, 2 * D_FF), D_MODEL ** -0.5),
        "w_ffb_down": nrm((DEPTH, D_FF, D_MODEL), D_FF ** -0.5),
        "w_conv_in": nrm((N_CONV_LAYERS, D_MODEL, 3 * D_MODEL), D_MODEL ** -0.5),
        "w_conv": nrm((N_CONV_LAYERS, CONV_W, D_MODEL), CONV_W ** -0.5),
        "w_conv_out": nrm((N_CONV_LAYERS, D_MODEL, D_MODEL), D_MODEL ** -0.5),
        "w_nsa_in": nrm((N_NSA_LAYERS, D_MODEL, NSA_IN), D_MODEL ** -0.5),
        "pe_cmp": nrm((N_NSA_LAYERS, L_CMP, HEAD_DIM), 0.1),
        "w_cmp_k1": nrm((N_NSA_LAYERS, L_CMP, HEAD_DIM, HEAD_DIM), (L_CMP * HEAD_DIM) ** -0.5),
        "w_cmp_k2": nrm((N_NSA_LAYERS, HEAD_DIM, HEAD_DIM), HEAD_DIM ** -0.5),
        "w_cmp_v1": nrm((N_NSA_LAYERS, L_CMP, HEAD_DIM, HEAD_DIM), (L_CMP * HEAD_DIM) ** -0.5),
        "w_cmp_v2": nrm((N_NSA_LAYERS, HEAD_DIM, HEAD_DIM), HEAD_DIM ** -0.5),
        "w_nsa_out": nrm((N_NSA_LAYERS, N_HEADS * HEAD_DIM, D_MODEL), (N_HEADS * HEAD_DIM) ** -0.5),
        "norm_final": gain((D_MODEL,)),
    }


def reference(x_prompt, x_sample, state_conv, cache_cmp_kv, cache_sel_kv, cache_win_kv, page_table,
              norm_ffa, w_ffa_gu, w_ffa_down, norm_mix, norm_ffb, w_ffb_gu, w_ffb_down,
              w_conv_in, w_conv, w_conv_out, w_nsa_in, pe_cmp, w_cmp_k1, w_cmp_k2, w_cmp_v1, w_cmp_v2,
              w_nsa_out, norm_final):
    xp, xs = x_prompt, x_sample
    conv_p, conv_s, cmp_p, cmp_s, sel_p, sel_s, win_p, win_s = [], [], [], [], [], [], [], []
    for i in range(DEPTH):
        xp = xp + 0.5 * swiglu(rmsnorm(xp, norm_ffa[i]), w_ffa_gu[i], w_ffa_down[i])
        xs = xs + 0.5 * swiglu(rmsnorm(xs, norm_ffa[i]), w_ffa_gu[i], w_ffa_down[i])
        hp = rmsnorm(xp, norm_mix[i])
        hs = rmsnorm(xs, norm_mix[i])
        j = i // N_MIXERS
        if i % N_MIXERS == 0:
            zero_state = jnp.zeros((hp.shape[0], CONV_W - 1, D_MODEL), hp.dtype)
            yp, st_p = short_conv_mixer(hp, zero_state, w_conv_in[j], w_conv[j], w_conv_out[j])
            ys, st_s = short_conv_mixer(hs, state_conv[j], w_conv_in[j], w_conv[j], w_conv_out[j])
            conv_p.append(st_p)
            conv_s.append(st_s)
        else:
            yp, c_p, s_p, w_p = nsa_prompt(hp, w_nsa_in[j], pe_cmp[j], w_cmp_k1[j], w_cmp_k2[j],
                                           w_cmp_v1[j], w_cmp_v2[j], w_nsa_out[j])
            ys, c_s, s_s, w_s = nsa_sample(hs, cache_cmp_kv[j], cache_sel_kv[j], cache_win_kv[j], page_table,
                                           w_nsa_in[j], pe_cmp[j], w_cmp_k1[j], w_cmp_k2[j],
                                           w_cmp_v1[j], w_cmp_v2[j], w_nsa_out[j])
            cmp_p.append(c_p)
            cmp_s.append(c_s)
            sel_p.append(s_p)
            sel_s.append(s_s)
            win_p.append(w_p)
            win_s.append(w_s)
        xp = xp + yp
        xs = xs + ys
        xp = xp + 0.5 * swiglu(rmsnorm(xp, norm_ffb[i]), w_ffb_gu[i], w_ffb_down[i])
        xs = xs + 0.5 * swiglu(rmsnorm(xs, norm_ffb[i]), w_ffb_gu[i], w_ffb_down[i])
    y_prompt = rmsnorm(xp, norm_final)
    y_sample = rmsnorm(xs, norm_final)
    return (y_prompt, y_sample,
            jnp.stack(conv_p), jnp.stack(conv_s),
            jnp.stack(cmp_p), jnp.stack(cmp_s),
            jnp.stack(sel_p), jnp.stack(sel_s),
            jnp.stack(win_p), jnp.stack(win_s))
```

```python
import numpy as np
from contextlib import ExitStack
import concourse.bass as bass
import concourse.mybir as mybir
from concourse.bass_utils import run_bass_kernel_spmd

F32 = mybir.dt.float32
BF16 = mybir.dt.bfloat16
I32 = mybir.dt.int32
AF = mybir.ActivationFunctionType
ALU = mybir.AluOpType
AX = mybir.AxisListType

D = 1024
FF = 2816
NCORE = 8
SEQ = 2048
NSEQ = 2
NB = 4
DT = 4
T = 512
NSA_IN = 2608
NPAGE = 128
WB = 512
EPS = 1e-6
NEGB = -30000.0


class Sched:
    ENG = ('pe', 'act', 'dve', 'pool', 'sp')

    def __init__(self, nc, es):
        self.nc = nc
        self.es = es
        self.prog = {e: [] for e in self.ENG}
        self.tl_sem = {e: es.enter_context(nc.semaphore('tl_' + e)) for e in self.ENG if e != 'sp'}
        self.count = {e: 0 for e in self.tl_sem}
        self.known = {e: {} for e in self.ENG}
        self.bufs = {}
        self.dsem = {}
        self.dcount = {}
        self.pe_strict = False
        import os as _os
        self.same_eng = _os.environ.get('K_SAME_ENG', '1') == '1'
        self.dma_rr = _os.environ.get('K_DMA_RR', '1') == '1'
        self.drr = {}

    DMA_K = {'cast': 8, 'st': 8, 'const': 4, 'cw': 2, 'ldx': 2, 'pg': 8}

    def _dma_sem(self, name):
        if name not in self.drr:
            self.drr[name] = 0
        k = (self.drr[name] % self.DMA_K.get(name, 1)) if self.dma_rr else 0
        self.drr[name] += 1
        sub = f"{name}_{k}"
        if sub not in self.dsem:
            self.dsem[sub] = self.es.enter_context(self.nc.semaphore('d_' + sub))
            self.dcount[sub] = 0
        return sub

    def _need(self, eng, src, val, waits):
        if self.known[eng].get(src, 0) >= val:
            return
        if waits.get(src, 0) < val:
            waits[src] = val

    def op(self, eng, fn, reads=(), writes=(), inc=True, dma=None):
        waits = {}
        for k in reads:
            b = self.bufs.get(k)
            if b and b['w']:
                self._need(eng, b['w'][0], b['w'][1], waits)
        for k in writes:
            b = self.bufs.get(k)
            if b:
                if b['w']:
                    self._need(eng, b['w'][0], b['w'][1], waits)
                for src, val in b['r'].items():
                    self._need(eng, src, val, waits)
        me_src = ('tl', eng)
        if not self.same_eng and me_src in waits:
            rv = 0
            for k in reads:
                b = self.bufs.get(k)
                if b and b['w'] and b['w'][0] == me_src:
                    rv = max(rv, b['w'][1])
            if rv > self.known[eng].get(me_src, 0) and rv <= self.count.get(eng, 0):
                waits[me_src] = rv
            else:
                del waits[me_src]
        if me_src in waits and (waits[me_src] > self.count.get(eng, 0) or eng == 'pe'):
            if eng == 'pe' and waits[me_src] <= self.count['pe'] and self.pe_strict:
                pass
            else:
                del waits[me_src]
        for src, val in waits.items():
            self.known[eng][src] = val
        if dma is not None:
            sub = self._dma_sem(dma)
            if self.dma_rr and self.dcount[sub] > 0 and self.known[eng].get(('d', sub), 0) < self.dcount[sub]:
                waits[('d', sub)] = self.dcount[sub]
                self.known[eng][('d', sub)] = self.dcount[sub]
            sem = self.dsem[sub]
            self.dcount[sub] += 16
            me = (('d', sub), self.dcount[sub])
        else:
            if inc:
                self.count[eng] += 1
                me = (('tl', eng), self.count[eng])
            else:
                me = (('tl', eng), self.count[eng] + 1)
            sem = self.tl_sem[eng]
        for k in reads:
            b = self.bufs.setdefault(k, {'w': None, 'r': {}})
            if b['r'].get(me[0], 0) < me[1]:
                b['r'][me[0]] = me[1]
        for k in writes:
            self.bufs[k] = {'w': me, 'r': {}}
        wl = [((self.tl_sem[s[1]] if s[0] == 'tl' else self.dsem[s[1]]), v) for s, v in waits.items()]
        self.prog[eng].append((wl, fn, sem if (inc or dma is not None) else None, 16 if dma is not None else 1))

    def barrier(self, engines, keys):
        for eng in engines:
            waits = {}
            for k in keys:
                b = self.bufs.get(k)
                if not b:
                    continue
                if b['w']:
                    self._need(eng, b['w'][0], b['w'][1], waits)
                for src, val in b['r'].items():
                    self._need(eng, src, val, waits)
            me_src = ('tl', eng)
            if me_src in waits and waits[me_src] > self.count.get(eng, 0):
                del waits[me_src]
            for src, val in waits.items():
                self.known[eng][src] = val
            wl = [((self.tl_sem[s[1]] if s[0] == 'tl' else self.dsem[s[1]]), v) for s, v in waits.items()]
            if wl:
                self.prog[eng].append((wl, None, None, 0))

    def finish(self):
        wl = [(self.dsem[n], self.dcount[n]) for n in self.dsem]
        self.prog['sp'].append((wl, None, None, 0))

    def emit(self):
        nc = self.nc
        emap = {'pe': 'tensor', 'act': 'scalar', 'dve': 'vector', 'pool': 'gpsimd', 'sp': 'sync'}
        with nc.Block() as block:
            for e in self.ENG:
                prog = self.prog[e]

                def body(engobj, prog=prog):
                    for wl, fn, sem, incv in prog:
                        for s, v in wl:
                            engobj.wait_ge(s, v)
                        if fn is not None:
                            ins = fn(engobj)
                            if sem is not None:
                                ins.then_inc(sem, incv)
                getattr(block, emap[e])(body)


def build_nc(nseq=NSEQ, seqlen=SEQ, with_sample=True, nphys=5120, dbg_stage=9):
    nc = bass.Bass("TRN2", target_bir_lowering=False)
    es = ExitStack()

    def din(name, shape, dt=F32):
        return nc.dram_tensor(name, list(shape), dt, kind="ExternalInput").ap()

    def dout(name, shape, dt=F32):
        return nc.dram_tensor(name, list(shape), dt, kind="ExternalOutput").ap()

    def dscr(name, shape, dt=BF16):
        return nc.dram_tensor(name, list(shape), dt, kind="Internal").ap()

    xp = din("xp", [nseq * seqlen, D])
    xs = din("xs", [NB * DT, D])
    stc = din("stc", [NB, 2, D])
    cache_cmp = din("cache_cmp", [nphys * 128, 512])
    cache_sel = din("cache_sel", [nphys * 128, 512])
    cache_win = din("cache_win", [NB, WB, 512])
    ptab = din("ptab", [NB, NPAGE], I32)
    norm_ffa = din("norm_ffa", [2, D]); norm_mix = din("norm_mix", [2, D]); norm_ffb = din("norm_ffb", [2, D])
    w_ffa_gu = din("w_ffa_gu", [2, D, 2 * FF]); w_ffa_down = din("w_ffa_down", [2, FF, D])
    w_ffb_gu = din("w_ffb_gu", [2, D, 2 * FF]); w_ffb_down = din("w_ffb_down", [2, FF, D])
    w_conv_in = din("w_conv_in", [D, 3 * D]); w_conv = din("w_conv", [3, D]); w_conv_out = din("w_conv_out", [D, D])
    w_nsa_in = din("w_nsa_in", [D, NSA_IN]); pe_cmp = din("pe_cmp", [32, 64])
    w_k1 = din("w_k1", [32, 64, 64]); w_k2 = din("w_k2", [64, 64]); w_v1 = din("w_v1", [32, 64, 64]); w_v2 = din("w_v2", [64, 64])
    w_nsa_out = din("w_nsa_out", [D, D]); norm_final = din("norm_final", [D])

    yp = dout("yp", [nseq * seqlen, D]); ys = dout("ys", [NB * DT, D])
    csp = dout("csp", [nseq, 2, D]); css = dout("css", [NB, 2, D])
    cmp_p = dout("cmp_p", [nseq * seqlen, 512]); cmp_s = dout("cmp_s", [NB * DT, 512])
    sel_p = dout("sel_p", [nseq * seqlen, 512]); sel_s = dout("sel_s", [NB * DT, 512])
    win_p = dout("win_p", [nseq, WB, 512]); win_s = dout("win_s", [NB, WB, 512])

    s_gu = {(l, ab): dscr(f"s_gu{l}{ab}", [22, 128, 8, 2, 128]) for l in range(2) for ab in 'ab'}
    s_dn = {(l, ab): dscr(f"s_dn{l}{ab}", [2, 128, 22, 512]) for l in range(2) for ab in 'ab'}
    s_cin = dscr("s_cin", [8, 128, 8, 3, 128])
    s_cout = dscr("s_cout", [128, 8, D])
    s_nq = dscr("s_nq", [4, 128, 8, 256])
    s_nk = dscr("s_nk", [6, 128, 8, 256])
    s_nkv = dscr("s_nkv", [128, 8, 1536])
    s_ng = dscr("s_ng", [128, 8, 48])
    s_nout = dscr("s_nout", [128, 8, D])

    with es:
        S = Sched(nc, es)

        def sb(name, shape, dt):
            return es.enter_context(nc.sbuf_tensor(name, list(shape), dt))

        def ps(name, shape, dt):
            return es.enter_context(nc.psum_tensor(name, list(shape), dt))

        X = sb("X", [128, 4, D], F32)
        hb = sb("hb", [128, D], BF16)
        junk = sb("junk", [128, D], BF16)
        hT = sb("hT", [128, 8, T], BF16)
        ARENA = 29 * 1024
        arena = sb("arena", [128, ARENA], mybir.dt.uint8)

        def aview(off, shape, dt, parts=128):
            nbytes = int(np.prod(shape)) * (2 if dt == BF16 else 4)
            assert off % 4 == 0 and off + nbytes <= ARENA, (off, nbytes, shape)
            v = arena[0:parts, off:off + nbytes].bitcast(dt)
            if len(shape) == 1:
                return v
            names = " ".join(f"a{i}" for i in range(len(shape)))
            kw = {f"a{i}": shape[i] for i in range(1, len(shape))}
            return v.rearrange(f"p ({names}) -> p {names}", **kw)

        KB = 1024
        import os as _os3
        NOARENA = _os3.environ.get('K_ARENA', '1') == '0'
        if NOARENA:
            aT = sb("aT", [128, 22, T], BF16)
            U = sb("U", [128, 8, T + 2], F32)
            zT = sb("zT", [128, 8, T], BF16)
            cgs = sb("cgs", [128, T], F32)
            cacc = sb("cacc", [128, T], F32)
        else:
            aT = aview(0, [22, T], BF16)
            U = aview(0, [8, T + 2], F32)
            zT = aview(17 * KB, [8, T], BF16)
            cgs = aview(25 * KB, [T], F32)
            cacc = aview(27 * KB, [T], F32)
        Qxb = [aview(0, [4, T], BF16), aview(25 * KB, [4, T], BF16)]
        kcT = aview(4 * KB, [2, T], BF16)
        Ob = aview(6 * KB, [4, D], BF16)
        PTs = [aview(14 * KB + k * KB, [T], BF16) for k in range(3)]
        EcT = [aview(17 * KB + k * KB, [T], BF16) for k in range(2)]
        Ec = aview(19 * KB, [4, 64], F32)
        imp4 = aview(20 * KB, [64], F32)
        imps = aview(20 * KB + 256, [32], F32)
        score = aview(20 * KB + 384, [32], F32)
        work = aview(20 * KB + 512, [32], F32)
        selm = aview(20 * KB + 640, [32], F32)
        m1 = aview(20 * KB + 768, [8], F32)
        m2 = aview(20 * KB + 800, [8], F32)
        sums = aview(20 * KB + 832, [4], F32)
        rsf = aview(20 * KB + 848, [3, 4], F32)
        hx = aview(21 * KB, [16], F32)
        hx2 = aview(21 * KB + 64, [16], F32)
        hx3 = aview(21 * KB + 128, [16], F32)
        hid = aview(21 * KB + 192, [16], BF16)
        tmpA = aview(22 * KB, [4, 64], F32)
        tmpB = aview(23 * KB, [4, 64], F32)
        tmpC = aview(24 * KB, [4, 64], F32)
        ARENA2 = 53632
        arena2 = sb("arena2", [128, ARENA2], mybir.dt.uint8)

        def aview2(off, shape, dt, parts=128):
            nbytes = int(np.prod(shape)) * (2 if dt == BF16 else 4)
            assert off % 4 == 0 and off + nbytes <= ARENA2, (off, nbytes)
            v = arena2[0:parts, off:off + nbytes].bitcast(dt)
            names = " ".join(f"a{i}" for i in range(len(shape)))
            kw = {f"a{i}": shape[i] for i in range(1, len(shape))}
            return v.rearrange(f"p ({names}) -> p {names}", **kw) if len(shape) > 1 else v

        KsT = aview2(0, [4, SEQ], BF16)
        KwT = aview2(16384, [4, 2 * T], BF16)
        Vs = aview2(24576, [16, 4, 66], BF16)
        Vw = aview2(33024, [8, 4, 66], BF16)
        Mc = aview2(37248, [16, 64], F32)
        McT = aview2(41344, [4, T], BF16, parts=64)
        E32 = aview2(45440, [SEQ], BF16, parts=32)
        Acst = aview2(49536, [16, 32], F32)
        Bcst = aview2(51584, [16, 32], F32)
        KcT = sb("KcT", [64, 4, 64], BF16)
        hidV = sb("hidV", [64, 4, 64], BF16)
        Vc = sb("Vc", [64, 4, 66], BF16)
        CG = aview2(0, [8, 512], BF16)
        XT = aview2(8192, [4, 1024], BF16)
        W1bd = [aview2(16384 + 8192 * k, [32, 128], BF16) for k in range(2)]
        hidS = aview2(32768, [4, 512], BF16)
        CGw = aview2(36864, [4, 512], BF16)
        XTw = aview2(40960, [2, 512], BF16)
        mbAll = aview2(43008, [4, 256], F32, parts=16)
        OBall = aview2(47104, [4, 3, 64], F32, parts=16)
        ssumS = aview2(50176, [4, 33], F32, parts=16)
        ObS = aview2(50720, [D], BF16, parts=16)
        Ec16_g = aview2(52768, [192], F32)
        OTk = aview2(0, [3, 16, 64], F32, parts=16)
        KcS = aview(0, [4, 512], BF16, parts=64)
        VcS = aview(4096, [4, 4, 64], BF16)
        IDX = aview(6144, [NB, NPAGE], I32)
        PTF = aview(8192, [NB, NPAGE], F32)
        Ec16 = aview(10240, [512], F32, parts=16)
        SM = aview(12288, [512], F32, parts=16)
        Pb = aview(14336, [512], BF16, parts=16)
        PsT = aview(15360, [4, 16], BF16)
        mb16 = aview(15616, [256], F32, parts=16)
        imp16 = aview(16640, [512], F32, parts=16)
        imps16 = aview(18688, [256], F32, parts=16)
        work16 = aview(19712, [256], F32, parts=16)
        m1s = aview(20736, [8], F32, parts=16)
        m2s = aview(20768, [8], F32, parts=16)
        ssum = aview(20800, [48], F32, parts=16)
        rsS = aview(20992, [4], F32, parts=16)
        QB = aview(21056, [4, 16], BF16, parts=64)
        QBp = aview(21184, [4, 16], BF16)
        OB3 = aview(21312, [3, 64], F32, parts=16)
        Qs = aview(22080, [16, 16], BF16, parts=64)
        KsN = aview(22592, [4, 16], BF16, parts=64)
        KwN = aview(22720, [4, 16], BF16, parts=64)
        VnS = aview(22848, [NB, 4, 64], BF16, parts=4)
        VnW = aview(24896, [NB, 4, 64], BF16, parts=4)
        WM16 = aview(26944, [512], F32, parts=16)
        CM16 = aview(28992, [4], F32, parts=16)
        Gsum = sb("Gsum", [16, 16], BF16)
        Gsumf = sb("Gsumf", [16, 16], F32)
        W2sel = sb("W2sel", [128, 2, 2, 64], BF16)
        peT2 = sb("peT2", [128, 32], F32)
        iot_i = sb("iot_i", [128, 1], I32)
        iot_f = sb("iot_f", [128, 1], F32)
        CM4 = sb("CM4", [4, 4], BF16)
        WM4 = sb("WM4", [4, 512], BF16)
        pt_i = sb("pt_i", [128, NB * NPAGE], I32)
        G = sb("G", [128, 4, 48], F32)
        W1k = None if NOARENA else sb("W1k", [64, 32, 64], BF16); W1v = None if NOARENA else sb("W1v", [64, 32, 64], BF16)
        W2k = sb("W2k", [64, 64], BF16); W2v = sb("W2v", [64, 64], BF16)
        peT = sb("peT", [64, 32], F32)
        pe_nat = sb("pe_nat", [32, 64], F32)
        TriGE = sb("TriGE", [128, 128], BF16); TriLT = sb("TriLT", [128, 128], BF16)
        TriGEb = sb("TriGEb", [128, 128], BF16); TriLTb = sb("TriLTb", [128, 128], BF16)
        SH = sb("SH", [128, 128], BF16)
        MBt = sb("MBt", [128, 128], BF16)
        ucarS = sb("ucarS", [128, NB, 8, 2], F32)
        ringg = [sb(f"ringg{i}", [128, 8 * 384], BF16) for i in range(3)]
        ringb = [sb(f"ringb{i}", [128, 12288], BF16) for i in range(2)]
        ident = sb("ident", [128, 128], BF16)
        identf = sb("identf", [128, 128], F32)
        gT = sb("gT", [128, 6, 8], F32)
        gfin = sb("gfin", [128, D], F32)
        wcv = sb("wcv", [128, 3, 8], F32)
        ss = sb("ss", [128, 4], F32)
        rstd = sb("rstd", [128, 4], F32)
        ucar = sb("ucar", [128, 8, 2], F32)
        sgs = [sb(f"sg{i}", [128, T], F32) for i in range(2)]
        stage = [sb(f"stage{i}", [128, 512], F32) for i in range(2)]

        PT = ps("PT", [128, 1024], BF16)
        PB = [ps(f"PB{i}", [128, 512], F32) for i in range(7)]

        def cast(dst, src, key):
            S.op('pool', lambda e: e.dma_start(out=dst, in_=src), writes=[key], dma='cast')

        def cast_gu(l, ab):
            w = (w_ffa_gu if ab == 'a' else w_ffb_gu)[l]
            for f in range(22):
                for u in range(2):
                    cast(s_gu[(l, ab)][f, :, :, u, :],
                         w[:, u * FF + f * 128:u * FF + (f + 1) * 128].rearrange("(c p) j -> p c j", p=128), ('s_gu', l, ab, f, u))

        def cast_dn(l, ab):
            w = (w_ffa_down if ab == 'a' else w_ffb_down)[l]
            for h in range(2):
                cast(s_dn[(l, ab)][h], w[:, h * 512:(h + 1) * 512].rearrange("(f p) n -> p f n", p=128), ('s_dn', l, ab, h))

        cast_gu(0, 'a'); cast_dn(0, 'a')
        for s3 in range(3):
            for ee in range(8):
                cast(s_cin[ee, :, :, s3, :], w_conv_in[:, s3 * D + ee * 128:s3 * D + (ee + 1) * 128].rearrange("(c p) j -> p c j", p=128), 's_cin')
        cast(s_cout, w_conv_out.rearrange("(c p) n -> p c n", p=128), 's_cout')
        cast_gu(0, 'b'); cast_dn(0, 'b')
        cast_gu(1, 'a'); cast_dn(1, 'a')
        for m in range(4):
            cast(s_nq[m], w_nsa_in[:, m * 256:(m + 1) * 256].rearrange("(c p) j -> p c j", p=128), 's_nq')
        for m in range(6):
            cast(s_nk[m], w_nsa_in[:, 1024 + m * 256:1024 + (m + 1) * 256].rearrange("(c p) j -> p c j", p=128), 's_nk')
        cast(s_nkv, w_nsa_in[:, 1024:2560].rearrange("(c p) n -> p c n", p=128), 's_nkv')
        cast(s_ng, w_nsa_in[:, 2560:2608].rearrange("(c p) n -> p c n", p=128), 's_ng')
        cast(s_nout, w_nsa_out.rearrange("(c p) n -> p c n", p=128), 's_nout')
        cast_gu(1, 'b'); cast_dn(1, 'b')

        S.op('pool', lambda e: e.memset(identf[:], 0.0), writes=['identf'])
        S.op('pool', lambda e: e.affine_select(out=identf[:], in_=identf[:], pattern=[[-1, 128]], compare_op=ALU.not_equal,
                                               fill=1.0, base=0, channel_multiplier=1), reads=['identf'], writes=['identf'])
        S.op('dve', lambda e: e.tensor_copy(out=ident[:], in_=identf[:]), reads=['identf'], writes=['ident'])
        for k, nt in enumerate([norm_ffa, norm_mix, norm_ffb]):
            for l in range(2):
                S.op('sp', lambda e, k=k, l=l, nt=nt: e.dma_start(out=gT[:, 2 * k + l, :], in_=nt[l].rearrange("(c p) -> p c", p=128),
                                                                  allow_slow_non_contiguous=True), writes=['gT'], dma='const')
        S.op('sp', lambda e: e.dma_start(out=gfin[:], in_=norm_final.partition_broadcast(128)), writes=['gfin'], dma='const')
        for i3 in range(3):
            S.op('sp', lambda e, i3=i3: e.dma_start(out=wcv[:, i3, :], in_=w_conv[i3].rearrange("(c p) -> p c", p=128),
                                                    allow_slow_non_contiguous=True), writes=['wcv'], dma='const')

        import os as _os2
        KC = int(_os2.environ.get('K_CONST', '3'))
        if KC >= 1:
            def pool_op(fn, reads=(), writes=()):
                S.op('pool', fn, reads=reads, writes=writes)

            def aff(out_ap, pattern, base, cm, key, fill=0.0, op=ALU.is_ge):
                pool_op(lambda e: e.affine_select(out=out_ap, in_=out_ap, pattern=pattern, compare_op=op, fill=fill, base=base, channel_multiplier=cm),
                        reads=[key], writes=[key])

            pool_op(lambda e: e.memset(TriGE[:], 1.0), writes=['TriGE'])
            aff(TriGE[:], [[1, 128]], 0, -1, 'TriGE')
            pool_op(lambda e: e.memset(TriLT[:], 1.0), writes=['TriLT'])
            aff(TriLT[:], [[-1, 128]], -1, 1, 'TriLT')
            S.op('dve', lambda e: e.tensor_scalar(out=TriGEb[:], in0=TriGE[:], scalar1=-1.0, scalar2=-NEGB, op0=ALU.add, op1=ALU.mult), reads=['TriGE'], writes=['TriGEb'])
            S.op('dve', lambda e: e.tensor_scalar(out=TriLTb[:], in0=TriLT[:], scalar1=-1.0, scalar2=-NEGB, op0=ALU.add, op1=ALU.mult), reads=['TriLT'], writes=['TriLTb'])
            pool_op(lambda e: e.memset(SH[:], 0.0), writes=['SH'])
            aff(SH[:], [[-1, 128]], 64, 1, 'SH', fill=1.0, op=ALU.not_equal)
            pool_op(lambda e: e.memset(E32[:], 1.0), writes=['E32'])
            aff(E32[:], [[1, SEQ]], 0, -64, 'E32')
            aff(E32[:], [[-1, SEQ]], 63, 64, 'E32')
            pool_op(lambda e: e.memset(Mc[:], 1.0), writes=['Mc'])
            for pos in range(16):
                aff(Mc[:, pos, :], [[-32, 64]], 128 * pos - 31, 1, 'Mc')
            pool_op(lambda e: e.memset(McT[:], 1.0), writes=['McT'])
            for i4 in range(4):
                aff(McT[:, i4, :], [[1, T]], T * i4 - 31, -32, 'McT')
            pool_op(lambda e: e.memset(Acst[:], 0.0), writes=['Acst'])
            pool_op(lambda e: e.memset(Bcst[:], -1e4), writes=['Bcst'])
            for pos in range(16):
                for half in range(2):
                    cur = 2 * pos + half
                    pr = slice(64 * half, 64 * half + 64)
                    if cur - 1 > 1:
                        pool_op(lambda e, pr=pr, pos=pos, cur=cur: e.memset(Acst[pr, pos, 1:cur - 1], 1.0), reads=['Acst'], writes=['Acst'])
                        pool_op(lambda e, pr=pr, pos=pos, cur=cur: e.memset(Bcst[pr, pos, 1:cur - 1], 0.0), reads=['Bcst'], writes=['Bcst'])
                    pool_op(lambda e, pr=pr, pos=pos, cur=cur: e.memset(Bcst[pr, pos, max(cur - 1, 0):cur + 1], 1e4), reads=['Bcst'], writes=['Bcst'])
                    pool_op(lambda e, pr=pr, pos=pos: e.memset(Bcst[pr, pos, 0:1], 1e4), reads=['Bcst'], writes=['Bcst'])
            pool_op(lambda e: e.memset(MBt[:], 0.0), writes=['MBt'])
            pool_op(lambda e: e.memset(Vs[:], 1.0), writes=['Vs'])
            pool_op(lambda e: e.memset(Vw[:], 1.0), writes=['Vw'])
            pool_op(lambda e: e.memset(Vc[:], 1.0), writes=['Vc'])
            pool_op(lambda e: e.memset(KsT[:], 0.0), writes=['KsT'])
            pool_op(lambda e: e.memset(KwT[:], 0.0), writes=['KwT'])
            pool_op(lambda e: e.memset(KcT[:], 0.0), writes=['KcT'])
            pool_op(lambda e: e.memset(hidV[:], 0.0), writes=['hidV'])
        if KC >= 2:
            S.op('pool', lambda e: e.dma_start(out=W1k[:], in_=w_k1.rearrange("l d e -> d l e")), writes=['W1k'], dma='cw')
            S.op('pool', lambda e: e.dma_start(out=W1v[:], in_=w_v1.rearrange("l d e -> d l e")), writes=['W1v'], dma='cw')
            S.op('pool', lambda e: e.dma_start(out=W2k[:], in_=w_k2), writes=['W2k'], dma='cw')
            S.op('pool', lambda e: e.dma_start(out=W2v[:], in_=w_v2), writes=['W2v'], dma='cw')
            S.op('sp', lambda e: e.dma_start(out=pe_nat[:], in_=pe_cmp), writes=['pe_nat'], dma='const')
            S.op('pe', lambda e: e.transpose(out=PB[1][0:64, 0:32], in_=pe_nat[:, :], identity=identf[0:32, 0:32]), reads=['pe_nat', 'identf'], writes=[('PB', 1)])
            S.op('dve', lambda e: e.tensor_copy(out=peT[:, :], in_=PB[1][0:64, 0:32]), reads=[('PB', 1)], writes=['peT'])
        if KC >= 3:
            for kc4 in range(SEQ // 512):
                S.op('pe', lambda e, kc4=kc4: e.matmul(PB[0][:, :], lhsT=SH[0:32, :], rhs=E32[:, kc4 * 512:(kc4 + 1) * 512], start=True, stop=True),
                     reads=['SH', 'E32'], writes=[('PB', 0)])
                for kv in range(4):
                    S.op('dve', lambda e, kc4=kc4, kv=kv: e.tensor_copy(out=KsT[64:96, kv, kc4 * 512:(kc4 + 1) * 512], in_=PB[0][64:96, :]),
                         reads=[('PB', 0)], writes=['KsT', 'KsTE'])

        rg = {'i': 0}
        rb = {'i': 0}

        def load_g(src_ap, ncols, skey):
            slot = rg['i'] % 3
            rg['i'] += 1
            dst = ringg[slot][:, 0:8 * ncols]
            S.op('sp', lambda e: e.dma_start(out=dst, in_=src_ap), reads=(skey if isinstance(skey, list) else [skey]), writes=[('rg', slot)], dma=f'rg{slot}')
            return ringg[slot][:, 0:8 * ncols].rearrange("p (c n) -> p c n", c=8), ('rg', slot)

        def load_b(src_ap, nelem, skey):
            slot = rb['i'] % 2
            rb['i'] += 1
            dst = ringb[slot][:, 0:nelem]
            S.op('sp', lambda e: e.dma_start(out=dst, in_=src_ap), reads=(skey if isinstance(skey, list) else [skey]), writes=[('rb', slot)], dma=f'rb{slot}')
            return ringb[slot], ('rb', slot)

        class Stream:
            def __init__(self, specs, ahead):
                self.specs = specs
                self.loaded = []
                self.ahead = ahead

            def get(self, i):
                while len(self.loaded) < min(len(self.specs), i + 1 + self.ahead):
                    self.loaded.append(self.specs[len(self.loaded)]())
                return self.loaded[i]

        bank_rr = {'i': 0}

        def bank(n=7):
            b = bank_rr['i'] % n
            bank_rr['i'] += 1
            return PB[b], ('PB', b)

        def norm_hT(nsub, npart, gi):
            for j in range(nsub):
                S.op('act', lambda e, j=j: e.activation(out=junk[:npart, :], in_=X[:npart, j, :], func=AF.Square, scale=1.0 / 32.0,
                                                         accum_out=ss[:npart, j:j + 1]), reads=['X'], writes=['junk', 'ss'])
            S.op('dve', lambda e: e.tensor_scalar(out=rstd[:npart, :nsub], in0=ss[:npart, :nsub], scalar1=EPS, scalar2=None, op0=ALU.add),
                 reads=['ss'], writes=['rstd'])
            S.op('act', lambda e: e.activation(out=rstd[:npart, :nsub], in_=rstd[:npart, :nsub], func=AF.Sqrt), reads=['rstd'], writes=['rstd'])
            S.op('dve', lambda e: e.reciprocal(out=rstd[:npart, :nsub], in_=rstd[:npart, :nsub]), reads=['rstd'], writes=['rstd'])
            for j in range(nsub):
                S.op('dve', lambda e, j=j: e.tensor_scalar(out=hb[:npart, :], in0=X[:npart, j, :], scalar1=rstd[:npart, j:j + 1], scalar2=None,
                                                           op0=ALU.mult), reads=['X', 'rstd'], writes=['hb'])
                for c in range(8):
                    S.op('pe', lambda e, c=c: e.transpose(out=PT[:, c * 128:c * 128 + npart], in_=hb[:npart, c * 128:(c + 1) * 128],
                                                          identity=ident[:npart, :npart]), reads=['hb', 'ident'], writes=['PT'], inc=(c == 7))
                S.op('dve', lambda e, j=j: e.tensor_tensor(out=hT[:, :, j * npart:(j + 1) * npart],
                                                           in0=PT[:, :].rearrange("p (c n) -> p c n", c=8)[:, :, 0:npart],
                                                           in1=gT[:, gi, :].unsqueeze(2).to_broadcast([128, 8, npart]), op=ALU.mult),
                     reads=['PT', 'gT'], writes=['hT'])

        def out_proj_add(srcT, wfun, nsub, npart, scale):
            nchunk = srcT.shape[1]
            for h in range(2):
                wap, wkey = wfun(h)
                for j in range(nsub):
                    pb, pk = bank()
                    for f in range(nchunk):
                        S.op('pe', lambda e, f=f, j=j, pb=pb, wap=wap: e.matmul(pb[:npart, :], lhsT=srcT[:, f, j * npart:(j + 1) * npart], rhs=wap[:, f, :],
                                                                             start=(f == 0), stop=(f == nchunk - 1)),
                             reads=[wkey, 'srcT'], writes=[pk], inc=(f == nchunk - 1))
                    S.op('dve', lambda e, j=j, h=h, pb=pb: e.scalar_tensor_tensor(out=X[:npart, j, h * 512:(h + 1) * 512], in0=pb[:npart, :], scalar=scale,
                                                                                  in1=X[:npart, j, h * 512:(h + 1) * 512], op0=ALU.mult, op1=ALU.add),
                         reads=[pk, 'X'], writes=['X'])

        def ffn(l, ab, nsub, npart):
            TT = nsub * npart
            gi = (0 if ab == 'a' else 4) + l
            norm_hT(nsub, npart, gi)
            sg_key = (l, ab)
            gus = Stream([(lambda f=f: load_g(s_gu[(l, ab)][f].rearrange("p c u j -> p (c u j)"), 256, [('s_gu', l, ab, f, 0), ('s_gu', l, ab, f, 1)])) for f in range(22)], 2)
            dns = Stream([(lambda h=h: load_b(s_dn[(l, ab)][h].rearrange("p f n -> p (f n)"), 22 * 512, [('s_dn', l, ab, h)])) for h in range(2)], 1)
            for f in range(22):
                wg, wk = gus.get(f)
                if f == 18:
                    dns.get(0)
                pg, pgk = bank()
                pu, puk = bank()
                for u, (pp, ppk) in enumerate([(pg, pgk), (pu, puk)]):
                    for c in range(8):
                        S.op('pe', lambda e, c=c, u=u, pp=pp, wg=wg: e.matmul(pp[:, :TT], lhsT=wg[:, c, u * 128:(u + 1) * 128], rhs=hT[:, c, :TT],
                                                                            start=(c == 0), stop=(c == 7)),
                             reads=[wk, 'hT'], writes=[ppk], inc=(c == 7))
                sg = sgs[f % 2]
                S.op('act', lambda e, pg=pg, sg=sg: e.activation(out=sg[:, :TT], in_=pg[:, :TT], func=AF.Silu), reads=[pgk], writes=[('sg', f % 2)])
                S.op('dve', lambda e, f=f, pu=pu, sg=sg: e.tensor_tensor(out=aT[:, f, :TT], in0=sg[:, :TT], in1=pu[:, :TT], op=ALU.mult),
                     reads=[('sg', f % 2), puk], writes=['srcT'])

            def wfun(h):
                ap, k = dns.get(h)
                return ap[:, 0:22 * 512].rearrange("p (f n) -> p f n", f=22), k
            out_proj_add(aT, wfun, nsub, npart, 0.5)

        def conv_layer(nsub, npart, first, sample, cs_out):
            TT = nsub * npart
            norm_hT(nsub, npart, 2)
            if first and not sample:
                S.op('pool', lambda e: e.memset(ucar[:], 0.0), writes=['ucar'])
            cins = Stream([(lambda ee=ee: load_g(s_cin[ee].rearrange("p c s j -> p (c s j)"), 384, 's_cin')) for ee in range(8)], 2)
            couts = Stream([lambda: load_b(s_cout.rearrange("p c n -> p (c n)"), 8 * D, 's_cout')], 0)
            for ee in range(8):
                w, wk = cins.get(ee)
                if ee == 5:
                    couts.get(0)
                w4 = w.rearrange("p c (s j) -> p c s j", s=3)
                pbs = []
                for s3 in range(3):
                    pb, pk = bank()
                    for c in range(8):
                        S.op('pe', lambda e, c=c, s3=s3, pb=pb, w4=w4: e.matmul(pb[:, :TT], lhsT=w4[:, c, s3, :], rhs=hT[:, c, :TT], start=(c == 0), stop=(c == 7)),
                             reads=[wk, 'hT'], writes=[pk], inc=(c == 7))
                    pbs.append((pb, pk))
                (pbg, kbg), (pcg, kcg), (pv, kv_) = pbs
                S.op('act', lambda e, pcg=pcg: e.activation(out=cgs[:, :TT], in_=pcg[:, :TT], func=AF.Copy), reads=[kcg], writes=['cgs'])
                if not sample:
                    S.op('dve', lambda e, ee=ee: e.tensor_copy(out=U[:, ee, 0:2], in_=ucar[:, ee, :]), reads=['ucar'], writes=['U'])
                    S.op('dve', lambda e, ee=ee, pv=pv: e.tensor_tensor(out=U[:, ee, 2:2 + TT], in0=cgs[:, :TT], in1=pv[:, :TT], op=ALU.mult),
                         reads=['cgs', kv_], writes=['U'])
                    segs = [(0, TT, 0)]
                else:
                    S.op('dve', lambda e, ee=ee: e.tensor_copy(out=U[:, ee, 0:6 * NB].rearrange("p (b k) -> p b k", k=6)[:, :, 0:2], in_=ucarS[:, :, ee, :]),
                         reads=['ucar'], writes=['U'])
                    S.op('dve', lambda e, ee=ee, pv=pv: e.tensor_tensor(out=U[:, ee, 0:6 * NB].rearrange("p (b k) -> p b k", k=6)[:, :, 2:6],
                                                                     in0=cgs[:, 0:TT].rearrange("p (b k) -> p b k", k=DT),
                                                                     in1=pv[:, 0:TT].rearrange("p (b k) -> p b k", k=DT), op=ALU.mult),
                         reads=['cgs', kv_], writes=['U'])
                    segs = [(6 * b, DT, DT * b) for b in range(NB)]
                for (u0, n, o0) in segs:
                    S.op('dve', lambda e, ee=ee, u0=u0, n=n, o0=o0: e.tensor_scalar(out=cacc[:, o0:o0 + n], in0=U[:, ee, u0 + 2:u0 + 2 + n], scalar1=wcv[:, 2, ee:ee + 1],
                                                                                    scalar2=None, op0=ALU.mult), reads=['U', 'wcv'], writes=['cacc'])
                    for i3 in (1, 0):
                        S.op('dve', lambda e, ee=ee, u0=u0, n=n, o0=o0, i3=i3: e.scalar_tensor_tensor(out=cacc[:, o0:o0 + n], in0=U[:, ee, u0 + i3:u0 + i3 + n],
                                                                                                       scalar=wcv[:, i3, ee:ee + 1], in1=cacc[:, o0:o0 + n],
                                                                                                       op0=ALU.mult, op1=ALU.add),
                             reads=['U', 'wcv', 'cacc'], writes=['cacc'])
                S.op('dve', lambda e, ee=ee, pbg=pbg: e.tensor_tensor(out=zT[:, ee, :TT], in0=cacc[:, :TT], in1=pbg[:, :TT], op=ALU.mult),
                     reads=['cacc', kbg], writes=['srcT'])
                if not sample:
                    S.op('pool', lambda e, ee=ee: e.tensor_copy(out=ucar[:, ee, :], in_=U[:, ee, TT:TT + 2]), reads=['U'], writes=['ucar'])
            if cs_out is not None:
                if not sample:
                    for t2 in range(2):
                        S.op('sp', lambda e, t2=t2: e.dma_start(out=cs_out[t2].rearrange("(c p) -> p c", p=128), in_=ucar[:, :, t2], allow_slow_non_contiguous=True),
                             reads=['ucar'], writes=['csout'], dma='st')
                else:
                    for b in range(NB):
                        for t2 in range(2):
                            S.op('sp', lambda e, b=b, t2=t2: e.dma_start(out=cs_out[b, t2].rearrange("(c p) -> p c", p=128), in_=U[:, :, 6 * b + 4 + t2],
                                                                         allow_slow_non_contiguous=True), reads=['U'], writes=['csout'], dma='st')

            def wfun(h):
                ap, k = couts.get(0)
                return ap[:, 0:8 * D].rearrange("p (c n) -> p c n", c=8)[:, :, h * 512:(h + 1) * 512], k
            out_proj_add(zT, wfun, nsub, npart, 1.0)

        def nsa_rows(nsub, npart, row0, outs, win_out, i=None):
            kvs = Stream([lambda: load_b(s_nkv.rearrange("p c n -> p (c n)"), 8 * 1536, 's_nkv')], 0)
            wap, wk = kvs.get(0)
            w3 = wap[:, 0:8 * 1536].rearrange("p (c n) -> p c n", c=8)
            for br in range(3):
                for j in range(nsub):
                    pb, pk = bank()
                    for c in range(8):
                        S.op('pe', lambda e, c=c, j=j, br=br, pb=pb: e.matmul(pb[:npart, :], lhsT=hT[:, c, j * npart:(j + 1) * npart], rhs=w3[:, c, br * 512:(br + 1) * 512],
                                                                           start=(c == 0), stop=(c == 7)), reads=[wk, 'hT'], writes=[pk], inc=(c == 7))
                    st = stage[j % 2]
                    S.op('act', lambda e, pb=pb, st=st: e.activation(out=st[:npart, :], in_=pb[:npart, :], func=AF.Copy), reads=[pk], writes=[('stage', j % 2)])
                    if i is not None and br == 1:
                        S.op('pool', lambda e, st=st, j=j: e.tensor_copy(out=Vs[:, 4 * i + j, :, 0:64], in_=st[:, 256:512].rearrange("p (k d) -> p k d", k=4)),
                             reads=[('stage', j % 2)], writes=['Vs'])
                    if i is not None and br == 2:
                        S.op('pool', lambda e, st=st, j=j: e.tensor_copy(out=Vw[:, (i % 2) * 4 + j, :, 0:64], in_=st[:, 256:512].rearrange("p (k d) -> p k d", k=4)),
                             reads=[('stage', j % 2)], writes=['Vw'])
                    r0 = row0 + j * npart
                    if br < 2:
                        S.op('sp', lambda e, st=st, br=br, r0=r0: e.dma_start(out=outs[br][r0:r0 + npart, :], in_=st[:npart, :]),
                             reads=[('stage', j % 2)], writes=['kvout'], dma='st')
                    elif win_out is not None:
                        for (oap, iap) in win_out(j, st):
                            S.op('sp', lambda e, oap=oap, iap=iap: e.dma_start(out=oap, in_=iap), reads=[('stage', j % 2)], writes=['kvout'], dma='st')

        NSA_KEYS = [('Qx', 0), ('Qx', 1), 'kcT', 'Ob', ('PTs', 0), ('PTs', 1), ('PTs', 2), ('EcT', 0), ('EcT', 1), 'Ec', 'imp', 'hx', 't123', 'rsf']

        def gelu_tanh(dst_bf16, src_f32, n, dkey='hx'):
            S.op('dve', lambda e: e.tensor_tensor(out=hx2[0:64, :n], in0=src_f32, in1=src_f32, op=ALU.mult), reads=['hx'], writes=['hx'])
            S.op('dve', lambda e: e.tensor_scalar(out=hx2[0:64, :n], in0=hx2[0:64, :n], scalar1=0.044715, scalar2=1.0, op0=ALU.mult, op1=ALU.add), reads=['hx'], writes=['hx'])
            S.op('dve', lambda e: e.tensor_tensor(out=hx2[0:64, :n], in0=hx2[0:64, :n], in1=src_f32, op=ALU.mult), reads=['hx'], writes=['hx'])
            S.op('act', lambda e: e.activation(out=hx3[0:64, :n], in_=hx2[0:64, :n], func=AF.Sigmoid, scale=1.5957691216), reads=['hx'], writes=['hx'])
            S.op('dve', lambda e: e.tensor_tensor(out=dst_bf16, in0=hx3[0:64, :n], in1=src_f32, op=ALU.mult), reads=['hx'], writes=[dkey])

        def nsa_prompt(i, row0, outs, win_out):
            t0 = i * T
            nb = 16 * (i + 1)
            slot = i % 2
            pslot = (i - 1) % 2
            norm_hT(4, 128, 3)
            nsa_rows(4, 128, row0, outs, win_out, i=i)
            if dbg_stage < 1:
                return
            gw, gk = load_g(s_ng.rearrange("p c n -> p (c n)"), 48, 's_ng')
            for j in range(4):
                pb, pk = bank(4)
                for c in range(8):
                    S.op('pe', lambda e, c=c, j=j, pb=pb: e.matmul(pb[:, 0:48], lhsT=hT[:, c, j * 128:(j + 1) * 128], rhs=gw[:, c, :], start=(c == 0), stop=(c == 7)),
                         reads=[gk, 'hT'], writes=[pk], inc=(c == 7))
                S.op('act', lambda e, j=j, pb=pb: e.activation(out=G[:, j, :], in_=pb[:, 0:48], func=AF.Sigmoid), reads=[pk], writes=['G'])
            for (m, dstT, c0) in ((2, KsT, t0), (4, KwT, slot * T)):
                w, wk = load_g(s_nk[m].rearrange("p c j -> p (c j)"), 256, 's_nk')
                for kv in range(4):
                    pb, pk = bank(4)
                    for c in range(8):
                        S.op('pe', lambda e, c=c, kv=kv, pb=pb, w=w: e.matmul(pb[0:64, :], lhsT=w[:, c, kv * 64:(kv + 1) * 64], rhs=hT[:, c, :], start=(c == 0), stop=(c == 7)),
                             reads=[wk, 'hT'], writes=[pk], inc=(c == 7))
                    S.op('dve', lambda e, kv=kv, pb=pb, dstT=dstT, c0=c0: e.tensor_copy(out=dstT[0:64, kv, c0:c0 + T], in_=pb[0:64, :]),
                         reads=[pk], writes=['KsT' if m == 2 else 'KwT'])
            ACCC, ACCS, ACCW = (PB[4], ('PB', 4)), (PB[5], ('PB', 5)), (PB[6], ('PB', 6))
            def do_kv(kv):
                Qx = Qxb[kv % 2]
                qxk = ('Qx', kv % 2)
                if dbg_stage < 2:
                    return None
                wkc, wkck = load_g(s_nk[0].rearrange("p c j -> p (c j)"), 256, 's_nk')
                wvc, wvck = load_g(s_nk[1].rearrange("p c j -> p (c j)"), 256, 's_nk')
                for kk, (w, wk) in enumerate(((wkc, wkck), (wvc, wvck))):
                    pb, pk = bank(4)
                    for c in range(8):
                        S.op('pe', lambda e, c=c, pb=pb, w=w: e.matmul(pb[0:64, :], lhsT=w[:, c, kv * 64:(kv + 1) * 64], rhs=hT[:, c, :], start=(c == 0), stop=(c == 7)),
                             reads=[wk, 'hT'], writes=[pk], inc=(c == 7))
                    S.op('dve', lambda e, kk=kk, pb=pb: e.tensor_tensor(out=kcT[0:64, kk, :].rearrange("p (n l) -> p n l", l=32),
                                                                        in0=pb[0:64, :].rearrange("p (n l) -> p n l", l=32),
                                                                        in1=peT[:, :].unsqueeze(1).to_broadcast([64, 16, 32]), op=ALU.add),
                         reads=[pk, 'peT'], writes=['kcT'])
                for kk, W1 in enumerate((W1k, W1v)):
                    pb, pk = bank(4)
                    kc3 = kcT[0:64, kk, :].rearrange("p (n l) -> p n l", l=32)
                    for l in range(32):
                        S.op('pe', lambda e, l=l, pb=pb, W1=W1, kc3=kc3: e.matmul(pb[0:64, 0:16], lhsT=W1[:, l, :], rhs=kc3[:, :, l], start=(l == 0), stop=(l == 31)),
                             reads=['kcT', 'W1k', 'W1v'], writes=[pk], inc=(l == 31))
                    S.op('act', lambda e, kk=kk, pb=pb: e.activation(out=hx[0:64, :], in_=pb[0:64, 0:16], func=AF.Copy), reads=[pk], writes=['hx'])
                    if kk == 0:
                        gelu_tanh(hid[0:64, :], hx[0:64, :], 16)
                        pb2, pk2 = bank(4)
                        S.op('pe', lambda e, pb2=pb2: e.matmul(pb2[0:64, 0:16], lhsT=W2k[:, :], rhs=hid[0:64, :], start=True, stop=True), reads=['hx', 'W2k'], writes=[pk2])
                        S.op('dve', lambda e, pb2=pb2: e.tensor_copy(out=KcT[:, kv, 16 * i:16 * i + 16], in_=pb2[0:64, 0:16]), reads=[pk2], writes=['KcT'])
                    else:
                        gelu_tanh(hidV[:, kv, 16 * i:16 * i + 16], hx[0:64, :], 16, dkey='hidV')
                        pb2, pk2 = bank(4)
                        S.op('pe', lambda e, pb2=pb2: e.matmul(pb2[0:nb, 0:64], lhsT=hidV[:, kv, 0:nb], rhs=W2v[:, :], start=True, stop=True), reads=['hidV', 'W2v'], writes=[pk2])
                        S.op('dve', lambda e, pb2=pb2: e.tensor_copy(out=Vc[0:nb, kv, 0:64], in_=pb2[0:nb, 0:64]), reads=[pk2], writes=['Vc'])
                if dbg_stage < 3:
                    return None
                qw, qk = load_g(s_nq[kv].rearrange("p c j -> p (c j)"), 256, 's_nq')
                for g in range(4):
                    pb, pk = bank(4)
                    for c in range(8):
                        S.op('pe', lambda e, c=c, g=g, pb=pb: e.matmul(pb[0:64, :], lhsT=qw[:, c, g * 64:(g + 1) * 64], rhs=hT[:, c, :], start=(c == 0), stop=(c == 7)),
                             reads=[qk, 'hT'], writes=[pk], inc=(c == 7))
                    S.op('act', lambda e, g=g, pb=pb: e.activation(out=Qx[0:64, g, :], in_=pb[0:64, :], func=AF.Copy, scale=0.125), reads=[pk], writes=[qxk])
                for j in range(4 if dbg_stage >= 4 else 0):
                    pos = 4 * i + j
                    pb, pk = bank(4)
                    for g in range(4):
                        S.op('pe', lambda e, g=g, j=j, pb=pb: e.matmul(pb[:, g * 64:g * 64 + nb], lhsT=Qx[0:64, g, j * 128:(j + 1) * 128], rhs=KcT[:, kv, 0:nb], start=True, stop=True),
                             reads=[qxk, 'KcT'], writes=[pk], inc=(g == 3))
                    pb3 = pb[:, 0:256].rearrange("p (g n) -> p g n", g=4)
                    S.op('act', lambda e, pb3=pb3: e.activation(out=Ec[:, :, 0:nb], in_=pb3[:, :, 0:nb], func=AF.Exp), reads=[pk], writes=['Ec'])
                    S.op('dve', lambda e, pos=pos: e.tensor_tensor(out=Ec[:, :, 0:nb], in0=Ec[:, :, 0:nb], in1=Mc[:, pos, 0:nb].unsqueeze(1).to_broadcast([128, 4, nb]), op=ALU.mult),
                         reads=['Ec', 'Mc'], writes=['Ec'])
                    S.op('dve', lambda e: e.tensor_reduce(out=sums[:, :], in_=Ec[:, :, 0:nb], axis=AX.X, op=ALU.add), reads=['Ec'], writes=['imp'])
                    S.op('dve', lambda e: e.tensor_scalar(out=sums[:, :], in0=sums[:, :], scalar1=1e-30, scalar2=None, op0=ALU.max), reads=['imp'], writes=['imp'])
                    S.op('dve', lambda e: e.reciprocal(out=sums[:, :], in_=sums[:, :]), reads=['imp'], writes=['imp'])
                    S.op('dve', lambda e: e.tensor_tensor(out=Ec[:, :, 0:nb], in0=Ec[:, :, 0:nb], in1=sums[:, :].unsqueeze(2).to_broadcast([128, 4, nb]), op=ALU.mult),
                         reads=['Ec', 'imp'], writes=['Ec'])
                    S.op('dve', lambda e: e.tensor_reduce(out=imp4[:, 0:nb], in_=Ec[:, :, 0:nb].rearrange("p g n -> p n g"), axis=AX.X, op=ALU.add), reads=['Ec'], writes=['imp'])
                    S.op('dve', lambda e: e.memset(imps[:, :], 0.0), reads=['imp'], writes=['imp'])
                    i2 = imp4[:, 0:nb].rearrange("p (m two) -> p m two", two=2)
                    S.op('dve', lambda e, i2=i2: e.tensor_tensor(out=imps[:, 0:nb // 2], in0=i2[:, :, 0], in1=i2[:, :, 1], op=ALU.add), reads=['imp'], writes=['imp'])
                    S.op('dve', lambda e, pos=pos: e.tensor_tensor(out=score[:, :], in0=imps[:, :], in1=Acst[:, pos, :], op=ALU.mult), reads=['imp', 'Acst'], writes=['imp'])
                    S.op('dve', lambda e, pos=pos: e.tensor_tensor(out=score[:, :], in0=score[:, :], in1=Bcst[:, pos, :], op=ALU.add), reads=['imp', 'Bcst'], writes=['imp'])
                    S.op('dve', lambda e: e.max(out=m1[:, :], in_=score[:, :]), reads=['imp'], writes=['imp'])
                    S.op('dve', lambda e: e.match_replace(out=work[:, :], in_to_replace=m1[:, :], in_values=score[:, :], imm_value=-3e4), reads=['imp'], writes=['imp'])
                    S.op('dve', lambda e: e.max(out=m2[:, :], in_=work[:, :]), reads=['imp'], writes=['imp'])
                    S.op('dve', lambda e: e.tensor_scalar(out=selm[:, :], in0=score[:, :], scalar1=m2[:, 7:8], scalar2=None, op0=ALU.is_ge), reads=['imp'], writes=['imp'])
                    S.op('dve', lambda e: e.tensor_scalar(out=MBt[:, 64:96], in0=selm[:, :], scalar1=-1.0, scalar2=-NEGB, op0=ALU.add, op1=ALU.mult), reads=['imp'], writes=['MBt'])
                    S.op('pe', lambda e: e.transpose(out=PT[:, 0:128], in_=MBt[:, :], identity=ident[:, :]), reads=['MBt', 'ident'], writes=['PT'])
                    S.op('dve', lambda e, j=j: e.tensor_copy(out=Qx[64:96, :, j * 128:(j + 1) * 128], in_=PT[64:96, 0:128].unsqueeze(1).to_broadcast([32, 4, 128])),
                         reads=['PT'], writes=[qxk])
                def do_head(g):
                    h = kv * 4 + g
                    pb, pk = bank(4)
                    S.op('pe', lambda e, g=g, pb=pb: e.matmul(pb[0:nb, :], lhsT=KcT[:, kv, 0:nb], rhs=Qx[0:64, g, :], start=True, stop=True), reads=[qxk, 'KcT'], writes=[pk])
                    ec = EcT[g % 2]; eck = ('EcT', g % 2)
                    S.op('act', lambda e, pb=pb, ec=ec: e.activation(out=ec[0:nb, :], in_=pb[0:nb, :], func=AF.Exp), reads=[pk], writes=[eck])
                    S.op('dve', lambda e, ec=ec: e.tensor_tensor(out=ec[0:nb, :], in0=ec[0:nb, :], in1=McT[0:nb, i, :], op=ALU.mult), reads=[eck, 'McT'], writes=[eck])
                    for j in range(4):
                        S.op('pe', lambda e, j=j, ec=ec: e.matmul(ACCC[0][:, j * 128:j * 128 + 65], lhsT=ec[0:nb, j * 128:(j + 1) * 128], rhs=Vc[0:nb, kv, 0:65], start=True, stop=True),
                             reads=[eck, 'Vc'], writes=[ACCC[1]], inc=(j == 3))
                    rr = 0
                    for kc in range(4 * i + 4):
                        md = kc - 4 * i
                        c0 = 128 * md if md > 0 else 0
                        pb, pk = bank(4)
                        S.op('pe', lambda e, kc=kc, c0=c0, pb=pb, g=g, md=md: e.matmul(pb[:, c0:T], lhsT=KsT[0:96, kv, kc * 128:(kc + 1) * 128], rhs=Qx[0:96, g, c0:T], start=True, stop=(md < 0)),
                             reads=[qxk, 'KsT', 'KsTE'], writes=[pk], inc=(md < 0))
                        if md >= 0:
                            S.op('pe', lambda e, c0=c0, pb=pb: e.matmul(pb[:, c0:c0 + 128], lhsT=ident[:, :], rhs=TriGEb[:, :], start=False, stop=True),
                                 reads=['ident', 'TriGEb'], writes=[pk])
                        ptk = rr % 3; rr += 1
                        pt = PTs[ptk]
                        S.op('act', lambda e, pb=pb, pt=pt, c0=c0: e.activation(out=pt[:, c0:T], in_=pb[:, c0:T], func=AF.Exp), reads=[pk], writes=[('PTs', ptk)])
                        for j in range(max(md, 0), 4):
                            S.op('pe', lambda e, j=j, kc=kc, pt=pt: e.matmul(ACCS[0][:, j * 128:j * 128 + 65], lhsT=pt[:, j * 128:(j + 1) * 128], rhs=Vs[:, kc, kv, 0:65],
                                                                            start=(kc == 0 and j == 0), stop=(kc == 4 * i + 3 and j == 3)),
                                 reads=[('PTs', ptk), 'Vs'], writes=[ACCS[1]], inc=(j == 3))
                    for m in range(0 if i > 0 else 4, 8):
                        if m < 4:
                            kcol = pslot * T + m * 128; vch = pslot * 4 + m
                            ca, cb = 0, 128 * (m + 1); mcol = 128 * m; tri = TriLT; js = range(0, m + 1)
                        else:
                            kcol = slot * T + (m - 4) * 128; vch = slot * 4 + (m - 4)
                            ca, cb = 128 * (m - 4), T; mcol = 128 * (m - 4); tri = TriGE; js = range(m - 4, 4)
                        pb, pk = bank(4)
                        trib = TriLTb if m < 4 else TriGEb
                        S.op('pe', lambda e, kcol=kcol, ca=ca, cb=cb, pb=pb, g=g: e.matmul(pb[:, ca:cb], lhsT=KwT[0:64, kv, kcol:kcol + 128], rhs=Qx[0:64, g, ca:cb], start=True, stop=False),
                             reads=[qxk, 'KwT'], writes=[pk], inc=False)
                        S.op('pe', lambda e, mcol=mcol, pb=pb, trib=trib: e.matmul(pb[:, mcol:mcol + 128], lhsT=ident[:, :], rhs=trib[:, :], start=False, stop=True),
                             reads=['ident', 'TriGEb', 'TriLTb'], writes=[pk])
                        ptk = rr % 3; rr += 1
                        pt = PTs[ptk]
                        S.op('act', lambda e, pb=pb, pt=pt, ca=ca, cb=cb: e.activation(out=pt[:, ca:cb], in_=pb[:, ca:cb], func=AF.Exp), reads=[pk], writes=[('PTs', ptk)])
                        for j in js:
                            first_m = j if i > 0 else 4
                            S.op('pe', lambda e, j=j, m=m, pt=pt, vch=vch, first_m=first_m: e.matmul(ACCW[0][:, j * 128:j * 128 + 65], lhsT=pt[:, j * 128:(j + 1) * 128], rhs=Vw[:, vch, kv, 0:65],
                                                                                                   start=(m == (0 if i > 0 else 4) and j == 0), stop=(m == 7 and j == 3)),
                                 reads=[('PTs', ptk), 'Vw'], writes=[ACCW[1]], inc=(j == js[-1]))
                    tmps = (tmpA, tmpB, tmpC)
                    for br, (acc, acck) in enumerate((ACCC, ACCS, ACCW)):
                        a3 = acc[:, :].rearrange("p (j d) -> p j d", j=4)
                        S.op('dve', lambda e, br=br, a3=a3: e.tensor_scalar(out=rsf[:, br, :], in0=a3[:, :, 64], scalar1=1e-30, scalar2=None, op0=ALU.max), reads=[acck], writes=['rsf'])
                        S.op('dve', lambda e, br=br: e.reciprocal(out=rsf[:, br, :], in_=rsf[:, br, :]), reads=['rsf'], writes=['rsf'])
                        S.op('dve', lambda e, br=br: e.tensor_tensor(out=rsf[:, br, :], in0=rsf[:, br, :], in1=G[:, :, br * 16 + h], op=ALU.mult), reads=['rsf', 'G'], writes=['rsf'])
                        S.op('dve', lambda e, br=br, a3=a3: e.tensor_tensor(out=tmps[br][:, :, :], in0=a3[:, :, 0:64], in1=rsf[:, br, :].unsqueeze(2).to_broadcast([128, 4, 64]), op=ALU.mult),
                             reads=[acck, 'rsf'], writes=['t123'])
                    S.op('dve', lambda e: e.tensor_tensor(out=tmpA[:, :, :], in0=tmpA[:, :, :], in1=tmpB[:, :, :], op=ALU.add), reads=['t123'], writes=['t123'])
                    S.op('dve', lambda e, h=h: e.tensor_tensor(out=Ob[:, :, h * 64:(h + 1) * 64], in0=tmpA[:, :, :], in1=tmpC[:, :, :], op=ALU.add), reads=['t123'], writes=['Ob'])
                return do_head
            heads = {}
            for kv in range(5):
                if kv < 4:
                    heads[kv] = do_kv(kv)
                if kv >= 1 and heads[kv - 1] is not None:
                    for g in range(4 if dbg_stage >= 5 else 0):
                        heads[kv - 1](g)
            for j in range(4):
                for c in range(8):
                    S.op('pe', lambda e, c=c, j=j: e.transpose(out=PT[:, c * 128:(c + 1) * 128], in_=Ob[:, j, c * 128:(c + 1) * 128], identity=ident[:, :]),
                         reads=['Ob', 'ident'], writes=['PT'], inc=(c == 7))
                S.op('dve', lambda e, j=j: e.tensor_copy(out=hT[:, :, j * 128:(j + 1) * 128], in_=PT[:, :].rearrange("p (c n) -> p c n", c=8)), reads=['PT'], writes=['srcT'])
            nouts = Stream([lambda: load_b(s_nout.rearrange("p c n -> p (c n)"), 8 * D, 's_nout')], 0)

            def wfun(hh):
                ap, k = nouts.get(0)
                return ap[:, 0:8 * D].rearrange("p (c n) -> p c n", c=8)[:, :, hh * 512:(hh + 1) * 512], k
            out_proj_add(hT, wfun, 4, 128, 1.0)

        o_scr = dscr("o_scr", [3, NB, 4, 4, DT, 64], F32)

        def nsa_sample(row0, outs, win_out):
            SALL = ['pe', 'act', 'dve', 'pool', 'sp']
            norm_hT(1, 16, 3)
            nsa_rows(1, 16, row0, outs, win_out)
            S.barrier(SALL, list(S.bufs.keys()))
            S.op('pool', lambda e: e.memset(Gsumf[:], 0.0), writes=['Gsum'])
            for k in range(-3, 4):
                S.op('pool', lambda e, k=k: e.affine_select(out=Gsumf[:], in_=Gsumf[:], pattern=[[-1, 16]], compare_op=ALU.not_equal, fill=1.0,
                                                            base=-4 * k, channel_multiplier=1), reads=['Gsum'], writes=['Gsum'])
            S.op('dve', lambda e: e.tensor_copy(out=Gsum[:], in_=Gsumf[:]), reads=['Gsum'], writes=['Gsumb'])
            S.op('pool', lambda e: e.memset(CM4[:], 0.0), writes=['CM4'])
            S.op('pool', lambda e: e.affine_select(out=CM4[:], in_=CM4[:], pattern=[[-1, 4]], compare_op=ALU.is_ge, fill=NEGB, base=0, channel_multiplier=1),
                 reads=['CM4'], writes=['CM4'])
            S.op('pool', lambda e: e.memset(WM4[:], 0.0), writes=['WM4'])
            S.op('pool', lambda e: e.affine_select(out=WM4[:, 0:4], in_=WM4[:, 0:4], pattern=[[1, 4]], compare_op=ALU.is_ge, fill=NEGB, base=-1, channel_multiplier=-1),
                 reads=['WM4'], writes=['WM4'])
            pb, pk = bank(4)
            S.op('pe', lambda e, pb=pb: e.matmul(pb[0:16, 0:4], lhsT=Gsum[0:4, :], rhs=CM4[:, :], start=True, stop=True), reads=['Gsumb', 'CM4'], writes=[pk])
            S.op('dve', lambda e, pb=pb: e.tensor_copy(out=CM16[:, :], in_=pb[0:16, 0:4]), reads=[pk], writes=['CM16'])
            pb, pk = bank(4)
            S.op('pe', lambda e, pb=pb: e.matmul(pb[0:16, :], lhsT=Gsum[0:4, :], rhs=WM4[:, :], start=True, stop=True), reads=['Gsumb', 'WM4'], writes=[pk])
            S.op('dve', lambda e, pb=pb: e.tensor_copy(out=WM16[:, :], in_=pb[0:16, :]), reads=[pk], writes=['WM16'])
            for k in range(2):
                S.op('pool', lambda e, k=k: e.memset(W1bd[k][:, :, :], 0.0), writes=[('W1bd', k)])
                src = (w_k1, w_v1)[k].rearrange("l d e -> d l e")
                S.op('pool', lambda e, k=k, src=src: e.dma_start(out=W1bd[k][0:64, :, 0:64], in_=src), reads=[('W1bd', k)], writes=[('W1bd', k)], dma='cw')
                S.op('pool', lambda e, k=k, src=src: e.dma_start(out=W1bd[k][64:128, :, 64:128], in_=src), reads=[('W1bd', k)], writes=[('W1bd', k)], dma='cw')
            S.op('pool', lambda e: e.memset(W2sel[:], 0.0), writes=['W2sel'])
            for k in range(2):
                src = (w_k2, w_v2)[k]
                S.op('pool', lambda e, k=k, src=src: e.dma_start(out=W2sel[0:64, k, 0, :], in_=src), reads=['W2sel'], writes=['W2sel'], dma='cw')
                S.op('pool', lambda e, k=k, src=src: e.dma_start(out=W2sel[64:128, k, 1, :], in_=src), reads=['W2sel'], writes=['W2sel'], dma='cw')
            S.op('sp', lambda e: e.dma_start(out=peT2[0:64, :], in_=peT[:, :]), reads=['peT'], writes=['peT2'], dma='const')
            S.op('sp', lambda e: e.dma_start(out=peT2[64:128, :], in_=peT[:, :]), reads=['peT'], writes=['peT2'], dma='const')
            S.op('pool', lambda e: e.dma_start(out=pt_i[:, :], in_=ptab.rearrange("b n -> (b n)").partition_broadcast(128)), writes=['pt_i'], dma='cw')
            S.op('pool', lambda e: e.iota(iot_i[:], pattern=[[0, 1]], base=0, channel_multiplier=1), writes=['iot'])
            S.op('dve', lambda e: e.tensor_copy(out=iot_f[:], in_=iot_i[:]), reads=['iot'], writes=['iotf'])
            S.op('dve', lambda e: e.tensor_copy(out=PTF[:, :, :].rearrange("p b n -> p (b n)"), in_=pt_i[:, :]), reads=['pt_i'], writes=['PTF'])
            S.op('dve', lambda e: e.tensor_scalar(out=PTF[:, :, :], in0=PTF[:, :, :], scalar1=128.0, scalar2=iot_f[:, 0:1], op0=ALU.mult, op1=ALU.add),
                 reads=['PTF', 'iotf'], writes=['PTF'])
            S.op('dve', lambda e: e.tensor_copy(out=IDX[:, :, :], in_=PTF[:, :, :]), reads=['PTF'], writes=['IDX'])
            gw, gk = load_g(s_ng.rearrange("p c n -> p (c n)"), 48, 's_ng')
            pb, pk = bank(4)
            for c in range(8):
                S.op('pe', lambda e, c=c, pb=pb: e.matmul(pb[0:16, 0:48], lhsT=hT[:, c, 0:16], rhs=gw[:, c, :], start=(c == 0), stop=(c == 7)), reads=[gk, 'hT'], writes=[pk], inc=(c == 7))
            S.op('act', lambda e, pb=pb: e.activation(out=G[0:16, 0, :], in_=pb[0:16, 0:48], func=AF.Sigmoid), reads=[pk], writes=['G'])
            for (m, dstT, key) in ((2, KsN, 'KsN'), (4, KwN, 'KwN')):
                w, wk = load_g(s_nk[m].rearrange("p c j -> p (c j)"), 256, 's_nk')
                for kv in range(4):
                    pb, pk = bank(4)
                    for c in range(8):
                        S.op('pe', lambda e, c=c, kv=kv, pb=pb, w=w: e.matmul(pb[0:64, 0:16], lhsT=w[:, c, kv * 64:(kv + 1) * 64], rhs=hT[:, c, 0:16], start=(c == 0), stop=(c == 7)),
                             reads=[wk, 'hT'], writes=[pk], inc=(c == 7))
                    S.op('dve', lambda e, kv=kv, pb=pb, dstT=dstT: e.tensor_copy(out=dstT[:, kv, :], in_=pb[0:64, 0:16]), reads=[pk], writes=[key])
            for kv in range(4):
                qw, qk = load_g(s_nq[kv].rearrange("p c j -> p (c j)"), 256, 's_nq')
                for g in range(4):
                    pb, pk = bank(4)
                    for c in range(8):
                        S.op('pe', lambda e, c=c, g=g, pb=pb, qw=qw: e.matmul(pb[0:64, 0:16], lhsT=qw[:, c, g * 64:(g + 1) * 64], rhs=hT[:, c, 0:16], start=(c == 0), stop=(c == 7)),
                             reads=[qk, 'hT'], writes=[pk], inc=(c == 7))
                    S.op('act', lambda e, g=g, kv=kv, pb=pb: e.activation(out=Qs[:, kv * 4 + g, :], in_=pb[0:64, 0:16], func=AF.Copy, scale=0.125), reads=[pk], writes=['Qs'])
            wap, wk3 = load_b(s_nkv.rearrange("p c n -> p (c n)"), 8 * 1536, 's_nkv')
            w3 = wap[:, 0:8 * 1536].rearrange("p (c n) -> p c n", c=8)
            for b in range(NB):
                for (br, dstV, key) in ((1, VnS, 'VnS'), (2, VnW, 'VnW')):
                    pb, pk = bank(4)
                    for c in range(8):
                        S.op('pe', lambda e, c=c, b=b, br=br, pb=pb: e.matmul(pb[0:4, 0:256], lhsT=hT[:, c, 4 * b:4 * b + 4], rhs=w3[:, c, br * 512 + 256:br * 512 + 512],
                                                                           start=(c == 0), stop=(c == 7)), reads=[wk3, 'hT'], writes=[pk], inc=(c == 7))
                    S.op('dve', lambda e, b=b, pb=pb, dstV=dstV: e.tensor_copy(out=dstV[:, b, :, :], in_=pb[0:4, 0:256].rearrange("p (k d) -> p k d", k=4)), reads=[pk], writes=[key])

            def transposeP(src16, ncols, dst):
                nk = ncols // 128
                for k4 in range(nk):
                    S.op('pe', lambda e, k4=k4: e.transpose(out=PT[:, k4 * 16:(k4 + 1) * 16], in_=src16[0:16, k4 * 128:(k4 + 1) * 128], identity=ident[0:16, 0:16]),
                         reads=['Pb', 'ident'], writes=['PT'], inc=(k4 == nk - 1))
                S.op('dve', lambda e: e.tensor_copy(out=dst[:, 0:nk, :], in_=PT[:, 0:nk * 16].rearrange("p (k q) -> p k q", q=16)), reads=['PT'], writes=['PsT'])

            ACC = [(PB[4], ('PB', 4)), (PB[5], ('PB', 5)), (PB[6], ('PB', 6))]
            ACCS4 = [PB[3], PB[4], PB[5], PB[6]]
            SHq = [ident[0:64, :], SH[0:64, :]]

            def gather_group(cache, b, grp, npg=8, slot0=0):
                for k in range(npg):
                    p = grp * npg + k
                    S.op('pool', lambda e, k=k, p=p: e.indirect_dma_start(out=CG[:, slot0 + k, :], out_offset=None, in_=cache[:, :],
                                                                          in_offset=bass.IndirectOffsetOnAxis(ap=IDX[:, b, p:p + 1], axis=0)),
                         reads=['IDX'], writes=[('CG', slot0 + k)], dma='pg')

            def transpose_group(ccs, add_pe, npair=4, slot0=0, xoff=0, xkey='XT'):
                for k2 in range(npair):
                    for ci, cc in enumerate(ccs):
                        for pg2 in range(2):
                            k = slot0 + 2 * k2 + pg2
                            S.op('pe', lambda e, k=k, cc=cc, ci=ci, pg2=pg2: e.transpose(out=PT[:, (ci * 2 + pg2) * 128:(ci * 2 + pg2 + 1) * 128], in_=CG[:, k, cc * 128:(cc + 1) * 128], identity=ident[:, :]),
                                 reads=[('CG', k), 'ident'], writes=['PT'], inc=(ci == len(ccs) - 1 and pg2 == 1))
                    src = PT[:, 0:len(ccs) * 256].rearrange("p (c r) -> p c r", c=len(ccs))
                    dst = XT[:, 0:len(ccs), xoff + k2 * 256:xoff + (k2 + 1) * 256]
                    if add_pe:
                        S.op('dve', lambda e, src=src, dst=dst: e.tensor_tensor(out=dst.rearrange("p c (n l) -> p c n l", l=32), in0=src.rearrange("p c (n l) -> p c n l", l=32),
                                                                                in1=peT2[:, :].unsqueeze(1).unsqueeze(1).to_broadcast([128, len(ccs), 8, 32]), op=ALU.add),
                             reads=['PT', 'peT2'], writes=[xkey])
                    else:
                        S.op('dve', lambda e, src=src, dst=dst: e.tensor_copy(out=dst, in_=src), reads=['PT'], writes=[xkey])

            for b in range(NB):
                for grp in range(16):
                    gather_group(cache_cmp, b, grp)
                    transpose_group([0, 1, 2, 3], True)
                    for cc in range(4):
                        pb, pk = bank(4)
                        x3 = XT[:, cc, :].rearrange("p (n l) -> p n l", l=32)
                        for l in range(32):
                            S.op('pe', lambda e, l=l, cc=cc, pb=pb, x3=x3: e.matmul(pb[:, 0:32], lhsT=W1bd[cc // 2][:, l, :], rhs=x3[:, :, l], start=(l == 0), stop=(l == 31)),
                                 reads=['XT', ('W1bd', 0), ('W1bd', 1)], writes=[pk], inc=(l == 31))
                        S.op('act', lambda e, pb=pb: e.activation(out=Ec16_g[:, 0:32], in_=pb[:, 0:32], func=AF.Copy), reads=[pk], writes=['gel'])
                        S.op('dve', lambda e: e.tensor_tensor(out=Ec16_g[:, 64:96], in0=Ec16_g[:, 0:32], in1=Ec16_g[:, 0:32], op=ALU.mult), reads=['gel'], writes=['gel'])
                        S.op('dve', lambda e: e.tensor_scalar(out=Ec16_g[:, 64:96], in0=Ec16_g[:, 64:96], scalar1=0.044715, scalar2=1.0, op0=ALU.mult, op1=ALU.add), reads=['gel'], writes=['gel'])
                        S.op('dve', lambda e: e.tensor_tensor(out=Ec16_g[:, 64:96], in0=Ec16_g[:, 64:96], in1=Ec16_g[:, 0:32], op=ALU.mult), reads=['gel'], writes=['gel'])
                        S.op('act', lambda e: e.activation(out=Ec16_g[:, 128:160], in_=Ec16_g[:, 64:96], func=AF.Sigmoid, scale=1.5957691216), reads=['gel'], writes=['gel'])
                        S.op('dve', lambda e, cc=cc, grp=grp: e.tensor_tensor(out=hidS[:, cc, grp * 32:(grp + 1) * 32], in0=Ec16_g[:, 128:160], in1=Ec16_g[:, 0:32], op=ALU.mult),
                             reads=['gel'], writes=['hidS'])
                for kv in range(4):
                    pb, pk = bank(4)
                    S.op('pe', lambda e, kv=kv, pb=pb: e.matmul(pb[0:64, :], lhsT=W2sel[:, 0, kv % 2, :], rhs=hidS[:, kv // 2, :], start=True, stop=True), reads=['hidS', 'W2sel'], writes=[pk])
                    S.op('dve', lambda e, kv=kv, pb=pb: e.tensor_copy(out=KcS[:, kv, :], in_=pb[0:64, :]), reads=[pk], writes=['KcS'])
                    pb, pk = bank(4)
                    for q4 in range(4):
                        S.op('pe', lambda e, kv=kv, q4=q4, pb=pb: e.matmul(pb[:, q4 * 64:(q4 + 1) * 64], lhsT=hidS[:, 2 + kv // 2, q4 * 128:(q4 + 1) * 128], rhs=W2sel[:, 1, kv % 2, :], start=True, stop=True),
                             reads=['hidS', 'W2sel'], writes=[pk], inc=(q4 == 3))
                    S.op('dve', lambda e, kv=kv, pb=pb: e.tensor_copy(out=VcS[:, :, kv, :], in_=pb[:, 0:256].rearrange("p (q d) -> p q d", q=4)), reads=[pk], writes=['VcS'])
                for q4 in range(4):
                    S.op('pool', lambda e, q4=q4, b=b: e.dma_start(out=CGw[:, q4, :], in_=cache_win[b, q4 * 128:(q4 + 1) * 128, :]), writes=['CGw'], dma='pg')
                for kv in range(4):
                    S.op('dve', lambda e, kv=kv, b=b: e.tensor_copy(out=QB[:, kv, :].rearrange("p (g t) -> p g t", g=4), in_=Qs[:, kv * 4:(kv + 1) * 4, 4 * b:4 * b + 4]), reads=['Qs'], writes=['QB'])
                    pb, pk = bank(4)
                    S.op('pe', lambda e, kv=kv, pb=pb: e.matmul(pb[:, 0:16], lhsT=SHq[kv % 2], rhs=QB[:, kv, :], start=True, stop=True), reads=['QB', 'SH', 'ident'], writes=[pk])
                    S.op('dve', lambda e, kv=kv, pb=pb: e.tensor_copy(out=QBp[:, kv, :], in_=pb[:, 0:16]), reads=[pk], writes=['QBp'])
                    pb, pk = bank(4)
                    S.op('pe', lambda e, kv=kv, pb=pb: e.matmul(pb[0:16, :], lhsT=QB[:, kv, :], rhs=KcS[:, kv, :], start=True, stop=True), reads=['QB', 'KcS'], writes=[pk])
                    S.op('act', lambda e, pb=pb: e.activation(out=Ec16[:, :], in_=pb[0:16, :], func=AF.Exp, accum_out=rsS[:, 0:1]), reads=[pk], writes=['Ec16', 'rsS'])
                    S.op('dve', lambda e: e.tensor_copy(out=Pb[:, :], in_=Ec16[:, :]), reads=['Ec16'], writes=['Pb'])
                    S.op('dve', lambda e: e.reciprocal(out=rsS[:, 0:1], in_=rsS[:, 0:1]), reads=['rsS'], writes=['rsS'])
                    S.op('dve', lambda e: e.tensor_scalar(out=Ec16[:, :], in0=Ec16[:, :], scalar1=rsS[:, 0:1], scalar2=None, op0=ALU.mult), reads=['Ec16', 'rsS'], writes=['Ec16'])
                    pb, pk = bank(4)
                    S.op('pe', lambda e, pb=pb: e.matmul(pb[0:16, :], lhsT=Gsumf[:, :], rhs=Ec16[:, :], start=True, stop=True), reads=['Ec16', 'Gsum'], writes=[pk])
                    p3 = pb[0:16, :].rearrange("p (m two) -> p m two", two=2)
                    S.op('dve', lambda e, p3=p3: e.tensor_copy(out=imp16[:, :].rearrange("p (m two) -> p m two", two=2), in_=p3), reads=[pk], writes=['imp16'])
                    i3 = imp16[:, :].rearrange("p (m two) -> p m two", two=2)
                    S.op('dve', lambda e, i3=i3: e.tensor_tensor(out=imps16[:, :], in0=i3[:, :, 0], in1=i3[:, :, 1], op=ALU.add), reads=['imp16'], writes=['imps16'])
                    S.op('dve', lambda e: e.memset(imps16[:, 0:1], 1e4), reads=['imps16'], writes=['imps16'])
                    S.op('dve', lambda e: e.memset(imps16[:, 255:256], 1e4), reads=['imps16'], writes=['imps16'])
                    S.op('dve', lambda e: e.max(out=m1s[:, :], in_=imps16[:, :]), reads=['imps16'], writes=['m1s'])
                    S.op('dve', lambda e: e.match_replace(out=work16[:, :], in_to_replace=m1s[:, :], in_values=imps16[:, :], imm_value=-3e4), reads=['imps16', 'm1s'], writes=['work16'])
                    S.op('dve', lambda e: e.max(out=m2s[:, :], in_=work16[:, :]), reads=['work16'], writes=['m2s'])
                    S.op('dve', lambda e: e.tensor_scalar(out=mb16[:, :], in0=imps16[:, :], scalar1=m2s[:, 6:7], scalar2=None, op0=ALU.is_ge), reads=['imps16', 'm2s'], writes=['mb16'])
                    S.op('dve', lambda e: e.tensor_scalar(out=mb16[:, :], in0=mb16[:, :], scalar1=-1.0, scalar2=-NEGB, op0=ALU.add, op1=ALU.mult), reads=['mb16'], writes=['mb16'])
                    transposeP(Pb, 512, PsT)
                    for q4 in range(4):
                        S.op('pe', lambda e, kv=kv, q4=q4: e.matmul(ACC[0][0][0:16, 0:64], lhsT=PsT[:, q4, :], rhs=VcS[:, q4, kv, :], start=(q4 == 0), stop=(q4 == 3)),
                             reads=['PsT', 'VcS'], writes=[ACC[0][1]], inc=(q4 == 3))
                    S.op('dve', lambda e: e.tensor_scalar(out=OB3[:, 0, :], in0=ACC[0][0][0:16, 0:64], scalar1=rsS[:, 0:1], scalar2=None, op0=ALU.mult), reads=[ACC[0][1], 'rsS'], writes=['OB3'])
                    if kv == 0:
                        for q4 in range(4):
                            for c2 in range(2):
                                S.op('pe', lambda e, q4=q4, c2=c2: e.transpose(out=PT[:, (q4 * 2 + c2) * 128:(q4 * 2 + c2 + 1) * 128], in_=CGw[:, q4, c2 * 128:(c2 + 1) * 128], identity=ident[:, :]),
                                     reads=['CGw', 'ident'], writes=['PT'], inc=(q4 == 3 and c2 == 1))
                        S.op('dve', lambda e: e.tensor_copy(out=XTw[:, :, :].rearrange("p c (q r) -> p q c r", q=4), in_=PT[:, :].rearrange("p (q c r) -> p q c r", q=4, c=2)), reads=['PT'], writes=['XTw'])
                    ncol = 0
                    pb, pk = bank(4)
                    S.op('pe', lambda e, kv=kv, pb=pb: e.matmul(pb[0:16, :], lhsT=QBp[:, kv, :], rhs=XTw[:, kv // 2, :], start=True, stop=True), reads=['QBp', 'XTw'], writes=[pk])
                    S.op('dve', lambda e, pb=pb: e.tensor_tensor(out=SM[:, :], in0=pb[0:16, :], in1=WM16[:, :], op=ALU.add), reads=[pk, 'WM16'], writes=['SM'])
                    S.op('act', lambda e: e.activation(out=Pb[:, :], in_=SM[:, :], func=AF.Exp, accum_out=ssum[:, 0:1]), reads=['SM'], writes=['Pb', 'ssum'])
                    transposeP(Pb, 512, PsT)
                    for q4 in range(4):
                        S.op('pe', lambda e, kv=kv, q4=q4: e.matmul(ACC[2][0][0:16, 0:64], lhsT=PsT[:, q4, :], rhs=CGw[:, q4, 256 + kv * 64:256 + (kv + 1) * 64], start=(q4 == 0), stop=False),
                             reads=['PsT', 'CGw'], writes=[ACC[2][1]], inc=(q4 == 3))
                    pb, pk = bank(4)
                    S.op('pe', lambda e, kv=kv, b=b, pb=pb: e.matmul(pb[0:16, 0:4], lhsT=QB[:, kv, :], rhs=KwN[:, kv, 4 * b:4 * b + 4], start=True, stop=True), reads=['QB', 'KwN'], writes=[pk])
                    S.op('dve', lambda e, pb=pb: e.tensor_tensor(out=SM[:, 0:4], in0=pb[0:16, 0:4], in1=CM16[:, :], op=ALU.add), reads=[pk, 'CM16'], writes=['SM'])
                    S.op('act', lambda e: e.activation(out=Pb[:, 0:4], in_=SM[:, 0:4], func=AF.Exp, accum_out=ssum[:, 1:2]), reads=['SM'], writes=['Pb', 'ssum'])
                    S.op('pe', lambda e: e.transpose(out=PT[0:4, 0:16], in_=Pb[0:16, 0:4], identity=ident[0:16, 0:16]), reads=['Pb', 'ident'], writes=['PT'])
                    S.op('dve', lambda e: e.tensor_copy(out=PsT[0:4, 0, :], in_=PT[0:4, 0:16]), reads=['PT'], writes=['PsT'])
                    S.op('pe', lambda e, kv=kv, b=b: e.matmul(ACC[2][0][0:16, 0:64], lhsT=PsT[0:4, 0, :], rhs=VnW[:, b, kv, :], start=False, stop=True), reads=['PsT', 'VnW'], writes=[ACC[2][1]])
                    S.op('dve', lambda e: e.tensor_tensor(out=rsS[:, 2:3], in0=ssum[:, 0:1], in1=ssum[:, 1:2], op=ALU.add), reads=['ssum'], writes=['rsS'])
                    S.op('dve', lambda e: e.reciprocal(out=rsS[:, 2:3], in_=rsS[:, 2:3]), reads=['rsS'], writes=['rsS'])
                    S.op('dve', lambda e: e.tensor_scalar(out=OB3[:, 2, :], in0=ACC[2][0][0:16, 0:64], scalar1=rsS[:, 2:3], scalar2=None, op0=ALU.mult), reads=[ACC[2][1], 'rsS'], writes=['OB3'])
                    S.op('dve', lambda e, kv=kv: e.tensor_copy(out=mbAll[:, kv, :], in_=mb16[:, :]), reads=['mb16'], writes=['mbAll'])
                    S.op('dve', lambda e, kv=kv: e.tensor_copy(out=OBall[:, kv, 0, :], in_=OB3[:, 0, :]), reads=['OB3'], writes=['OBall'])
                    S.op('dve', lambda e, kv=kv: e.tensor_copy(out=OBall[:, kv, 2, :], in_=OB3[:, 2, :]), reads=['OB3'], writes=['OBall'])
                ncols = {kv: 0 for kv in range(4)}
                for grp in range(32):
                    hf = grp % 2
                    xk = ('XTs', hf)
                    gather_group(cache_sel, b, grp, npg=4, slot0=4 * hf)
                    transpose_group([0, 1], False, npair=2, slot0=4 * hf, xoff=512 * hf, xkey=xk)
                    for kv in range(4):
                        pb, pk = bank(3)
                        S.op('pe', lambda e, kv=kv, hf=hf, pb=pb: e.matmul(pb[0:16, :], lhsT=QBp[:, kv, :], rhs=XT[:, kv // 2, hf * 512:(hf + 1) * 512], start=True, stop=True),
                             reads=['QBp', xk], writes=[pk])
                        blk0 = grp * 8
                        S.op('dve', lambda e, kv=kv, pb=pb, blk0=blk0: e.tensor_tensor(out=SM[:, :].rearrange("p (n l) -> p n l", l=64), in0=pb[0:16, :].rearrange("p (n l) -> p n l", l=64),
                                                                                     in1=mbAll[:, kv, blk0:blk0 + 8].unsqueeze(2).to_broadcast([16, 8, 64]), op=ALU.add),
                             reads=[pk, 'mbAll'], writes=['SM'])
                        S.op('act', lambda e, kv=kv, grp=grp: e.activation(out=Pb[:, :], in_=SM[:, :], func=AF.Exp, accum_out=ssumS[:, kv, grp:grp + 1]), reads=['SM'], writes=['Pb', 'ssumS'])
                        transposeP(Pb, 512, PsT)
                        for q4 in range(4):
                            first = (grp == 0 and q4 == 0)
                            S.op('pe', lambda e, kv=kv, q4=q4, hf=hf, first=first: e.matmul(ACCS4[kv][0:16, 0:64], lhsT=PsT[:, q4, :], rhs=CG[:, 4 * hf + q4, 256 + kv * 64:256 + (kv + 1) * 64],
                                                                                           start=first, stop=False),
                                 reads=['PsT', ('CG', 4 * hf + q4)], writes=[('PB', 3 + kv)], inc=(q4 == 3))
                for kv in range(4):
                    S.op('dve', lambda e, kv=kv, b=b: e.tensor_copy(out=QB[:, kv, :].rearrange("p (g t) -> p g t", g=4), in_=Qs[:, kv * 4:(kv + 1) * 4, 4 * b:4 * b + 4]), reads=['Qs'], writes=['QB'])
                    pb, pk = bank(3)
                    S.op('pe', lambda e, kv=kv, b=b, pb=pb: e.matmul(pb[0:16, 0:4], lhsT=QB[:, kv, :], rhs=KsN[:, kv, 4 * b:4 * b + 4], start=True, stop=True), reads=['QB', 'KsN'], writes=[pk])
                    S.op('dve', lambda e, pb=pb: e.tensor_tensor(out=SM[:, 0:4], in0=pb[0:16, 0:4], in1=CM16[:, :], op=ALU.add), reads=[pk, 'CM16'], writes=['SM'])
                    S.op('act', lambda e, kv=kv: e.activation(out=Pb[:, 0:4], in_=SM[:, 0:4], func=AF.Exp, accum_out=ssumS[:, kv, 32:33]), reads=['SM'], writes=['Pb', 'ssumS'])
                    S.op('pe', lambda e: e.transpose(out=PT[0:4, 0:16], in_=Pb[0:16, 0:4], identity=ident[0:16, 0:16]), reads=['Pb', 'ident'], writes=['PT'])
                    S.op('dve', lambda e: e.tensor_copy(out=PsT[0:4, 0, :], in_=PT[0:4, 0:16]), reads=['PT'], writes=['PsT'])
                    S.op('pe', lambda e, kv=kv, b=b: e.matmul(ACCS4[kv][0:16, 0:64], lhsT=PsT[0:4, 0, :], rhs=VnS[:, b, kv, :], start=False, stop=True), reads=['PsT', 'VnS'], writes=[('PB', 3 + kv)])
                    S.op('dve', lambda e, kv=kv: e.tensor_reduce(out=rsS[:, 1:2], in_=ssumS[:, kv, 0:33], axis=AX.X, op=ALU.add), reads=['ssumS'], writes=['rsS'])
                    S.op('dve', lambda e: e.reciprocal(out=rsS[:, 1:2], in_=rsS[:, 1:2]), reads=['rsS'], writes=['rsS'])
                    S.op('dve', lambda e, kv=kv: e.tensor_scalar(out=OBall[:, kv, 1, :], in0=ACCS4[kv][0:16, 0:64], scalar1=rsS[:, 1:2], scalar2=None, op0=ALU.mult), reads=[('PB', 3 + kv), 'rsS'], writes=['OBall'])
                for kv in range(4):
                    for br in range(3):
                        S.op('sp', lambda e, kv=kv, br=br, b=b: e.dma_start(out=o_scr[br, b, kv].rearrange("g t d -> (g t) d"), in_=OBall[:, kv, br, :]), reads=['OBall'], writes=['o_scr'], dma='st')
            S.barrier(SALL, [('CG', k) for k in range(8)] + ['XT', ('XTs', 0), ('XTs', 1), 'hidS', ('W1bd', 0), ('W1bd', 1), 'CGw', 'XTw'])
            for br in range(3):
                for b in range(NB):
                    for t in range(DT):
                        S.op('sp', lambda e, br=br, b=b, t=t: e.dma_start(out=OTk[4 * b + t:4 * b + t + 1, br, :, :], in_=o_scr[br, b, :, :, t, :].rearrange("k g d -> (k g) d").unsqueeze(0)),
                             reads=['o_scr'], writes=['OTk'], dma='ldx')
            for br in range(3):
                S.op('dve', lambda e, br=br: e.tensor_tensor(out=OTk[:, br, :, :], in0=OTk[:, br, :, :], in1=G[0:16, 0, br * 16:(br + 1) * 16].unsqueeze(2).to_broadcast([16, 16, 64]), op=ALU.mult),
                     reads=['OTk', 'G'], writes=['OTk'])
            S.op('dve', lambda e: e.tensor_tensor(out=OTk[:, 0, :, :], in0=OTk[:, 0, :, :], in1=OTk[:, 1, :, :], op=ALU.add), reads=['OTk'], writes=['OTk'])
            S.op('dve', lambda e: e.tensor_tensor(out=ObS[:, :].rearrange("p (h d) -> p h d", h=16), in0=OTk[:, 0, :, :], in1=OTk[:, 2, :, :], op=ALU.add), reads=['OTk'], writes=['ObS'])
            for c in range(8):
                S.op('pe', lambda e, c=c: e.transpose(out=PT[:, c * 128:c * 128 + 16], in_=ObS[0:16, c * 128:(c + 1) * 128], identity=ident[0:16, 0:16]), reads=['ObS', 'ident'], writes=['PT'], inc=(c == 7))
            S.op('dve', lambda e: e.tensor_copy(out=hT[:, :, 0:16], in_=PT[:, :].rearrange("p (c n) -> p c n", c=8)[:, :, 0:16]), reads=['PT'], writes=['srcT'])
            nouts = Stream([lambda: load_b(s_nout.rearrange("p c n -> p (c n)"), 8 * D, 's_nout')], 0)

            def wfun(hh):
                ap, k = nouts.get(0)
                return ap[:, 0:8 * D].rearrange("p (c n) -> p c n", c=8)[:, :, hh * 512:(hh + 1) * 512], k
            out_proj_add(hT, wfun, 1, 16, 1.0)

        def final_norm(nsub, npart, out_ap):
            for j in range(nsub):
                S.op('act', lambda e, j=j: e.activation(out=junk[:npart, :], in_=X[:npart, j, :], func=AF.Square, scale=1.0 / 32.0,
                                                         accum_out=ss[:npart, j:j + 1]), reads=['X'], writes=['junk', 'ss'])
            S.op('dve', lambda e: e.tensor_scalar(out=rstd[:npart, :nsub], in0=ss[:npart, :nsub], scalar1=EPS, scalar2=None, op0=ALU.add),
                 reads=['ss'], writes=['rstd'])
            S.op('act', lambda e: e.activation(out=rstd[:npart, :nsub], in_=rstd[:npart, :nsub], func=AF.Sqrt), reads=['rstd'], writes=['rstd'])
            S.op('dve', lambda e: e.reciprocal(out=rstd[:npart, :nsub], in_=rstd[:npart, :nsub]), reads=['rstd'], writes=['rstd'])
            for j in range(nsub):
                S.op('dve', lambda e, j=j: e.scalar_tensor_tensor(out=X[:npart, j, :], in0=X[:npart, j, :], scalar=rstd[:npart, j:j + 1], in1=gfin[:npart, :],
                                                                  op0=ALU.mult, op1=ALU.mult), reads=['X', 'rstd', 'gfin'], writes=['X'])
            S.op('sp', lambda e: e.dma_start(out=out_ap, in_=X[:npart, 0:nsub, :]), reads=['X'], writes=['yout'], dma='st')

        ARENA_KEYS = ['srcT', 'U', 'cgs', 'cacc']
        ALLENG = ['pe', 'act', 'dve', 'pool', 'sp']

        def run_tile(nsub, npart, x_ap, y_ap, first, sample, cs_out, row0, kv_outs, win_out, tile_i=None):
            S.op('sp', lambda e: e.dma_start(out=X[:npart, 0:nsub, :], in_=x_ap), writes=['X'], dma='ldx')
            ffn(0, 'a', nsub, npart)
            S.barrier(ALLENG, ARENA_KEYS)
            conv_layer(nsub, npart, first, sample, cs_out)
            S.barrier(ALLENG, ARENA_KEYS)
            ffn(0, 'b', nsub, npart)
            ffn(1, 'a', nsub, npart)
            S.barrier(ALLENG, ARENA_KEYS)
            if sample:
                nsa_sample(row0, kv_outs, win_out)
            else:
                nsa_prompt(tile_i, row0, kv_outs, win_out)
            S.barrier(ALLENG, ARENA_KEYS + NSA_KEYS)
            ffn(1, 'b', nsub, npart)
            final_norm(nsub, npart, y_ap)

        for s in range(nseq):
            for i in range(seqlen // T):
                r0 = s * seqlen + i * T
                last = (i == seqlen // T - 1)
                run_tile(4, 128, xp[r0:r0 + T, :].rearrange("(j p) d -> p j d", p=128),
                         yp[r0:r0 + T, :].rearrange("(j p) d -> p j d", p=128),
                         first=(i == 0), sample=False, cs_out=(csp[s] if last else None), row0=r0, kv_outs=(cmp_p, sel_p), tile_i=i,
                         win_out=((lambda j, st, s=s: [(win_p[s, j * 128:(j + 1) * 128, :], st[:, :])]) if last else None))
        for b in range(NB if with_sample else 0):
            for t2 in range(2):
                S.op('sp', lambda e, b=b, t2=t2: e.dma_start(out=ucarS[:, b, :, t2], in_=stc[b, t2].rearrange("(c p) -> p c", p=128), allow_slow_non_contiguous=True),
                     writes=['ucar'], dma='const')
        if with_sample:
          run_tile(1, NB * DT, xs.rearrange("(j p) d -> p j d", j=1), ys.rearrange("(j p) d -> p j d", j=1), first=False,
                 sample=True, cs_out=css, row0=0, kv_outs=(cmp_s, sel_s),
                 win_out=(lambda j, st: [(win_s[b, WB - DT:WB, :], st[DT * b:DT * (b + 1), :]) for b in range(NB)]))
        for b in range(NB if with_sample else 0):
            S.op('sp', lambda e, b=b: e.dma_start(out=win_s[b, 0:WB - DT, :], in_=cache_win[b, DT:WB, :]), writes=['wins'], dma='st')
        S.finish()
        S.emit()
    return nc


_NC_CACHE = {}


def kernel(**inp):
    f32 = lambda a: np.ascontiguousarray(np.asarray(a, dtype=np.float32))
    x_prompt = f32(inp["x_prompt"]); x_sample = f32(inp["x_sample"])
    state_conv = f32(inp["state_conv"])
    cache_cmp = f32(inp["cache_cmp_kv"]).reshape(5120 * 128, 512)
    cache_sel = f32(inp["cache_sel_kv"]).reshape(5120 * 128, 512)
    cache_win = f32(inp["cache_win_kv"]).reshape(32, WB, 512)
    page_table = np.ascontiguousarray(np.asarray(inp["page_table"], dtype=np.int32))
    if 'nc' not in _NC_CACHE:
        _NC_CACHE['nc'] = build_nc()
    nc = _NC_CACHE['nc']
    shared = {
        "cache_cmp": cache_cmp, "cache_sel": cache_sel,
        "norm_ffa": f32(inp["norm_ffa"]), "norm_mix": f32(inp["norm_mix"]), "norm_ffb": f32(inp["norm_ffb"]),
        "w_ffa_gu": f32(inp["w_ffa_gu"]), "w_ffa_down": f32(inp["w_ffa_down"]),
        "w_ffb_gu": f32(inp["w_ffb_gu"]), "w_ffb_down": f32(inp["w_ffb_down"]),
        "w_conv_in": f32(inp["w_conv_in"])[0], "w_conv": f32(inp["w_conv"])[0], "w_conv_out": f32(inp["w_conv_out"])[0],
        "w_nsa_in": f32(inp["w_nsa_in"])[0], "pe_cmp": f32(inp["pe_cmp"])[0],
        "w_k1": f32(inp["w_cmp_k1"])[0], "w_k2": f32(inp["w_cmp_k2"])[0], "w_v1": f32(inp["w_cmp_v1"])[0], "w_v2": f32(inp["w_cmp_v2"])[0],
        "w_nsa_out": f32(inp["w_nsa_out"])[0], "norm_final": f32(inp["norm_final"]),
    }
    in_maps = []
    for c in range(NCORE):
        m = dict(shared)
        m["xp"] = x_prompt[NSEQ * c:NSEQ * (c + 1)].reshape(NSEQ * SEQ, D)
        m["xs"] = x_sample[NB * c:NB * (c + 1)].reshape(NB * DT, D)
        m["stc"] = np.ascontiguousarray(state_conv[0, NB * c:NB * (c + 1)])
        m["cache_win"] = np.ascontiguousarray(cache_win[NB * c:NB * (c + 1)])
        m["ptab"] = np.ascontiguousarray(page_table[NB * c:NB * (c + 1)])
        in_maps.append(m)
    res = run_bass_kernel_spmd(nc, in_maps, core_ids=list(range(NCORE)))
    R = res.results
    cat = lambda k: np.concatenate([np.asarray(r[k]) for r in R], axis=0)
    y_prompt = cat("yp").reshape(16, SEQ, D)
    y_sample = cat("ys").reshape(32, DT, D)
    conv_p = cat("csp").reshape(1, 16, 2, D)
    conv_s = cat("css").reshape(1, 32, 2, D)
    cmp_p = cat("cmp_p").reshape(1, 16, SEQ, 2, 4, 64)
    cmp_s = cat("cmp_s").reshape(1, 32, DT, 2, 4, 64)
    sel_p = cat("sel_p").reshape(1, 16, SEQ, 2, 4, 64)
    sel_s = cat("sel_s").reshape(1, 32, DT, 2, 4, 64)
    win_p = cat("win_p").reshape(1, 16, WB, 2, 4, 64)
    win_s = cat("win_s").reshape(1, 32, WB, 2, 4, 64)
    return (y_prompt, y_sample, conv_p, conv_s, cmp_p, cmp_s, sel_p, sel_s, win_p, win_s)
```

```python
import numpy as np
from contextlib import ExitStack
import concourse.bass as bass
import concourse.mybir as mybir
from concourse.bass_utils import run_bass_kernel_spmd

F32 = mybir.dt.float32
BF16 = mybir.dt.bfloat16
I32 = mybir.dt.int32
AF = mybir.ActivationFunctionType
ALU = mybir.AluOpType
AX = mybir.AxisListType

D = 1024
FF = 2816
NCORE = 8
SEQ = 2048
NSEQ = 2
NB = 4
DT = 4
T = 512
NSA_IN = 2608
NPAGE = 128
WB = 512
EPS = 1e-6
NEGB = -30000.0


class Sched:
    ENG = ('pe', 'act', 'dve', 'pool', 'sp')

    def __init__(self, nc, es):
        self.nc = nc
        self.es = es
        self.prog = {e: [] for e in self.ENG}
        self.tl_sem = {e: es.enter_context(nc.semaphore('tl_' + e)) for e in self.ENG if e != 'sp'}
        self.count = {e: 0 for e in self.tl_sem}
        self.known = {e: {} for e in self.ENG}
        self.bufs = {}
        self.dsem = {}
        self.dcount = {}
        self.pe_strict = False
        import os as _os
        self.same_eng = _os.environ.get('K_SAME_ENG', '1') == '1'
        self.dma_rr = _os.environ.get('K_DMA_RR', '1') == '1'
        self.drr = {}

    DMA_K = {'cast': 8, 'st': 8, 'const': 4, 'cw': 2, 'ldx': 2, 'pg': 8}

    def _dma_sem(self, name):
        if name not in self.drr:
            self.drr[name] = 0
        k = (self.drr[name] % self.DMA_K.get(name, 1)) if self.dma_rr else 0
        self.drr[name] += 1
        sub = f"{name}_{k}"
        if sub not in self.dsem:
            self.dsem[sub] = self.es.enter_context(self.nc.semaphore('d_' + sub))
            self.dcount[sub] = 0
        return sub

    def _need(self, eng, src, val, waits):
        if self.known[eng].get(src, 0) >= val:
            return
        if waits.get(src, 0) < val:
            waits[src] = val

    def op(self, eng, fn, reads=(), writes=(), inc=True, dma=None):
        waits = {}
        for k in reads:
            b = self.bufs.get(k)
            if b and b['w']:
                self._need(eng, b['w'][0], b['w'][1], waits)
        for k in writes:
            b = self.bufs.get(k)
            if b:
                if b['w']:
                    self._need(eng, b['w'][0], b['w'][1], waits)
                for src, val in b['r'].items():
                    self._need(eng, src, val, waits)
        me_src = ('tl', eng)
        if not self.same_eng and me_src in waits:
            rv = 0
            for k in reads:
                b = self.bufs.get(k)
                if b and b['w'] and b['w'][0] == me_src:
                    rv = max(rv, b['w'][1])
            if rv > self.known[eng].get(me_src, 0) and rv <= self.count.get(eng, 0):
                waits[me_src] = rv
            else:
                del waits[me_src]
        if me_src in waits and (waits[me_src] > self.count.get(eng, 0) or eng == 'pe'):
            if eng == 'pe' and waits[me_src] <= self.count['pe'] and self.pe_strict:
                pass
            else:
                del waits[me_src]
        for src, val in waits.items():
            self.known[eng][src] = val
        if dma is not None:
            sub = self._dma_sem(dma)
            if self.dma_rr and self.dcount[sub] > 0 and self.known[eng].get(('d', sub), 0) < self.dcount[sub]:
                waits[('d', sub)] = self.dcount[sub]
                self.known[eng][('d', sub)] = self.dcount[sub]
            sem = self.dsem[sub]
            self.dcount[sub] += 16
            me = (('d', sub), self.dcount[sub])
        else:
            if inc:
                self.count[eng] += 1
                me = (('tl', eng), self.count[eng])
            else:
                me = (('tl', eng), self.count[eng] + 1)
            sem = self.tl_sem[eng]
        for k in reads:
            b = self.bufs.setdefault(k, {'w': None, 'r': {}})
            if b['r'].get(me[0], 0) < me[1]:
                b['r'][me[0]] = me[1]
        for k in writes:
            self.bufs[k] = {'w': me, 'r': {}}
        wl = [((self.tl_sem[s[1]] if s[0] == 'tl' else self.dsem[s[1]]), v) for s, v in waits.items()]
        self.prog[eng].append((wl, fn, sem if (inc or dma is not None) else None, 16 if dma is not None else 1))

    def barrier(self, engines, keys):
        for eng in engines:
            waits = {}
            for k in keys:
                b = self.bufs.get(k)
                if not b:
                    continue
                if b['w']:
                    self._need(eng, b['w'][0], b['w'][1], waits)
                for src, val in b['r'].items():
                    self._need(eng, src, val, waits)
            me_src = ('tl', eng)
            if me_src in waits and waits[me_src] > self.count.get(eng, 0):
                del waits[me_src]
            for src, val in waits.items():
                self.known[eng][src] = val
            wl = [((self.tl_sem[s[1]] if s[0] == 'tl' else self.dsem[s[1]]), v) for s, v in waits.items()]
            if wl:
                self.prog[eng].append((wl, None, None, 0))

    def finish(self):
        wl = [(self.dsem[n], self.dcount[n]) for n in self.dsem]
        self.prog['sp'].append((wl, None, None, 0))

    def emit(self):
        nc = self.nc
        emap = {'pe': 'tensor', 'act': 'scalar', 'dve': 'vector', 'pool': 'gpsimd', 'sp': 'sync'}
        with nc.Block() as block:
            for e in self.ENG:
                prog = self.prog[e]

                def body(engobj, prog=prog):
                    for wl, fn, sem, incv in prog:
                        for s, v in wl:
                            engobj.wait_ge(s, v)
                        if fn is not None:
                            ins = fn(engobj)
                            if sem is not None:
                                ins.then_inc(sem, incv)
                getattr(block, emap[e])(body)


def build_nc(nseq=NSEQ, seqlen=SEQ, with_sample=True, nphys=5120, dbg_stage=9):
    nc = bass.Bass("TRN2", target_bir_lowering=False)
    es = ExitStack()

    def din(name, shape, dt=F32):
        return nc.dram_tensor(name, list(shape), dt, kind="ExternalInput").ap()

    def dout(name, shape, dt=F32):
        return nc.dram_tensor(name, list(shape), dt, kind="ExternalOutput").ap()

    def dscr(name, shape, dt=BF16):
        return nc.dram_tensor(name, list(shape), dt, kind="Internal").ap()

    xp = din("xp", [nseq * seqlen, D])
    xs = din("xs", [NB * DT, D])
    stc = din("stc", [NB, 2, D])
    cache_cmp = din("cache_cmp", [nphys * 128, 512])
    cache_sel = din("cache_sel", [nphys * 128, 512])
    cache_win = din("cache_win", [NB, WB, 512])
    ptab = din("ptab", [NB, NPAGE], I32)
    norm_ffa = din("norm_ffa", [2, D]); norm_mix = din("norm_mix", [2, D]); norm_ffb = din("norm_ffb", [2, D])
    w_ffa_gu = din("w_ffa_gu", [2, D, 2 * FF]); w_ffa_down = din("w_ffa_down", [2, FF, D])
    w_ffb_gu = din("w_ffb_gu", [2, D, 2 * FF]); w_ffb_down = din("w_ffb_down", [2, FF, D])
    w_conv_in = din("w_conv_in", [D, 3 * D]); w_conv = din("w_conv", [3, D]); w_conv_out = din("w_conv_out", [D, D])
    w_nsa_in = din("w_nsa_in", [D, NSA_IN]); pe_cmp = din("pe_cmp", [32, 64])
    w_k1 = din("w_k1", [32, 64, 64]); w_k2 = din("w_k2", [64, 64]); w_v1 = din("w_v1", [32, 64, 64]); w_v2 = din("w_v2", [64, 64])
    w_nsa_out = din("w_nsa_out", [D, D]); norm_final = din("norm_final", [D])

    yp = dout("yp", [nseq * seqlen, D]); ys = dout("ys", [NB * DT, D])
    csp = dout("csp", [nseq, 2, D]); css = dout("css", [NB, 2, D])
    cmp_p = dout("cmp_p", [nseq * seqlen, 512]); cmp_s = dout("cmp_s", [NB * DT, 512])
    sel_p = dout("sel_p", [nseq * seqlen, 512]); sel_s = dout("sel_s", [NB * DT, 512])
    win_p = dout("win_p", [nseq, WB, 512]); win_s = dout("win_s", [NB, WB, 512])

    s_gu = {(l, ab): dscr(f"s_gu{l}{ab}", [22, 128, 8, 2, 128]) for l in range(2) for ab in 'ab'}
    s_dn = {(l, ab): dscr(f"s_dn{l}{ab}", [2, 128, 22, 512]) for l in range(2) for ab in 'ab'}
    s_cin = dscr("s_cin", [8, 128, 8, 3, 128])
    s_cout = dscr("s_cout", [128, 8, D])
    s_nq = dscr("s_nq", [4, 128, 8, 256])
    s_nk = dscr("s_nk", [6, 128, 8, 256])
    s_nkv = dscr("s_nkv", [128, 8, 1536])
    s_ng = dscr("s_ng", [128, 8, 48])
    s_nout = dscr("s_nout", [128, 8, D])

    with es:
        S = Sched(nc, es)

        def sb(name, shape, dt):
            return es.enter_context(nc.sbuf_tensor(name, list(shape), dt))

        def ps(name, shape, dt):
            return es.enter_context(nc.psum_tensor(name, list(shape), dt))

        X = sb("X", [128, 4, D], F32)
        hb = sb("hb", [128, D], BF16)
        junk = sb("junk", [128, D], BF16)
        hT = sb("hT", [128, 8, T], BF16)
        ARENA = 29 * 1024
        arena = sb("arena", [128, ARENA], mybir.dt.uint8)

        def aview(off, shape, dt, parts=128):
            nbytes = int(np.prod(shape)) * (2 if dt == BF16 else 4)
            assert off % 4 == 0 and off + nbytes <= ARENA, (off, nbytes, shape)
            v = arena[0:parts, off:off + nbytes].bitcast(dt)
            if len(shape) == 1:
                return v
            names = " ".join(f"a{i}" for i in range(len(shape)))
            kw = {f"a{i}": shape[i] for i in range(1, len(shape))}
            return v.rearrange(f"p ({names}) -> p {names}", **kw)

        KB = 1024
        import os as _os3
        NOARENA = _os3.environ.get('K_ARENA', '1') == '0'
        if NOARENA:
            aT = sb("aT", [128, 22, T], BF16)
            U = sb("U", [128, 8, T + 2], F32)
            zT = sb("zT", [128, 8, T], BF16)
            cgs = sb("cgs", [128, T], F32)
            cacc = sb("cacc", [128, T], F32)
        else:
            aT = aview(0, [22, T], BF16)
            U = aview(0, [8, T + 2], F32)
            zT = aview(17 * KB, [8, T], BF16)
            cgs = aview(25 * KB, [T], F32)
            cacc = aview(27 * KB, [T], F32)
        Qxb = [aview(0, [4, T], BF16), aview(25 * KB, [4, T], BF16)]
        kcT = aview(4 * KB, [2, T], BF16)
        Ob = aview(6 * KB, [4, D], BF16)
        PTs = [aview(14 * KB + k * KB, [T], BF16) for k in range(3)]
        EcT = [aview(17 * KB + k * KB, [T], BF16) for k in range(2)]
        Ec = aview(19 * KB, [4, 64], F32)
        imp4 = aview(20 * KB, [64], F32)
        imps = aview(20 * KB + 256, [32], F32)
        score = aview(20 * KB + 384, [32], F32)
        work = aview(20 * KB + 512, [32], F32)
        selm = aview(20 * KB + 640, [32], F32)
        m1 = aview(20 * KB + 768, [8], F32)
        m2 = aview(20 * KB + 800, [8], F32)
        sums = aview(20 * KB + 832, [4], F32)
        rsf = aview(20 * KB + 848, [3, 4], F32)
        hx = aview(21 * KB, [16], F32)
        hx2 = aview(21 * KB + 64, [16], F32)
        hx3 = aview(21 * KB + 128, [16], F32)
        hid = aview(21 * KB + 192, [16], BF16)
        tmpA = aview(22 * KB, [4, 64], F32)
        tmpB = aview(23 * KB, [4, 64], F32)
        tmpC = aview(24 * KB, [4, 64], F32)
        ARENA2 = 53632
        arena2 = sb("arena2", [128, ARENA2], mybir.dt.uint8)

        def aview2(off, shape, dt, parts=128):
            nbytes = int(np.prod(shape)) * (2 if dt == BF16 else 4)
            assert off % 4 == 0 and off + nbytes <= ARENA2, (off, nbytes)
            v = arena2[0:parts, off:off + nbytes].bitcast(dt)
            names = " ".join(f"a{i}" for i in range(len(shape)))
            kw = {f"a{i}": shape[i] for i in range(1, len(shape))}
            return v.rearrange(f"p ({names}) -> p {names}", **kw) if len(shape) > 1 else v

        KsT = aview2(0, [4, SEQ], BF16)
        KwT = aview2(16384, [4, 2 * T], BF16)
        Vs = aview2(24576, [16, 4, 66], BF16)
        Vw = aview2(33024, [8, 4, 66], BF16)
        Mc = aview2(37248, [16, 64], F32)
        McT = aview2(41344, [4, T], BF16, parts=64)
        E32 = aview2(45440, [SEQ], BF16, parts=32)
        Acst = aview2(49536, [16, 32], F32)
        Bcst = aview2(51584, [16, 32], F32)
        KcT = sb("KcT", [64, 4, 64], BF16)
        hidV = sb("hidV", [64, 4, 64], BF16)
        Vc = sb("Vc", [64, 4, 66], BF16)
        CG = aview2(0, [8, 512], BF16)
        XT = aview2(8192, [4, 1024], BF16)
        W1bd = [aview2(16384 + 8192 * k, [32, 128], BF16) for k in range(2)]
        hidS = aview2(32768, [4, 512], BF16)
        CGw = aview2(36864, [4, 512], BF16)
        XTw = aview2(40960, [2, 512], BF16)
        mbAll = aview2(43008, [4, 256], F32, parts=16)
        OBall = aview2(47104, [4, 3, 64], F32, parts=16)
        ssumS = aview2(50176, [4, 33], F32, parts=16)
        ObS = aview2(50720, [D], BF16, parts=16)
        Ec16_g = aview2(52768, [192], F32)
        OTk = aview2(0, [3, 16, 64], F32, parts=16)
        KcS = aview(0, [4, 512], BF16, parts=64)
        VcS = aview(4096, [4, 4, 64], BF16)
        IDX = aview(6144, [NB, NPAGE], I32)
        PTF = aview(8192, [NB, NPAGE], F32)
        Ec16 = aview(10240, [512], F32, parts=16)
        SM = aview(12288, [512], F32, parts=16)
        Pb = aview(14336, [512], BF16, parts=16)
        PsT = aview(15360, [4, 16], BF16)
        mb16 = aview(15616, [256], F32, parts=16)
        imp16 = aview(16640, [512], F32, parts=16)
        imps16 = aview(18688, [256], F32, parts=16)
        work16 = aview(19712, [256], F32, parts=16)
        m1s = aview(20736, [8], F32, parts=16)
        m2s = aview(20768, [8], F32, parts=16)
        ssum = aview(20800, [48], F32, parts=16)
        rsS = aview(20992, [4], F32, parts=16)
        QB = aview(21056, [4, 16], BF16, parts=64)
        QBp = aview(21184, [4, 16], BF16)
        OB3 = aview(21312, [3, 64], F32, parts=16)
        Qs = aview(22080, [16, 16], BF16, parts=64)
        KsN = aview(22592, [4, 16], BF16, parts=64)
        KwN = aview(22720, [4, 16], BF16, parts=64)
        VnS = aview(22848, [NB, 4, 64], BF16, parts=4)
        VnW = aview(24896, [NB, 4, 64], BF16, parts=4)
        WM16 = aview(26944, [512], F32, parts=16)
        CM16 = aview(28992, [4], F32, parts=16)
        Gsum = sb("Gsum", [16, 16], BF16)
        Gsumf = sb("Gsumf", [16, 16], F32)
        W2sel = sb("W2sel", [128, 2, 2, 64], BF16)
        peT2 = sb("peT2", [128, 32], F32)
        iot_i = sb("iot_i", [128, 1], I32)
        iot_f = sb("iot_f", [128, 1], F32)
        CM4 = sb("CM4", [4, 4], BF16)
        WM4 = sb("WM4", [4, 512], BF16)
        pt_i = sb("pt_i", [128, NB * NPAGE], I32)
        G = sb("G", [128, 4, 48], F32)
        W1k = None if NOARENA else sb("W1k", [64, 32, 64], BF16); W1v = None if NOARENA else sb("W1v", [64, 32, 64], BF16)
        W2k = sb("W2k", [64, 64], BF16); W2v = sb("W2v", [64, 64], BF16)
        peT = sb("peT", [64, 32], F32)
        pe_nat = sb("pe_nat", [32, 64], F32)
        TriGE = sb("TriGE", [128, 128], BF16); TriLT = sb("TriLT", [128, 128], BF16)
        TriGEb = sb("TriGEb", [128, 128], BF16); TriLTb = sb("TriLTb", [128, 128], BF16)
        SH = sb("SH", [128, 128], BF16)
        MBt = sb("MBt", [128, 128], BF16)
        ucarS = sb("ucarS", [128, NB, 8, 2], F32)
        ringg = [sb(f"ringg{i}", [128, 8 * 384], BF16) for i in range(3)]
        ringb = [sb(f"ringb{i}", [128, 12288], BF16) for i in range(2)]
        ident = sb("ident", [128, 128], BF16)
        identf = sb("identf", [128, 128], F32)
        gT = sb("gT", [128, 6, 8], F32)
        gfin = sb("gfin", [128, D], F32)
        wcv = sb("wcv", [128, 3, 8], F32)
        ss = sb("ss", [128, 4], F32)
        rstd = sb("rstd", [128, 4], F32)
        ucar = sb("ucar", [128, 8, 2], F32)
        sgs = [sb(f"sg{i}", [128, T], F32) for i in range(2)]
        stage = [sb(f"stage{i}", [128, 512], F32) for i in range(2)]

        PT = ps("PT", [128, 1024], BF16)
        PB = [ps(f"PB{i}", [128, 512], F32) for i in range(7)]

        def cast(dst, src, key):
            S.op('pool', lambda e: e.dma_start(out=dst, in_=src), writes=[key], dma='cast')

        def cast_gu(l, ab):
            w = (w_ffa_gu if ab == 'a' else w_ffb_gu)[l]
            for f in range(22):
                for u in range(2):
                    cast(s_gu[(l, ab)][f, :, :, u, :],
                         w[:, u * FF + f * 128:u * FF + (f + 1) * 128].rearrange("(c p) j -> p c j", p=128), ('s_gu', l, ab, f, u))

        def cast_dn(l, ab):
            w = (w_ffa_down if ab == 'a' else w_ffb_down)[l]
            for h in range(2):
                cast(s_dn[(l, ab)][h], w[:, h * 512:(h + 1) * 512].rearrange("(f p) n -> p f n", p=128), ('s_dn', l, ab, h))

        cast_gu(0, 'a'); cast_dn(0, 'a')
        for s3 in range(3):
            for ee in range(8):
                cast(s_cin[ee, :, :, s3, :], w_conv_in[:, s3 * D + ee * 128:s3 * D + (ee + 1) * 128].rearrange("(c p) j -> p c j", p=128), 's_cin')
        cast(s_cout, w_conv_out.rearrange("(c p) n -> p c n", p=128), 's_cout')
        cast_gu(0, 'b'); cast_dn(0, 'b')
        cast_gu(1, 'a'); cast_dn(1, 'a')
        for m in range(4):
            cast(s_nq[m], w_nsa_in[:, m * 256:(m + 1) * 256].rearrange("(c p) j -> p c j", p=128), 's_nq')
        for m in range(6):
            cast(s_nk[m], w_nsa_in[:, 1024 + m * 256:1024 + (m + 1) * 256].rearrange("(c p) j -> p c j", p=128), 's_nk')
        cast(s_nkv, w_nsa_in[:, 1024:2560].rearrange("(c p) n -> p c n", p=128), 's_nkv')
        cast(s_ng, w_nsa_in[:, 2560:2608].rearrange("(c p) n -> p c n", p=128), 's_ng')
        cast(s_nout, w_nsa_out.rearrange("(c p) n -> p c n", p=128), 's_nout')
        cast_gu(1, 'b'); cast_dn(1, 'b')

        S.op('pool', lambda e: e.memset(identf[:], 0.0), writes=['identf'])
        S.op('pool', lambda e: e.affine_select(out=identf[:], in_=identf[:], pattern=[[-1, 128]], compare_op=ALU.not_equal,
                                               fill=1.0, base=0, channel_multiplier=1), reads=['identf'], writes=['identf'])
        S.op('dve', lambda e: e.tensor_copy(out=ident[:], in_=identf[:]), reads=['identf'], writes=['ident'])
        for k, nt in enumerate([norm_ffa, norm_mix, norm_ffb]):
            for l in range(2):
                S.op('sp', lambda e, k=k, l=l, nt=nt: e.dma_start(out=gT[:, 2 * k + l, :], in_=nt[l].rearrange("(c p) -> p c", p=128),
                                                                  allow_slow_non_contiguous=True), writes=['gT'], dma='const')
        S.op('sp', lambda e: e.dma_start(out=gfin[:], in_=norm_final.partition_broadcast(128)), writes=['gfin'], dma='const')
        for i3 in range(3):
            S.op('sp', lambda e, i3=i3: e.dma_start(out=wcv[:, i3, :], in_=w_conv[i3].rearrange("(c p) -> p c", p=128),
                                                    allow_slow_non_contiguous=True), writes=['wcv'], dma='const')

        import os as _os2
        KC = int(_os2.environ.get('K_CONST', '3'))
        if KC >= 1:
            def pool_op(fn, reads=(), writes=()):
                S.op('pool', fn, reads=reads, writes=writes)

            def aff(out_ap, pattern, base, cm, key, fill=0.0, op=ALU.is_ge):
                pool_op(lambda e: e.affine_select(out=out_ap, in_=out_ap, pattern=pattern, compare_op=op, fill=fill, base=base, channel_multiplier=cm),
                        reads=[key], writes=[key])

            pool_op(lambda e: e.memset(TriGE[:], 1.0), writes=['TriGE'])
            aff(TriGE[:], [[1, 128]], 0, -1, 'TriGE')
            pool_op(lambda e: e.memset(TriLT[:], 1.0), writes=['TriLT'])
            aff(TriLT[:], [[-1, 128]], -1, 1, 'TriLT')
            S.op('dve', lambda e: e.tensor_scalar(out=TriGEb[:], in0=TriGE[:], scalar1=-1.0, scalar2=-NEGB, op0=ALU.add, op1=ALU.mult), reads=['TriGE'], writes=['TriGEb'])
            S.op('dve', lambda e: e.tensor_scalar(out=TriLTb[:], in0=TriLT[:], scalar1=-1.0, scalar2=-NEGB, op0=ALU.add, op1=ALU.mult), reads=['TriLT'], writes=['TriLTb'])
            pool_op(lambda e: e.memset(SH[:], 0.0), writes=['SH'])
            aff(SH[:], [[-1, 128]], 64, 1, 'SH', fill=1.0, op=ALU.not_equal)
            pool_op(lambda e: e.memset(E32[:], 1.0), writes=['E32'])
            aff(E32[:], [[1, SEQ]], 0, -64, 'E32')
            aff(E32[:], [[-1, SEQ]], 63, 64, 'E32')
            pool_op(lambda e: e.memset(Mc[:], 1.0), writes=['Mc'])
            for pos in range(16):
                aff(Mc[:, pos, :], [[-32, 64]], 128 * pos - 31, 1, 'Mc')
            pool_op(lambda e: e.memset(McT[:], 1.0), writes=['McT'])
            for i4 in range(4):
                aff(McT[:, i4, :], [[1, T]], T * i4 - 31, -32, 'McT')
            pool_op(lambda e: e.memset(Acst[:], 0.0), writes=['Acst'])
            pool_op(lambda e: e.memset(Bcst[:], -1e4), writes=['Bcst'])
            for pos in range(16):
                for half in range(2):
                    cur = 2 * pos + half
                    pr = slice(64 * half, 64 * half + 64)
                    if cur - 1 > 1:
                        pool_op(lambda e, pr=pr, pos=pos, cur=cur: e.memset(Acst[pr, pos, 1:cur - 1], 1.0), reads=['Acst'], writes=['Acst'])
                        pool_op(lambda e, pr=pr, pos=pos, cur=cur: e.memset(Bcst[pr, pos, 1:cur - 1], 0.0), reads=['Bcst'], writes=['Bcst'])
                    pool_op(lambda e, pr=pr, pos=pos, cur=cur: e.memset(Bcst[pr, pos, max(cur - 1, 0):cur + 1], 1e4), reads=['Bcst'], writes=['Bcst'])
                    pool_op(lambda e, pr=pr, pos=pos: e.memset(Bcst[pr, pos, 0:1], 1e4), reads=['Bcst'], writes=['Bcst'])
            pool_op(lambda e: e.memset(MBt[:], 0.0), writes=['MBt'])
            pool_op(lambda e: e.memset(Vs[:], 1.0), writes=['Vs'])
            pool_op(lambda e: e.memset(Vw[:], 1.0), writes=['Vw'])
            pool_op(lambda e: e.memset(Vc[:], 1.0), writes=['Vc'])
            pool_op(lambda e: e.memset(KsT[:], 0.0), writes=['KsT'])
            pool_op(lambda e: e.memset(KwT[:], 0.0), writes=['KwT'])
            pool_op(lambda e: e.memset(KcT[:], 0.0), writes=['KcT'])
            pool_op(lambda e: e.memset(hidV[:], 0.0), writes=['hidV'])
        if KC >= 2:
            S.op('pool', lambda e: e.dma_start(out=W1k[:], in_=w_k1.rearrange("l d e -> d l e")), writes=['W1k'], dma='cw')
            S.op('pool', lambda e: e.dma_start(out=W1v[:], in_=w_v1.rearrange("l d e -> d l e")), writes=['W1v'], dma='cw')
            S.op('pool', lambda e: e.dma_start(out=W2k[:], in_=w_k2), writes=['W2k'], dma='cw')
            S.op('pool', lambda e: e.dma_start(out=W2v[:], in_=w_v2), writes=['W2v'], dma='cw')
            S.op('sp', lambda e: e.dma_start(out=pe_nat[:], in_=pe_cmp), writes=['pe_nat'], dma='const')
            S.op('pe', lambda e: e.transpose(out=PB[1][0:64, 0:32], in_=pe_nat[:, :], identity=identf[0:32, 0:32]), reads=['pe_nat', 'identf'], writes=[('PB', 1)])
            S.op('dve', lambda e: e.tensor_copy(out=peT[:, :], in_=PB[1][0:64, 0:32]), reads=[('PB', 1)], writes=['peT'])
        if KC >= 3:
            for kc4 in range(SEQ // 512):
                S.op('pe', lambda e, kc4=kc4: e.matmul(PB[0][:, :], lhsT=SH[0:32, :], rhs=E32[:, kc4 * 512:(kc4 + 1) * 512], start=True, stop=True),
                     reads=['SH', 'E32'], writes=[('PB', 0)])
                for kv in range(4):
                    S.op('dve', lambda e, kc4=kc4, kv=kv: e.tensor_copy(out=KsT[64:96, kv, kc4 * 512:(kc4 + 1) * 512], in_=PB[0][64:96, :]),
                         reads=[('PB', 0)], writes=['KsT', 'KsTE'])

        rg = {'i': 0}
        rb = {'i': 0}

        def load_g(src_ap, ncols, skey):
            slot = rg['i'] % 3
            rg['i'] += 1
            dst = ringg[slot][:, 0:8 * ncols]
            S.op('sp', lambda e: e.dma_start(out=dst, in_=src_ap), reads=(skey if isinstance(skey, list) else [skey]), writes=[('rg', slot)], dma=f'rg{slot}')
            return ringg[slot][:, 0:8 * ncols].rearrange("p (c n) -> p c n", c=8), ('rg', slot)

        def load_b(src_ap, nelem, skey):
            slot = rb['i'] % 2
            rb['i'] += 1
            dst = ringb[slot][:, 0:nelem]
            S.op('sp', lambda e: e.dma_start(out=dst, in_=src_ap), reads=(skey if isinstance(skey, list) else [skey]), writes=[('rb', slot)], dma=f'rb{slot}')
            return ringb[slot], ('rb', slot)

        class Stream:
            def __init__(self, specs, ahead):
                self.specs = specs
                self.loaded = []
                self.ahead = ahead

            def get(self, i):
                while len(self.loaded) < min(len(self.specs), i + 1 + self.ahead):
                    self.loaded.append(self.specs[len(self.loaded)]())
                return self.loaded[i]

        bank_rr = {'i': 0}

        def bank(n=7):
            b = bank_rr['i'] % n
            bank_rr['i'] += 1
            return PB[b], ('PB', b)

        def norm_hT(nsub, npart, gi):
            for j in range(nsub):
                S.op('act', lambda e, j=j: e.activation(out=junk[:npart, :], in_=X[:npart, j, :], func=AF.Square, scale=1.0 / 32.0,
                                                         accum_out=ss[:npart, j:j + 1]), reads=['X'], writes=['junk', 'ss'])
            S.op('dve', lambda e: e.tensor_scalar(out=rstd[:npart, :nsub], in0=ss[:npart, :nsub], scalar1=EPS, scalar2=None, op0=ALU.add),
                 reads=['ss'], writes=['rstd'])
            S.op('act', lambda e: e.activation(out=rstd[:npart, :nsub], in_=rstd[:npart, :nsub], func=AF.Sqrt), reads=['rstd'], writes=['rstd'])
            S.op('dve', lambda e: e.reciprocal(out=rstd[:npart, :nsub], in_=rstd[:npart, :nsub]), reads=['rstd'], writes=['rstd'])
            for j in range(nsub):
                S.op('dve', lambda e, j=j: e.tensor_scalar(out=hb[:npart, :], in0=X[:npart, j, :], scalar1=rstd[:npart, j:j + 1], scalar2=None,
                                                           op0=ALU.mult), reads=['X', 'rstd'], writes=['hb'])
                for c in range(8):
                    S.op('pe', lambda e, c=c: e.transpose(out=PT[:, c * 128:c * 128 + npart], in_=hb[:npart, c * 128:(c + 1) * 128],
                                                          identity=ident[:npart, :npart]), reads=['hb', 'ident'], writes=['PT'], inc=(c == 7))
                S.op('dve', lambda e, j=j: e.tensor_tensor(out=hT[:, :, j * npart:(j + 1) * npart],
                                                           in0=PT[:, :].rearrange("p (c n) -> p c n", c=8)[:, :, 0:npart],
                                                           in1=gT[:, gi, :].unsqueeze(2).to_broadcast([128, 8, npart]), op=ALU.mult),
                     reads=['PT', 'gT'], writes=['hT'])

        def out_proj_add(srcT, wfun, nsub, npart, scale):
            nchunk = srcT.shape[1]
            for h in range(2):
                wap, wkey = wfun(h)
                for j in range(nsub):
                    pb, pk = bank()
                    for f in range(nchunk):
                        S.op('pe', lambda e, f=f, j=j, pb=pb, wap=wap: e.matmul(pb[:npart, :], lhsT=srcT[:, f, j * npart:(j + 1) * npart], rhs=wap[:, f, :],
                                                                             start=(f == 0), stop=(f == nchunk - 1)),
                             reads=[wkey, 'srcT'], writes=[pk], inc=(f == nchunk - 1))
                    S.op('dve', lambda e, j=j, h=h, pb=pb: e.scalar_tensor_tensor(out=X[:npart, j, h * 512:(h + 1) * 512], in0=pb[:npart, :], scalar=scale,
                                                                                  in1=X[:npart, j, h * 512:(h + 1) * 512], op0=ALU.mult, op1=ALU.add),
                         reads=[pk, 'X'], writes=['X'])

        def ffn(l, ab, nsub, npart):
            TT = nsub * npart
            gi = (0 if ab == 'a' else 4) + l
            norm_hT(nsub, npart, gi)
            sg_key = (l, ab)
            gus = Stream([(lambda f=f: load_g(s_gu[(l, ab)][f].rearrange("p c u j -> p (c u j)"), 256, [('s_gu', l, ab, f, 0), ('s_gu', l, ab, f, 1)])) for f in range(22)], 2)
            dns = Stream([(lambda h=h: load_b(s_dn[(l, ab)][h].rearrange("p f n -> p (f n)"), 22 * 512, [('s_dn', l, ab, h)])) for h in range(2)], 1)
            for f in range(22):
                wg, wk = gus.get(f)
                if f == 18:
                    dns.get(0)
                pg, pgk = bank()
                pu, puk = bank()
                for u, (pp, ppk) in enumerate([(pg, pgk), (pu, puk)]):
                    for c in range(8):
                        S.op('pe', lambda e, c=c, u=u, pp=pp, wg=wg: e.matmul(pp[:, :TT], lhsT=wg[:, c, u * 128:(u + 1) * 128], rhs=hT[:, c, :TT],
                                                                            start=(c == 0), stop=(c == 7)),
                             reads=[wk, 'hT'], writes=[ppk], inc=(c == 7))
                sg = sgs[f % 2]
                S.op('act', lambda e, pg=pg, sg=sg: e.activation(out=sg[:, :TT], in_=pg[:, :TT], func=AF.Silu), reads=[pgk], writes=[('sg', f % 2)])
                S.op('dve', lambda e, f=f, pu=pu, sg=sg: e.tensor_tensor(out=aT[:, f, :TT], in0=sg[:, :TT], in1=pu[:, :TT], op=ALU.mult),
                     reads=[('sg', f % 2), puk], writes=['srcT'])

            def wfun(h):
                ap, k = dns.get(h)
                return ap[:, 0:22 * 512].rearrange("p (f n) -> p f n", f=22), k
            out_proj_add(aT, wfun, nsub, npart, 0.5)

        def conv_layer(nsub, npart, first, sample, cs_out):
            TT = nsub * npart
            norm_hT(nsub, npart, 2)
            if first and not sample:
                S.op('pool', lambda e: e.memset(ucar[:], 0.0), writes=['ucar'])
            cins = Stream([(lambda ee=ee: load_g(s_cin[ee].rearrange("p c s j -> p (c s j)"), 384, 's_cin')) for ee in range(8)], 2)
            couts = Stream([lambda: load_b(s_cout.rearrange("p c n -> p (c n)"), 8 * D, 's_cout')], 0)
            for ee in range(8):
                w, wk = cins.get(ee)
                if ee == 5:
                    couts.get(0)
                w4 = w.rearrange("p c (s j) -> p c s j", s=3)
                pbs = []
                for s3 in range(3):
                    pb, pk = bank()
                    for c in range(8):
                        S.op('pe', lambda e, c=c, s3=s3, pb=pb, w4=w4: e.matmul(pb[:, :TT], lhsT=w4[:, c, s3, :], rhs=hT[:, c, :TT], start=(c == 0), stop=(c == 7)),
                             reads=[wk, 'hT'], writes=[pk], inc=(c == 7))
                    pbs.append((pb, pk))
                (pbg, kbg), (pcg, kcg), (pv, kv_) = pbs
                S.op('act', lambda e, pcg=pcg: e.activation(out=cgs[:, :TT], in_=pcg[:, :TT], func=AF.Copy), reads=[kcg], writes=['cgs'])
                if not sample:
                    S.op('dve', lambda e, ee=ee: e.tensor_copy(out=U[:, ee, 0:2], in_=ucar[:, ee, :]), reads=['ucar'], writes=['U'])
                    S.op('dve', lambda e, ee=ee, pv=pv: e.tensor_tensor(out=U[:, ee, 2:2 + TT], in0=cgs[:, :TT], in1=pv[:, :TT], op=ALU.mult),
                         reads=['cgs', kv_], writes=['U'])
                    segs = [(0, TT, 0)]
                else:
                    S.op('dve', lambda e, ee=ee: e.tensor_copy(out=U[:, ee, 0:6 * NB].rearrange("p (b k) -> p b k", k=6)[:, :, 0:2], in_=ucarS[:, :, ee, :]),
                         reads=['ucar'], writes=['U'])
                    S.op('dve', lambda e, ee=ee, pv=pv: e.tensor_tensor(out=U[:, ee, 0:6 * NB].rearrange("p (b k) -> p b k", k=6)[:, :, 2:6],
                                                                     in0=cgs[:, 0:TT].rearrange("p (b k) -> p b k", k=DT),
                                                                     in1=pv[:, 0:TT].rearrange("p (b k) -> p b k", k=DT), op=ALU.mult),
                         reads=['cgs', kv_], writes=['U'])
                    segs = [(6 * b, DT, DT * b) for b in range(NB)]
                for (u0, n, o0) in segs:
                    S.op('dve', lambda e, ee=ee, u0=u0, n=n, o0=o0: e.tensor_scalar(out=cacc[:, o0:o0 + n], in0=U[:, ee, u0 + 2:u0 + 2 + n], scalar1=wcv[:, 2, ee:ee + 1],
                                                                                    scalar2=None, op0=ALU.mult), reads=['U', 'wcv'], writes=['cacc'])
                    for i3 in (1, 0):
                        S.op('dve', lambda e, ee=ee, u0=u0, n=n, o0=o0, i3=i3: e.scalar_tensor_tensor(out=cacc[:, o0:o0 + n], in0=U[:, ee, u0 + i3:u0 + i3 + n],
                                                                                                       scalar=wcv[:, i3, ee:ee + 1], in1=cacc[:, o0:o0 + n],
                                                                                                       op0=ALU.mult, op1=ALU.add),
                             reads=['U', 'wcv', 'cacc'], writes=['cacc'])
                S.op('dve', lambda e, ee=ee, pbg=pbg: e.tensor_tensor(out=zT[:, ee, :TT], in0=cacc[:, :TT], in1=pbg[:, :TT], op=ALU.mult),
                     reads=['cacc', kbg], writes=['srcT'])
                if not sample:
                    S.op('pool', lambda e, ee=ee: e.tensor_copy(out=ucar[:, ee, :], in_=U[:, ee, TT:TT + 2]), reads=['U'], writes=['ucar'])
            if cs_out is not None:
                if not sample:
                    for t2 in range(2):
                        S.op('sp', lambda e, t2=t2: e.dma_start(out=cs_out[t2].rearrange("(c p) -> p c", p=128), in_=ucar[:, :, t2], allow_slow_non_contiguous=True),
                             reads=['ucar'], writes=['csout'], dma='st')
                else:
                    for b in range(NB):
                        for t2 in range(2):
                            S.op('sp', lambda e, b=b, t2=t2: e.dma_start(out=cs_out[b, t2].rearrange("(c p) -> p c", p=128), in_=U[:, :, 6 * b + 4 + t2],
                                                                         allow_slow_non_contiguous=True), reads=['U'], writes=['csout'], dma='st')

            def wfun(h):
                ap, k = couts.get(0)
                return ap[:, 0:8 * D].rearrange("p (c n) -> p c n", c=8)[:, :, h * 512:(h + 1) * 512], k
            out_proj_add(zT, wfun, nsub, npart, 1.0)

        def nsa_rows(nsub, npart, row0, outs, win_out, i=None):
            kvs = Stream([lambda: load_b(s_nkv.rearrange("p c n -> p (c n)"), 8 * 1536, 's_nkv')], 0)
            wap, wk = kvs.get(0)
            w3 = wap[:, 0:8 * 1536].rearrange("p (c n) -> p c n", c=8)
            for br in range(3):
                for j in range(nsub):
                    pb, pk = bank()
                    for c in range(8):
                        S.op('pe', lambda e, c=c, j=j, br=br, pb=pb: e.matmul(pb[:npart, :], lhsT=hT[:, c, j * npart:(j + 1) * npart], rhs=w3[:, c, br * 512:(br + 1) * 512],
                                                                           start=(c == 0), stop=(c == 7)), reads=[wk, 'hT'], writes=[pk], inc=(c == 7))
                    st = stage[j % 2]
                    S.op('act', lambda e, pb=pb, st=st: e.activation(out=st[:npart, :], in_=pb[:npart, :], func=AF.Copy), reads=[pk], writes=[('stage', j % 2)])
                    if i is not None and br == 1:
                        S.op('pool', lambda e, st=st, j=j: e.tensor_copy(out=Vs[:, 4 * i + j, :, 0:64], in_=st[:, 256:512].rearrange("p (k d) -> p k d", k=4)),
                             reads=[('stage', j % 2)], writes=['Vs'])
                    if i is not None and br == 2:
                        S.op('pool', lambda e, st=st, j=j: e.tensor_copy(out=Vw[:, (i % 2) * 4 + j, :, 0:64], in_=st[:, 256:512].rearrange("p (k d) -> p k d", k=4)),
                             reads=[('stage', j % 2)], writes=['Vw'])
                    r0 = row0 + j * npart
                    if br < 2:
                        S.op('sp', lambda e, st=st, br=br, r0=r0: e.dma_start(out=outs[br][r0:r0 + npart, :], in_=st[:npart, :]),
                             reads=[('stage', j % 2)], writes=['kvout'], dma='st')
                    elif win_out is not None:
                        for (oap, iap) in win_out(j, st):
                            S.op('sp', lambda e, oap=oap, iap=iap: e.dma_start(out=oap, in_=iap), reads=[('stage', j % 2)], writes=['kvout'], dma='st')

        NSA_KEYS = [('Qx', 0), ('Qx', 1), 'kcT', 'Ob', ('PTs', 0), ('PTs', 1), ('PTs', 2), ('EcT', 0), ('EcT', 1), 'Ec', 'imp', 'hx', 't123', 'rsf']

        def gelu_tanh(dst_bf16, src_f32, n, dkey='hx'):
            S.op('dve', lambda e: e.tensor_tensor(out=hx2[0:64, :n], in0=src_f32, in1=src_f32, op=ALU.mult), reads=['hx'], writes=['hx'])
            S.op('dve', lambda e: e.tensor_scalar(out=hx2[0:64, :n], in0=hx2[0:64, :n], scalar1=0.044715, scalar2=1.0, op0=ALU.mult, op1=ALU.add), reads=['hx'], writes=['hx'])
            S.op('dve', lambda e: e.tensor_tensor(out=hx2[0:64, :n], in0=hx2[0:64, :n], in1=src_f32, op=ALU.mult), reads=['hx'], writes=['hx'])
            S.op('act', lambda e: e.activation(out=hx3[0:64, :n], in_=hx2[0:64, :n], func=AF.Sigmoid, scale=1.5957691216), reads=['hx'], writes=['hx'])
            S.op('dve', lambda e: e.tensor_tensor(out=dst_bf16, in0=hx3[0:64, :n], in1=src_f32, op=ALU.mult), reads=['hx'], writes=[dkey])

        def nsa_prompt(i, row0, outs, win_out):
            t0 = i * T
            nb = 16 * (i + 1)
            slot = i % 2
            pslot = (i - 1) % 2
            norm_hT(4, 128, 3)
            nsa_rows(4, 128, row0, outs, win_out, i=i)
            if dbg_stage < 1:
                return
            gw, gk = load_g(s_ng.rearrange("p c n -> p (c n)"), 48, 's_ng')
            for j in range(4):
                pb, pk = bank(4)
                for c in range(8):
                    S.op('pe', lambda e, c=c, j=j, pb=pb: e.matmul(pb[:, 0:48], lhsT=hT[:, c, j * 128:(j + 1) * 128], rhs=gw[:, c, :], start=(c == 0), stop=(c == 7)),
                         reads=[gk, 'hT'], writes=[pk], inc=(c == 7))
                S.op('act', lambda e, j=j, pb=pb: e.activation(out=G[:, j, :], in_=pb[:, 0:48], func=AF.Sigmoid), reads=[pk], writes=['G'])
            for (m, dstT, c0) in ((2, KsT, t0), (4, KwT, slot * T)):
                w, wk = load_g(s_nk[m].rearrange("p c j -> p (c j)"), 256, 's_nk')
                for kv in range(4):
                    pb, pk = bank(4)
                    for c in range(8):
                        S.op('pe', lambda e, c=c, kv=kv, pb=pb, w=w: e.matmul(pb[0:64, :], lhsT=w[:, c, kv * 64:(kv + 1) * 64], rhs=hT[:, c, :], start=(c == 0), stop=(c == 7)),
                             reads=[wk, 'hT'], writes=[pk], inc=(c == 7))
                    S.op('dve', lambda e, kv=kv, pb=pb, dstT=dstT, c0=c0: e.tensor_copy(out=dstT[0:64, kv, c0:c0 + T], in_=pb[0:64, :]),
                         reads=[pk], writes=['KsT' if m == 2 else 'KwT'])
            ACCC, ACCS, ACCW = (PB[4], ('PB', 4)), (PB[5], ('PB', 5)), (PB[6], ('PB', 6))
            def do_kv(kv):
                Qx = Qxb[kv % 2]
                qxk = ('Qx', kv % 2)
                if dbg_stage < 2:
                    return None
                wkc, wkck = load_g(s_nk[0].rearrange("p c j -> p (c j)"), 256, 's_nk')
                wvc, wvck = load_g(s_nk[1].rearrange("p c j -> p (c j)"), 256, 's_nk')
                for kk, (w, wk) in enumerate(((wkc, wkck), (wvc, wvck))):
                    pb, pk = bank(4)
                    for c in range(8):
                        S.op('pe', lambda e, c=c, pb=pb, w=w: e.matmul(pb[0:64, :], lhsT=w[:, c, kv * 64:(kv + 1) * 64], rhs=hT[:, c, :], start=(c == 0), stop=(c == 7)),
                             reads=[wk, 'hT'], writes=[pk], inc=(c == 7))
                    S.op('dve', lambda e, kk=kk, pb=pb: e.tensor_tensor(out=kcT[0:64, kk, :].rearrange("p (n l) -> p n l", l=32),
                                                                        in0=pb[0:64, :].rearrange("p (n l) -> p n l", l=32),
                                                                        in1=peT[:, :].unsqueeze(1).to_broadcast([64, 16, 32]), op=ALU.add),
                         reads=[pk, 'peT'], writes=['kcT'])
                for kk, W1 in enumerate((W1k, W1v)):
                    pb, pk = bank(4)
                    kc3 = kcT[0:64, kk, :].rearrange("p (n l) -> p n l", l=32)
                    for l in range(32):
                        S.op('pe', lambda e, l=l, pb=pb, W1=W1, kc3=kc3: e.matmul(pb[0:64, 0:16], lhsT=W1[:, l, :], rhs=kc3[:, :, l], start=(l == 0), stop=(l == 31)),
                             reads=['kcT', 'W1k', 'W1v'], writes=[pk], inc=(l == 31))
                    S.op('act', lambda e, kk=kk, pb=pb: e.activation(out=hx[0:64, :], in_=pb[0:64, 0:16], func=AF.Copy), reads=[pk], writes=['hx'])
                    if kk == 0:
                        gelu_tanh(hid[0:64, :], hx[0:64, :], 16)
                        pb2, pk2 = bank(4)
                        S.op('pe', lambda e, pb2=pb2: e.matmul(pb2[0:64, 0:16], lhsT=W2k[:, :], rhs=hid[0:64, :], start=True, stop=True), reads=['hx', 'W2k'], writes=[pk2])
                        S.op('dve', lambda e, pb2=pb2: e.tensor_copy(out=KcT[:, kv, 16 * i:16 * i + 16], in_=pb2[0:64, 0:16]), reads=[pk2], writes=['KcT'])
                    else:
                        gelu_tanh(hidV[:, kv, 16 * i:16 * i + 16], hx[0:64, :], 16, dkey='hidV')
                        pb2, pk2 = bank(4)
                        S.op('pe', lambda e, pb2=pb2: e.matmul(pb2[0:nb, 0:64], lhsT=hidV[:, kv, 0:nb], rhs=W2v[:, :], start=True, stop=True), reads=['hidV', 'W2v'], writes=[pk2])
                        S.op('dve', lambda e, pb2=pb2: e.tensor_copy(out=Vc[0:nb, kv, 0:64], in_=pb2[0:nb, 0:64]), reads=[pk2], writes=['Vc'])
                if dbg_stage < 3:
                    return None
                qw, qk = load_g(s_nq[kv].rearrange("p c j -> p (c j)"), 256, 's_nq')
                for g in range(4):
                    pb, pk = bank(4)
                    for c in range(8):
                        S.op('pe', lambda e, c=c, g=g, pb=pb: e.matmul(pb[0:64, :], lhsT=qw[:, c, g * 64:(g + 1) * 64], rhs=hT[:, c, :], start=(c == 0), stop=(c == 7)),
                             reads=[qk, 'hT'], writes=[pk], inc=(c == 7))
                    S.op('act', lambda e, g=g, pb=pb: e.activation(out=Qx[0:64, g, :], in_=pb[0:64, :], func=AF.Copy, scale=0.125), reads=[pk], writes=[qxk])
                for j in range(4 if dbg_stage >= 4 else 0):
                    pos = 4 * i + j
                    pb, pk = bank(4)
                    for g in range(4):
                        S.op('pe', lambda e, g=g, j=j, pb=pb: e.matmul(pb[:, g * 64:g * 64 + nb], lhsT=Qx[0:64, g, j * 128:(j + 1) * 128], rhs=KcT[:, kv, 0:nb], start=True, stop=True),
                             reads=[qxk, 'KcT'], writes=[pk], inc=(g == 3))
                    pb3 = pb[:, 0:256].rearrange("p (g n) -> p g n", g=4)
                    S.op('act', lambda e, pb3=pb3: e.activation(out=Ec[:, :, 0:nb], in_=pb3[:, :, 0:nb], func=AF.Exp), reads=[pk], writes=['Ec'])
                    S.op('dve', lambda e, pos=pos: e.tensor_tensor(out=Ec[:, :, 0:nb], in0=Ec[:, :, 0:nb], in1=Mc[:, pos, 0:nb].unsqueeze(1).to_broadcast([128, 4, nb]), op=ALU.mult),
                         reads=['Ec', 'Mc'], writes=['Ec'])
                    S.op('dve', lambda e: e.tensor_reduce(out=sums[:, :], in_=Ec[:, :, 0:nb], axis=AX.X, op=ALU.add), reads=['Ec'], writes=['imp'])
                    S.op('dve', lambda e: e.tensor_scalar(out=sums[:, :], in0=sums[:, :], scalar1=1e-30, scalar2=None, op0=ALU.max), reads=['imp'], writes=['imp'])
                    S.op('dve', lambda e: e.reciprocal(out=sums[:, :], in_=sums[:, :]), reads=['imp'], writes=['imp'])
                    S.op('dve', lambda e: e.tensor_tensor(out=Ec[:, :, 0:nb], in0=Ec[:, :, 0:nb], in1=sums[:, :].unsqueeze(2).to_broadcast([128, 4, nb]), op=ALU.mult),
                         reads=['Ec', 'imp'], writes=['Ec'])
                    S.op('dve', lambda e: e.tensor_reduce(out=imp4[:, 0:nb], in_=Ec[:, :, 0:nb].rearrange("p g n -> p n g"), axis=AX.X, op=ALU.add), reads=['Ec'], writes=['imp'])
                    S.op('dve', lambda e: e.memset(imps[:, :], 0.0), reads=['imp'], writes=['imp'])
                    i2 = imp4[:, 0:nb].rearrange("p (m two) -> p m two", two=2)
                    S.op('dve', lambda e, i2=i2: e.tensor_tensor(out=imps[:, 0:nb // 2], in0=i2[:, :, 0], in1=i2[:, :, 1], op=ALU.add), reads=['imp'], writes=['imp'])
                    S.op('dve', lambda e, pos=pos: e.tensor_tensor(out=score[:, :], in0=imps[:, :], in1=Acst[:, pos, :], op=ALU.mult), reads=['imp', 'Acst'], writes=['imp'])
                    S.op('dve', lambda e, pos=pos: e.tensor_tensor(out=score[:, :], in0=score[:, :], in1=Bcst[:, pos, :], op=ALU.add), reads=['imp', 'Bcst'], writes=['imp'])
                    S.op('dve', lambda e: e.max(out=m1[:, :], in_=score[:, :]), reads=['imp'], writes=['imp'])
                    S.op('dve', lambda e: e.match_replace(out=work[:, :], in_to_replace=m1[:, :], in_values=score[:, :], imm_value=-3e4), reads=['imp'], writes=['imp'])
                    S.op('dve', lambda e: e.max(out=m2[:, :], in_=work[:, :]), reads=['imp'], writes=['imp'])
                    S.op('dve', lambda e: e.tensor_scalar(out=selm[:, :], in0=score[:, :], scalar1=m2[:, 7:8], scalar2=None, op0=ALU.is_ge), reads=['imp'], writes=['imp'])
                    S.op('dve', lambda e: e.tensor_scalar(out=MBt[:, 64:96], in0=selm[:, :], scalar1=-1.0, scalar2=-NEGB, op0=ALU.add, op1=ALU.mult), reads=['imp'], writes=['MBt'])
                    S.op('pe', lambda e: e.transpose(out=PT[:, 0:128], in_=MBt[:, :], identity=ident[:, :]), reads=['MBt', 'ident'], writes=['PT'])
                    S.op('dve', lambda e, j=j: e.tensor_copy(out=Qx[64:96, :, j * 128:(j + 1) * 128], in_=PT[64:96, 0:128].unsqueeze(1).to_broadcast([32, 4, 128])),
                         reads=['PT'], writes=[qxk])
                def do_head(g):
                    h = kv * 4 + g
                    stages = []
                    ec = EcT[g % 2]; eck = ('EcT', g % 2)

                    def cmpA():
                        pb, pk = bank(4)
                        S.op('pe', lambda e, pb=pb: e.matmul(pb[0:nb, :], lhsT=KcT[:, kv, 0:nb], rhs=Qx[0:64, g, :], start=True, stop=True), reads=[qxk, 'KcT'], writes=[pk])
                        S.op('act', lambda e, pb=pb: e.activation(out=ec[0:nb, :], in_=pb[0:nb, :], func=AF.Exp), reads=[pk], writes=[eck])
                        S.op('dve', lambda e: e.tensor_tensor(out=ec[0:nb, :], in0=ec[0:nb, :], in1=McT[0:nb, i, :], op=ALU.mult), reads=[eck, 'McT'], writes=[eck])

                    def cmpB():
                        for j in range(4):
                            S.op('pe', lambda e, j=j: e.matmul(ACCC[0][:, j * 128:j * 128 + 65], lhsT=ec[0:nb, j * 128:(j + 1) * 128], rhs=Vc[0:nb, kv, 0:65], start=True, stop=True),
                                 reads=[eck, 'Vc'], writes=[ACCC[1]], inc=(j == 3))
                    stages.append((cmpA, cmpB))
                    rr = 0
                    for kc in range(4 * i + 4):
                        md = kc - 4 * i
                        c0 = 128 * md if md > 0 else 0
                        ptk = rr % 3; rr += 1

                        def selA(kc=kc, md=md, c0=c0, ptk=ptk):
                            pb, pk = bank(4)
                            pt = PTs[ptk]
                            S.op('pe', lambda e, pb=pb: e.matmul(pb[:, c0:T], lhsT=KsT[0:96, kv, kc * 128:(kc + 1) * 128], rhs=Qx[0:96, g, c0:T], start=True, stop=(md < 0)),
                                 reads=[qxk, 'KsT', 'KsTE'], writes=[pk], inc=(md < 0))
                            if md >= 0:
                                S.op('pe', lambda e, pb=pb: e.matmul(pb[:, c0:c0 + 128], lhsT=ident[:, :], rhs=TriGEb[:, :], start=False, stop=True),
                                     reads=['ident', 'TriGEb'], writes=[pk])
                            S.op('act', lambda e, pb=pb, pt=pt: e.activation(out=pt[:, c0:T], in_=pb[:, c0:T], func=AF.Exp), reads=[pk], writes=[('PTs', ptk)])

                        def selB(kc=kc, md=md, ptk=ptk):
                            pt = PTs[ptk]
                            for j in range(max(md, 0), 4):
                                S.op('pe', lambda e, j=j, pt=pt: e.matmul(ACCS[0][:, j * 128:j * 128 + 65], lhsT=pt[:, j * 128:(j + 1) * 128], rhs=Vs[:, kc, kv, 0:65],
                                                                        start=(kc == 0 and j == 0), stop=(kc == 4 * i + 3 and j == 3)),
                                     reads=[('PTs', ptk), 'Vs'], writes=[ACCS[1]], inc=(j == 3))
                        stages.append((selA, selB))
                    for m in range(0 if i > 0 else 4, 8):
                        if m < 4:
                            kcol = pslot * T + m * 128; vch = pslot * 4 + m
                            ca, cb = 0, 128 * (m + 1); mcol = 128 * m; js = range(0, m + 1)
                        else:
                            kcol = slot * T + (m - 4) * 128; vch = slot * 4 + (m - 4)
                            ca, cb = 128 * (m - 4), T; mcol = 128 * (m - 4); js = range(m - 4, 4)
                        trib = TriLTb if m < 4 else TriGEb
                        ptk = rr % 3; rr += 1

                        def winA(kcol=kcol, ca=ca, cb=cb, mcol=mcol, trib=trib, ptk=ptk):
                            pb, pk = bank(4)
                            pt = PTs[ptk]
                            S.op('pe', lambda e, pb=pb: e.matmul(pb[:, ca:cb], lhsT=KwT[0:64, kv, kcol:kcol + 128], rhs=Qx[0:64, g, ca:cb], start=True, stop=False),
                                 reads=[qxk, 'KwT'], writes=[pk], inc=False)
                            S.op('pe', lambda e, pb=pb: e.matmul(pb[:, mcol:mcol + 128], lhsT=ident[:, :], rhs=trib[:, :], start=False, stop=True),
                                 reads=['ident', 'TriGEb', 'TriLTb'], writes=[pk])
                            S.op('act', lambda e, pb=pb, pt=pt: e.activation(out=pt[:, ca:cb], in_=pb[:, ca:cb], func=AF.Exp), reads=[pk], writes=[('PTs', ptk)])

                        def winB(m=m, js=js, vch=vch, ptk=ptk):
                            pt = PTs[ptk]
                            for j in js:
                                S.op('pe', lambda e, j=j, pt=pt: e.matmul(ACCW[0][:, j * 128:j * 128 + 65], lhsT=pt[:, j * 128:(j + 1) * 128], rhs=Vw[:, vch, kv, 0:65],
                                                                        start=(m == (0 if i > 0 else 4) and j == 0), stop=(m == 7 and j == 3)),
                                     reads=[('PTs', ptk), 'Vw'], writes=[ACCW[1]], inc=(j == js[-1]))
                        stages.append((winA, winB))
                    LA = 2
                    for n in range(len(stages) + LA):
                        if n < len(stages):
                            stages[n][0]()
                        if n >= LA:
                            stages[n - LA][1]()
                    tmps = (tmpA, tmpB, tmpC)
                    for br, (acc, acck) in enumerate((ACCC, ACCS, ACCW)):
                        a3 = acc[:, :].rearrange("p (j d) -> p j d", j=4)
                        S.op('dve', lambda e, br=br, a3=a3: e.tensor_scalar(out=rsf[:, br, :], in0=a3[:, :, 64], scalar1=1e-30, scalar2=None, op0=ALU.max), reads=[acck], writes=['rsf'])
                        S.op('dve', lambda e, br=br: e.reciprocal(out=rsf[:, br, :], in_=rsf[:, br, :]), reads=['rsf'], writes=['rsf'])
                        S.op('dve', lambda e, br=br: e.tensor_tensor(out=rsf[:, br, :], in0=rsf[:, br, :], in1=G[:, :, br * 16 + h], op=ALU.mult), reads=['rsf', 'G'], writes=['rsf'])
                        S.op('dve', lambda e, br=br, a3=a3: e.tensor_tensor(out=tmps[br][:, :, :], in0=a3[:, :, 0:64], in1=rsf[:, br, :].unsqueeze(2).to_broadcast([128, 4, 64]), op=ALU.mult),
                             reads=[acck, 'rsf'], writes=['t123'])
                    S.op('dve', lambda e: e.tensor_tensor(out=tmpA[:, :, :], in0=tmpA[:, :, :], in1=tmpB[:, :, :], op=ALU.add), reads=['t123'], writes=['t123'])
                    S.op('dve', lambda e, h=h: e.tensor_tensor(out=Ob[:, :, h * 64:(h + 1) * 64], in0=tmpA[:, :, :], in1=tmpC[:, :, :], op=ALU.add), reads=['t123'], writes=['Ob'])
                return do_head
            heads = {}
            for kv in range(5):
                if kv < 4:
                    heads[kv] = do_kv(kv)
                if kv >= 1 and heads[kv - 1] is not None:
                    for g in range(4 if dbg_stage >= 5 else 0):
                        heads[kv - 1](g)
            for j in range(4):
                for c in range(8):
                    S.op('pe', lambda e, c=c, j=j: e.transpose(out=PT[:, c * 128:(c + 1) * 128], in_=Ob[:, j, c * 128:(c + 1) * 128], identity=ident[:, :]),
                         reads=['Ob', 'ident'], writes=['PT'], inc=(c == 7))
                S.op('dve', lambda e, j=j: e.tensor_copy(out=hT[:, :, j * 128:(j + 1) * 128], in_=PT[:, :].rearrange("p (c n) -> p c n", c=8)), reads=['PT'], writes=['srcT'])
            nouts = Stream([lambda: load_b(s_nout.rearrange("p c n -> p (c n)"), 8 * D, 's_nout')], 0)

            def wfun(hh):
                ap, k = nouts.get(0)
                return ap[:, 0:8 * D].rearrange("p (c n) -> p c n", c=8)[:, :, hh * 512:(hh + 1) * 512], k
            out_proj_add(hT, wfun, 4, 128, 1.0)

        o_scr = dscr("o_scr", [3, NB, 4, 4, DT, 64], F32)

        def nsa_sample(row0, outs, win_out):
            SALL = ['pe', 'act', 'dve', 'pool', 'sp']
            norm_hT(1, 16, 3)
            nsa_rows(1, 16, row0, outs, win_out)
            S.barrier(SALL, list(S.bufs.keys()))
            S.op('pool', lambda e: e.memset(Gsumf[:], 0.0), writes=['Gsum'])
            for k in range(-3, 4):
                S.op('pool', lambda e, k=k: e.affine_select(out=Gsumf[:], in_=Gsumf[:], pattern=[[-1, 16]], compare_op=ALU.not_equal, fill=1.0,
                                                            base=-4 * k, channel_multiplier=1), reads=['Gsum'], writes=['Gsum'])
            S.op('dve', lambda e: e.tensor_copy(out=Gsum[:], in_=Gsumf[:]), reads=['Gsum'], writes=['Gsumb'])
            S.op('pool', lambda e: e.memset(CM4[:], 0.0), writes=['CM4'])
            S.op('pool', lambda e: e.affine_select(out=CM4[:], in_=CM4[:], pattern=[[-1, 4]], compare_op=ALU.is_ge, fill=NEGB, base=0, channel_multiplier=1),
                 reads=['CM4'], writes=['CM4'])
            S.op('pool', lambda e: e.memset(WM4[:], 0.0), writes=['WM4'])
            S.op('pool', lambda e: e.affine_select(out=WM4[:, 0:4], in_=WM4[:, 0:4], pattern=[[1, 4]], compare_op=ALU.is_ge, fill=NEGB, base=-1, channel_multiplier=-1),
                 reads=['WM4'], writes=['WM4'])
            pb, pk = bank(4)
            S.op('pe', lambda e, pb=pb: e.matmul(pb[0:16, 0:4], lhsT=Gsum[0:4, :], rhs=CM4[:, :], start=True, stop=True), reads=['Gsumb', 'CM4'], writes=[pk])
            S.op('dve', lambda e, pb=pb: e.tensor_copy(out=CM16[:, :], in_=pb[0:16, 0:4]), reads=[pk], writes=['CM16'])
            pb, pk = bank(4)
            S.op('pe', lambda e, pb=pb: e.matmul(pb[0:16, :], lhsT=Gsum[0:4, :], rhs=WM4[:, :], start=True, stop=True), reads=['Gsumb', 'WM4'], writes=[pk])
            S.op('dve', lambda e, pb=pb: e.tensor_copy(out=WM16[:, :], in_=pb[0:16, :]), reads=[pk], writes=['WM16'])
            for k in range(2):
                S.op('pool', lambda e, k=k: e.memset(W1bd[k][:, :, :], 0.0), writes=[('W1bd', k)])
                src = (w_k1, w_v1)[k].rearrange("l d e -> d l e")
                S.op('pool', lambda e, k=k, src=src: e.dma_start(out=W1bd[k][0:64, :, 0:64], in_=src), reads=[('W1bd', k)], writes=[('W1bd', k)], dma='cw')
                S.op('pool', lambda e, k=k, src=src: e.dma_start(out=W1bd[k][64:128, :, 64:128], in_=src), reads=[('W1bd', k)], writes=[('W1bd', k)], dma='cw')
            S.op('pool', lambda e: e.memset(W2sel[:], 0.0), writes=['W2sel'])
            for k in range(2):
                src = (w_k2, w_v2)[k]
                S.op('pool', lambda e, k=k, src=src: e.dma_start(out=W2sel[0:64, k, 0, :], in_=src), reads=['W2sel'], writes=['W2sel'], dma='cw')
                S.op('pool', lambda e, k=k, src=src: e.dma_start(out=W2sel[64:128, k, 1, :], in_=src), reads=['W2sel'], writes=['W2sel'], dma='cw')
            S.op('sp', lambda e: e.dma_start(out=peT2[0:64, :], in_=peT[:, :]), reads=['peT'], writes=['peT2'], dma='const')
            S.op('sp', lambda e: e.dma_start(out=peT2[64:128, :], in_=peT[:, :]), reads=['peT'], writes=['peT2'], dma='const')
            S.op('pool', lambda e: e.dma_start(out=pt_i[:, :], in_=ptab.rearrange("b n -> (b n)").partition_broadcast(128)), writes=['pt_i'], dma='cw')
            S.op('pool', lambda e: e.iota(iot_i[:], pattern=[[0, 1]], base=0, channel_multiplier=1), writes=['iot'])
            S.op('dve', lambda e: e.tensor_copy(out=iot_f[:], in_=iot_i[:]), reads=['iot'], writes=['iotf'])
            S.op('dve', lambda e: e.tensor_copy(out=PTF[:, :, :].rearrange("p b n -> p (b n)"), in_=pt_i[:, :]), reads=['pt_i'], writes=['PTF'])
            S.op('dve', lambda e: e.tensor_scalar(out=PTF[:, :, :], in0=PTF[:, :, :], scalar1=128.0, scalar2=iot_f[:, 0:1], op0=ALU.mult, op1=ALU.add),
                 reads=['PTF', 'iotf'], writes=['PTF'])
            S.op('dve', lambda e: e.tensor_copy(out=IDX[:, :, :], in_=PTF[:, :, :]), reads=['PTF'], writes=['IDX'])
            gw, gk = load_g(s_ng.rearrange("p c n -> p (c n)"), 48, 's_ng')
            pb, pk = bank(4)
            for c in range(8):
                S.op('pe', lambda e, c=c, pb=pb: e.matmul(pb[0:16, 0:48], lhsT=hT[:, c, 0:16], rhs=gw[:, c, :], start=(c == 0), stop=(c == 7)), reads=[gk, 'hT'], writes=[pk], inc=(c == 7))
            S.op('act', lambda e, pb=pb: e.activation(out=G[0:16, 0, :], in_=pb[0:16, 0:48], func=AF.Sigmoid), reads=[pk], writes=['G'])
            for (m, dstT, key) in ((2, KsN, 'KsN'), (4, KwN, 'KwN')):
                w, wk = load_g(s_nk[m].rearrange("p c j -> p (c j)"), 256, 's_nk')
                for kv in range(4):
                    pb, pk = bank(4)
                    for c in range(8):
                        S.op('pe', lambda e, c=c, kv=kv, pb=pb, w=w: e.matmul(pb[0:64, 0:16], lhsT=w[:, c, kv * 64:(kv + 1) * 64], rhs=hT[:, c, 0:16], start=(c == 0), stop=(c == 7)),
                             reads=[wk, 'hT'], writes=[pk], inc=(c == 7))
                    S.op('dve', lambda e, kv=kv, pb=pb, dstT=dstT: e.tensor_copy(out=dstT[:, kv, :], in_=pb[0:64, 0:16]), reads=[pk], writes=[key])
            for kv in range(4):
                qw, qk = load_g(s_nq[kv].rearrange("p c j -> p (c j)"), 256, 's_nq')
                for g in range(4):
                    pb, pk = bank(4)
                    for c in range(8):
                        S.op('pe', lambda e, c=c, g=g, pb=pb, qw=qw: e.matmul(pb[0:64, 0:16], lhsT=qw[:, c, g * 64:(g + 1) * 64], rhs=hT[:, c, 0:16], start=(c == 0), stop=(c == 7)),
                             reads=[qk, 'hT'], writes=[pk], inc=(c == 7))
                    S.op('act', lambda e, g=g, kv=kv, pb=pb: e.activation(out=Qs[:, kv * 4 + g, :], in_=pb[0:64, 0:16], func=AF.Copy, scale=0.125), reads=[pk], writes=['Qs'])
            wap, wk3 = load_b(s_nkv.rearrange("p c n -> p (c n)"), 8 * 1536, 's_nkv')
            w3 = wap[:, 0:8 * 1536].rearrange("p (c n) -> p c n", c=8)
            for b in range(NB):
                for (br, dstV, key) in ((1, VnS, 'VnS'), (2, VnW, 'VnW')):
                    pb, pk = bank(4)
                    for c in range(8):
                        S.op('pe', lambda e, c=c, b=b, br=br, pb=pb: e.matmul(pb[0:4, 0:256], lhsT=hT[:, c, 4 * b:4 * b + 4], rhs=w3[:, c, br * 512 + 256:br * 512 + 512],
                                                                           start=(c == 0), stop=(c == 7)), reads=[wk3, 'hT'], writes=[pk], inc=(c == 7))
                    S.op('dve', lambda e, b=b, pb=pb, dstV=dstV: e.tensor_copy(out=dstV[:, b, :, :], in_=pb[0:4, 0:256].rearrange("p (k d) -> p k d", k=4)), reads=[pk], writes=[key])

            def transposeP(src16, ncols, dst):
                nk = ncols // 128
                for k4 in range(nk):
                    S.op('pe', lambda e, k4=k4: e.transpose(out=PT[:, k4 * 16:(k4 + 1) * 16], in_=src16[0:16, k4 * 128:(k4 + 1) * 128], identity=ident[0:16, 0:16]),
                         reads=['Pb', 'ident'], writes=['PT'], inc=(k4 == nk - 1))
                S.op('dve', lambda e: e.tensor_copy(out=dst[:, 0:nk, :], in_=PT[:, 0:nk * 16].rearrange("p (k q) -> p k q", q=16)), reads=['PT'], writes=['PsT'])

            ACC = [(PB[4], ('PB', 4)), (PB[5], ('PB', 5)), (PB[6], ('PB', 6))]
            ACCS4 = [PB[3], PB[4], PB[5], PB[6]]
            SHq = [ident[0:64, :], SH[0:64, :]]

            def gather_group(cache, b, grp, npg=8, slot0=0):
                for k in range(npg):
                    p = grp * npg + k
                    S.op('pool', lambda e, k=k, p=p: e.indirect_dma_start(out=CG[:, slot0 + k, :], out_offset=None, in_=cache[:, :],
                                                                          in_offset=bass.IndirectOffsetOnAxis(ap=IDX[:, b, p:p + 1], axis=0)),
                         reads=['IDX'], writes=[('CG', slot0 + k)], dma='pg')

            def transpose_group(ccs, add_pe, npair=4, slot0=0, xoff=0, xkey='XT'):
                for k2 in range(npair):
                    for ci, cc in enumerate(ccs):
                        for pg2 in range(2):
                            k = slot0 + 2 * k2 + pg2
                            S.op('pe', lambda e, k=k, cc=cc, ci=ci, pg2=pg2: e.transpose(out=PT[:, (ci * 2 + pg2) * 128:(ci * 2 + pg2 + 1) * 128], in_=CG[:, k, cc * 128:(cc + 1) * 128], identity=ident[:, :]),
                                 reads=[('CG', k), 'ident'], writes=['PT'], inc=(ci == len(ccs) - 1 and pg2 == 1))
                    src = PT[:, 0:len(ccs) * 256].rearrange("p (c r) -> p c r", c=len(ccs))
                    dst = XT[:, 0:len(ccs), xoff + k2 * 256:xoff + (k2 + 1) * 256]
                    if add_pe:
                        S.op('dve', lambda e, src=src, dst=dst: e.tensor_tensor(out=dst.rearrange("p c (n l) -> p c n l", l=32), in0=src.rearrange("p c (n l) -> p c n l", l=32),
                                                                                in1=peT2[:, :].unsqueeze(1).unsqueeze(1).to_broadcast([128, len(ccs), 8, 32]), op=ALU.add),
                             reads=['PT', 'peT2'], writes=[xkey])
                    else:
                        S.op('dve', lambda e, src=src, dst=dst: e.tensor_copy(out=dst, in_=src), reads=['PT'], writes=[xkey])

            for b in range(NB):
                for grp in range(16):
                    gather_group(cache_cmp, b, grp)
                    transpose_group([0, 1, 2, 3], True)
                    for cc in range(4):
                        pb, pk = bank(4)
                        x3 = XT[:, cc, :].rearrange("p (n l) -> p n l", l=32)
                        for l in range(32):
                            S.op('pe', lambda e, l=l, cc=cc, pb=pb, x3=x3: e.matmul(pb[:, 0:32], lhsT=W1bd[cc // 2][:, l, :], rhs=x3[:, :, l], start=(l == 0), stop=(l == 31)),
                                 reads=['XT', ('W1bd', 0), ('W1bd', 1)], writes=[pk], inc=(l == 31))
                        S.op('act', lambda e, pb=pb: e.activation(out=Ec16_g[:, 0:32], in_=pb[:, 0:32], func=AF.Copy), reads=[pk], writes=['gel'])
                        S.op('dve', lambda e: e.tensor_tensor(out=Ec16_g[:, 64:96], in0=Ec16_g[:, 0:32], in1=Ec16_g[:, 0:32], op=ALU.mult), reads=['gel'], writes=['gel'])
                        S.op('dve', lambda e: e.tensor_scalar(out=Ec16_g[:, 64:96], in0=Ec16_g[:, 64:96], scalar1=0.044715, scalar2=1.0, op0=ALU.mult, op1=ALU.add), reads=['gel'], writes=['gel'])
                        S.op('dve', lambda e: e.tensor_tensor(out=Ec16_g[:, 64:96], in0=Ec16_g[:, 64:96], in1=Ec16_g[:, 0:32], op=ALU.mult), reads=['gel'], writes=['gel'])
                        S.op('act', lambda e: e.activation(out=Ec16_g[:, 128:160], in_=Ec16_g[:, 64:96], func=AF.Sigmoid, scale=1.5957691216), reads=['gel'], writes=['gel'])
                        S.op('dve', lambda e, cc=cc, grp=grp: e.tensor_tensor(out=hidS[:, cc, grp * 32:(grp + 1) * 32], in0=Ec16_g[:, 128:160], in1=Ec16_g[:, 0:32], op=ALU.mult),
                             reads=['gel'], writes=['hidS'])
                for kv in range(4):
                    pb, pk = bank(4)
                    S.op('pe', lambda e, kv=kv, pb=pb: e.matmul(pb[0:64, :], lhsT=W2sel[:, 0, kv % 2, :], rhs=hidS[:, kv // 2, :], start=True, stop=True), reads=['hidS', 'W2sel'], writes=[pk])
                    S.op('dve', lambda e, kv=kv, pb=pb: e.tensor_copy(out=KcS[:, kv, :], in_=pb[0:64, :]), reads=[pk], writes=['KcS'])
                    pb, pk = bank(4)
                    for q4 in range(4):
                        S.op('pe', lambda e, kv=kv, q4=q4, pb=pb: e.matmul(pb[:, q4 * 64:(q4 + 1) * 64], lhsT=hidS[:, 2 + kv // 2, q4 * 128:(q4 + 1) * 128], rhs=W2sel[:, 1, kv % 2, :], start=True, stop=True),
                             reads=['hidS', 'W2sel'], writes=[pk], inc=(q4 == 3))
                    S.op('dve', lambda e, kv=kv, pb=pb: e.tensor_copy(out=VcS[:, :, kv, :], in_=pb[:, 0:256].rearrange("p (q d) -> p q d", q=4)), reads=[pk], writes=['VcS'])
                for q4 in range(4):
                    S.op('pool', lambda e, q4=q4, b=b: e.dma_start(out=CGw[:, q4, :], in_=cache_win[b, q4 * 128:(q4 + 1) * 128, :]), writes=['CGw'], dma='pg')
                for kv in range(4):
                    S.op('dve', lambda e, kv=kv, b=b: e.tensor_copy(out=QB[:, kv, :].rearrange("p (g t) -> p g t", g=4), in_=Qs[:, kv * 4:(kv + 1) * 4, 4 * b:4 * b + 4]), reads=['Qs'], writes=['QB'])
                    pb, pk = bank(4)
                    S.op('pe', lambda e, kv=kv, pb=pb: e.matmul(pb[:, 0:16], lhsT=SHq[kv % 2], rhs=QB[:, kv, :], start=True, stop=True), reads=['QB', 'SH', 'ident'], writes=[pk])
                    S.op('dve', lambda e, kv=kv, pb=pb: e.tensor_copy(out=QBp[:, kv, :], in_=pb[:, 0:16]), reads=[pk], writes=['QBp'])
                    pb, pk = bank(4)
                    S.op('pe', lambda e, kv=kv, pb=pb: e.matmul(pb[0:16, :], lhsT=QB[:, kv, :], rhs=KcS[:, kv, :], start=True, stop=True), reads=['QB', 'KcS'], writes=[pk])
                    S.op('act', lambda e, pb=pb: e.activation(out=Ec16[:, :], in_=pb[0:16, :], func=AF.Exp, accum_out=rsS[:, 0:1]), reads=[pk], writes=['Ec16', 'rsS'])
                    S.op('dve', lambda e: e.tensor_copy(out=Pb[:, :], in_=Ec16[:, :]), reads=['Ec16'], writes=['Pb'])
                    S.op('dve', lambda e: e.reciprocal(out=rsS[:, 0:1], in_=rsS[:, 0:1]), reads=['rsS'], writes=['rsS'])
                    S.op('dve', lambda e: e.tensor_scalar(out=Ec16[:, :], in0=Ec16[:, :], scalar1=rsS[:, 0:1], scalar2=None, op0=ALU.mult), reads=['Ec16', 'rsS'], writes=['Ec16'])
                    pb, pk = bank(4)
                    S.op('pe', lambda e, pb=pb: e.matmul(pb[0:16, :], lhsT=Gsumf[:, :], rhs=Ec16[:, :], start=True, stop=True), reads=['Ec16', 'Gsum'], writes=[pk])
                    p3 = pb[0:16, :].rearrange("p (m two) -> p m two", two=2)
                    S.op('dve', lambda e, p3=p3: e.tensor_copy(out=imp16[:, :].rearrange("p (m two) -> p m two", two=2), in_=p3), reads=[pk], writes=['imp16'])
                    i3 = imp16[:, :].rearrange("p (m two) -> p m two", two=2)
                    S.op('dve', lambda e, i3=i3: e.tensor_tensor(out=imps16[:, :], in0=i3[:, :, 0], in1=i3[:, :, 1], op=ALU.add), reads=['imp16'], writes=['imps16'])
                    S.op('dve', lambda e: e.memset(imps16[:, 0:1], 1e4), reads=['imps16'], writes=['imps16'])
                    S.op('dve', lambda e: e.memset(imps16[:, 255:256], 1e4), reads=['imps16'], writes=['imps16'])
                    S.op('dve', lambda e: e.max(out=m1s[:, :], in_=imps16[:, :]), reads=['imps16'], writes=['m1s'])
                    S.op('dve', lambda e: e.match_replace(out=work16[:, :], in_to_replace=m1s[:, :], in_values=imps16[:, :], imm_value=-3e4), reads=['imps16', 'm1s'], writes=['work16'])
                    S.op('dve', lambda e: e.max(out=m2s[:, :], in_=work16[:, :]), reads=['work16'], writes=['m2s'])
                    S.op('dve', lambda e: e.tensor_scalar(out=mb16[:, :], in0=imps16[:, :], scalar1=m2s[:, 6:7], scalar2=None, op0=ALU.is_ge), reads=['imps16', 'm2s'], writes=['mb16'])
                    S.op('dve', lambda e: e.tensor_scalar(out=mb16[:, :], in0=mb16[:, :], scalar1=-1.0, scalar2=-NEGB, op0=ALU.add, op1=ALU.mult), reads=['mb16'], writes=['mb16'])
                    transposeP(Pb, 512, PsT)
                    for q4 in range(4):
                        S.op('pe', lambda e, kv=kv, q4=q4: e.matmul(ACC[0][0][0:16, 0:64], lhsT=PsT[:, q4, :], rhs=VcS[:, q4, kv, :], start=(q4 == 0), stop=(q4 == 3)),
                             reads=['PsT', 'VcS'], writes=[ACC[0][1]], inc=(q4 == 3))
                    S.op('dve', lambda e: e.tensor_scalar(out=OB3[:, 0, :], in0=ACC[0][0][0:16, 0:64], scalar1=rsS[:, 0:1], scalar2=None, op0=ALU.mult), reads=[ACC[0][1], 'rsS'], writes=['OB3'])
                    if kv == 0:
                        for q4 in range(4):
                            for c2 in range(2):
                                S.op('pe', lambda e, q4=q4, c2=c2: e.transpose(out=PT[:, (q4 * 2 + c2) * 128:(q4 * 2 + c2 + 1) * 128], in_=CGw[:, q4, c2 * 128:(c2 + 1) * 128], identity=ident[:, :]),
                                     reads=['CGw', 'ident'], writes=['PT'], inc=(q4 == 3 and c2 == 1))
                        S.op('dve', lambda e: e.tensor_copy(out=XTw[:, :, :].rearrange("p c (q r) -> p q c r", q=4), in_=PT[:, :].rearrange("p (q c r) -> p q c r", q=4, c=2)), reads=['PT'], writes=['XTw'])
                    ncol = 0
                    pb, pk = bank(4)
                    S.op('pe', lambda e, kv=kv, pb=pb: e.matmul(pb[0:16, :], lhsT=QBp[:, kv, :], rhs=XTw[:, kv // 2, :], start=True, stop=True), reads=['QBp', 'XTw'], writes=[pk])
                    S.op('dve', lambda e, pb=pb: e.tensor_tensor(out=SM[:, :], in0=pb[0:16, :], in1=WM16[:, :], op=ALU.add), reads=[pk, 'WM16'], writes=['SM'])
                    S.op('act', lambda e: e.activation(out=Pb[:, :], in_=SM[:, :], func=AF.Exp, accum_out=ssum[:, 0:1]), reads=['SM'], writes=['Pb', 'ssum'])
                    transposeP(Pb, 512, PsT)
                    for q4 in range(4):
                        S.op('pe', lambda e, kv=kv, q4=q4: e.matmul(ACC[2][0][0:16, 0:64], lhsT=PsT[:, q4, :], rhs=CGw[:, q4, 256 + kv * 64:256 + (kv + 1) * 64], start=(q4 == 0), stop=False),
                             reads=['PsT', 'CGw'], writes=[ACC[2][1]], inc=(q4 == 3))
                    pb, pk = bank(4)
                    S.op('pe', lambda e, kv=kv, b=b, pb=pb: e.matmul(pb[0:16, 0:4], lhsT=QB[:, kv, :], rhs=KwN[:, kv, 4 * b:4 * b + 4], start=True, stop=True), reads=['QB', 'KwN'], writes=[pk])
                    S.op('dve', lambda e, pb=pb: e.tensor_tensor(out=SM[:, 0:4], in0=pb[0:16, 0:4], in1=CM16[:, :], op=ALU.add), reads=[pk, 'CM16'], writes=['SM'])
                    S.op('act', lambda e: e.activation(out=Pb[:, 0:4], in_=SM[:, 0:4], func=AF.Exp, accum_out=ssum[:, 1:2]), reads=['SM'], writes=['Pb', 'ssum'])
                    S.op('pe', lambda e: e.transpose(out=PT[0:4, 0:16], in_=Pb[0:16, 0:4], identity=ident[0:16, 0:16]), reads=['Pb', 'ident'], writes=['PT'])
                    S.op('dve', lambda e: e.tensor_copy(out=PsT[0:4, 0, :], in_=PT[0:4, 0:16]), reads=['PT'], writes=['PsT'])
                    S.op('pe', lambda e, kv=kv, b=b: e.matmul(ACC[2][0][0:16, 0:64], lhsT=PsT[0:4, 0, :], rhs=VnW[:, b, kv, :], start=False, stop=True), reads=['PsT', 'VnW'], writes=[ACC[2][1]])
                    S.op('dve', lambda e: e.tensor_tensor(out=rsS[:, 2:3], in0=ssum[:, 0:1], in1=ssum[:, 1:2], op=ALU.add), reads=['ssum'], writes=['rsS'])
                    S.op('dve', lambda e: e.reciprocal(out=rsS[:, 2:3], in_=rsS[:, 2:3]), reads=['rsS'], writes=['rsS'])
                    S.op('dve', lambda e: e.tensor_scalar(out=OB3[:, 2, :], in0=ACC[2][0][0:16, 0:64], scalar1=rsS[:, 2:3], scalar2=None, op0=ALU.mult), reads=[ACC[2][1], 'rsS'], writes=['OB3'])
                    S.op('dve', lambda e, kv=kv: e.tensor_copy(out=mbAll[:, kv, :], in_=mb16[:, :]), reads=['mb16'], writes=['mbAll'])
                    S.op('dve', lambda e, kv=kv: e.tensor_copy(out=OBall[:, kv, 0, :], in_=OB3[:, 0, :]), reads=['OB3'], writes=['OBall'])
                    S.op('dve', lambda e, kv=kv: e.tensor_copy(out=OBall[:, kv, 2, :], in_=OB3[:, 2, :]), reads=['OB3'], writes=['OBall'])
                ncols = {kv: 0 for kv in range(4)}
                for grp in range(32):
                    hf = grp % 2
                    xk = ('XTs', hf)
                    gather_group(cache_sel, b, grp, npg=4, slot0=4 * hf)
                    transpose_group([0, 1], False, npair=2, slot0=4 * hf, xoff=512 * hf, xkey=xk)
                    for kv in range(4):
                        pb, pk = bank(3)
                        S.op('pe', lambda e, kv=kv, hf=hf, pb=pb: e.matmul(pb[0:16, :], lhsT=QBp[:, kv, :], rhs=XT[:, kv // 2, hf * 512:(hf + 1) * 512], start=True, stop=True),
                             reads=['QBp', xk], writes=[pk])
                        blk0 = grp * 8
                        S.op('dve', lambda e, kv=kv, pb=pb, blk0=blk0: e.tensor_tensor(out=SM[:, :].rearrange("p (n l) -> p n l", l=64), in0=pb[0:16, :].rearrange("p (n l) -> p n l", l=64),
                                                                                     in1=mbAll[:, kv, blk0:blk0 + 8].unsqueeze(2).to_broadcast([16, 8, 64]), op=ALU.add),
                             reads=[pk, 'mbAll'], writes=['SM'])
                        S.op('act', lambda e, kv=kv, grp=grp: e.activation(out=Pb[:, :], in_=SM[:, :], func=AF.Exp, accum_out=ssumS[:, kv, grp:grp + 1]), reads=['SM'], writes=['Pb', 'ssumS'])
                        transposeP(Pb, 512, PsT)
                        for q4 in range(4):
                            first = (grp == 0 and q4 == 0)
                            S.op('pe', lambda e, kv=kv, q4=q4, hf=hf, first=first: e.matmul(ACCS4[kv][0:16, 0:64], lhsT=PsT[:, q4, :], rhs=CG[:, 4 * hf + q4, 256 + kv * 64:256 + (kv + 1) * 64],
                                                                                           start=first, stop=False),
                                 reads=['PsT', ('CG', 4 * hf + q4)], writes=[('PB', 3 + kv)], inc=(q4 == 3))
                for kv in range(4):
                    S.op('dve', lambda e, kv=kv, b=b: e.tensor_copy(out=QB[:, kv, :].rearrange("p (g t) -> p g t", g=4), in_=Qs[:, kv * 4:(kv + 1) * 4, 4 * b:4 * b + 4]), reads=['Qs'], writes=['QB'])
                    pb, pk = bank(3)
                    S.op('pe', lambda e, kv=kv, b=b, pb=pb: e.matmul(pb[0:16, 0:4], lhsT=QB[:, kv, :], rhs=KsN[:, kv, 4 * b:4 * b + 4], start=True, stop=True), reads=['QB', 'KsN'], writes=[pk])
                    S.op('dve', lambda e, pb=pb: e.tensor_tensor(out=SM[:, 0:4], in0=pb[0:16, 0:4], in1=CM16[:, :], op=ALU.add), reads=[pk, 'CM16'], writes=['SM'])
                    S.op('act', lambda e, kv=kv: e.activation(out=Pb[:, 0:4], in_=SM[:, 0:4], func=AF.Exp, accum_out=ssumS[:, kv, 32:33]), reads=['SM'], writes=['Pb', 'ssumS'])
                    S.op('pe', lambda e: e.transpose(out=PT[0:4, 0:16], in_=Pb[0:16, 0:4], identity=ident[0:16, 0:16]), reads=['Pb', 'ident'], writes=['PT'])
                    S.op('dve', lambda e: e.tensor_copy(out=PsT[0:4, 0, :], in_=PT[0:4, 0:16]), reads=['PT'], writes=['PsT'])
                    S.op('pe', lambda e, kv=kv, b=b: e.matmul(ACCS4[kv][0:16, 0:64], lhsT=PsT[0:4, 0, :], rhs=VnS[:, b, kv, :], start=False, stop=True), reads=['PsT', 'VnS'], writes=[('PB', 3 + kv)])
                    S.op('dve', lambda e, kv=kv: e.tensor_reduce(out=rsS[:, 1:2], in_=ssumS[:, kv, 0:33], axis=AX.X, op=ALU.add), reads=['ssumS'], writes=['rsS'])
                    S.op('dve', lambda e: e.reciprocal(out=rsS[:, 1:2], in_=rsS[:, 1:2]), reads=['rsS'], writes=['rsS'])
                    S.op('dve', lambda e, kv=kv: e.tensor_scalar(out=OBall[:, kv, 1, :], in0=ACCS4[kv][0:16, 0:64], scalar1=rsS[:, 1:2], scalar2=None, op0=ALU.mult), reads=[('PB', 3 + kv), 'rsS'], writes=['OBall'])
                for kv in range(4):
                    for br in range(3):
                        S.op('sp', lambda e, kv=kv, br=br, b=b: e.dma_start(out=o_scr[br, b, kv].rearrange("g t d -> (g t) d"), in_=OBall[:, kv, br, :]), reads=['OBall'], writes=['o_scr'], dma='st')
            S.barrier(SALL, [('CG', k) for k in range(8)] + ['XT', ('XTs', 0), ('XTs', 1), 'hidS', ('W1bd', 0), ('W1bd', 1), 'CGw', 'XTw'])
            for br in range(3):
                for b in range(NB):
                    for t in range(DT):
                        S.op('sp', lambda e, br=br, b=b, t=t: e.dma_start(out=OTk[4 * b + t:4 * b + t + 1, br, :, :], in_=o_scr[br, b, :, :, t, :].rearrange("k g d -> (k g) d").unsqueeze(0)),
                             reads=['o_scr'], writes=['OTk'], dma='ldx')
            for br in range(3):
                S.op('dve', lambda e, br=br: e.tensor_tensor(out=OTk[:, br, :, :], in0=OTk[:, br, :, :], in1=G[0:16, 0, br * 16:(br + 1) * 16].unsqueeze(2).to_broadcast([16, 16, 64]), op=ALU.mult),
                     reads=['OTk', 'G'], writes=['OTk'])
            S.op('dve', lambda e: e.tensor_tensor(out=OTk[:, 0, :, :], in0=OTk[:, 0, :, :], in1=OTk[:, 1, :, :], op=ALU.add), reads=['OTk'], writes=['OTk'])
            S.op('dve', lambda e: e.tensor_tensor(out=ObS[:, :].rearrange("p (h d) -> p h d", h=16), in0=OTk[:, 0, :, :], in1=OTk[:, 2, :, :], op=ALU.add), reads=['OTk'], writes=['ObS'])
            for c in range(8):
                S.op('pe', lambda e, c=c: e.transpose(out=PT[:, c * 128:c * 128 + 16], in_=ObS[0:16, c * 128:(c + 1) * 128], identity=ident[0:16, 0:16]), reads=['ObS', 'ident'], writes=['PT'], inc=(c == 7))
            S.op('dve', lambda e: e.tensor_copy(out=hT[:, :, 0:16], in_=PT[:, :].rearrange("p (c n) -> p c n", c=8)[:, :, 0:16]), reads=['PT'], writes=['srcT'])
            nouts = Stream([lambda: load_b(s_nout.rearrange("p c n -> p (c n)"), 8 * D, 's_nout')], 0)

            def wfun(hh):
                ap, k = nouts.get(0)
                return ap[:, 0:8 * D].rearrange("p (c n) -> p c n", c=8)[:, :, hh * 512:(hh + 1) * 512], k
            out_proj_add(hT, wfun, 1, 16, 1.0)

        def final_norm(nsub, npart, out_ap):
            for j in range(nsub):
                S.op('act', lambda e, j=j: e.activation(out=junk[:npart, :], in_=X[:npart, j, :], func=AF.Square, scale=1.0 / 32.0,
                                                         accum_out=ss[:npart, j:j + 1]), reads=['X'], writes=['junk', 'ss'])
            S.op('dve', lambda e: e.tensor_scalar(out=rstd[:npart, :nsub], in0=ss[:npart, :nsub], scalar1=EPS, scalar2=None, op0=ALU.add),
                 reads=['ss'], writes=['rstd'])
            S.op('act', lambda e: e.activation(out=rstd[:npart, :nsub], in_=rstd[:npart, :nsub], func=AF.Sqrt), reads=['rstd'], writes=['rstd'])
            S.op('dve', lambda e: e.reciprocal(out=rstd[:npart, :nsub], in_=rstd[:npart, :nsub]), reads=['rstd'], writes=['rstd'])
            for j in range(nsub):
                S.op('dve', lambda e, j=j: e.scalar_tensor_tensor(out=X[:npart, j, :], in0=X[:npart, j, :], scalar=rstd[:npart, j:j + 1], in1=gfin[:npart, :],
                                                                  op0=ALU.mult, op1=ALU.mult), reads=['X', 'rstd', 'gfin'], writes=['X'])
            S.op('sp', lambda e: e.dma_start(out=out_ap, in_=X[:npart, 0:nsub, :]), reads=['X'], writes=['yout'], dma='st')

        ARENA_KEYS = ['srcT', 'U', 'cgs', 'cacc']
        ALLENG = ['pe', 'act', 'dve', 'pool', 'sp']

        def run_tile(nsub, npart, x_ap, y_ap, first, sample, cs_out, row0, kv_outs, win_out, tile_i=None):
            S.op('sp', lambda e: e.dma_start(out=X[:npart, 0:nsub, :], in_=x_ap), writes=['X'], dma='ldx')
            ffn(0, 'a', nsub, npart)
            S.barrier(ALLENG, ARENA_KEYS)
            conv_layer(nsub, npart, first, sample, cs_out)
            S.barrier(ALLENG, ARENA_KEYS)
            ffn(0, 'b', nsub, npart)
            ffn(1, 'a', nsub, npart)
            S.barrier(ALLENG, ARENA_KEYS)
            if sample:
                nsa_sample(row0, kv_outs, win_out)
            else:
                nsa_prompt(tile_i, row0, kv_outs, win_out)
            S.barrier(ALLENG, ARENA_KEYS + NSA_KEYS)
            ffn(1, 'b', nsub, npart)
            final_norm(nsub, npart, y_ap)

        for s in range(nseq):
            for i in range(seqlen // T):
                r0 = s * seqlen + i * T
                last = (i == seqlen // T - 1)
                run_tile(4, 128, xp[r0:r0 + T, :].rearrange("(j p) d -> p j d", p=128),
                         yp[r0:r0 + T, :].rearrange("(j p) d -> p j d", p=128),
                         first=(i == 0), sample=False, cs_out=(csp[s] if last else None), row0=r0, kv_outs=(cmp_p, sel_p), tile_i=i,
                         win_out=((lambda j, st, s=s: [(win_p[s, j * 128:(j + 1) * 128, :], st[:, :])]) if last else None))
        for b in range(NB if with_sample else 0):
            for t2 in range(2):
                S.op('sp', lambda e, b=b, t2=t2: e.dma_start(out=ucarS[:, b, :, t2], in_=stc[b, t2].rearrange("(c p) -> p c", p=128), allow_slow_non_contiguous=True),
                     writes=['ucar'], dma='const')
        if with_sample:
          run_tile(1, NB * DT, xs.rearrange("(j p) d -> p j d", j=1), ys.rearrange("(j p) d -> p j d", j=1), first=False,
                 sample=True, cs_out=css, row0=0, kv_outs=(cmp_s, sel_s),
                 win_out=(lambda j, st: [(win_s[b, WB - DT:WB, :], st[DT * b:DT * (b + 1), :]) for b in range(NB)]))
        for b in range(NB if with_sample else 0):
            S.op('sp', lambda e, b=b: e.dma_start(out=win_s[b, 0:WB - DT, :], in_=cache_win[b, DT:WB, :]), writes=['wins'], dma='st')
        S.finish()
        S.emit()
    return nc


_NC_CACHE = {}


def kernel(**inp):
    f32 = lambda a: np.ascontiguousarray(np.asarray(a, dtype=np.float32))
    x_prompt = f32(inp["x_prompt"]); x_sample = f32(inp["x_sample"])
    state_conv = f32(inp["state_conv"])
    cache_cmp = f32(inp["cache_cmp_kv"]).reshape(5120 * 128, 512)
    cache_sel = f32(inp["cache_sel_kv"]).reshape(5120 * 128, 512)
    cache_win = f32(inp["cache_win_kv"]).reshape(32, WB, 512)
    page_table = np.ascontiguousarray(np.asarray(inp["page_table"], dtype=np.int32))
    if 'nc' not in _NC_CACHE:
        _NC_CACHE['nc'] = build_nc()
    nc = _NC_CACHE['nc']
    shared = {
        "cache_cmp": cache_cmp, "cache_sel": cache_sel,
        "norm_ffa": f32(inp["norm_ffa"]), "norm_mix": f32(inp["norm_mix"]), "norm_ffb": f32(inp["norm_ffb"]),
        "w_ffa_gu": f32(inp["w_ffa_gu"]), "w_ffa_down": f32(inp["w_ffa_down"]),
        "w_ffb_gu": f32(inp["w_ffb_gu"]), "w_ffb_down": f32(inp["w_ffb_down"]),
        "w_conv_in": f32(inp["w_conv_in"])[0], "w_conv": f32(inp["w_conv"])[0], "w_conv_out": f32(inp["w_conv_out"])[0],
        "w_nsa_in": f32(inp["w_nsa_in"])[0], "pe_cmp": f32(inp["pe_cmp"])[0],
        "w_k1": f32(inp["w_cmp_k1"])[0], "w_k2": f32(inp["w_cmp_k2"])[0], "w_v1": f32(inp["w_cmp_v1"])[0], "w_v2": f32(inp["w_cmp_v2"])[0],
        "w_nsa_out": f32(inp["w_nsa_out"])[0], "norm_final": f32(inp["norm_final"]),
    }
    in_maps = []
    for c in range(NCORE):
        m = dict(shared)
        m["xp"] = x_prompt[NSEQ * c:NSEQ * (c + 1)].reshape(NSEQ * SEQ, D)
        m["xs"] = x_sample[NB * c:NB * (c + 1)].reshape(NB * DT, D)
        m["stc"] = np.ascontiguousarray(state_conv[0, NB * c:NB * (c + 1)])
        m["cache_win"] = np.ascontiguousarray(cache_win[NB * c:NB * (c + 1)])
        m["ptab"] = np.ascontiguousarray(page_table[NB * c:NB * (c + 1)])
        in_maps.append(m)
    res = run_bass_kernel_spmd(nc, in_maps, core_ids=list(range(NCORE)))
    R = res.results
    cat = lambda k: np.concatenate([np.asarray(r[k]) for r in R], axis=0)
    y_prompt = cat("yp").reshape(16, SEQ, D)
    y_sample = cat("ys").reshape(32, DT, D)
    conv_p = cat("csp").reshape(1, 16, 2, D)
    conv_s = cat("css").reshape(1, 32, 2, D)
    cmp_p = cat("cmp_p").reshape(1, 16, SEQ, 2, 4, 64)
    cmp_s = cat("cmp_s").reshape(1, 32, DT, 2, 4, 64)
    sel_p = cat("sel_p").reshape(1, 16, SEQ, 2, 4, 64)
    sel_s = cat("sel_s").reshape(1, 32, DT, 2, 4, 64)
    win_p = cat("win_p").reshape(1, 16, WB, 2, 4, 64)
    win_s = cat("win_s").reshape(1, 32, WB, 2, 4, 64)
    return (y_prompt, y_sample, conv_p, conv_s, cmp_p, cmp_s, sel_p, sel_s, win_p, win_s)
```

```python
import numpy as np
from contextlib import ExitStack
import concourse.bass as bass
import concourse.mybir as mybir
from concourse.bass_utils import run_bass_kernel_spmd

F32 = mybir.dt.float32
BF16 = mybir.dt.bfloat16
I32 = mybir.dt.int32
AF = mybir.ActivationFunctionType
ALU = mybir.AluOpType
AX = mybir.AxisListType

D = 1024
FF = 2816
NCORE = 8
SEQ = 2048
NSEQ = 2
NB = 4
DT = 4
T = 512
NSA_IN = 2608
NPAGE = 128
WB = 512
EPS = 1e-6
NEGB = -30000.0


class Sched:
    ENG = ('pe', 'act', 'dve', 'pool', 'sp')

    def __init__(self, nc, es):
        self.nc = nc
        self.es = es
        self.prog = {e: [] for e in self.ENG}
        self.tl_sem = {e: es.enter_context(nc.semaphore('tl_' + e)) for e in self.ENG if e != 'sp'}
        self.count = {e: 0 for e in self.tl_sem}
        self.known = {e: {} for e in self.ENG}
        self.bufs = {}
        self.dsem = {}
        self.dcount = {}
        self.pe_strict = False
        import os as _os
        self.same_eng = _os.environ.get('K_SAME_ENG', '1') == '1'
        self.dma_rr = _os.environ.get('K_DMA_RR', '1') == '1'
        self.drr = {}

    DMA_K = {'cast': 8, 'st': 8, 'const': 4, 'cw': 2, 'ldx': 2, 'pg': 8}

    def _dma_sem(self, name):
        if name not in self.drr:
            self.drr[name] = 0
        k = (self.drr[name] % self.DMA_K.get(name, 1)) if self.dma_rr else 0
        self.drr[name] += 1
        sub = f"{name}_{k}"
        if sub not in self.dsem:
            self.dsem[sub] = self.es.enter_context(self.nc.semaphore('d_' + sub))
            self.dcount[sub] = 0
        return sub

    def _need(self, eng, src, val, waits):
        if self.known[eng].get(src, 0) >= val:
            return
        if waits.get(src, 0) < val:
            waits[src] = val

    def op(self, eng, fn, reads=(), writes=(), inc=True, dma=None):
        waits = {}
        for k in reads:
            b = self.bufs.get(k)
            if b and b['w']:
                self._need(eng, b['w'][0], b['w'][1], waits)
        for k in writes:
            b = self.bufs.get(k)
            if b:
                if b['w']:
                    self._need(eng, b['w'][0], b['w'][1], waits)
                for src, val in b['r'].items():
                    self._need(eng, src, val, waits)
        me_src = ('tl', eng)
        if not self.same_eng and me_src in waits:
            rv = 0
            for k in reads:
                b = self.bufs.get(k)
                if b and b['w'] and b['w'][0] == me_src:
                    rv = max(rv, b['w'][1])
            if rv > self.known[eng].get(me_src, 0) and rv <= self.count.get(eng, 0):
                waits[me_src] = rv
            else:
                del waits[me_src]
        if me_src in waits and (waits[me_src] > self.count.get(eng, 0) or eng == 'pe'):
            if eng == 'pe' and waits[me_src] <= self.count['pe'] and self.pe_strict:
                pass
            else:
                del waits[me_src]
        for src, val in waits.items():
            self.known[eng][src] = val
        if dma is not None:
            sub = self._dma_sem(dma)
            if self.dma_rr and self.dcount[sub] > 0 and self.known[eng].get(('d', sub), 0) < self.dcount[sub]:
                waits[('d', sub)] = self.dcount[sub]
                self.known[eng][('d', sub)] = self.dcount[sub]
            sem = self.dsem[sub]
            self.dcount[sub] += 16
            me = (('d', sub), self.dcount[sub])
        else:
            if inc:
                self.count[eng] += 1
                me = (('tl', eng), self.count[eng])
            else:
                me = (('tl', eng), self.count[eng] + 1)
            sem = self.tl_sem[eng]
        for k in reads:
            b = self.bufs.setdefault(k, {'w': None, 'r': {}})
            if b['r'].get(me[0], 0) < me[1]:
                b['r'][me[0]] = me[1]
        for k in writes:
            self.bufs[k] = {'w': me, 'r': {}}
        wl = [((self.tl_sem[s[1]] if s[0] == 'tl' else self.dsem[s[1]]), v) for s, v in waits.items()]
        self.prog[eng].append((wl, fn, sem if (inc or dma is not None) else None, 16 if dma is not None else 1))

    def barrier(self, engines, keys):
        for eng in engines:
            waits = {}
            for k in keys:
                b = self.bufs.get(k)
                if not b:
                    continue
                if b['w']:
                    self._need(eng, b['w'][0], b['w'][1], waits)
                for src, val in b['r'].items():
                    self._need(eng, src, val, waits)
            me_src = ('tl', eng)
            if me_src in waits and waits[me_src] > self.count.get(eng, 0):
                del waits[me_src]
            for src, val in waits.items():
                self.known[eng][src] = val
            wl = [((self.tl_sem[s[1]] if s[0] == 'tl' else self.dsem[s[1]]), v) for s, v in waits.items()]
            if wl:
                self.prog[eng].append((wl, None, None, 0))

    def finish(self):
        wl = [(self.dsem[n], self.dcount[n]) for n in self.dsem]
        self.prog['sp'].append((wl, None, None, 0))

    def emit(self):
        nc = self.nc
        emap = {'pe': 'tensor', 'act': 'scalar', 'dve': 'vector', 'pool': 'gpsimd', 'sp': 'sync'}
        with nc.Block() as block:
            for e in self.ENG:
                prog = self.prog[e]

                def body(engobj, prog=prog):
                    for wl, fn, sem, incv in prog:
                        for s, v in wl:
                            engobj.wait_ge(s, v)
                        if fn is not None:
                            ins = fn(engobj)
                            if sem is not None:
                                ins.then_inc(sem, incv)
                getattr(block, emap[e])(body)


def build_nc(nseq=NSEQ, seqlen=SEQ, with_sample=True, nphys=5120, dbg_stage=9):
    nc = bass.Bass("TRN2", target_bir_lowering=False)
    es = ExitStack()

    def din(name, shape, dt=F32):
        return nc.dram_tensor(name, list(shape), dt, kind="ExternalInput").ap()

    def dout(name, shape, dt=F32):
        return nc.dram_tensor(name, list(shape), dt, kind="ExternalOutput").ap()

    def dscr(name, shape, dt=BF16):
        return nc.dram_tensor(name, list(shape), dt, kind="Internal").ap()

    xp = din("xp", [nseq * seqlen, D])
    xs = din("xs", [NB * DT, D])
    stc = din("stc", [NB, 2, D])
    cache_cmp = din("cache_cmp", [nphys * 128, 512])
    cache_sel = din("cache_sel", [nphys * 128, 512])
    cache_win = din("cache_win", [NB, WB, 512])
    ptab = din("ptab", [NB, NPAGE], I32)
    norm_ffa = din("norm_ffa", [2, D]); norm_mix = din("norm_mix", [2, D]); norm_ffb = din("norm_ffb", [2, D])
    w_ffa_gu = din("w_ffa_gu", [2, D, 2 * FF]); w_ffa_down = din("w_ffa_down", [2, FF, D])
    w_ffb_gu = din("w_ffb_gu", [2, D, 2 * FF]); w_ffb_down = din("w_ffb_down", [2, FF, D])
    w_conv_in = din("w_conv_in", [D, 3 * D]); w_conv = din("w_conv", [3, D]); w_conv_out = din("w_conv_out", [D, D])
    w_nsa_in = din("w_nsa_in", [D, NSA_IN]); pe_cmp = din("pe_cmp", [32, 64])
    w_k1 = din("w_k1", [32, 64, 64]); w_k2 = din("w_k2", [64, 64]); w_v1 = din("w_v1", [32, 64, 64]); w_v2 = din("w_v2", [64, 64])
    w_nsa_out = din("w_nsa_out", [D, D]); norm_final = din("norm_final", [D])

    yp = dout("yp", [nseq * seqlen, D]); ys = dout("ys", [NB * DT, D])
    csp = dout("csp", [nseq, 2, D]); css = dout("css", [NB, 2, D])
    cmp_p = dout("cmp_p", [nseq * seqlen, 512]); cmp_s = dout("cmp_s", [NB * DT, 512])
    sel_p = dout("sel_p", [nseq * seqlen, 512]); sel_s = dout("sel_s", [NB * DT, 512])
    win_p = dout("win_p", [nseq, WB, 512]); win_s = dout("win_s", [NB, WB, 512])

    s_gu = {(l, ab): dscr(f"s_gu{l}{ab}", [22, 128, 8, 2, 128]) for l in range(2) for ab in 'ab'}
    s_dn = {(l, ab): dscr(f"s_dn{l}{ab}", [2, 128, 22, 512]) for l in range(2) for ab in 'ab'}
    s_cin = dscr("s_cin", [8, 128, 8, 3, 128])
    s_cout = dscr("s_cout", [128, 8, D])
    s_nq = dscr("s_nq", [4, 128, 8, 256])
    s_nk = dscr("s_nk", [6, 128, 8, 256])
    s_nkv = dscr("s_nkv", [128, 8, 1536])
    s_ng = dscr("s_ng", [128, 8, 48])
    s_nout = dscr("s_nout", [128, 8, D])

    with es:
        S = Sched(nc, es)

        def sb(name, shape, dt):
            return es.enter_context(nc.sbuf_tensor(name, list(shape), dt))

        def ps(name, shape, dt):
            return es.enter_context(nc.psum_tensor(name, list(shape), dt))

        X = sb("X", [128, 4, D], F32)
        hb = sb("hb", [128, D], BF16)
        junk = sb("junk", [128, D], BF16)
        hT = sb("hT", [128, 8, T], BF16)
        ARENA = 29 * 1024
        arena = sb("arena", [128, ARENA], mybir.dt.uint8)

        def aview(off, shape, dt, parts=128):
            nbytes = int(np.prod(shape)) * (2 if dt == BF16 else 4)
            assert off % 4 == 0 and off + nbytes <= ARENA, (off, nbytes, shape)
            v = arena[0:parts, off:off + nbytes].bitcast(dt)
            if len(shape) == 1:
                return v
            names = " ".join(f"a{i}" for i in range(len(shape)))
            kw = {f"a{i}": shape[i] for i in range(1, len(shape))}
            return v.rearrange(f"p ({names}) -> p {names}", **kw)

        KB = 1024
        import os as _os3
        NOARENA = _os3.environ.get('K_ARENA', '1') == '0'
        if NOARENA:
            aT = sb("aT", [128, 22, T], BF16)
            U = sb("U", [128, 8, T + 2], F32)
            zT = sb("zT", [128, 8, T], BF16)
            cgs = sb("cgs", [128, T], F32)
            cacc = sb("cacc", [128, T], F32)
        else:
            aT = aview(0, [22, T], BF16)
            U = aview(0, [8, T + 2], F32)
            zT = aview(17 * KB, [8, T], BF16)
            cgs = aview(25 * KB, [T], F32)
            cacc = aview(27 * KB, [T], F32)
        Qxb = [aview(0, [4, T], BF16), aview(25 * KB, [4, T], BF16)]
        kcT = aview(4 * KB, [2, T], BF16)
        Ob = aview(6 * KB, [4, D], BF16)
        PTs = [aview(14 * KB + k * KB, [T], BF16) for k in range(3)]
        EcT = [aview(17 * KB + k * KB, [T], BF16) for k in range(2)]
        Ec = aview(19 * KB, [4, 64], F32)
        imp4 = aview(20 * KB, [64], F32)
        imps = aview(20 * KB + 256, [32], F32)
        score = aview(20 * KB + 384, [32], F32)
        work = aview(20 * KB + 512, [32], F32)
        selm = aview(20 * KB + 640, [32], F32)
        m1 = aview(20 * KB + 768, [8], F32)
        m2 = aview(20 * KB + 800, [8], F32)
        sums = aview(20 * KB + 832, [4], F32)
        rsf = aview(20 * KB + 848, [3, 4], F32)
        hx = aview(21 * KB, [16], F32)
        hx2 = aview(21 * KB + 64, [16], F32)
        hx3 = aview(21 * KB + 128, [16], F32)
        hid = aview(21 * KB + 192, [16], BF16)
        tmpA = aview(22 * KB, [4, 64], F32)
        tmpB = aview(23 * KB, [4, 64], F32)
        tmpC = aview(24 * KB, [4, 64], F32)
        ARENA2 = 53632
        arena2 = sb("arena2", [128, ARENA2], mybir.dt.uint8)

        def aview2(off, shape, dt, parts=128):
            nbytes = int(np.prod(shape)) * (2 if dt == BF16 else 4)
            assert off % 4 == 0 and off + nbytes <= ARENA2, (off, nbytes)
            v = arena2[0:parts, off:off + nbytes].bitcast(dt)
            names = " ".join(f"a{i}" for i in range(len(shape)))
            kw = {f"a{i}": shape[i] for i in range(1, len(shape))}
            return v.rearrange(f"p ({names}) -> p {names}", **kw) if len(shape) > 1 else v

        KsT = aview2(0, [4, SEQ], BF16)
        KwT = aview2(16384, [4, 2 * T], BF16)
        Vs = aview2(24576, [16, 4, 66], BF16)
        Vw = aview2(33024, [8, 4, 66], BF16)
        Mc = aview2(37248, [16, 64], F32)
        McT = aview2(41344, [4, T], BF16, parts=64)
        E32 = aview2(45440, [SEQ], BF16, parts=32)
        Acst = aview2(49536, [16, 32], F32)
        Bcst = aview2(51584, [16, 32], F32)
        KcT = sb("KcT", [64, 4, 64], BF16)
        hidV = sb("hidV", [64, 4, 64], BF16)
        Vc = sb("Vc", [64, 4, 66], BF16)
        CG = aview2(0, [8, 512], BF16)
        XT = aview2(8192, [4, 1024], BF16)
        W1bd = [aview2(16384 + 8192 * k, [32, 128], BF16) for k in range(2)]
        hidS = aview2(32768, [4, 512], BF16)
        CGw = aview2(36864, [4, 512], BF16)
        XTw = aview2(40960, [2, 512], BF16)
        mbAll = aview2(43008, [4, 256], F32, parts=16)
        OBall = aview2(47104, [4, 3, 64], F32, parts=16)
        ssumS = aview2(50176, [4, 33], F32, parts=16)
        ObS = aview2(50720, [D], BF16, parts=16)
        Ec16_g = aview2(52768, [192], F32)
        OTk = aview2(0, [3, 16, 64], F32, parts=16)
        KcS = aview(0, [4, 512], BF16, parts=64)
        VcS = aview(4096, [4, 4, 64], BF16)
        IDX = aview(6144, [NB, NPAGE], I32)
        PTF = aview(8192, [NB, NPAGE], F32)
        Ec16 = aview(10240, [512], F32, parts=16)
        SM = aview(12288, [512], F32, parts=16)
        Pb = aview(14336, [512], BF16, parts=16)
        PsT = aview(15360, [4, 16], BF16)
        mb16 = aview(15616, [256], F32, parts=16)
        imp16 = aview(16640, [512], F32, parts=16)
        imps16 = aview(18688, [256], F32, parts=16)
        work16 = aview(19712, [256], F32, parts=16)
        m1s = aview(20736, [8], F32, parts=16)
        m2s = aview(20768, [8], F32, parts=16)
        ssum = aview(20800, [48], F32, parts=16)
        rsS = aview(20992, [4], F32, parts=16)
        QB = aview(21056, [4, 16], BF16, parts=64)
        QBp = aview(21184, [4, 16], BF16)
        OB3 = aview(21312, [3, 64], F32, parts=16)
        Qs = aview(22080, [16, 16], BF16, parts=64)
        KsN = aview(22592, [4, 16], BF16, parts=64)
        KwN = aview(22720, [4, 16], BF16, parts=64)
        VnS = aview(22848, [NB, 4, 64], BF16, parts=4)
        VnW = aview(24896, [NB, 4, 64], BF16, parts=4)
        WM16 = aview(26944, [512], F32, parts=16)
        CM16 = aview(28992, [4], F32, parts=16)
        Gsum = sb("Gsum", [16, 16], BF16)
        Gsumf = sb("Gsumf", [16, 16], F32)
        W2sel = sb("W2sel", [128, 2, 2, 64], BF16)
        peT2 = sb("peT2", [128, 32], F32)
        iot_i = sb("iot_i", [128, 1], I32)
        iot_f = sb("iot_f", [128, 1], F32)
        CM4 = sb("CM4", [4, 4], BF16)
        WM4 = sb("WM4", [4, 512], BF16)
        pt_i = sb("pt_i", [128, NB * NPAGE], I32)
        G = sb("G", [128, 4, 48], F32)
        W1k = None if NOARENA else sb("W1k", [64, 32, 64], BF16); W1v = None if NOARENA else sb("W1v", [64, 32, 64], BF16)
        W2k = sb("W2k", [64, 64], BF16); W2v = sb("W2v", [64, 64], BF16)
        peT = sb("peT", [64, 32], F32)
        pe_nat = sb("pe_nat", [32, 64], F32)
        TriGE = sb("TriGE", [128, 128], BF16); TriLT = sb("TriLT", [128, 128], BF16)
        TriGEb = sb("TriGEb", [128, 128], BF16); TriLTb = sb("TriLTb", [128, 128], BF16)
        SH = sb("SH", [128, 128], BF16)
        MBt = sb("MBt", [128, 128], BF16)
        ucarS = sb("ucarS", [128, NB, 8, 2], F32)
        ringg = [sb(f"ringg{i}", [128, 8 * 384], BF16) for i in range(3)]
        ringb = [sb(f"ringb{i}", [128, 12288], BF16) for i in range(2)]
        ident = sb("ident", [128, 128], BF16)
        identf = sb("identf", [128, 128], F32)
        gT = sb("gT", [128, 6, 8], F32)
        gfin = sb("gfin", [128, D], F32)
        wcv = sb("wcv", [128, 3, 8], F32)
        ss = sb("ss", [128, 4], F32)
        rstd = sb("rstd", [128, 4], F32)
        ucar = sb("ucar", [128, 8, 2], F32)
        sgs = [sb(f"sg{i}", [128, T], F32) for i in range(2)]
        stage = [sb(f"stage{i}", [128, 512], F32) for i in range(2)]

        PT = ps("PT", [128, 1024], BF16)
        PB = [ps(f"PB{i}", [128, 512], F32) for i in range(7)]

        def cast(dst, src, key):
            S.op('pool', lambda e: e.dma_start(out=dst, in_=src), writes=[key], dma='cast')

        def cast_gu(l, ab):
            w = (w_ffa_gu if ab == 'a' else w_ffb_gu)[l]
            for f in range(22):
                for u in range(2):
                    cast(s_gu[(l, ab)][f, :, :, u, :],
                         w[:, u * FF + f * 128:u * FF + (f + 1) * 128].rearrange("(c p) j -> p c j", p=128), ('s_gu', l, ab, f, u))

        def cast_dn(l, ab):
            w = (w_ffa_down if ab == 'a' else w_ffb_down)[l]
            for h in range(2):
                cast(s_dn[(l, ab)][h], w[:, h * 512:(h + 1) * 512].rearrange("(f p) n -> p f n", p=128), ('s_dn', l, ab, h))

        cast_gu(0, 'a'); cast_dn(0, 'a')
        for s3 in range(3):
            for ee in range(8):
                cast(s_cin[ee, :, :, s3, :], w_conv_in[:, s3 * D + ee * 128:s3 * D + (ee + 1) * 128].rearrange("(c p) j -> p c j", p=128), 's_cin')
        cast(s_cout, w_conv_out.rearrange("(c p) n -> p c n", p=128), 's_cout')
        cast_gu(0, 'b'); cast_dn(0, 'b')
        cast_gu(1, 'a'); cast_dn(1, 'a')
        for m in range(4):
            cast(s_nq[m], w_nsa_in[:, m * 256:(m + 1) * 256].rearrange("(c p) j -> p c j", p=128), 's_nq')
        for m in range(6):
            cast(s_nk[m], w_nsa_in[:, 1024 + m * 256:1024 + (m + 1) * 256].rearrange("(c p) j -> p c j", p=128), 's_nk')
        cast(s_nkv, w_nsa_in[:, 1024:2560].rearrange("(c p) n -> p c n", p=128), 's_nkv')
        cast(s_ng, w_nsa_in[:, 2560:2608].rearrange("(c p) n -> p c n", p=128), 's_ng')
        cast(s_nout, w_nsa_out.rearrange("(c p) n -> p c n", p=128), 's_nout')
        cast_gu(1, 'b'); cast_dn(1, 'b')

        S.op('pool', lambda e: e.memset(identf[:], 0.0), writes=['identf'])
        S.op('pool', lambda e: e.affine_select(out=identf[:], in_=identf[:], pattern=[[-1, 128]], compare_op=ALU.not_equal,
                                               fill=1.0, base=0, channel_multiplier=1), reads=['identf'], writes=['identf'])
        S.op('dve', lambda e: e.tensor_copy(out=ident[:], in_=identf[:]), reads=['identf'], writes=['ident'])
        for k, nt in enumerate([norm_ffa, norm_mix, norm_ffb]):
            for l in range(2):
                S.op('sp', lambda e, k=k, l=l, nt=nt: e.dma_start(out=gT[:, 2 * k + l, :], in_=nt[l].rearrange("(c p) -> p c", p=128),
                                                                  allow_slow_non_contiguous=True), writes=['gT'], dma='const')
        S.op('sp', lambda e: e.dma_start(out=gfin[:], in_=norm_final.partition_broadcast(128)), writes=['gfin'], dma='const')
        for i3 in range(3):
            S.op('sp', lambda e, i3=i3: e.dma_start(out=wcv[:, i3, :], in_=w_conv[i3].rearrange("(c p) -> p c", p=128),
                                                    allow_slow_non_contiguous=True), writes=['wcv'], dma='const')

        import os as _os2
        KC = int(_os2.environ.get('K_CONST', '3'))
        if KC >= 1:
            def pool_op(fn, reads=(), writes=()):
                S.op('pool', fn, reads=reads, writes=writes)

            def aff(out_ap, pattern, base, cm, key, fill=0.0, op=ALU.is_ge):
                pool_op(lambda e: e.affine_select(out=out_ap, in_=out_ap, pattern=pattern, compare_op=op, fill=fill, base=base, channel_multiplier=cm),
                        reads=[key], writes=[key])

            pool_op(lambda e: e.memset(TriGE[:], 1.0), writes=['TriGE'])
            aff(TriGE[:], [[1, 128]], 0, -1, 'TriGE')
            pool_op(lambda e: e.memset(TriLT[:], 1.0), writes=['TriLT'])
            aff(TriLT[:], [[-1, 128]], -1, 1, 'TriLT')
            S.op('dve', lambda e: e.tensor_scalar(out=TriGEb[:], in0=TriGE[:], scalar1=-1.0, scalar2=-NEGB, op0=ALU.add, op1=ALU.mult), reads=['TriGE'], writes=['TriGEb'])
            S.op('dve', lambda e: e.tensor_scalar(out=TriLTb[:], in0=TriLT[:], scalar1=-1.0, scalar2=-NEGB, op0=ALU.add, op1=ALU.mult), reads=['TriLT'], writes=['TriLTb'])
            pool_op(lambda e: e.memset(SH[:], 0.0), writes=['SH'])
            aff(SH[:], [[-1, 128]], 64, 1, 'SH', fill=1.0, op=ALU.not_equal)
            pool_op(lambda e: e.memset(E32[:], 1.0), writes=['E32'])
            aff(E32[:], [[1, SEQ]], 0, -64, 'E32')
            aff(E32[:], [[-1, SEQ]], 63, 64, 'E32')
            pool_op(lambda e: e.memset(Mc[:], 1.0), writes=['Mc'])
            for pos in range(16):
                aff(Mc[:, pos, :], [[-32, 64]], 128 * pos - 31, 1, 'Mc')
            pool_op(lambda e: e.memset(McT[:], 1.0), writes=['McT'])
            for i4 in range(4):
                aff(McT[:, i4, :], [[1, T]], T * i4 - 31, -32, 'McT')
            pool_op(lambda e: e.memset(Acst[:], 0.0), writes=['Acst'])
            pool_op(lambda e: e.memset(Bcst[:], -1e4), writes=['Bcst'])
            for pos in range(16):
                for half in range(2):
                    cur = 2 * pos + half
                    pr = slice(64 * half, 64 * half + 64)
                    if cur - 1 > 1:
                        pool_op(lambda e, pr=pr, pos=pos, cur=cur: e.memset(Acst[pr, pos, 1:cur - 1], 1.0), reads=['Acst'], writes=['Acst'])
                        pool_op(lambda e, pr=pr, pos=pos, cur=cur: e.memset(Bcst[pr, pos, 1:cur - 1], 0.0), reads=['Bcst'], writes=['Bcst'])
                    pool_op(lambda e, pr=pr, pos=pos, cur=cur: e.memset(Bcst[pr, pos, max(cur - 1, 0):cur + 1], 1e4), reads=['Bcst'], writes=['Bcst'])
                    pool_op(lambda e, pr=pr, pos=pos: e.memset(Bcst[pr, pos, 0:1], 1e4), reads=['Bcst'], writes=['Bcst'])
            pool_op(lambda e: e.memset(MBt[:], 0.0), writes=['MBt'])
            pool_op(lambda e: e.memset(Vs[:], 1.0), writes=['Vs'])
            pool_op(lambda e: e.memset(Vw[:], 1.0), writes=['Vw'])
            pool_op(lambda e: e.memset(Vc[:], 1.0), writes=['Vc'])
            pool_op(lambda e: e.memset(KsT[:], 0.0), writes=['KsT'])
            pool_op(lambda e: e.memset(KwT[:], 0.0), writes=['KwT'])
            pool_op(lambda e: e.memset(KcT[:], 0.0), writes=['KcT'])
            pool_op(lambda e: e.memset(hidV[:], 0.0), writes=['hidV'])
        if KC >= 2:
            S.op('pool', lambda e: e.dma_start(out=W1k[:], in_=w_k1.rearrange("l d e -> d l e")), writes=['W1k'], dma='cw')
            S.op('pool', lambda e: e.dma_start(out=W1v[:], in_=w_v1.rearrange("l d e -> d l e")), writes=['W1v'], dma='cw')
            S.op('pool', lambda e: e.dma_start(out=W2k[:], in_=w_k2), writes=['W2k'], dma='cw')
            S.op('pool', lambda e: e.dma_start(out=W2v[:], in_=w_v2), writes=['W2v'], dma='cw')
            S.op('sp', lambda e: e.dma_start(out=pe_nat[:], in_=pe_cmp), writes=['pe_nat'], dma='const')
            S.op('pe', lambda e: e.transpose(out=PB[1][0:64, 0:32], in_=pe_nat[:, :], identity=identf[0:32, 0:32]), reads=['pe_nat', 'identf'], writes=[('PB', 1)])
            S.op('dve', lambda e: e.tensor_copy(out=peT[:, :], in_=PB[1][0:64, 0:32]), reads=[('PB', 1)], writes=['peT'])
        if KC >= 3:
            for kc4 in range(SEQ // 512):
                S.op('pe', lambda e, kc4=kc4: e.matmul(PB[0][:, :], lhsT=SH[0:32, :], rhs=E32[:, kc4 * 512:(kc4 + 1) * 512], start=True, stop=True),
                     reads=['SH', 'E32'], writes=[('PB', 0)])
                for kv in range(4):
                    S.op('dve', lambda e, kc4=kc4, kv=kv: e.tensor_copy(out=KsT[64:96, kv, kc4 * 512:(kc4 + 1) * 512], in_=PB[0][64:96, :]),
                         reads=[('PB', 0)], writes=['KsT', 'KsTE'])

        rg = {'i': 0}
        rb = {'i': 0}

        def load_g(src_ap, ncols, skey):
            slot = rg['i'] % 3
            rg['i'] += 1
            dst = ringg[slot][:, 0:8 * ncols]
            S.op('sp', lambda e: e.dma_start(out=dst, in_=src_ap), reads=(skey if isinstance(skey, list) else [skey]), writes=[('rg', slot)], dma=f'rg{slot}')
            return ringg[slot][:, 0:8 * ncols].rearrange("p (c n) -> p c n", c=8), ('rg', slot)

        def load_b(src_ap, nelem, skey):
            slot = rb['i'] % 2
            rb['i'] += 1
            dst = ringb[slot][:, 0:nelem]
            S.op('sp', lambda e: e.dma_start(out=dst, in_=src_ap), reads=(skey if isinstance(skey, list) else [skey]), writes=[('rb', slot)], dma=f'rb{slot}')
            return ringb[slot], ('rb', slot)

        class Stream:
            def __init__(self, specs, ahead):
                self.specs = specs
                self.loaded = []
                self.ahead = ahead

            def get(self, i):
                while len(self.loaded) < min(len(self.specs), i + 1 + self.ahead):
                    self.loaded.append(self.specs[len(self.loaded)]())
                return self.loaded[i]

        bank_rr = {'i': 0}

        def bank(n=7):
            b = bank_rr['i'] % n
            bank_rr['i'] += 1
            return PB[b], ('PB', b)

        def norm_hT(nsub, npart, gi):
            for j in range(nsub):
                S.op('act', lambda e, j=j: e.activation(out=junk[:npart, :], in_=X[:npart, j, :], func=AF.Square, scale=1.0 / 32.0,
                                                         accum_out=ss[:npart, j:j + 1]), reads=['X'], writes=['junk', 'ss'])
            S.op('dve', lambda e: e.tensor_scalar(out=rstd[:npart, :nsub], in0=ss[:npart, :nsub], scalar1=EPS, scalar2=None, op0=ALU.add),
                 reads=['ss'], writes=['rstd'])
            S.op('act', lambda e: e.activation(out=rstd[:npart, :nsub], in_=rstd[:npart, :nsub], func=AF.Sqrt), reads=['rstd'], writes=['rstd'])
            S.op('dve', lambda e: e.reciprocal(out=rstd[:npart, :nsub], in_=rstd[:npart, :nsub]), reads=['rstd'], writes=['rstd'])
            for j in range(nsub):
                S.op('dve', lambda e, j=j: e.tensor_scalar(out=hb[:npart, :], in0=X[:npart, j, :], scalar1=rstd[:npart, j:j + 1], scalar2=None,
                                                           op0=ALU.mult), reads=['X', 'rstd'], writes=['hb'])
                for c in range(8):
                    S.op('pe', lambda e, c=c: e.transpose(out=PT[:, c * 128:c * 128 + npart], in_=hb[:npart, c * 128:(c + 1) * 128],
                                                          identity=ident[:npart, :npart]), reads=['hb', 'ident'], writes=['PT'], inc=(c == 7))
                S.op('dve', lambda e, j=j: e.tensor_tensor(out=hT[:, :, j * npart:(j + 1) * npart],
                                                           in0=PT[:, :].rearrange("p (c n) -> p c n", c=8)[:, :, 0:npart],
                                                           in1=gT[:, gi, :].unsqueeze(2).to_broadcast([128, 8, npart]), op=ALU.mult),
                     reads=['PT', 'gT'], writes=['hT'])

        def out_proj_add(srcT, wfun, nsub, npart, scale):
            nchunk = srcT.shape[1]
            for h in range(2):
                wap, wkey = wfun(h)
                for j in range(nsub):
                    pb, pk = bank()
                    for f in range(nchunk):
                        S.op('pe', lambda e, f=f, j=j, pb=pb, wap=wap: e.matmul(pb[:npart, :], lhsT=srcT[:, f, j * npart:(j + 1) * npart], rhs=wap[:, f, :],
                                                                             start=(f == 0), stop=(f == nchunk - 1)),
                             reads=[wkey, 'srcT'], writes=[pk], inc=(f == nchunk - 1))
                    S.op('dve', lambda e, j=j, h=h, pb=pb: e.scalar_tensor_tensor(out=X[:npart, j, h * 512:(h + 1) * 512], in0=pb[:npart, :], scalar=scale,
                                                                                  in1=X[:npart, j, h * 512:(h + 1) * 512], op0=ALU.mult, op1=ALU.add),
                         reads=[pk, 'X'], writes=['X'])

        def ffn(l, ab, nsub, npart):
            TT = nsub * npart
            gi = (0 if ab == 'a' else 4) + l
            norm_hT(nsub, npart, gi)
            sg_key = (l, ab)
            gus = Stream([(lambda f=f: load_g(s_gu[(l, ab)][f].rearrange("p c u j -> p (c u j)"), 256, [('s_gu', l, ab, f, 0), ('s_gu', l, ab, f, 1)])) for f in range(22)], 2)
            dns = Stream([(lambda h=h: load_b(s_dn[(l, ab)][h].rearrange("p f n -> p (f n)"), 22 * 512, [('s_dn', l, ab, h)])) for h in range(2)], 1)
            for f in range(22):
                wg, wk = gus.get(f)
                if f == 18:
                    dns.get(0)
                pg, pgk = bank()
                pu, puk = bank()
                for u, (pp, ppk) in enumerate([(pg, pgk), (pu, puk)]):
                    for c in range(8):
                        S.op('pe', lambda e, c=c, u=u, pp=pp, wg=wg: e.matmul(pp[:, :TT], lhsT=wg[:, c, u * 128:(u + 1) * 128], rhs=hT[:, c, :TT],
                                                                            start=(c == 0), stop=(c == 7)),
                             reads=[wk, 'hT'], writes=[ppk], inc=(c == 7))
                sg = sgs[f % 2]
                S.op('act', lambda e, pg=pg, sg=sg: e.activation(out=sg[:, :TT], in_=pg[:, :TT], func=AF.Silu), reads=[pgk], writes=[('sg', f % 2)])
                S.op('dve', lambda e, f=f, pu=pu, sg=sg: e.tensor_tensor(out=aT[:, f, :TT], in0=sg[:, :TT], in1=pu[:, :TT], op=ALU.mult),
                     reads=[('sg', f % 2), puk], writes=['srcT'])

            def wfun(h):
                ap, k = dns.get(h)
                return ap[:, 0:22 * 512].rearrange("p (f n) -> p f n", f=22), k
            out_proj_add(aT, wfun, nsub, npart, 0.5)

        def conv_layer(nsub, npart, first, sample, cs_out):
            TT = nsub * npart
            norm_hT(nsub, npart, 2)
            if first and not sample:
                S.op('pool', lambda e: e.memset(ucar[:], 0.0), writes=['ucar'])
            cins = Stream([(lambda ee=ee: load_g(s_cin[ee].rearrange("p c s j -> p (c s j)"), 384, 's_cin')) for ee in range(8)], 2)
            couts = Stream([lambda: load_b(s_cout.rearrange("p c n -> p (c n)"), 8 * D, 's_cout')], 0)
            for ee in range(8):
                w, wk = cins.get(ee)
                if ee == 5:
                    couts.get(0)
                w4 = w.rearrange("p c (s j) -> p c s j", s=3)
                pbs = []
                for s3 in range(3):
                    pb, pk = bank()
                    for c in range(8):
                        S.op('pe', lambda e, c=c, s3=s3, pb=pb, w4=w4: e.matmul(pb[:, :TT], lhsT=w4[:, c, s3, :], rhs=hT[:, c, :TT], start=(c == 0), stop=(c == 7)),
                             reads=[wk, 'hT'], writes=[pk], inc=(c == 7))
                    pbs.append((pb, pk))
                (pbg, kbg), (pcg, kcg), (pv, kv_) = pbs
                S.op('act', lambda e, pcg=pcg: e.activation(out=cgs[:, :TT], in_=pcg[:, :TT], func=AF.Copy), reads=[kcg], writes=['cgs'])
                if not sample:
                    S.op('dve', lambda e, ee=ee: e.tensor_copy(out=U[:, ee, 0:2], in_=ucar[:, ee, :]), reads=['ucar'], writes=['U'])
                    S.op('dve', lambda e, ee=ee, pv=pv: e.tensor_tensor(out=U[:, ee, 2:2 + TT], in0=cgs[:, :TT], in1=pv[:, :TT], op=ALU.mult),
                         reads=['cgs', kv_], writes=['U'])
                    segs = [(0, TT, 0)]
                else:
                    S.op('dve', lambda e, ee=ee: e.tensor_copy(out=U[:, ee, 0:6 * NB].rearrange("p (b k) -> p b k", k=6)[:, :, 0:2], in_=ucarS[:, :, ee, :]),
                         reads=['ucar'], writes=['U'])
                    S.op('dve', lambda e, ee=ee, pv=pv: e.tensor_tensor(out=U[:, ee, 0:6 * NB].rearrange("p (b k) -> p b k", k=6)[:, :, 2:6],
                                                                     in0=cgs[:, 0:TT].rearrange("p (b k) -> p b k", k=DT),
                                                                     in1=pv[:, 0:TT].rearrange("p (b k) -> p b k", k=DT), op=ALU.mult),
                         reads=['cgs', kv_], writes=['U'])
                    segs = [(6 * b, DT, DT * b) for b in range(NB)]
                for (u0, n, o0) in segs:
                    S.op('dve', lambda e, ee=ee, u0=u0, n=n, o0=o0: e.tensor_scalar(out=cacc[:, o0:o0 + n], in0=U[:, ee, u0 + 2:u0 + 2 + n], scalar1=wcv[:, 2, ee:ee + 1],
                                                                                    scalar2=None, op0=ALU.mult), reads=['U', 'wcv'], writes=['cacc'])
                    for i3 in (1, 0):
                        S.op('dve', lambda e, ee=ee, u0=u0, n=n, o0=o0, i3=i3: e.scalar_tensor_tensor(out=cacc[:, o0:o0 + n], in0=U[:, ee, u0 + i3:u0 + i3 + n],
                                                                                                       scalar=wcv[:, i3, ee:ee + 1], in1=cacc[:, o0:o0 + n],
                                                                                                       op0=ALU.mult, op1=ALU.add),
                             reads=['U', 'wcv', 'cacc'], writes=['cacc'])
                S.op('dve', lambda e, ee=ee, pbg=pbg: e.tensor_tensor(out=zT[:, ee, :TT], in0=cacc[:, :TT], in1=pbg[:, :TT], op=ALU.mult),
                     reads=['cacc', kbg], writes=['srcT'])
                if not sample:
                    S.op('pool', lambda e, ee=ee: e.tensor_copy(out=ucar[:, ee, :], in_=U[:, ee, TT:TT + 2]), reads=['U'], writes=['ucar'])
            if cs_out is not None:
                if not sample:
                    for t2 in range(2):
                        S.op('sp', lambda e, t2=t2: e.dma_start(out=cs_out[t2].rearrange("(c p) -> p c", p=128), in_=ucar[:, :, t2], allow_slow_non_contiguous=True),
                             reads=['ucar'], writes=['csout'], dma='st')
                else:
                    for b in range(NB):
                        for t2 in range(2):
                            S.op('sp', lambda e, b=b, t2=t2: e.dma_start(out=cs_out[b, t2].rearrange("(c p) -> p c", p=128), in_=U[:, :, 6 * b + 4 + t2],
                                                                         allow_slow_non_contiguous=True), reads=['U'], writes=['csout'], dma='st')

            def wfun(h):
                ap, k = couts.get(0)
                return ap[:, 0:8 * D].rearrange("p (c n) -> p c n", c=8)[:, :, h * 512:(h + 1) * 512], k
            out_proj_add(zT, wfun, nsub, npart, 1.0)

        def nsa_rows(nsub, npart, row0, outs, win_out, i=None):
            kvs = Stream([lambda: load_b(s_nkv.rearrange("p c n -> p (c n)"), 8 * 1536, 's_nkv')], 0)
            wap, wk = kvs.get(0)
            w3 = wap[:, 0:8 * 1536].rearrange("p (c n) -> p c n", c=8)
            for br in range(3):
                for j in range(nsub):
                    pb, pk = bank()
                    for c in range(8):
                        S.op('pe', lambda e, c=c, j=j, br=br, pb=pb: e.matmul(pb[:npart, :], lhsT=hT[:, c, j * npart:(j + 1) * npart], rhs=w3[:, c, br * 512:(br + 1) * 512],
                                                                           start=(c == 0), stop=(c == 7)), reads=[wk, 'hT'], writes=[pk], inc=(c == 7))
                    st = stage[j % 2]
                    S.op('act', lambda e, pb=pb, st=st: e.activation(out=st[:npart, :], in_=pb[:npart, :], func=AF.Copy), reads=[pk], writes=[('stage', j % 2)])
                    if i is not None and br == 1:
                        S.op('pool', lambda e, st=st, j=j: e.tensor_copy(out=Vs[:, 4 * i + j, :, 0:64], in_=st[:, 256:512].rearrange("p (k d) -> p k d", k=4)),
                             reads=[('stage', j % 2)], writes=['Vs'])
                    if i is not None and br == 2:
                        S.op('pool', lambda e, st=st, j=j: e.tensor_copy(out=Vw[:, (i % 2) * 4 + j, :, 0:64], in_=st[:, 256:512].rearrange("p (k d) -> p k d", k=4)),
                             reads=[('stage', j % 2)], writes=['Vw'])
                    r0 = row0 + j * npart
                    if br < 2:
                        S.op('sp', lambda e, st=st, br=br, r0=r0: e.dma_start(out=outs[br][r0:r0 + npart, :], in_=st[:npart, :]),
                             reads=[('stage', j % 2)], writes=['kvout'], dma='st')
                    elif win_out is not None:
                        for (oap, iap) in win_out(j, st):
                            S.op('sp', lambda e, oap=oap, iap=iap: e.dma_start(out=oap, in_=iap), reads=[('stage', j % 2)], writes=['kvout'], dma='st')

        NSA_KEYS = [('Qx', 0), ('Qx', 1), 'kcT', 'Ob', ('PTs', 0), ('PTs', 1), ('PTs', 2), ('EcT', 0), ('EcT', 1), 'Ec', 'imp', 'hx', 't123', 'rsf']

        def gelu_tanh(dst_bf16, src_f32, n, dkey='hx'):
            S.op('dve', lambda e: e.tensor_tensor(out=hx2[0:64, :n], in0=src_f32, in1=src_f32, op=ALU.mult), reads=['hx'], writes=['hx'])
            S.op('dve', lambda e: e.tensor_scalar(out=hx2[0:64, :n], in0=hx2[0:64, :n], scalar1=0.044715, scalar2=1.0, op0=ALU.mult, op1=ALU.add), reads=['hx'], writes=['hx'])
            S.op('dve', lambda e: e.tensor_tensor(out=hx2[0:64, :n], in0=hx2[0:64, :n], in1=src_f32, op=ALU.mult), reads=['hx'], writes=['hx'])
            S.op('act', lambda e: e.activation(out=hx3[0:64, :n], in_=hx2[0:64, :n], func=AF.Sigmoid, scale=1.5957691216), reads=['hx'], writes=['hx'])
            S.op('dve', lambda e: e.tensor_tensor(out=dst_bf16, in0=hx3[0:64, :n], in1=src_f32, op=ALU.mult), reads=['hx'], writes=[dkey])

        def nsa_prompt(i, row0, outs, win_out):
            t0 = i * T
            nb = 16 * (i + 1)
            slot = i % 2
            pslot = (i - 1) % 2
            norm_hT(4, 128, 3)
            nsa_rows(4, 128, row0, outs, win_out, i=i)
            if dbg_stage < 1:
                return
            gw, gk = load_g(s_ng.rearrange("p c n -> p (c n)"), 48, 's_ng')
            for j in range(4):
                pb, pk = bank(4)
                for c in range(8):
                    S.op('pe', lambda e, c=c, j=j, pb=pb: e.matmul(pb[:, 0:48], lhsT=hT[:, c, j * 128:(j + 1) * 128], rhs=gw[:, c, :], start=(c == 0), stop=(c == 7)),
                         reads=[gk, 'hT'], writes=[pk], inc=(c == 7))
                S.op('act', lambda e, j=j, pb=pb: e.activation(out=G[:, j, :], in_=pb[:, 0:48], func=AF.Sigmoid), reads=[pk], writes=['G'])
            for (m, dstT, c0) in ((2, KsT, t0), (4, KwT, slot * T)):
                w, wk = load_g(s_nk[m].rearrange("p c j -> p (c j)"), 256, 's_nk')
                for kv in range(4):
                    pb, pk = bank(4)
                    for c in range(8):
                        S.op('pe', lambda e, c=c, kv=kv, pb=pb, w=w: e.matmul(pb[0:64, :], lhsT=w[:, c, kv * 64:(kv + 1) * 64], rhs=hT[:, c, :], start=(c == 0), stop=(c == 7)),
                             reads=[wk, 'hT'], writes=[pk], inc=(c == 7))
                    S.op('dve', lambda e, kv=kv, pb=pb, dstT=dstT, c0=c0: e.tensor_copy(out=dstT[0:64, kv, c0:c0 + T], in_=pb[0:64, :]),
                         reads=[pk], writes=['KsT' if m == 2 else 'KwT'])
            ACCC, ACCS, ACCW = (PB[4], ('PB', 4)), (PB[5], ('PB', 5)), (PB[6], ('PB', 6))
            def do_kv(kv):
                Qx = Qxb[kv % 2]
                qxk = ('Qx', kv % 2)
                if dbg_stage < 2:
                    return None
                wkc, wkck = load_g(s_nk[0].rearrange("p c j -> p (c j)"), 256, 's_nk')
                wvc, wvck = load_g(s_nk[1].rearrange("p c j -> p (c j)"), 256, 's_nk')
                for kk, (w, wk) in enumerate(((wkc, wkck), (wvc, wvck))):
                    pb, pk = bank(4)
                    for c in range(8):
                        S.op('pe', lambda e, c=c, pb=pb, w=w: e.matmul(pb[0:64, :], lhsT=w[:, c, kv * 64:(kv + 1) * 64], rhs=hT[:, c, :], start=(c == 0), stop=(c == 7)),
                             reads=[wk, 'hT'], writes=[pk], inc=(c == 7))
                    S.op('dve', lambda e, kk=kk, pb=pb: e.tensor_tensor(out=kcT[0:64, kk, :].rearrange("p (n l) -> p n l", l=32),
                                                                        in0=pb[0:64, :].rearrange("p (n l) -> p n l", l=32),
                                                                        in1=peT[:, :].unsqueeze(1).to_broadcast([64, 16, 32]), op=ALU.add),
                         reads=[pk, 'peT'], writes=['kcT'])
                for kk, W1 in enumerate((W1k, W1v)):
                    pb, pk = bank(4)
                    kc3 = kcT[0:64, kk, :].rearrange("p (n l) -> p n l", l=32)
                    for l in range(32):
                        S.op('pe', lambda e, l=l, pb=pb, W1=W1, kc3=kc3: e.matmul(pb[0:64, 0:16], lhsT=W1[:, l, :], rhs=kc3[:, :, l], start=(l == 0), stop=(l == 31)),
                             reads=['kcT', 'W1k', 'W1v'], writes=[pk], inc=(l == 31))
                    S.op('act', lambda e, kk=kk, pb=pb: e.activation(out=hx[0:64, :], in_=pb[0:64, 0:16], func=AF.Copy), reads=[pk], writes=['hx'])
                    if kk == 0:
                        gelu_tanh(hid[0:64, :], hx[0:64, :], 16)
                        pb2, pk2 = bank(4)
                        S.op('pe', lambda e, pb2=pb2: e.matmul(pb2[0:64, 0:16], lhsT=W2k[:, :], rhs=hid[0:64, :], start=True, stop=True), reads=['hx', 'W2k'], writes=[pk2])
                        S.op('dve', lambda e, pb2=pb2: e.tensor_copy(out=KcT[:, kv, 16 * i:16 * i + 16], in_=pb2[0:64, 0:16]), reads=[pk2], writes=['KcT'])
                    else:
                        gelu_tanh(hidV[:, kv, 16 * i:16 * i + 16], hx[0:64, :], 16, dkey='hidV')
                        pb2, pk2 = bank(4)
                        S.op('pe', lambda e, pb2=pb2: e.matmul(pb2[0:nb, 0:64], lhsT=hidV[:, kv, 0:nb], rhs=W2v[:, :], start=True, stop=True), reads=['hidV', 'W2v'], writes=[pk2])
                        S.op('dve', lambda e, pb2=pb2: e.tensor_copy(out=Vc[0:nb, kv, 0:64], in_=pb2[0:nb, 0:64]), reads=[pk2], writes=['Vc'])
                if dbg_stage < 3:
                    return None
                qw, qk = load_g(s_nq[kv].rearrange("p c j -> p (c j)"), 256, 's_nq')
                for g in range(4):
                    pb, pk = bank(4)
                    for c in range(8):
                        S.op('pe', lambda e, c=c, g=g, pb=pb: e.matmul(pb[0:64, :], lhsT=qw[:, c, g * 64:(g + 1) * 64], rhs=hT[:, c, :], start=(c == 0), stop=(c == 7)),
                             reads=[qk, 'hT'], writes=[pk], inc=(c == 7))
                    S.op('act', lambda e, g=g, pb=pb: e.activation(out=Qx[0:64, g, :], in_=pb[0:64, :], func=AF.Copy, scale=0.125), reads=[pk], writes=[qxk])
                for j in range(4 if dbg_stage >= 4 else 0):
                    pos = 4 * i + j
                    pb, pk = bank(4)
                    for g in range(4):
                        S.op('pe', lambda e, g=g, j=j, pb=pb: e.matmul(pb[:, g * 64:g * 64 + nb], lhsT=Qx[0:64, g, j * 128:(j + 1) * 128], rhs=KcT[:, kv, 0:nb], start=True, stop=True),
                             reads=[qxk, 'KcT'], writes=[pk], inc=(g == 3))
                    pb3 = pb[:, 0:256].rearrange("p (g n) -> p g n", g=4)
                    S.op('act', lambda e, pb3=pb3: e.activation(out=Ec[:, :, 0:nb], in_=pb3[:, :, 0:nb], func=AF.Exp), reads=[pk], writes=['Ec'])
                    S.op('dve', lambda e, pos=pos: e.tensor_tensor(out=Ec[:, :, 0:nb], in0=Ec[:, :, 0:nb], in1=Mc[:, pos, 0:nb].unsqueeze(1).to_broadcast([128, 4, nb]), op=ALU.mult),
                         reads=['Ec', 'Mc'], writes=['Ec'])
                    S.op('dve', lambda e: e.tensor_reduce(out=sums[:, :], in_=Ec[:, :, 0:nb], axis=AX.X, op=ALU.add), reads=['Ec'], writes=['imp'])
                    S.op('dve', lambda e: e.tensor_scalar(out=sums[:, :], in0=sums[:, :], scalar1=1e-30, scalar2=None, op0=ALU.max), reads=['imp'], writes=['imp'])
                    S.op('dve', lambda e: e.reciprocal(out=sums[:, :], in_=sums[:, :]), reads=['imp'], writes=['imp'])
                    S.op('dve', lambda e: e.tensor_tensor(out=Ec[:, :, 0:nb], in0=Ec[:, :, 0:nb], in1=sums[:, :].unsqueeze(2).to_broadcast([128, 4, nb]), op=ALU.mult),
                         reads=['Ec', 'imp'], writes=['Ec'])
                    S.op('dve', lambda e: e.tensor_reduce(out=imp4[:, 0:nb], in_=Ec[:, :, 0:nb].rearrange("p g n -> p n g"), axis=AX.X, op=ALU.add), reads=['Ec'], writes=['imp'])
                    S.op('dve', lambda e: e.memset(imps[:, :], 0.0), reads=['imp'], writes=['imp'])
                    i2 = imp4[:, 0:nb].rearrange("p (m two) -> p m two", two=2)
                    S.op('dve', lambda e, i2=i2: e.tensor_tensor(out=imps[:, 0:nb // 2], in0=i2[:, :, 0], in1=i2[:, :, 1], op=ALU.add), reads=['imp'], writes=['imp'])
                    S.op('dve', lambda e, pos=pos: e.tensor_tensor(out=score[:, :], in0=imps[:, :], in1=Acst[:, pos, :], op=ALU.mult), reads=['imp', 'Acst'], writes=['imp'])
                    S.op('dve', lambda e, pos=pos: e.tensor_tensor(out=score[:, :], in0=score[:, :], in1=Bcst[:, pos, :], op=ALU.add), reads=['imp', 'Bcst'], writes=['imp'])
                    S.op('dve', lambda e: e.max(out=m1[:, :], in_=score[:, :]), reads=['imp'], writes=['imp'])
                    S.op('dve', lambda e: e.match_replace(out=work[:, :], in_to_replace=m1[:, :], in_values=score[:, :], imm_value=-3e4), reads=['imp'], writes=['imp'])
                    S.op('dve', lambda e: e.max(out=m2[:, :], in_=work[:, :]), reads=['imp'], writes=['imp'])
                    S.op('dve', lambda e: e.tensor_scalar(out=selm[:, :], in0=score[:, :], scalar1=m2[:, 7:8], scalar2=None, op0=ALU.is_ge), reads=['imp'], writes=['imp'])
                    S.op('dve', lambda e: e.tensor_scalar(out=MBt[:, 64:96], in0=selm[:, :], scalar1=-1.0, scalar2=-NEGB, op0=ALU.add, op1=ALU.mult), reads=['imp'], writes=['MBt'])
                    S.op('pe', lambda e: e.transpose(out=PT[:, 0:128], in_=MBt[:, :], identity=ident[:, :]), reads=['MBt', 'ident'], writes=['PT'])
                    S.op('dve', lambda e, j=j: e.tensor_copy(out=Qx[64:96, :, j * 128:(j + 1) * 128], in_=PT[64:96, 0:128].unsqueeze(1).to_broadcast([32, 4, 128])),
                         reads=['PT'], writes=[qxk])
                def do_head(g):
                    h = kv * 4 + g
                    stages = []
                    ec = EcT[g % 2]; eck = ('EcT', g % 2)

                    def cmpA():
                        pb, pk = bank(4)
                        S.op('pe', lambda e, pb=pb: e.matmul(pb[0:nb, :], lhsT=KcT[:, kv, 0:nb], rhs=Qx[0:64, g, :], start=True, stop=True), reads=[qxk, 'KcT'], writes=[pk])
                        S.op('act', lambda e, pb=pb: e.activation(out=ec[0:nb, :], in_=pb[0:nb, :], func=AF.Exp), reads=[pk], writes=[eck])
                        S.op('dve', lambda e: e.tensor_tensor(out=ec[0:nb, :], in0=ec[0:nb, :], in1=McT[0:nb, i, :], op=ALU.mult), reads=[eck, 'McT'], writes=[eck])

                    def cmpB():
                        for j in range(4):
                            S.op('pe', lambda e, j=j: e.matmul(ACCC[0][:, j * 128:j * 128 + 65], lhsT=ec[0:nb, j * 128:(j + 1) * 128], rhs=Vc[0:nb, kv, 0:65], start=True, stop=True),
                                 reads=[eck, 'Vc'], writes=[ACCC[1]], inc=(j == 3))
                    stages.append((cmpA, cmpB))
                    rr = 0
                    for kc in range(4 * i + 4):
                        md = kc - 4 * i
                        c0 = 128 * md if md > 0 else 0
                        ptk = rr % 3; rr += 1

                        def selA(kc=kc, md=md, c0=c0, ptk=ptk):
                            pb, pk = bank(4)
                            pt = PTs[ptk]
                            S.op('pe', lambda e, pb=pb: e.matmul(pb[:, c0:T], lhsT=KsT[0:96, kv, kc * 128:(kc + 1) * 128], rhs=Qx[0:96, g, c0:T], start=True, stop=(md < 0)),
                                 reads=[qxk, 'KsT', 'KsTE'], writes=[pk], inc=(md < 0))
                            if md >= 0:
                                S.op('pe', lambda e, pb=pb: e.matmul(pb[:, c0:c0 + 128], lhsT=ident[:, :], rhs=TriGEb[:, :], start=False, stop=True),
                                     reads=['ident', 'TriGEb'], writes=[pk])
                            S.op('act', lambda e, pb=pb, pt=pt: e.activation(out=pt[:, c0:T], in_=pb[:, c0:T], func=AF.Exp), reads=[pk], writes=[('PTs', ptk)])

                        def selB(kc=kc, md=md, ptk=ptk):
                            pt = PTs[ptk]
                            for j in range(max(md, 0), 4):
                                S.op('pe', lambda e, j=j, pt=pt: e.matmul(ACCS[0][:, j * 128:j * 128 + 65], lhsT=pt[:, j * 128:(j + 1) * 128], rhs=Vs[:, kc, kv, 0:65],
                                                                        start=(kc == 0 and j == 0), stop=(kc == 4 * i + 3 and j == 3)),
                                     reads=[('PTs', ptk), 'Vs'], writes=[ACCS[1]], inc=(j == 3))
                        stages.append((selA, selB))
                    for m in range(0 if i > 0 else 4, 8):
                        if m < 4:
                            kcol = pslot * T + m * 128; vch = pslot * 4 + m
                            ca, cb = 0, 128 * (m + 1); mcol = 128 * m; js = range(0, m + 1)
                        else:
                            kcol = slot * T + (m - 4) * 128; vch = slot * 4 + (m - 4)
                            ca, cb = 128 * (m - 4), T; mcol = 128 * (m - 4); js = range(m - 4, 4)
                        trib = TriLTb if m < 4 else TriGEb
                        ptk = rr % 3; rr += 1

                        def winA(kcol=kcol, ca=ca, cb=cb, mcol=mcol, trib=trib, ptk=ptk):
                            pb, pk = bank(4)
                            pt = PTs[ptk]
                            S.op('pe', lambda e, pb=pb: e.matmul(pb[:, ca:cb], lhsT=KwT[0:64, kv, kcol:kcol + 128], rhs=Qx[0:64, g, ca:cb], start=True, stop=False),
                                 reads=[qxk, 'KwT'], writes=[pk], inc=False)
                            S.op('pe', lambda e, pb=pb: e.matmul(pb[:, mcol:mcol + 128], lhsT=ident[:, :], rhs=trib[:, :], start=False, stop=True),
                                 reads=['ident', 'TriGEb', 'TriLTb'], writes=[pk])
                            S.op('act', lambda e, pb=pb, pt=pt: e.activation(out=pt[:, ca:cb], in_=pb[:, ca:cb], func=AF.Exp), reads=[pk], writes=[('PTs', ptk)])

                        def winB(m=m, js=js, vch=vch, ptk=ptk):
                            pt = PTs[ptk]
                            for j in js:
                                S.op('pe', lambda e, j=j, pt=pt: e.matmul(ACCW[0][:, j * 128:j * 128 + 65], lhsT=pt[:, j * 128:(j + 1) * 128], rhs=Vw[:, vch, kv, 0:65],
                                                                        start=(m == (0 if i > 0 else 4) and j == 0), stop=(m == 7 and j == 3)),
                                     reads=[('PTs', ptk), 'Vw'], writes=[ACCW[1]], inc=(j == js[-1]))
                        stages.append((winA, winB))
                    LA = 2
                    for n in range(len(stages) + LA):
                        if n < len(stages):
                            stages[n][0]()
                        if n >= LA:
                            stages[n - LA][1]()
                    tmps = (tmpA, tmpB, tmpC)
                    for br, (acc, acck) in enumerate((ACCC, ACCS, ACCW)):
                        a3 = acc[:, :].rearrange("p (j d) -> p j d", j=4)
                        S.op('dve', lambda e, br=br, a3=a3: e.tensor_scalar(out=rsf[:, br, :], in0=a3[:, :, 64], scalar1=1e-30, scalar2=None, op0=ALU.max), reads=[acck], writes=['rsf'])
                        S.op('dve', lambda e, br=br: e.reciprocal(out=rsf[:, br, :], in_=rsf[:, br, :]), reads=['rsf'], writes=['rsf'])
                        S.op('dve', lambda e, br=br: e.tensor_tensor(out=rsf[:, br, :], in0=rsf[:, br, :], in1=G[:, :, br * 16 + h], op=ALU.mult), reads=['rsf', 'G'], writes=['rsf'])
                        S.op('dve', lambda e, br=br, a3=a3: e.tensor_tensor(out=tmps[br][:, :, :], in0=a3[:, :, 0:64], in1=rsf[:, br, :].unsqueeze(2).to_broadcast([128, 4, 64]), op=ALU.mult),
                             reads=[acck, 'rsf'], writes=['t123'])
                    S.op('dve', lambda e: e.tensor_tensor(out=tmpA[:, :, :], in0=tmpA[:, :, :], in1=tmpB[:, :, :], op=ALU.add), reads=['t123'], writes=['t123'])
                    S.op('dve', lambda e, h=h: e.tensor_tensor(out=Ob[:, :, h * 64:(h + 1) * 64], in0=tmpA[:, :, :], in1=tmpC[:, :, :], op=ALU.add), reads=['t123'], writes=['Ob'])
                return do_head
            heads = {}
            for kv in range(5):
                if kv < 4:
                    heads[kv] = do_kv(kv)
                if kv >= 1 and heads[kv - 1] is not None:
                    for g in range(4 if dbg_stage >= 5 else 0):
                        heads[kv - 1](g)
            for j in range(4):
                for c in range(8):
                    S.op('pe', lambda e, c=c, j=j: e.transpose(out=PT[:, c * 128:(c + 1) * 128], in_=Ob[:, j, c * 128:(c + 1) * 128], identity=ident[:, :]),
                         reads=['Ob', 'ident'], writes=['PT'], inc=(c == 7))
                S.op('dve', lambda e, j=j: e.tensor_copy(out=hT[:, :, j * 128:(j + 1) * 128], in_=PT[:, :].rearrange("p (c n) -> p c n", c=8)), reads=['PT'], writes=['srcT'])
            nouts = Stream([lambda: load_b(s_nout.rearrange("p c n -> p (c n)"), 8 * D, 's_nout')], 0)

            def wfun(hh):
                ap, k = nouts.get(0)
                return ap[:, 0:8 * D].rearrange("p (c n) -> p c n", c=8)[:, :, hh * 512:(hh + 1) * 512], k
            out_proj_add(hT, wfun, 4, 128, 1.0)

        o_scr = dscr("o_scr", [3, NB, 4, 4, DT, 64], F32)

        def nsa_sample(row0, outs, win_out):
            SALL = ['pe', 'act', 'dve', 'pool', 'sp']
            norm_hT(1, 16, 3)
            nsa_rows(1, 16, row0, outs, win_out)
            S.barrier(SALL, list(S.bufs.keys()))
            S.op('pool', lambda e: e.memset(Gsumf[:], 0.0), writes=['Gsum'])
            for k in range(-3, 4):
                S.op('pool', lambda e, k=k: e.affine_select(out=Gsumf[:], in_=Gsumf[:], pattern=[[-1, 16]], compare_op=ALU.not_equal, fill=1.0,
                                                            base=-4 * k, channel_multiplier=1), reads=['Gsum'], writes=['Gsum'])
            S.op('dve', lambda e: e.tensor_copy(out=Gsum[:], in_=Gsumf[:]), reads=['Gsum'], writes=['Gsumb'])
            S.op('pool', lambda e: e.memset(CM4[:], 0.0), writes=['CM4'])
            S.op('pool', lambda e: e.affine_select(out=CM4[:], in_=CM4[:], pattern=[[-1, 4]], compare_op=ALU.is_ge, fill=NEGB, base=0, channel_multiplier=1),
                 reads=['CM4'], writes=['CM4'])
            S.op('pool', lambda e: e.memset(WM4[:], 0.0), writes=['WM4'])
            S.op('pool', lambda e: e.affine_select(out=WM4[:, 0:4], in_=WM4[:, 0:4], pattern=[[1, 4]], compare_op=ALU.is_ge, fill=NEGB, base=-1, channel_multiplier=-1),
                 reads=['WM4'], writes=['WM4'])
            pb, pk = bank(4)
            S.op('pe', lambda e, pb=pb: e.matmul(pb[0:16, 0:4], lhsT=Gsum[0:4, :], rhs=CM4[:, :], start=True, stop=True), reads=['Gsumb', 'CM4'], writes=[pk])
            S.op('dve', lambda e, pb=pb: e.tensor_copy(out=CM16[:, :], in_=pb[0:16, 0:4]), reads=[pk], writes=['CM16'])
            pb, pk = bank(4)
            S.op('pe', lambda e, pb=pb: e.matmul(pb[0:16, :], lhsT=Gsum[0:4, :], rhs=WM4[:, :], start=True, stop=True), reads=['Gsumb', 'WM4'], writes=[pk])
            S.op('dve', lambda e, pb=pb: e.tensor_copy(out=WM16[:, :], in_=pb[0:16, :]), reads=[pk], writes=['WM16'])
            for k in range(2):
                S.op('pool', lambda e, k=k: e.memset(W1bd[k][:, :, :], 0.0), writes=[('W1bd', k)])
                src = (w_k1, w_v1)[k].rearrange("l d e -> d l e")
                S.op('pool', lambda e, k=k, src=src: e.dma_start(out=W1bd[k][0:64, :, 0:64], in_=src), reads=[('W1bd', k)], writes=[('W1bd', k)], dma='cw')
                S.op('pool', lambda e, k=k, src=src: e.dma_start(out=W1bd[k][64:128, :, 64:128], in_=src), reads=[('W1bd', k)], writes=[('W1bd', k)], dma='cw')
            S.op('pool', lambda e: e.memset(W2sel[:], 0.0), writes=['W2sel'])
            for k in range(2):
                src = (w_k2, w_v2)[k]
                S.op('pool', lambda e, k=k, src=src: e.dma_start(out=W2sel[0:64, k, 0, :], in_=src), reads=['W2sel'], writes=['W2sel'], dma='cw')
                S.op('pool', lambda e, k=k, src=src: e.dma_start(out=W2sel[64:128, k, 1, :], in_=src), reads=['W2sel'], writes=['W2sel'], dma='cw')
            S.op('sp', lambda e: e.dma_start(out=peT2[0:64, :], in_=peT[:, :]), reads=['peT'], writes=['peT2'], dma='const')
            S.op('sp', lambda e: e.dma_start(out=peT2[64:128, :], in_=peT[:, :]), reads=['peT'], writes=['peT2'], dma='const')
            S.op('pool', lambda e: e.dma_start(out=pt_i[:, :], in_=ptab.rearrange("b n -> (b n)").partition_broadcast(128)), writes=['pt_i'], dma='cw')
            S.op('pool', lambda e: e.iota(iot_i[:], pattern=[[0, 1]], base=0, channel_multiplier=1), writes=['iot'])
            S.op('dve', lambda e: e.tensor_copy(out=iot_f[:], in_=iot_i[:]), reads=['iot'], writes=['iotf'])
            S.op('dve', lambda e: e.tensor_copy(out=PTF[:, :, :].rearrange("p b n -> p (b n)"), in_=pt_i[:, :]), reads=['pt_i'], writes=['PTF'])
            S.op('dve', lambda e: e.tensor_scalar(out=PTF[:, :, :], in0=PTF[:, :, :], scalar1=128.0, scalar2=iot_f[:, 0:1], op0=ALU.mult, op1=ALU.add),
                 reads=['PTF', 'iotf'], writes=['PTF'])
            S.op('dve', lambda e: e.tensor_copy(out=IDX[:, :, :], in_=PTF[:, :, :]), reads=['PTF'], writes=['IDX'])
            gw, gk = load_g(s_ng.rearrange("p c n -> p (c n)"), 48, 's_ng')
            pb, pk = bank(4)
            for c in range(8):
                S.op('pe', lambda e, c=c, pb=pb: e.matmul(pb[0:16, 0:48], lhsT=hT[:, c, 0:16], rhs=gw[:, c, :], start=(c == 0), stop=(c == 7)), reads=[gk, 'hT'], writes=[pk], inc=(c == 7))
            S.op('act', lambda e, pb=pb: e.activation(out=G[0:16, 0, :], in_=pb[0:16, 0:48], func=AF.Sigmoid), reads=[pk], writes=['G'])
            for (m, dstT, key) in ((2, KsN, 'KsN'), (4, KwN, 'KwN')):
                w, wk = load_g(s_nk[m].rearrange("p c j -> p (c j)"), 256, 's_nk')
                for kv in range(4):
                    pb, pk = bank(4)
                    for c in range(8):
                        S.op('pe', lambda e, c=c, kv=kv, pb=pb, w=w: e.matmul(pb[0:64, 0:16], lhsT=w[:, c, kv * 64:(kv + 1) * 64], rhs=hT[:, c, 0:16], start=(c == 0), stop=(c == 7)),
                             reads=[wk, 'hT'], writes=[pk], inc=(c == 7))
                    S.op('dve', lambda e, kv=kv, pb=pb, dstT=dstT: e.tensor_copy(out=dstT[:, kv, :], in_=pb[0:64, 0:16]), reads=[pk], writes=[key])
            for kv in range(4):
                qw, qk = load_g(s_nq[kv].rearrange("p c j -> p (c j)"), 256, 's_nq')
                for g in range(4):
                    pb, pk = bank(4)
                    for c in range(8):
                        S.op('pe', lambda e, c=c, g=g, pb=pb, qw=qw: e.matmul(pb[0:64, 0:16], lhsT=qw[:, c, g * 64:(g + 1) * 64], rhs=hT[:, c, 0:16], start=(c == 0), stop=(c == 7)),
                             reads=[qk, 'hT'], writes=[pk], inc=(c == 7))
                    S.op('act', lambda e, g=g, kv=kv, pb=pb: e.activation(out=Qs[:, kv * 4 + g, :], in_=pb[0:64, 0:16], func=AF.Copy, scale=0.125), reads=[pk], writes=['Qs'])
            wap, wk3 = load_b(s_nkv.rearrange("p c n -> p (c n)"), 8 * 1536, 's_nkv')
            w3 = wap[:, 0:8 * 1536].rearrange("p (c n) -> p c n", c=8)
            for b in range(NB):
                for (br, dstV, key) in ((1, VnS, 'VnS'), (2, VnW, 'VnW')):
                    pb, pk = bank(4)
                    for c in range(8):
                        S.op('pe', lambda e, c=c, b=b, br=br, pb=pb: e.matmul(pb[0:4, 0:256], lhsT=hT[:, c, 4 * b:4 * b + 4], rhs=w3[:, c, br * 512 + 256:br * 512 + 512],
                                                                           start=(c == 0), stop=(c == 7)), reads=[wk3, 'hT'], writes=[pk], inc=(c == 7))
                    S.op('dve', lambda e, b=b, pb=pb, dstV=dstV: e.tensor_copy(out=dstV[:, b, :, :], in_=pb[0:4, 0:256].rearrange("p (k d) -> p k d", k=4)), reads=[pk], writes=[key])

            def transposeP(src16, ncols, dst, skey='Pb', dkey='PsT'):
                nk = ncols // 128
                for k4 in range(nk):
                    S.op('pe', lambda e, k4=k4: e.transpose(out=PT[:, k4 * 16:(k4 + 1) * 16], in_=src16[0:16, k4 * 128:(k4 + 1) * 128], identity=ident[0:16, 0:16]),
                         reads=[skey, 'ident'], writes=['PT'], inc=(k4 == nk - 1))
                S.op('dve', lambda e: e.tensor_copy(out=dst[:, 0:nk, :], in_=PT[:, 0:nk * 16].rearrange("p (k q) -> p k q", q=16)), reads=['PT'], writes=[dkey])

            sg1b = sgs[1][:, :].bitcast(BF16)
            SMs = [SM, sgs[0][0:16, :]]
            Pbs = [Pb, sg1b[0:16, 0:512]]
            PsTs = [PsT, sg1b[:, 512:576].rearrange("p (k q) -> p k q", q=16)]
            SMk = ['SM', ('sg', 0)]; Pbk = ['Pb', ('sg', 1)]; PsTk = ['PsT', 'PsT1']

            ACC = [(PB[4], ('PB', 4)), (PB[5], ('PB', 5)), (PB[6], ('PB', 6))]
            ACCS4 = [PB[3], PB[4], PB[5], PB[6]]
            SHq = [ident[0:64, :], SH[0:64, :]]

            def gather_group(cache, b, grp, npg=8, slot0=0):
                for k in range(npg):
                    p = grp * npg + k
                    S.op('pool', lambda e, k=k, p=p: e.indirect_dma_start(out=CG[:, slot0 + k, :], out_offset=None, in_=cache[:, :],
                                                                          in_offset=bass.IndirectOffsetOnAxis(ap=IDX[:, b, p:p + 1], axis=0)),
                         reads=['IDX'], writes=[('CG', slot0 + k)], dma='pg')

            def transpose_group(ccs, add_pe, npair=4, slot0=0, xoff=0, xkey='XT'):
                for k2 in range(npair):
                    for ci, cc in enumerate(ccs):
                        for pg2 in range(2):
                            k = slot0 + 2 * k2 + pg2
                            S.op('pe', lambda e, k=k, cc=cc, ci=ci, pg2=pg2: e.transpose(out=PT[:, (ci * 2 + pg2) * 128:(ci * 2 + pg2 + 1) * 128], in_=CG[:, k, cc * 128:(cc + 1) * 128], identity=ident[:, :]),
                                 reads=[('CG', k), 'ident'], writes=['PT'], inc=(ci == len(ccs) - 1 and pg2 == 1))
                    src = PT[:, 0:len(ccs) * 256].rearrange("p (c r) -> p c r", c=len(ccs))
                    dst = XT[:, 0:len(ccs), xoff + k2 * 256:xoff + (k2 + 1) * 256]
                    if add_pe:
                        S.op('dve', lambda e, src=src, dst=dst: e.tensor_tensor(out=dst.rearrange("p c (n l) -> p c n l", l=32), in0=src.rearrange("p c (n l) -> p c n l", l=32),
                                                                                in1=peT2[:, :].unsqueeze(1).unsqueeze(1).to_broadcast([128, len(ccs), 8, 32]), op=ALU.add),
                             reads=['PT', 'peT2'], writes=[xkey])
                    else:
                        S.op('dve', lambda e, src=src, dst=dst: e.tensor_copy(out=dst, in_=src), reads=['PT'], writes=[xkey])

            for b in range(NB):
                for grp in range(16):
                    gather_group(cache_cmp, b, grp)
                    transpose_group([0, 1, 2, 3], True)
                    for cc in range(4):
                        pb, pk = bank(4)
                        x3 = XT[:, cc, :].rearrange("p (n l) -> p n l", l=32)
                        for l in range(32):
                            S.op('pe', lambda e, l=l, cc=cc, pb=pb, x3=x3: e.matmul(pb[:, 0:32], lhsT=W1bd[cc // 2][:, l, :], rhs=x3[:, :, l], start=(l == 0), stop=(l == 31)),
                                 reads=['XT', ('W1bd', 0), ('W1bd', 1)], writes=[pk], inc=(l == 31))
                        S.op('act', lambda e, pb=pb: e.activation(out=Ec16_g[:, 0:32], in_=pb[:, 0:32], func=AF.Copy), reads=[pk], writes=['gel'])
                        S.op('dve', lambda e: e.tensor_tensor(out=Ec16_g[:, 64:96], in0=Ec16_g[:, 0:32], in1=Ec16_g[:, 0:32], op=ALU.mult), reads=['gel'], writes=['gel'])
                        S.op('dve', lambda e: e.tensor_scalar(out=Ec16_g[:, 64:96], in0=Ec16_g[:, 64:96], scalar1=0.044715, scalar2=1.0, op0=ALU.mult, op1=ALU.add), reads=['gel'], writes=['gel'])
                        S.op('dve', lambda e: e.tensor_tensor(out=Ec16_g[:, 64:96], in0=Ec16_g[:, 64:96], in1=Ec16_g[:, 0:32], op=ALU.mult), reads=['gel'], writes=['gel'])
                        S.op('act', lambda e: e.activation(out=Ec16_g[:, 128:160], in_=Ec16_g[:, 64:96], func=AF.Sigmoid, scale=1.5957691216), reads=['gel'], writes=['gel'])
                        S.op('dve', lambda e, cc=cc, grp=grp: e.tensor_tensor(out=hidS[:, cc, grp * 32:(grp + 1) * 32], in0=Ec16_g[:, 128:160], in1=Ec16_g[:, 0:32], op=ALU.mult),
                             reads=['gel'], writes=['hidS'])
                for kv in range(4):
                    pb, pk = bank(4)
                    S.op('pe', lambda e, kv=kv, pb=pb: e.matmul(pb[0:64, :], lhsT=W2sel[:, 0, kv % 2, :], rhs=hidS[:, kv // 2, :], start=True, stop=True), reads=['hidS', 'W2sel'], writes=[pk])
                    S.op('dve', lambda e, kv=kv, pb=pb: e.tensor_copy(out=KcS[:, kv, :], in_=pb[0:64, :]), reads=[pk], writes=['KcS'])
                    pb, pk = bank(4)
                    for q4 in range(4):
                        S.op('pe', lambda e, kv=kv, q4=q4, pb=pb: e.matmul(pb[:, q4 * 64:(q4 + 1) * 64], lhsT=hidS[:, 2 + kv // 2, q4 * 128:(q4 + 1) * 128], rhs=W2sel[:, 1, kv % 2, :], start=True, stop=True),
                             reads=['hidS', 'W2sel'], writes=[pk], inc=(q4 == 3))
                    S.op('dve', lambda e, kv=kv, pb=pb: e.tensor_copy(out=VcS[:, :, kv, :], in_=pb[:, 0:256].rearrange("p (q d) -> p q d", q=4)), reads=[pk], writes=['VcS'])
                for q4 in range(4):
                    S.op('pool', lambda e, q4=q4, b=b: e.dma_start(out=CGw[:, q4, :], in_=cache_win[b, q4 * 128:(q4 + 1) * 128, :]), writes=['CGw'], dma='pg')
                for kv in range(4):
                    S.op('dve', lambda e, kv=kv, b=b: e.tensor_copy(out=QB[:, kv, :].rearrange("p (g t) -> p g t", g=4), in_=Qs[:, kv * 4:(kv + 1) * 4, 4 * b:4 * b + 4]), reads=['Qs'], writes=['QB'])
                    pb, pk = bank(4)
                    S.op('pe', lambda e, kv=kv, pb=pb: e.matmul(pb[:, 0:16], lhsT=SHq[kv % 2], rhs=QB[:, kv, :], start=True, stop=True), reads=['QB', 'SH', 'ident'], writes=[pk])
                    S.op('dve', lambda e, kv=kv, pb=pb: e.tensor_copy(out=QBp[:, kv, :], in_=pb[:, 0:16]), reads=[pk], writes=['QBp'])
                    pb, pk = bank(4)
                    S.op('pe', lambda e, kv=kv, pb=pb: e.matmul(pb[0:16, :], lhsT=QB[:, kv, :], rhs=KcS[:, kv, :], start=True, stop=True), reads=['QB', 'KcS'], writes=[pk])
                    S.op('act', lambda e, pb=pb: e.activation(out=Ec16[:, :], in_=pb[0:16, :], func=AF.Exp, accum_out=rsS[:, 0:1]), reads=[pk], writes=['Ec16', 'rsS'])
                    S.op('dve', lambda e: e.tensor_copy(out=Pb[:, :], in_=Ec16[:, :]), reads=['Ec16'], writes=['Pb'])
                    S.op('dve', lambda e: e.reciprocal(out=rsS[:, 0:1], in_=rsS[:, 0:1]), reads=['rsS'], writes=['rsS'])
                    S.op('dve', lambda e: e.tensor_scalar(out=Ec16[:, :], in0=Ec16[:, :], scalar1=rsS[:, 0:1], scalar2=None, op0=ALU.mult), reads=['Ec16', 'rsS'], writes=['Ec16'])
                    pb, pk = bank(4)
                    S.op('pe', lambda e, pb=pb: e.matmul(pb[0:16, :], lhsT=Gsumf[:, :], rhs=Ec16[:, :], start=True, stop=True), reads=['Ec16', 'Gsum'], writes=[pk])
                    p3 = pb[0:16, :].rearrange("p (m two) -> p m two", two=2)
                    S.op('dve', lambda e, p3=p3: e.tensor_copy(out=imp16[:, :].rearrange("p (m two) -> p m two", two=2), in_=p3), reads=[pk], writes=['imp16'])
                    i3 = imp16[:, :].rearrange("p (m two) -> p m two", two=2)
                    S.op('dve', lambda e, i3=i3: e.tensor_tensor(out=imps16[:, :], in0=i3[:, :, 0], in1=i3[:, :, 1], op=ALU.add), reads=['imp16'], writes=['imps16'])
                    S.op('dve', lambda e: e.memset(imps16[:, 0:1], 1e4), reads=['imps16'], writes=['imps16'])
                    S.op('dve', lambda e: e.memset(imps16[:, 255:256], 1e4), reads=['imps16'], writes=['imps16'])
                    S.op('dve', lambda e: e.max(out=m1s[:, :], in_=imps16[:, :]), reads=['imps16'], writes=['m1s'])
                    S.op('dve', lambda e: e.match_replace(out=work16[:, :], in_to_replace=m1s[:, :], in_values=imps16[:, :], imm_value=-3e4), reads=['imps16', 'm1s'], writes=['work16'])
                    S.op('dve', lambda e: e.max(out=m2s[:, :], in_=work16[:, :]), reads=['work16'], writes=['m2s'])
                    S.op('dve', lambda e: e.tensor_scalar(out=mb16[:, :], in0=imps16[:, :], scalar1=m2s[:, 6:7], scalar2=None, op0=ALU.is_ge), reads=['imps16', 'm2s'], writes=['mb16'])
                    S.op('dve', lambda e: e.tensor_scalar(out=mb16[:, :], in0=mb16[:, :], scalar1=-1.0, scalar2=-NEGB, op0=ALU.add, op1=ALU.mult), reads=['mb16'], writes=['mb16'])
                    transposeP(Pb, 512, PsT)
                    for q4 in range(4):
                        S.op('pe', lambda e, kv=kv, q4=q4: e.matmul(ACC[0][0][0:16, 0:64], lhsT=PsT[:, q4, :], rhs=VcS[:, q4, kv, :], start=(q4 == 0), stop=(q4 == 3)),
                             reads=['PsT', 'VcS'], writes=[ACC[0][1]], inc=(q4 == 3))
                    S.op('dve', lambda e: e.tensor_scalar(out=OB3[:, 0, :], in0=ACC[0][0][0:16, 0:64], scalar1=rsS[:, 0:1], scalar2=None, op0=ALU.mult), reads=[ACC[0][1], 'rsS'], writes=['OB3'])
                    if kv == 0:
                        for q4 in range(4):
                            for c2 in range(2):
                                S.op('pe', lambda e, q4=q4, c2=c2: e.transpose(out=PT[:, (q4 * 2 + c2) * 128:(q4 * 2 + c2 + 1) * 128], in_=CGw[:, q4, c2 * 128:(c2 + 1) * 128], identity=ident[:, :]),
                                     reads=['CGw', 'ident'], writes=['PT'], inc=(q4 == 3 and c2 == 1))
                        S.op('dve', lambda e: e.tensor_copy(out=XTw[:, :, :].rearrange("p c (q r) -> p q c r", q=4), in_=PT[:, :].rearrange("p (q c r) -> p q c r", q=4, c=2)), reads=['PT'], writes=['XTw'])
                    ncol = 0
                    pb, pk = bank(4)
                    S.op('pe', lambda e, kv=kv, pb=pb: e.matmul(pb[0:16, :], lhsT=QBp[:, kv, :], rhs=XTw[:, kv // 2, :], start=True, stop=True), reads=['QBp', 'XTw'], writes=[pk])
                    S.op('dve', lambda e, pb=pb: e.tensor_tensor(out=SM[:, :], in0=pb[0:16, :], in1=WM16[:, :], op=ALU.add), reads=[pk, 'WM16'], writes=['SM'])
                    S.op('act', lambda e: e.activation(out=Pb[:, :], in_=SM[:, :], func=AF.Exp, accum_out=ssum[:, 0:1]), reads=['SM'], writes=['Pb', 'ssum'])
                    transposeP(Pb, 512, PsT)
                    for q4 in range(4):
                        S.op('pe', lambda e, kv=kv, q4=q4: e.matmul(ACC[2][0][0:16, 0:64], lhsT=PsT[:, q4, :], rhs=CGw[:, q4, 256 + kv * 64:256 + (kv + 1) * 64], start=(q4 == 0), stop=False),
                             reads=['PsT', 'CGw'], writes=[ACC[2][1]], inc=(q4 == 3))
                    pb, pk = bank(4)
                    S.op('pe', lambda e, kv=kv, b=b, pb=pb: e.matmul(pb[0:16, 0:4], lhsT=QB[:, kv, :], rhs=KwN[:, kv, 4 * b:4 * b + 4], start=True, stop=True), reads=['QB', 'KwN'], writes=[pk])
                    S.op('dve', lambda e, pb=pb: e.tensor_tensor(out=SM[:, 0:4], in0=pb[0:16, 0:4], in1=CM16[:, :], op=ALU.add), reads=[pk, 'CM16'], writes=['SM'])
                    S.op('act', lambda e: e.activation(out=Pb[:, 0:4], in_=SM[:, 0:4], func=AF.Exp, accum_out=ssum[:, 1:2]), reads=['SM'], writes=['Pb', 'ssum'])
                    S.op('pe', lambda e: e.transpose(out=PT[0:4, 0:16], in_=Pb[0:16, 0:4], identity=ident[0:16, 0:16]), reads=['Pb', 'ident'], writes=['PT'])
                    S.op('dve', lambda e: e.tensor_copy(out=PsT[0:4, 0, :], in_=PT[0:4, 0:16]), reads=['PT'], writes=['PsT'])
                    S.op('pe', lambda e, kv=kv, b=b: e.matmul(ACC[2][0][0:16, 0:64], lhsT=PsT[0:4, 0, :], rhs=VnW[:, b, kv, :], start=False, stop=True), reads=['PsT', 'VnW'], writes=[ACC[2][1]])
                    S.op('dve', lambda e: e.tensor_tensor(out=rsS[:, 2:3], in0=ssum[:, 0:1], in1=ssum[:, 1:2], op=ALU.add), reads=['ssum'], writes=['rsS'])
                    S.op('dve', lambda e: e.reciprocal(out=rsS[:, 2:3], in_=rsS[:, 2:3]), reads=['rsS'], writes=['rsS'])
                    S.op('dve', lambda e: e.tensor_scalar(out=OB3[:, 2, :], in0=ACC[2][0][0:16, 0:64], scalar1=rsS[:, 2:3], scalar2=None, op0=ALU.mult), reads=[ACC[2][1], 'rsS'], writes=['OB3'])
                    S.op('dve', lambda e, kv=kv: e.tensor_copy(out=mbAll[:, kv, :], in_=mb16[:, :]), reads=['mb16'], writes=['mbAll'])
                    S.op('dve', lambda e, kv=kv: e.tensor_copy(out=OBall[:, kv, 0, :], in_=OB3[:, 0, :]), reads=['OB3'], writes=['OBall'])
                    S.op('dve', lambda e, kv=kv: e.tensor_copy(out=OBall[:, kv, 2, :], in_=OB3[:, 2, :]), reads=['OB3'], writes=['OBall'])
                ncols = {kv: 0 for kv in range(4)}
                sel_stages = []
                nst = 0
                for grp in range(32):
                    for kv in range(4):
                        kb = nst % 2
                        nst += 1

                        def selA(grp=grp, kv=kv, kb=kb, b=b):
                            hf = grp % 2
                            xk = ('XTs', hf)
                            if kv == 0:
                                gather_group(cache_sel, b, grp, npg=4, slot0=4 * hf)
                                transpose_group([0, 1], False, npair=2, slot0=4 * hf, xoff=512 * hf, xkey=xk)
                            pb, pk = bank(3)
                            S.op('pe', lambda e, pb=pb: e.matmul(pb[0:16, :], lhsT=QBp[:, kv, :], rhs=XT[:, kv // 2, hf * 512:(hf + 1) * 512], start=True, stop=True),
                                 reads=['QBp', xk], writes=[pk])
                            blk0 = grp * 8
                            S.op('dve', lambda e, pb=pb: e.tensor_tensor(out=SMs[kb][:, :].rearrange("p (n l) -> p n l", l=64), in0=pb[0:16, :].rearrange("p (n l) -> p n l", l=64),
                                                                          in1=mbAll[:, kv, blk0:blk0 + 8].unsqueeze(2).to_broadcast([16, 8, 64]), op=ALU.add),
                                 reads=[pk, 'mbAll'], writes=[SMk[kb]])
                            S.op('act', lambda e: e.activation(out=Pbs[kb][:, :], in_=SMs[kb][:, :], func=AF.Exp, accum_out=ssumS[:, kv, grp:grp + 1]), reads=[SMk[kb]], writes=[Pbk[kb], 'ssumS'])

                        def selB(grp=grp, kv=kv, kb=kb):
                            hf = grp % 2
                            transposeP(Pbs[kb], 512, PsTs[kb], skey=Pbk[kb], dkey=PsTk[kb])
                            for q4 in range(4):
                                first = (grp == 0 and q4 == 0)
                                S.op('pe', lambda e, q4=q4, first=first: e.matmul(ACCS4[kv][0:16, 0:64], lhsT=PsTs[kb][:, q4, :], rhs=CG[:, 4 * hf + q4, 256 + kv * 64:256 + (kv + 1) * 64],
                                                                             start=first, stop=False),
                                     reads=[PsTk[kb], ('CG', 4 * hf + q4)], writes=[('PB', 3 + kv)], inc=(q4 == 3))
                        sel_stages.append((selA, selB))
                for n in range(len(sel_stages) + 1):
                    if n < len(sel_stages):
                        sel_stages[n][0]()
                    if n >= 1:
                        sel_stages[n - 1][1]()
                for kv in range(4):
                    S.op('dve', lambda e, kv=kv, b=b: e.tensor_copy(out=QB[:, kv, :].rearrange("p (g t) -> p g t", g=4), in_=Qs[:, kv * 4:(kv + 1) * 4, 4 * b:4 * b + 4]), reads=['Qs'], writes=['QB'])
                    pb, pk = bank(3)
                    S.op('pe', lambda e, kv=kv, b=b, pb=pb: e.matmul(pb[0:16, 0:4], lhsT=QB[:, kv, :], rhs=KsN[:, kv, 4 * b:4 * b + 4], start=True, stop=True), reads=['QB', 'KsN'], writes=[pk])
                    S.op('dve', lambda e, pb=pb: e.tensor_tensor(out=SM[:, 0:4], in0=pb[0:16, 0:4], in1=CM16[:, :], op=ALU.add), reads=[pk, 'CM16'], writes=['SM'])
                    S.op('act', lambda e, kv=kv: e.activation(out=Pb[:, 0:4], in_=SM[:, 0:4], func=AF.Exp, accum_out=ssumS[:, kv, 32:33]), reads=['SM'], writes=['Pb', 'ssumS'])
                    S.op('pe', lambda e: e.transpose(out=PT[0:4, 0:16], in_=Pb[0:16, 0:4], identity=ident[0:16, 0:16]), reads=['Pb', 'ident'], writes=['PT'])
                    S.op('dve', lambda e: e.tensor_copy(out=PsT[0:4, 0, :], in_=PT[0:4, 0:16]), reads=['PT'], writes=['PsT'])
                    S.op('pe', lambda e, kv=kv, b=b: e.matmul(ACCS4[kv][0:16, 0:64], lhsT=PsT[0:4, 0, :], rhs=VnS[:, b, kv, :], start=False, stop=True), reads=['PsT', 'VnS'], writes=[('PB', 3 + kv)])
                    S.op('dve', lambda e, kv=kv: e.tensor_reduce(out=rsS[:, 1:2], in_=ssumS[:, kv, 0:33], axis=AX.X, op=ALU.add), reads=['ssumS'], writes=['rsS'])
                    S.op('dve', lambda e: e.reciprocal(out=rsS[:, 1:2], in_=rsS[:, 1:2]), reads=['rsS'], writes=['rsS'])
                    S.op('dve', lambda e, kv=kv: e.tensor_scalar(out=OBall[:, kv, 1, :], in0=ACCS4[kv][0:16, 0:64], scalar1=rsS[:, 1:2], scalar2=None, op0=ALU.mult), reads=[('PB', 3 + kv), 'rsS'], writes=['OBall'])
                for kv in range(4):
                    for br in range(3):
                        S.op('sp', lambda e, kv=kv, br=br, b=b: e.dma_start(out=o_scr[br, b, kv].rearrange("g t d -> (g t) d"), in_=OBall[:, kv, br, :]), reads=['OBall'], writes=['o_scr'], dma='st')
            S.barrier(SALL, [('CG', k) for k in range(8)] + ['XT', ('XTs', 0), ('XTs', 1), 'hidS', ('W1bd', 0), ('W1bd', 1), 'CGw', 'XTw'])
            for br in range(3):
                for b in range(NB):
                    for t in range(DT):
                        S.op('sp', lambda e, br=br, b=b, t=t: e.dma_start(out=OTk[4 * b + t:4 * b + t + 1, br, :, :], in_=o_scr[br, b, :, :, t, :].rearrange("k g d -> (k g) d").unsqueeze(0)),
                             reads=['o_scr'], writes=['OTk'], dma='ldx')
            for br in range(3):
                S.op('dve', lambda e, br=br: e.tensor_tensor(out=OTk[:, br, :, :], in0=OTk[:, br, :, :], in1=G[0:16, 0, br * 16:(br + 1) * 16].unsqueeze(2).to_broadcast([16, 16, 64]), op=ALU.mult),
                     reads=['OTk', 'G'], writes=['OTk'])
            S.op('dve', lambda e: e.tensor_tensor(out=OTk[:, 0, :, :], in0=OTk[:, 0, :, :], in1=OTk[:, 1, :, :], op=ALU.add), reads=['OTk'], writes=['OTk'])
            S.op('dve', lambda e: e.tensor_tensor(out=ObS[:, :].rearrange("p (h d) -> p h d", h=16), in0=OTk[:, 0, :, :], in1=OTk[:, 2, :, :], op=ALU.add), reads=['OTk'], writes=['ObS'])
            for c in range(8):
                S.op('pe', lambda e, c=c: e.transpose(out=PT[:, c * 128:c * 128 + 16], in_=ObS[0:16, c * 128:(c + 1) * 128], identity=ident[0:16, 0:16]), reads=['ObS', 'ident'], writes=['PT'], inc=(c == 7))
            S.op('dve', lambda e: e.tensor_copy(out=hT[:, :, 0:16], in_=PT[:, :].rearrange("p (c n) -> p c n", c=8)[:, :, 0:16]), reads=['PT'], writes=['srcT'])
            nouts = Stream([lambda: load_b(s_nout.rearrange("p c n -> p (c n)"), 8 * D, 's_nout')], 0)

            def wfun(hh):
                ap, k = nouts.get(0)
                return ap[:, 0:8 * D].rearrange("p (c n) -> p c n", c=8)[:, :, hh * 512:(hh + 1) * 512], k
            out_proj_add(hT, wfun, 1, 16, 1.0)
            S.barrier(SALL, [('sg', 0), ('sg', 1), 'PsT1'])

        def final_norm(nsub, npart, out_ap):
            for j in range(nsub):
                S.op('act', lambda e, j=j: e.activation(out=junk[:npart, :], in_=X[:npart, j, :], func=AF.Square, scale=1.0 / 32.0,
                                                         accum_out=ss[:npart, j:j + 1]), reads=['X'], writes=['junk', 'ss'])
            S.op('dve', lambda e: e.tensor_scalar(out=rstd[:npart, :nsub], in0=ss[:npart, :nsub], scalar1=EPS, scalar2=None, op0=ALU.add),
                 reads=['ss'], writes=['rstd'])
            S.op('act', lambda e: e.activation(out=rstd[:npart, :nsub], in_=rstd[:npart, :nsub], func=AF.Sqrt), reads=['rstd'], writes=['rstd'])
            S.op('dve', lambda e: e.reciprocal(out=rstd[:npart, :nsub], in_=rstd[:npart, :nsub]), reads=['rstd'], writes=['rstd'])
            for j in range(nsub):
                S.op('dve', lambda e, j=j: e.scalar_tensor_tensor(out=X[:npart, j, :], in0=X[:npart, j, :], scalar=rstd[:npart, j:j + 1], in1=gfin[:npart, :],
                                                                  op0=ALU.mult, op1=ALU.mult), reads=['X', 'rstd', 'gfin'], writes=['X'])
            S.op('sp', lambda e: e.dma_start(out=out_ap, in_=X[:npart, 0:nsub, :]), reads=['X'], writes=['yout'], dma='st')

        ARENA_KEYS = ['srcT', 'U', 'cgs', 'cacc']
        ALLENG = ['pe', 'act', 'dve', 'pool', 'sp']

        def run_tile(nsub, npart, x_ap, y_ap, first, sample, cs_out, row0, kv_outs, win_out, tile_i=None):
            S.op('sp', lambda e: e.dma_start(out=X[:npart, 0:nsub, :], in_=x_ap), writes=['X'], dma='ldx')
            ffn(0, 'a', nsub, npart)
            S.barrier(ALLENG, ARENA_KEYS)
            conv_layer(nsub, npart, first, sample, cs_out)
            S.barrier(ALLENG, ARENA_KEYS)
            ffn(0, 'b', nsub, npart)
            ffn(1, 'a', nsub, npart)
            S.barrier(ALLENG, ARENA_KEYS)
            if sample:
                nsa_sample(row0, kv_outs, win_out)
            else:
                nsa_prompt(tile_i, row0, kv_outs, win_out)
            S.barrier(ALLENG, ARENA_KEYS + NSA_KEYS)
            ffn(1, 'b', nsub, npart)
            final_norm(nsub, npart, y_ap)

        for s in range(nseq):
            for i in range(seqlen // T):
                r0 = s * seqlen + i * T
                last = (i == seqlen // T - 1)
                run_tile(4, 128, xp[r0:r0 + T, :].rearrange("(j p) d -> p j d", p=128),
                         yp[r0:r0 + T, :].rearrange("(j p) d -> p j d", p=128),
                         first=(i == 0), sample=False, cs_out=(csp[s] if last else None), row0=r0, kv_outs=(cmp_p, sel_p), tile_i=i,
                         win_out=((lambda j, st, s=s: [(win_p[s, j * 128:(j + 1) * 128, :], st[:, :])]) if last else None))
        for b in range(NB if with_sample else 0):
            for t2 in range(2):
                S.op('sp', lambda e, b=b, t2=t2: e.dma_start(out=ucarS[:, b, :, t2], in_=stc[b, t2].rearrange("(c p) -> p c", p=128), allow_slow_non_contiguous=True),
                     writes=['ucar'], dma='const')
        if with_sample:
          run_tile(1, NB * DT, xs.rearrange("(j p) d -> p j d", j=1), ys.rearrange("(j p) d -> p j d", j=1), first=False,
                 sample=True, cs_out=css, row0=0, kv_outs=(cmp_s, sel_s),
                 win_out=(lambda j, st: [(win_s[b, WB - DT:WB, :], st[DT * b:DT * (b + 1), :]) for b in range(NB)]))
        for b in range(NB if with_sample else 0):
            S.op('sp', lambda e, b=b: e.dma_start(out=win_s[b, 0:WB - DT, :], in_=cache_win[b, DT:WB, :]), writes=['wins'], dma='st')
        S.finish()
        S.emit()
    return nc


_NC_CACHE = {}


def kernel(**inp):
    f32 = lambda a: np.ascontiguousarray(np.asarray(a, dtype=np.float32))
    x_prompt = f32(inp["x_prompt"]); x_sample = f32(inp["x_sample"])
    state_conv = f32(inp["state_conv"])
    cache_cmp = f32(inp["cache_cmp_kv"]).reshape(5120 * 128, 512)
    cache_sel = f32(inp["cache_sel_kv"]).reshape(5120 * 128, 512)
    cache_win = f32(inp["cache_win_kv"]).reshape(32, WB, 512)
    page_table = np.ascontiguousarray(np.asarray(inp["page_table"], dtype=np.int32))
    if 'nc' not in _NC_CACHE:
        _NC_CACHE['nc'] = build_nc()
    nc = _NC_CACHE['nc']
    shared = {
        "cache_cmp": cache_cmp, "cache_sel": cache_sel,
        "norm_ffa": f32(inp["norm_ffa"]), "norm_mix": f32(inp["norm_mix"]), "norm_ffb": f32(inp["norm_ffb"]),
        "w_ffa_gu": f32(inp["w_ffa_gu"]), "w_ffa_down": f32(inp["w_ffa_down"]),
        "w_ffb_gu": f32(inp["w_ffb_gu"]), "w_ffb_down": f32(inp["w_ffb_down"]),
        "w_conv_in": f32(inp["w_conv_in"])[0], "w_conv": f32(inp["w_conv"])[0], "w_conv_out": f32(inp["w_conv_out"])[0],
        "w_nsa_in": f32(inp["w_nsa_in"])[0], "pe_cmp": f32(inp["pe_cmp"])[0],
        "w_k1": f32(inp["w_cmp_k1"])[0], "w_k2": f32(inp["w_cmp_k2"])[0], "w_v1": f32(inp["w_cmp_v1"])[0], "w_v2": f32(inp["w_cmp_v2"])[0],
        "w_nsa_out": f32(inp["w_nsa_out"])[0], "norm_final": f32(inp["norm_final"]),
    }
    in_maps = []
    for c in range(NCORE):
        m = dict(shared)
        m["xp"] = x_prompt[NSEQ * c:NSEQ * (c + 1)].reshape(NSEQ * SEQ, D)
        m["xs"] = x_sample[NB * c:NB * (c + 1)].reshape(NB * DT, D)
        m["stc"] = np.ascontiguousarray(state_conv[0, NB * c:NB * (c + 1)])
        m["cache_win"] = np.ascontiguousarray(cache_win[NB * c:NB * (c + 1)])
        m["ptab"] = np.ascontiguousarray(page_table[NB * c:NB * (c + 1)])
        in_maps.append(m)
    res = run_bass_kernel_spmd(nc, in_maps, core_ids=list(range(NCORE)))
    R = res.results
    cat = lambda k: np.concatenate([np.asarray(r[k]) for r in R], axis=0)
    y_prompt = cat("yp").reshape(16, SEQ, D)
    y_sample = cat("ys").reshape(32, DT, D)
    conv_p = cat("csp").reshape(1, 16, 2, D)
    conv_s = cat("css").reshape(1, 32, 2, D)
    cmp_p = cat("cmp_p").reshape(1, 16, SEQ, 2, 4, 64)
    cmp_s = cat("cmp_s").reshape(1, 32, DT, 2, 4, 64)
    sel_p = cat("sel_p").reshape(1, 16, SEQ, 2, 4, 64)
    sel_s = cat("sel_s").reshape(1, 32, DT, 2, 4, 64)
    win_p = cat("win_p").reshape(1, 16, WB, 2, 4, 64)
    win_s = cat("win_s").reshape(1, 32, WB, 2, 4, 64)
    return (y_prompt, y_sample, conv_p, conv_s, cmp_p, cmp_s, sel_p, sel_s, win_p, win_s)
```

```python
import numpy as np
from contextlib import ExitStack
import concourse.bass as bass
import concourse.mybir as mybir
from concourse.bass_utils import run_bass_kernel_spmd

F32 = mybir.dt.float32
BF16 = mybir.dt.bfloat16
I32 = mybir.dt.int32
AF = mybir.ActivationFunctionType
ALU = mybir.AluOpType
AX = mybir.AxisListType

D = 1024
FF = 2816
NCORE = 8
SEQ = 2048
NSEQ = 2
NB = 4
DT = 4
T = 512
NSA_IN = 2608
NPAGE = 128
WB = 512
EPS = 1e-6
NEGB = -30000.0


class Sched:
    ENG = ('pe', 'act', 'dve', 'pool', 'sp')

    def __init__(self, nc, es):
        self.nc = nc
        self.es = es
        self.prog = {e: [] for e in self.ENG}
        self.tl_sem = {e: es.enter_context(nc.semaphore('tl_' + e)) for e in self.ENG if e != 'sp'}
        self.count = {e: 0 for e in self.tl_sem}
        self.known = {e: {} for e in self.ENG}
        self.bufs = {}
        self.dsem = {}
        self.dcount = {}
        self.pe_strict = False
        import os as _os
        self.same_eng = _os.environ.get('K_SAME_ENG', '1') == '1'
        self.dma_rr = _os.environ.get('K_DMA_RR', '1') == '1'
        self.drr = {}

    DMA_K = {'cast': 8, 'st': 8, 'const': 4, 'cw': 2, 'ldx': 2, 'pg': 8}

    def _dma_sem(self, name):
        if name not in self.drr:
            self.drr[name] = 0
        k = (self.drr[name] % self.DMA_K.get(name, 1)) if self.dma_rr else 0
        self.drr[name] += 1
        sub = f"{name}_{k}"
        if sub not in self.dsem:
            self.dsem[sub] = self.es.enter_context(self.nc.semaphore('d_' + sub))
            self.dcount[sub] = 0
        return sub

    def _need(self, eng, src, val, waits):
        if self.known[eng].get(src, 0) >= val:
            return
        if waits.get(src, 0) < val:
            waits[src] = val

    def op(self, eng, fn, reads=(), writes=(), inc=True, dma=None):
        waits = {}
        for k in reads:
            b = self.bufs.get(k)
            if b and b['w']:
                self._need(eng, b['w'][0], b['w'][1], waits)
        for k in writes:
            b = self.bufs.get(k)
            if b:
                if b['w']:
                    self._need(eng, b['w'][0], b['w'][1], waits)
                for src, val in b['r'].items():
                    self._need(eng, src, val, waits)
        me_src = ('tl', eng)
        if not self.same_eng and me_src in waits:
            rv = 0
            for k in reads:
                b = self.bufs.get(k)
                if b and b['w'] and b['w'][0] == me_src:
                    rv = max(rv, b['w'][1])
            if rv > self.known[eng].get(me_src, 0) and rv <= self.count.get(eng, 0):
                waits[me_src] = rv
            else:
                del waits[me_src]
        if me_src in waits and (waits[me_src] > self.count.get(eng, 0) or eng == 'pe'):
            if eng == 'pe' and waits[me_src] <= self.count['pe'] and self.pe_strict:
                pass
            else:
                del waits[me_src]
        for src, val in waits.items():
            self.known[eng][src] = val
        if dma is not None:
            sub = self._dma_sem(dma)
            if self.dma_rr and self.dcount[sub] > 0 and self.known[eng].get(('d', sub), 0) < self.dcount[sub]:
                waits[('d', sub)] = self.dcount[sub]
                self.known[eng][('d', sub)] = self.dcount[sub]
            sem = self.dsem[sub]
            self.dcount[sub] += 16
            me = (('d', sub), self.dcount[sub])
        else:
            if inc:
                self.count[eng] += 1
                me = (('tl', eng), self.count[eng])
            else:
                me = (('tl', eng), self.count[eng] + 1)
            sem = self.tl_sem[eng]
        for k in reads:
            b = self.bufs.setdefault(k, {'w': None, 'r': {}})
            if b['r'].get(me[0], 0) < me[1]:
                b['r'][me[0]] = me[1]
        for k in writes:
            self.bufs[k] = {'w': me, 'r': {}}
        wl = [((self.tl_sem[s[1]] if s[0] == 'tl' else self.dsem[s[1]]), v) for s, v in waits.items()]
        self.prog[eng].append((wl, fn, sem if (inc or dma is not None) else None, 16 if dma is not None else 1))

    def barrier(self, engines, keys):
        for eng in engines:
            waits = {}
            for k in keys:
                b = self.bufs.get(k)
                if not b:
                    continue
                if b['w']:
                    self._need(eng, b['w'][0], b['w'][1], waits)
                for src, val in b['r'].items():
                    self._need(eng, src, val, waits)
            me_src = ('tl', eng)
            if me_src in waits and waits[me_src] > self.count.get(eng, 0):
                del waits[me_src]
            for src, val in waits.items():
                self.known[eng][src] = val
            wl = [((self.tl_sem[s[1]] if s[0] == 'tl' else self.dsem[s[1]]), v) for s, v in waits.items()]
            if wl:
                self.prog[eng].append((wl, None, None, 0))

    def finish(self):
        wl = [(self.dsem[n], self.dcount[n]) for n in self.dsem]
        self.prog['sp'].append((wl, None, None, 0))

    def emit(self):
        nc = self.nc
        emap = {'pe': 'tensor', 'act': 'scalar', 'dve': 'vector', 'pool': 'gpsimd', 'sp': 'sync'}
        with nc.Block() as block:
            for e in self.ENG:
                prog = self.prog[e]

                def body(engobj, prog=prog):
                    for wl, fn, sem, incv in prog:
                        for s, v in wl:
                            engobj.wait_ge(s, v)
                        if fn is not None:
                            ins = fn(engobj)
                            if sem is not None:
                                ins.then_inc(sem, incv)
                getattr(block, emap[e])(body)


def build_nc(nseq=NSEQ, seqlen=SEQ, with_sample=True, nphys=5120, dbg_stage=9):
    nc = bass.Bass("TRN2", target_bir_lowering=False)
    es = ExitStack()

    def din(name, shape, dt=F32):
        return nc.dram_tensor(name, list(shape), dt, kind="ExternalInput").ap()

    def dout(name, shape, dt=F32):
        return nc.dram_tensor(name, list(shape), dt, kind="ExternalOutput").ap()

    def dscr(name, shape, dt=BF16):
        return nc.dram_tensor(name, list(shape), dt, kind="Internal").ap()

    xp = din("xp", [nseq * seqlen, D])
    xs = din("xs", [NB * DT, D])
    stc = din("stc", [NB, 2, D])
    cache_cmp = din("cache_cmp", [nphys * 128, 512])
    cache_sel = din("cache_sel", [nphys * 128, 512])
    cache_win = din("cache_win", [NB, WB, 512])
    ptab = din("ptab", [NB, NPAGE], I32)
    norm_ffa = din("norm_ffa", [2, D]); norm_mix = din("norm_mix", [2, D]); norm_ffb = din("norm_ffb", [2, D])
    w_ffa_gu = din("w_ffa_gu", [2, D, 2 * FF]); w_ffa_down = din("w_ffa_down", [2, FF, D])
    w_ffb_gu = din("w_ffb_gu", [2, D, 2 * FF]); w_ffb_down = din("w_ffb_down", [2, FF, D])
    w_conv_in = din("w_conv_in", [D, 3 * D]); w_conv = din("w_conv", [3, D]); w_conv_out = din("w_conv_out", [D, D])
    w_nsa_in = din("w_nsa_in", [D, NSA_IN]); pe_cmp = din("pe_cmp", [32, 64])
    w_k1 = din("w_k1", [32, 64, 64]); w_k2 = din("w_k2", [64, 64]); w_v1 = din("w_v1", [32, 64, 64]); w_v2 = din("w_v2", [64, 64])
    w_nsa_out = din("w_nsa_out", [D, D]); norm_final = din("norm_final", [D])

    yp = dout("yp", [nseq * seqlen, D]); ys = dout("ys", [NB * DT, D])
    csp = dout("csp", [nseq, 2, D]); css = dout("css", [NB, 2, D])
    cmp_p = dout("cmp_p", [nseq * seqlen, 512]); cmp_s = dout("cmp_s", [NB * DT, 512])
    sel_p = dout("sel_p", [nseq * seqlen, 512]); sel_s = dout("sel_s", [NB * DT, 512])
    win_p = dout("win_p", [nseq, WB, 512]); win_s = dout("win_s", [NB, WB, 512])

    s_gu = {(l, ab): dscr(f"s_gu{l}{ab}", [22, 128, 8, 2, 128]) for l in range(2) for ab in 'ab'}
    s_dn = {(l, ab): dscr(f"s_dn{l}{ab}", [2, 128, 22, 512]) for l in range(2) for ab in 'ab'}
    s_cin = dscr("s_cin", [8, 128, 8, 3, 128])
    s_cout = dscr("s_cout", [128, 8, D])
    s_nq = dscr("s_nq", [4, 128, 8, 256])
    s_nk = dscr("s_nk", [6, 128, 8, 256])
    s_nkv = dscr("s_nkv", [128, 8, 1536])
    s_ng = dscr("s_ng", [128, 8, 48])
    s_nout = dscr("s_nout", [128, 8, D])

    with es:
        S = Sched(nc, es)

        def sb(name, shape, dt):
            return es.enter_context(nc.sbuf_tensor(name, list(shape), dt))

        def ps(name, shape, dt):
            return es.enter_context(nc.psum_tensor(name, list(shape), dt))

        X = sb("X", [128, 4, D], F32)
        hb = sb("hb", [128, D], BF16)
        junk = sb("junk", [128, D], BF16)
        hT = sb("hT", [128, 8, T], BF16)
        ARENA = 29 * 1024
        arena = sb("arena", [128, ARENA], mybir.dt.uint8)

        def aview(off, shape, dt, parts=128):
            nbytes = int(np.prod(shape)) * (2 if dt == BF16 else 4)
            assert off % 4 == 0 and off + nbytes <= ARENA, (off, nbytes, shape)
            v = arena[0:parts, off:off + nbytes].bitcast(dt)
            if len(shape) == 1:
                return v
            names = " ".join(f"a{i}" for i in range(len(shape)))
            kw = {f"a{i}": shape[i] for i in range(1, len(shape))}
            return v.rearrange(f"p ({names}) -> p {names}", **kw)

        KB = 1024
        import os as _os3
        NOARENA = _os3.environ.get('K_ARENA', '1') == '0'
        if NOARENA:
            aT = sb("aT", [128, 22, T], BF16)
            U = sb("U", [128, 8, T + 2], F32)
            zT = sb("zT", [128, 8, T], BF16)
            cgs = sb("cgs", [128, T], F32)
            cacc = sb("cacc", [128, T], F32)
        else:
            aT = aview(0, [22, T], BF16)
            U = aview(0, [8, T + 2], F32)
            zT = aview(17 * KB, [8, T], BF16)
            cgs = aview(25 * KB, [T], F32)
            cacc = aview(27 * KB, [T], F32)
        Qxb = [aview(0, [4, T], BF16), aview(25 * KB, [4, T], BF16)]
        kcT = aview(4 * KB, [2, T], BF16)
        Ob = aview(6 * KB, [4, D], BF16)
        PTs = [aview(14 * KB + k * KB, [T], BF16) for k in range(3)]
        EcT = [aview(17 * KB + k * KB, [T], BF16) for k in range(2)]
        Ec = aview(19 * KB, [4, 64], F32)
        imp4 = aview(20 * KB, [64], F32)
        imps = aview(20 * KB + 256, [32], F32)
        score = aview(20 * KB + 384, [32], F32)
        work = aview(20 * KB + 512, [32], F32)
        selm = aview(20 * KB + 640, [32], F32)
        m1 = aview(20 * KB + 768, [8], F32)
        m2 = aview(20 * KB + 800, [8], F32)
        sums = aview(20 * KB + 832, [4], F32)
        rsf = aview(20 * KB + 848, [3, 4], F32)
        hx = aview(21 * KB, [16], F32)
        hx2 = aview(21 * KB + 64, [16], F32)
        hx3 = aview(21 * KB + 128, [16], F32)
        hid = aview(21 * KB + 192, [16], BF16)
        tmpA = aview(22 * KB, [4, 64], F32)
        tmpB = aview(23 * KB, [4, 64], F32)
        tmpC = aview(24 * KB, [4, 64], F32)
        ARENA2 = 53632
        arena2 = sb("arena2", [128, ARENA2], mybir.dt.uint8)

        def aview2(off, shape, dt, parts=128):
            nbytes = int(np.prod(shape)) * (2 if dt == BF16 else 4)
            assert off % 4 == 0 and off + nbytes <= ARENA2, (off, nbytes)
            v = arena2[0:parts, off:off + nbytes].bitcast(dt)
            names = " ".join(f"a{i}" for i in range(len(shape)))
            kw = {f"a{i}": shape[i] for i in range(1, len(shape))}
            return v.rearrange(f"p ({names}) -> p {names}", **kw) if len(shape) > 1 else v

        KsT = aview2(0, [4, SEQ], BF16)
        KwT = aview2(16384, [4, 2 * T], BF16)
        Vs = aview2(24576, [16, 4, 66], BF16)
        Vw = aview2(33024, [8, 4, 66], BF16)
        Mc = aview2(37248, [16, 64], F32)
        McT = aview2(41344, [4, T], BF16, parts=64)
        E32 = aview2(45440, [SEQ], BF16, parts=32)
        Acst = aview2(49536, [16, 32], F32)
        Bcst = aview2(51584, [16, 32], F32)
        KcT = sb("KcT", [64, 4, 64], BF16)
        hidV = sb("hidV", [64, 4, 64], BF16)
        Vc = sb("Vc", [64, 4, 66], BF16)
        CG = aview2(0, [8, 512], BF16)
        XT = aview2(8192, [4, 1024], BF16)
        W1bd = [aview2(16384 + 8192 * k, [32, 128], BF16) for k in range(2)]
        hidS = aview2(32768, [4, 512], BF16)
        CGw = aview2(36864, [4, 512], BF16)
        XTw = aview2(40960, [2, 512], BF16)
        mbAll = aview2(43008, [4, 256], F32, parts=16)
        OBall = aview2(47104, [4, 3, 64], F32, parts=16)
        ssumS = aview2(50176, [4, 33], F32, parts=16)
        ObS = aview2(50720, [D], BF16, parts=16)
        Ec16_g = aview2(52768, [192], F32)
        OTk = aview2(0, [3, 16, 64], F32, parts=16)
        KcS = aview(0, [4, 512], BF16, parts=64)
        VcS = aview(4096, [4, 4, 64], BF16)
        IDX = aview(6144, [NB, NPAGE], I32)
        PTF = aview(8192, [NB, NPAGE], F32)
        Ec16 = aview(10240, [512], F32, parts=16)
        SM = aview(12288, [512], F32, parts=16)
        Pb = aview(14336, [512], BF16, parts=16)
        PsT = aview(15360, [4, 16], BF16)
        mb16 = aview(15616, [256], F32, parts=16)
        imp16 = aview(16640, [512], F32, parts=16)
        imps16 = aview(18688, [256], F32, parts=16)
        work16 = aview(19712, [256], F32, parts=16)
        m1s = aview(20736, [8], F32, parts=16)
        m2s = aview(20768, [8], F32, parts=16)
        ssum = aview(20800, [48], F32, parts=16)
        rsS = aview(20992, [4], F32, parts=16)
        QB = aview(21056, [4, 16], BF16, parts=64)
        QBp = aview(21184, [4, 16], BF16)
        OB3 = aview(21312, [3, 64], F32, parts=16)
        Qs = aview(22080, [16, 16], BF16, parts=64)
        KsN = aview(22592, [4, 16], BF16, parts=64)
        KwN = aview(22720, [4, 16], BF16, parts=64)
        VnS = aview(22848, [NB, 4, 64], BF16, parts=4)
        VnW = aview(24896, [NB, 4, 64], BF16, parts=4)
        WM16 = aview(26944, [512], F32, parts=16)
        CM16 = aview(28992, [4], F32, parts=16)
        Gsum = sb("Gsum", [16, 16], BF16)
        Gsumf = sb("Gsumf", [16, 16], F32)
        W2sel = sb("W2sel", [128, 2, 2, 64], BF16)
        peT2 = sb("peT2", [128, 32], F32)
        iot_i = sb("iot_i", [128, 1], I32)
        iot_f = sb("iot_f", [128, 1], F32)
        CM4 = sb("CM4", [4, 4], BF16)
        WM4 = sb("WM4", [4, 512], BF16)
        pt_i = sb("pt_i", [128, NB * NPAGE], I32)
        G = sb("G", [128, 4, 48], F32)
        W1k = None if NOARENA else sb("W1k", [64, 32, 64], BF16); W1v = None if NOARENA else sb("W1v", [64, 32, 64], BF16)
        W2k = sb("W2k", [64, 64], BF16); W2v = sb("W2v", [64, 64], BF16)
        peT = sb("peT", [64, 32], F32)
        pe_nat = sb("pe_nat", [32, 64], F32)
        TriGE = sb("TriGE", [128, 128], BF16); TriLT = sb("TriLT", [128, 128], BF16)
        TriGEb = sb("TriGEb", [128, 128], BF16); TriLTb = sb("TriLTb", [128, 128], BF16)
        SH = sb("SH", [128, 128], BF16)
        MBt = sb("MBt", [128, 128], BF16)
        ucarS = sb("ucarS", [128, NB, 8, 2], F32)
        ringg = [sb(f"ringg{i}", [128, 8 * 384], BF16) for i in range(3)]
        ringb = [sb(f"ringb{i}", [128, 12288], BF16) for i in range(2)]
        ident = sb("ident", [128, 128], BF16)
        identf = sb("identf", [128, 128], F32)
        gT = sb("gT", [128, 6, 8], F32)
        gfin = sb("gfin", [128, D], F32)
        wcv = sb("wcv", [128, 3, 8], F32)
        ss = sb("ss", [128, 4], F32)
        rstd = sb("rstd", [128, 4], F32)
        ucar = sb("ucar", [128, 8, 2], F32)
        sgs = [sb(f"sg{i}", [128, T], F32) for i in range(2)]
        stage = [sb(f"stage{i}", [128, 512], F32) for i in range(2)]

        PT = ps("PT", [128, 1024], BF16)
        PB = [ps(f"PB{i}", [128, 512], F32) for i in range(7)]

        def cast(dst, src, key):
            S.op('pool', lambda e: e.dma_start(out=dst, in_=src), writes=[key], dma='cast')

        def cast_gu(l, ab):
            w = (w_ffa_gu if ab == 'a' else w_ffb_gu)[l]
            for f in range(22):
                for u in range(2):
                    cast(s_gu[(l, ab)][f, :, :, u, :],
                         w[:, u * FF + f * 128:u * FF + (f + 1) * 128].rearrange("(c p) j -> p c j", p=128), ('s_gu', l, ab, f, u))

        def cast_dn(l, ab):
            w = (w_ffa_down if ab == 'a' else w_ffb_down)[l]
            for h in range(2):
                cast(s_dn[(l, ab)][h], w[:, h * 512:(h + 1) * 512].rearrange("(f p) n -> p f n", p=128), ('s_dn', l, ab, h))

        cast_gu(0, 'a'); cast_dn(0, 'a')
        for s3 in range(3):
            for ee in range(8):
                cast(s_cin[ee, :, :, s3, :], w_conv_in[:, s3 * D + ee * 128:s3 * D + (ee + 1) * 128].rearrange("(c p) j -> p c j", p=128), 's_cin')
        cast(s_cout, w_conv_out.rearrange("(c p) n -> p c n", p=128), 's_cout')
        cast_gu(0, 'b'); cast_dn(0, 'b')
        cast_gu(1, 'a'); cast_dn(1, 'a')
        for m in range(4):
            cast(s_nq[m], w_nsa_in[:, m * 256:(m + 1) * 256].rearrange("(c p) j -> p c j", p=128), 's_nq')
        for m in range(6):
            cast(s_nk[m], w_nsa_in[:, 1024 + m * 256:1024 + (m + 1) * 256].rearrange("(c p) j -> p c j", p=128), 's_nk')
        cast(s_nkv, w_nsa_in[:, 1024:2560].rearrange("(c p) n -> p c n", p=128), 's_nkv')
        cast(s_ng, w_nsa_in[:, 2560:2608].rearrange("(c p) n -> p c n", p=128), 's_ng')
        cast(s_nout, w_nsa_out.rearrange("(c p) n -> p c n", p=128), 's_nout')
        cast_gu(1, 'b'); cast_dn(1, 'b')

        S.op('pool', lambda e: e.memset(identf[:], 0.0), writes=['identf'])
        S.op('pool', lambda e: e.affine_select(out=identf[:], in_=identf[:], pattern=[[-1, 128]], compare_op=ALU.not_equal,
                                               fill=1.0, base=0, channel_multiplier=1), reads=['identf'], writes=['identf'])
        S.op('dve', lambda e: e.tensor_copy(out=ident[:], in_=identf[:]), reads=['identf'], writes=['ident'])
        for k, nt in enumerate([norm_ffa, norm_mix, norm_ffb]):
            for l in range(2):
                S.op('sp', lambda e, k=k, l=l, nt=nt: e.dma_start(out=gT[:, 2 * k + l, :], in_=nt[l].rearrange("(c p) -> p c", p=128),
                                                                  allow_slow_non_contiguous=True), writes=['gT'], dma='const')
        S.op('sp', lambda e: e.dma_start(out=gfin[:], in_=norm_final.partition_broadcast(128)), writes=['gfin'], dma='const')
        for i3 in range(3):
            S.op('sp', lambda e, i3=i3: e.dma_start(out=wcv[:, i3, :], in_=w_conv[i3].rearrange("(c p) -> p c", p=128),
                                                    allow_slow_non_contiguous=True), writes=['wcv'], dma='const')

        import os as _os2
        KC = int(_os2.environ.get('K_CONST', '3'))
        if KC >= 1:
            def pool_op(fn, reads=(), writes=()):
                S.op('pool', fn, reads=reads, writes=writes)

            def aff(out_ap, pattern, base, cm, key, fill=0.0, op=ALU.is_ge):
                pool_op(lambda e: e.affine_select(out=out_ap, in_=out_ap, pattern=pattern, compare_op=op, fill=fill, base=base, channel_multiplier=cm),
                        reads=[key], writes=[key])

            pool_op(lambda e: e.memset(TriGE[:], 1.0), writes=['TriGE'])
            aff(TriGE[:], [[1, 128]], 0, -1, 'TriGE')
            pool_op(lambda e: e.memset(TriLT[:], 1.0), writes=['TriLT'])
            aff(TriLT[:], [[-1, 128]], -1, 1, 'TriLT')
            S.op('dve', lambda e: e.tensor_scalar(out=TriGEb[:], in0=TriGE[:], scalar1=-1.0, scalar2=-NEGB, op0=ALU.add, op1=ALU.mult), reads=['TriGE'], writes=['TriGEb'])
            S.op('dve', lambda e: e.tensor_scalar(out=TriLTb[:], in0=TriLT[:], scalar1=-1.0, scalar2=-NEGB, op0=ALU.add, op1=ALU.mult), reads=['TriLT'], writes=['TriLTb'])
            pool_op(lambda e: e.memset(SH[:], 0.0), writes=['SH'])
            aff(SH[:], [[-1, 128]], 64, 1, 'SH', fill=1.0, op=ALU.not_equal)
            pool_op(lambda e: e.memset(E32[:], 1.0), writes=['E32'])
            aff(E32[:], [[1, SEQ]], 0, -64, 'E32')
            aff(E32[:], [[-1, SEQ]], 63, 64, 'E32')
            pool_op(lambda e: e.memset(Mc[:], 1.0), writes=['Mc'])
            for pos in range(16):
                aff(Mc[:, pos, :], [[-32, 64]], 128 * pos - 31, 1, 'Mc')
            pool_op(lambda e: e.memset(McT[:], 1.0), writes=['McT'])
            for i4 in range(4):
                aff(McT[:, i4, :], [[1, T]], T * i4 - 31, -32, 'McT')
            pool_op(lambda e: e.memset(Acst[:], 0.0), writes=['Acst'])
            pool_op(lambda e: e.memset(Bcst[:], -1e4), writes=['Bcst'])
            for pos in range(16):
                for half in range(2):
                    cur = 2 * pos + half
                    pr = slice(64 * half, 64 * half + 64)
                    if cur - 1 > 1:
                        pool_op(lambda e, pr=pr, pos=pos, cur=cur: e.memset(Acst[pr, pos, 1:cur - 1], 1.0), reads=['Acst'], writes=['Acst'])
                        pool_op(lambda e, pr=pr, pos=pos, cur=cur: e.memset(Bcst[pr, pos, 1:cur - 1], 0.0), reads=['Bcst'], writes=['Bcst'])
                    pool_op(lambda e, pr=pr, pos=pos, cur=cur: e.memset(Bcst[pr, pos, max(cur - 1, 0):cur + 1], 1e4), reads=['Bcst'], writes=['Bcst'])
                    pool_op(lambda e, pr=pr, pos=pos: e.memset(Bcst[pr, pos, 0:1], 1e4), reads=['Bcst'], writes=['Bcst'])
            pool_op(lambda e: e.memset(MBt[:], 0.0), writes=['MBt'])
            pool_op(lambda e: e.memset(Vs[:], 1.0), writes=['Vs'])
            pool_op(lambda e: e.memset(Vw[:], 1.0), writes=['Vw'])
            pool_op(lambda e: e.memset(Vc[:], 1.0), writes=['Vc'])
            pool_op(lambda e: e.memset(KsT[:], 0.0), writes=['KsT'])
            pool_op(lambda e: e.memset(KwT[:], 0.0), writes=['KwT'])
            pool_op(lambda e: e.memset(KcT[:], 0.0), writes=['KcT'])
            pool_op(lambda e: e.memset(hidV[:], 0.0), writes=['hidV'])
        if KC >= 2:
            S.op('pool', lambda e: e.dma_start(out=W1k[:], in_=w_k1.rearrange("l d e -> d l e")), writes=['W1k'], dma='cw')
            S.op('pool', lambda e: e.dma_start(out=W1v[:], in_=w_v1.rearrange("l d e -> d l e")), writes=['W1v'], dma='cw')
            S.op('pool', lambda e: e.dma_start(out=W2k[:], in_=w_k2), writes=['W2k'], dma='cw')
            S.op('pool', lambda e: e.dma_start(out=W2v[:], in_=w_v2), writes=['W2v'], dma='cw')
            S.op('sp', lambda e: e.dma_start(out=pe_nat[:], in_=pe_cmp), writes=['pe_nat'], dma='const')
            S.op('pe', lambda e: e.transpose(out=PB[1][0:64, 0:32], in_=pe_nat[:, :], identity=identf[0:32, 0:32]), reads=['pe_nat', 'identf'], writes=[('PB', 1)])
            S.op('dve', lambda e: e.tensor_copy(out=peT[:, :], in_=PB[1][0:64, 0:32]), reads=[('PB', 1)], writes=['peT'])
        if KC >= 3:
            for kc4 in range(SEQ // 512):
                S.op('pe', lambda e, kc4=kc4: e.matmul(PB[0][:, :], lhsT=SH[0:32, :], rhs=E32[:, kc4 * 512:(kc4 + 1) * 512], start=True, stop=True),
                     reads=['SH', 'E32'], writes=[('PB', 0)])
                for kv in range(4):
                    S.op('dve', lambda e, kc4=kc4, kv=kv: e.tensor_copy(out=KsT[64:96, kv, kc4 * 512:(kc4 + 1) * 512], in_=PB[0][64:96, :]),
                         reads=[('PB', 0)], writes=['KsT', 'KsTE'])

        rg = {'i': 0}
        rb = {'i': 0}

        def load_g(src_ap, ncols, skey):
            slot = rg['i'] % 3
            rg['i'] += 1
            dst = ringg[slot][:, 0:8 * ncols]
            S.op('sp', lambda e: e.dma_start(out=dst, in_=src_ap), reads=(skey if isinstance(skey, list) else [skey]), writes=[('rg', slot)], dma=f'rg{slot}')
            return ringg[slot][:, 0:8 * ncols].rearrange("p (c n) -> p c n", c=8), ('rg', slot)

        def load_b(src_ap, nelem, skey):
            slot = rb['i'] % 2
            rb['i'] += 1
            dst = ringb[slot][:, 0:nelem]
            S.op('sp', lambda e: e.dma_start(out=dst, in_=src_ap), reads=(skey if isinstance(skey, list) else [skey]), writes=[('rb', slot)], dma=f'rb{slot}')
            return ringb[slot], ('rb', slot)

        class Stream:
            def __init__(self, specs, ahead):
                self.specs = specs
                self.loaded = []
                self.ahead = ahead

            def get(self, i):
                while len(self.loaded) < min(len(self.specs), i + 1 + self.ahead):
                    self.loaded.append(self.specs[len(self.loaded)]())
                return self.loaded[i]

        bank_rr = {'i': 0}

        def bank(n=7):
            b = bank_rr['i'] % n
            bank_rr['i'] += 1
            return PB[b], ('PB', b)

        def norm_hT(nsub, npart, gi):
            for j in range(nsub):
                S.op('act', lambda e, j=j: e.activation(out=junk[:npart, :], in_=X[:npart, j, :], func=AF.Square, scale=1.0 / 32.0,
                                                         accum_out=ss[:npart, j:j + 1]), reads=['X'], writes=['junk', 'ss'])
            S.op('dve', lambda e: e.tensor_scalar(out=rstd[:npart, :nsub], in0=ss[:npart, :nsub], scalar1=EPS, scalar2=None, op0=ALU.add),
                 reads=['ss'], writes=['rstd'])
            S.op('act', lambda e: e.activation(out=rstd[:npart, :nsub], in_=rstd[:npart, :nsub], func=AF.Sqrt), reads=['rstd'], writes=['rstd'])
            S.op('dve', lambda e: e.reciprocal(out=rstd[:npart, :nsub], in_=rstd[:npart, :nsub]), reads=['rstd'], writes=['rstd'])
            for j in range(nsub):
                S.op('dve', lambda e, j=j: e.tensor_scalar(out=hb[:npart, :], in0=X[:npart, j, :], scalar1=rstd[:npart, j:j + 1], scalar2=None,
                                                           op0=ALU.mult), reads=['X', 'rstd'], writes=['hb'])
                for c in range(8):
                    S.op('pe', lambda e, c=c: e.transpose(out=PT[:, c * 128:c * 128 + npart], in_=hb[:npart, c * 128:(c + 1) * 128],
                                                          identity=ident[:npart, :npart]), reads=['hb', 'ident'], writes=['PT'], inc=(c == 7))
                S.op('dve', lambda e, j=j: e.tensor_tensor(out=hT[:, :, j * npart:(j + 1) * npart],
                                                           in0=PT[:, :].rearrange("p (c n) -> p c n", c=8)[:, :, 0:npart],
                                                           in1=gT[:, gi, :].unsqueeze(2).to_broadcast([128, 8, npart]), op=ALU.mult),
                     reads=['PT', 'gT'], writes=['hT'])

        def out_proj_add(srcT, wfun, nsub, npart, scale):
            nchunk = srcT.shape[1]
            for h in range(2):
                wap, wkey = wfun(h)
                for j in range(nsub):
                    pb, pk = bank()
                    for f in range(nchunk):
                        S.op('pe', lambda e, f=f, j=j, pb=pb, wap=wap: e.matmul(pb[:npart, :], lhsT=srcT[:, f, j * npart:(j + 1) * npart], rhs=wap[:, f, :],
                                                                             start=(f == 0), stop=(f == nchunk - 1)),
                             reads=[wkey, 'srcT'], writes=[pk], inc=(f == nchunk - 1))
                    S.op('dve', lambda e, j=j, h=h, pb=pb: e.scalar_tensor_tensor(out=X[:npart, j, h * 512:(h + 1) * 512], in0=pb[:npart, :], scalar=scale,
                                                                                  in1=X[:npart, j, h * 512:(h + 1) * 512], op0=ALU.mult, op1=ALU.add),
                         reads=[pk, 'X'], writes=['X'])

        def ffn(l, ab, nsub, npart):
            TT = nsub * npart
            gi = (0 if ab == 'a' else 4) + l
            norm_hT(nsub, npart, gi)
            sg_key = (l, ab)
            gus = Stream([(lambda f=f: load_g(s_gu[(l, ab)][f].rearrange("p c u j -> p (c u j)"), 256, [('s_gu', l, ab, f, 0), ('s_gu', l, ab, f, 1)])) for f in range(22)], 2)
            dns = Stream([(lambda h=h: load_b(s_dn[(l, ab)][h].rearrange("p f n -> p (f n)"), 22 * 512, [('s_dn', l, ab, h)])) for h in range(2)], 1)
            for f in range(22):
                wg, wk = gus.get(f)
                if f == 18:
                    dns.get(0)
                pg, pgk = bank()
                pu, puk = bank()
                for u, (pp, ppk) in enumerate([(pg, pgk), (pu, puk)]):
                    for c in range(8):
                        S.op('pe', lambda e, c=c, u=u, pp=pp, wg=wg: e.matmul(pp[:, :TT], lhsT=wg[:, c, u * 128:(u + 1) * 128], rhs=hT[:, c, :TT],
                                                                            start=(c == 0), stop=(c == 7)),
                             reads=[wk, 'hT'], writes=[ppk], inc=(c == 7))
                sg = sgs[f % 2]
                S.op('act', lambda e, pg=pg, sg=sg: e.activation(out=sg[:, :TT], in_=pg[:, :TT], func=AF.Silu), reads=[pgk], writes=[('sg', f % 2)])
                S.op('dve', lambda e, f=f, pu=pu, sg=sg: e.tensor_tensor(out=aT[:, f, :TT], in0=sg[:, :TT], in1=pu[:, :TT], op=ALU.mult),
                     reads=[('sg', f % 2), puk], writes=['srcT'])

            def wfun(h):
                ap, k = dns.get(h)
                return ap[:, 0:22 * 512].rearrange("p (f n) -> p f n", f=22), k
            out_proj_add(aT, wfun, nsub, npart, 0.5)

        def conv_layer(nsub, npart, first, sample, cs_out):
            TT = nsub * npart
            norm_hT(nsub, npart, 2)
            if first and not sample:
                S.op('pool', lambda e: e.memset(ucar[:], 0.0), writes=['ucar'])
            cins = Stream([(lambda ee=ee: load_g(s_cin[ee].rearrange("p c s j -> p (c s j)"), 384, 's_cin')) for ee in range(8)], 2)
            couts = Stream([lambda: load_b(s_cout.rearrange("p c n -> p (c n)"), 8 * D, 's_cout')], 0)
            for ee in range(8):
                w, wk = cins.get(ee)
                if ee == 5:
                    couts.get(0)
                w4 = w.rearrange("p c (s j) -> p c s j", s=3)
                pbs = []
                for s3 in range(3):
                    pb, pk = bank()
                    for c in range(8):
                        S.op('pe', lambda e, c=c, s3=s3, pb=pb, w4=w4: e.matmul(pb[:, :TT], lhsT=w4[:, c, s3, :], rhs=hT[:, c, :TT], start=(c == 0), stop=(c == 7)),
                             reads=[wk, 'hT'], writes=[pk], inc=(c == 7))
                    pbs.append((pb, pk))
                (pbg, kbg), (pcg, kcg), (pv, kv_) = pbs
                S.op('act', lambda e, pcg=pcg: e.activation(out=cgs[:, :TT], in_=pcg[:, :TT], func=AF.Copy), reads=[kcg], writes=['cgs'])
                if not sample:
                    S.op('dve', lambda e, ee=ee: e.tensor_copy(out=U[:, ee, 0:2], in_=ucar[:, ee, :]), reads=['ucar'], writes=['U'])
                    S.op('dve', lambda e, ee=ee, pv=pv: e.tensor_tensor(out=U[:, ee, 2:2 + TT], in0=cgs[:, :TT], in1=pv[:, :TT], op=ALU.mult),
                         reads=['cgs', kv_], writes=['U'])
                    segs = [(0, TT, 0)]
                else:
                    S.op('dve', lambda e, ee=ee: e.tensor_copy(out=U[:, ee, 0:6 * NB].rearrange("p (b k) -> p b k", k=6)[:, :, 0:2], in_=ucarS[:, :, ee, :]),
                         reads=['ucar'], writes=['U'])
                    S.op('dve', lambda e, ee=ee, pv=pv: e.tensor_tensor(out=U[:, ee, 0:6 * NB].rearrange("p (b k) -> p b k", k=6)[:, :, 2:6],
                                                                     in0=cgs[:, 0:TT].rearrange("p (b k) -> p b k", k=DT),
                                                                     in1=pv[:, 0:TT].rearrange("p (b k) -> p b k", k=DT), op=ALU.mult),
                         reads=['cgs', kv_], writes=['U'])
                    segs = [(6 * b, DT, DT * b) for b in range(NB)]
                for (u0, n, o0) in segs:
                    S.op('dve', lambda e, ee=ee, u0=u0, n=n, o0=o0: e.tensor_scalar(out=cacc[:, o0:o0 + n], in0=U[:, ee, u0 + 2:u0 + 2 + n], scalar1=wcv[:, 2, ee:ee + 1],
                                                                                    scalar2=None, op0=ALU.mult), reads=['U', 'wcv'], writes=['cacc'])
                    for i3 in (1, 0):
                        S.op('dve', lambda e, ee=ee, u0=u0, n=n, o0=o0, i3=i3: e.scalar_tensor_tensor(out=cacc[:, o0:o0 + n], in0=U[:, ee, u0 + i3:u0 + i3 + n],
                                                                                                       scalar=wcv[:, i3, ee:ee + 1], in1=cacc[:, o0:o0 + n],
                                                                                                       op0=ALU.mult, op1=ALU.add),
                             reads=['U', 'wcv', 'cacc'], writes=['cacc'])
                S.op('dve', lambda e, ee=ee, pbg=pbg: e.tensor_tensor(out=zT[:, ee, :TT], in0=cacc[:, :TT], in1=pbg[:, :TT], op=ALU.mult),
                     reads=['cacc', kbg], writes=['srcT'])
                if not sample:
                    S.op('pool', lambda e, ee=ee: e.tensor_copy(out=ucar[:, ee, :], in_=U[:, ee, TT:TT + 2]), reads=['U'], writes=['ucar'])
            if cs_out is not None:
                if not sample:
                    for t2 in range(2):
                        S.op('sp', lambda e, t2=t2: e.dma_start(out=cs_out[t2].rearrange("(c p) -> p c", p=128), in_=ucar[:, :, t2], allow_slow_non_contiguous=True),
                             reads=['ucar'], writes=['csout'], dma='st')
                else:
                    for b in range(NB):
                        for t2 in range(2):
                            S.op('sp', lambda e, b=b, t2=t2: e.dma_start(out=cs_out[b, t2].rearrange("(c p) -> p c", p=128), in_=U[:, :, 6 * b + 4 + t2],
                                                                         allow_slow_non_contiguous=True), reads=['U'], writes=['csout'], dma='st')

            def wfun(h):
                ap, k = couts.get(0)
                return ap[:, 0:8 * D].rearrange("p (c n) -> p c n", c=8)[:, :, h * 512:(h + 1) * 512], k
            out_proj_add(zT, wfun, nsub, npart, 1.0)

        def nsa_rows(nsub, npart, row0, outs, win_out, i=None):
            kvs = Stream([lambda: load_b(s_nkv.rearrange("p c n -> p (c n)"), 8 * 1536, 's_nkv')], 0)
            wap, wk = kvs.get(0)
            w3 = wap[:, 0:8 * 1536].rearrange("p (c n) -> p c n", c=8)
            for br in range(3):
                for j in range(nsub):
                    pb, pk = bank()
                    for c in range(8):
                        S.op('pe', lambda e, c=c, j=j, br=br, pb=pb: e.matmul(pb[:npart, :], lhsT=hT[:, c, j * npart:(j + 1) * npart], rhs=w3[:, c, br * 512:(br + 1) * 512],
                                                                           start=(c == 0), stop=(c == 7)), reads=[wk, 'hT'], writes=[pk], inc=(c == 7))
                    st = stage[j % 2]
                    S.op('act', lambda e, pb=pb, st=st: e.activation(out=st[:npart, :], in_=pb[:npart, :], func=AF.Copy), reads=[pk], writes=[('stage', j % 2)])
                    if i is not None and br == 1:
                        S.op('pool', lambda e, st=st, j=j: e.tensor_copy(out=Vs[:, 4 * i + j, :, 0:64], in_=st[:, 256:512].rearrange("p (k d) -> p k d", k=4)),
                             reads=[('stage', j % 2)], writes=['Vs'])
                    if i is not None and br == 2:
                        S.op('pool', lambda e, st=st, j=j: e.tensor_copy(out=Vw[:, (i % 2) * 4 + j, :, 0:64], in_=st[:, 256:512].rearrange("p (k d) -> p k d", k=4)),
                             reads=[('stage', j % 2)], writes=['Vw'])
                    r0 = row0 + j * npart
                    if br < 2:
                        S.op('pool', lambda e, st=st, br=br, r0=r0: e.dma_start(out=outs[br][r0:r0 + npart, :], in_=st[:npart, :]),
                             reads=[('stage', j % 2)], writes=['kvout'], dma='st')
                    elif win_out is not None:
                        for (oap, iap) in win_out(j, st):
                            S.op('pool', lambda e, oap=oap, iap=iap: e.dma_start(out=oap, in_=iap), reads=[('stage', j % 2)], writes=['kvout'], dma='st')

        NSA_KEYS = [('Qx', 0), ('Qx', 1), 'kcT', 'Ob', ('PTs', 0), ('PTs', 1), ('PTs', 2), ('EcT', 0), ('EcT', 1), 'Ec', 'imp', 'hx', 't123', 'rsf']

        def gelu_tanh(dst_bf16, src_f32, n, dkey='hx'):
            S.op('dve', lambda e: e.tensor_tensor(out=hx2[0:64, :n], in0=src_f32, in1=src_f32, op=ALU.mult), reads=['hx'], writes=['hx'])
            S.op('dve', lambda e: e.tensor_scalar(out=hx2[0:64, :n], in0=hx2[0:64, :n], scalar1=0.044715, scalar2=1.0, op0=ALU.mult, op1=ALU.add), reads=['hx'], writes=['hx'])
            S.op('dve', lambda e: e.tensor_tensor(out=hx2[0:64, :n], in0=hx2[0:64, :n], in1=src_f32, op=ALU.mult), reads=['hx'], writes=['hx'])
            S.op('act', lambda e: e.activation(out=hx3[0:64, :n], in_=hx2[0:64, :n], func=AF.Sigmoid, scale=1.5957691216), reads=['hx'], writes=['hx'])
            S.op('dve', lambda e: e.tensor_tensor(out=dst_bf16, in0=hx3[0:64, :n], in1=src_f32, op=ALU.mult), reads=['hx'], writes=[dkey])

        def nsa_prompt(i, row0, outs, win_out):
            t0 = i * T
            nb = 16 * (i + 1)
            slot = i % 2
            pslot = (i - 1) % 2
            norm_hT(4, 128, 3)
            nsa_rows(4, 128, row0, outs, win_out, i=i)
            if dbg_stage < 1:
                return
            gw, gk = load_g(s_ng.rearrange("p c n -> p (c n)"), 48, 's_ng')
            for j in range(4):
                pb, pk = bank(4)
                for c in range(8):
                    S.op('pe', lambda e, c=c, j=j, pb=pb: e.matmul(pb[:, 0:48], lhsT=hT[:, c, j * 128:(j + 1) * 128], rhs=gw[:, c, :], start=(c == 0), stop=(c == 7)),
                         reads=[gk, 'hT'], writes=[pk], inc=(c == 7))
                S.op('act', lambda e, j=j, pb=pb: e.activation(out=G[:, j, :], in_=pb[:, 0:48], func=AF.Sigmoid), reads=[pk], writes=['G'])
            for (m, dstT, c0) in ((2, KsT, t0), (4, KwT, slot * T)):
                w, wk = load_g(s_nk[m].rearrange("p c j -> p (c j)"), 256, 's_nk')
                for kv in range(4):
                    pb, pk = bank(4)
                    for c in range(8):
                        S.op('pe', lambda e, c=c, kv=kv, pb=pb, w=w: e.matmul(pb[0:64, :], lhsT=w[:, c, kv * 64:(kv + 1) * 64], rhs=hT[:, c, :], start=(c == 0), stop=(c == 7)),
                             reads=[wk, 'hT'], writes=[pk], inc=(c == 7))
                    S.op('dve', lambda e, kv=kv, pb=pb, dstT=dstT, c0=c0: e.tensor_copy(out=dstT[0:64, kv, c0:c0 + T], in_=pb[0:64, :]),
                         reads=[pk], writes=['KsT' if m == 2 else 'KwT'])
            ACCC, ACCS, ACCW = (PB[4], ('PB', 4)), (PB[5], ('PB', 5)), (PB[6], ('PB', 6))
            def do_kv(kv):
                Qx = Qxb[kv % 2]
                qxk = ('Qx', kv % 2)
                if dbg_stage < 2:
                    return None
                wkc, wkck = load_g(s_nk[0].rearrange("p c j -> p (c j)"), 256, 's_nk')
                wvc, wvck = load_g(s_nk[1].rearrange("p c j -> p (c j)"), 256, 's_nk')
                for kk, (w, wk) in enumerate(((wkc, wkck), (wvc, wvck))):
                    pb, pk = bank(4)
                    for c in range(8):
                        S.op('pe', lambda e, c=c, pb=pb, w=w: e.matmul(pb[0:64, :], lhsT=w[:, c, kv * 64:(kv + 1) * 64], rhs=hT[:, c, :], start=(c == 0), stop=(c == 7)),
                             reads=[wk, 'hT'], writes=[pk], inc=(c == 7))
                    S.op('dve', lambda e, kk=kk, pb=pb: e.tensor_tensor(out=kcT[0:64, kk, :].rearrange("p (n l) -> p n l", l=32),
                                                                        in0=pb[0:64, :].rearrange("p (n l) -> p n l", l=32),
                                                                        in1=peT[:, :].unsqueeze(1).to_broadcast([64, 16, 32]), op=ALU.add),
                         reads=[pk, 'peT'], writes=['kcT'])
                for kk, W1 in enumerate((W1k, W1v)):
                    pb, pk = bank(4)
                    kc3 = kcT[0:64, kk, :].rearrange("p (n l) -> p n l", l=32)
                    for l in range(32):
                        S.op('pe', lambda e, l=l, pb=pb, W1=W1, kc3=kc3: e.matmul(pb[0:64, 0:16], lhsT=W1[:, l, :], rhs=kc3[:, :, l], start=(l == 0), stop=(l == 31)),
                             reads=['kcT', 'W1k', 'W1v'], writes=[pk], inc=(l == 31))
                    S.op('act', lambda e, kk=kk, pb=pb: e.activation(out=hx[0:64, :], in_=pb[0:64, 0:16], func=AF.Copy), reads=[pk], writes=['hx'])
                    if kk == 0:
                        gelu_tanh(hid[0:64, :], hx[0:64, :], 16)
                        pb2, pk2 = bank(4)
                        S.op('pe', lambda e, pb2=pb2: e.matmul(pb2[0:64, 0:16], lhsT=W2k[:, :], rhs=hid[0:64, :], start=True, stop=True), reads=['hx', 'W2k'], writes=[pk2])
                        S.op('dve', lambda e, pb2=pb2: e.tensor_copy(out=KcT[:, kv, 16 * i:16 * i + 16], in_=pb2[0:64, 0:16]), reads=[pk2], writes=['KcT'])
                    else:
                        gelu_tanh(hidV[:, kv, 16 * i:16 * i + 16], hx[0:64, :], 16, dkey='hidV')
                        pb2, pk2 = bank(4)
                        S.op('pe', lambda e, pb2=pb2: e.matmul(pb2[0:nb, 0:64], lhsT=hidV[:, kv, 0:nb], rhs=W2v[:, :], start=True, stop=True), reads=['hidV', 'W2v'], writes=[pk2])
                        S.op('dve', lambda e, pb2=pb2: e.tensor_copy(out=Vc[0:nb, kv, 0:64], in_=pb2[0:nb, 0:64]), reads=[pk2], writes=['Vc'])
                if dbg_stage < 3:
                    return None
                qw, qk = load_g(s_nq[kv].rearrange("p c j -> p (c j)"), 256, 's_nq')
                for g in range(4):
                    pb, pk = bank(4)
                    for c in range(8):
                        S.op('pe', lambda e, c=c, g=g, pb=pb: e.matmul(pb[0:64, :], lhsT=qw[:, c, g * 64:(g + 1) * 64], rhs=hT[:, c, :], start=(c == 0), stop=(c == 7)),
                             reads=[qk, 'hT'], writes=[pk], inc=(c == 7))
                    S.op('act', lambda e, g=g, pb=pb: e.activation(out=Qx[0:64, g, :], in_=pb[0:64, :], func=AF.Copy, scale=0.125), reads=[pk], writes=[qxk])
                for j in range(4 if dbg_stage >= 4 else 0):
                    pos = 4 * i + j
                    pb, pk = bank(4)
                    for g in range(4):
                        S.op('pe', lambda e, g=g, j=j, pb=pb: e.matmul(pb[:, g * 64:g * 64 + nb], lhsT=Qx[0:64, g, j * 128:(j + 1) * 128], rhs=KcT[:, kv, 0:nb], start=True, stop=True),
                             reads=[qxk, 'KcT'], writes=[pk], inc=(g == 3))
                    pb3 = pb[:, 0:256].rearrange("p (g n) -> p g n", g=4)
                    S.op('act', lambda e, pb3=pb3: e.activation(out=Ec[:, :, 0:nb], in_=pb3[:, :, 0:nb], func=AF.Exp), reads=[pk], writes=['Ec'])
                    S.op('dve', lambda e, pos=pos: e.tensor_tensor(out=Ec[:, :, 0:nb], in0=Ec[:, :, 0:nb], in1=Mc[:, pos, 0:nb].unsqueeze(1).to_broadcast([128, 4, nb]), op=ALU.mult),
                         reads=['Ec', 'Mc'], writes=['Ec'])
                    S.op('dve', lambda e: e.tensor_reduce(out=sums[:, :], in_=Ec[:, :, 0:nb], axis=AX.X, op=ALU.add), reads=['Ec'], writes=['imp'])
                    S.op('dve', lambda e: e.tensor_scalar(out=sums[:, :], in0=sums[:, :], scalar1=1e-30, scalar2=None, op0=ALU.max), reads=['imp'], writes=['imp'])
                    S.op('dve', lambda e: e.reciprocal(out=sums[:, :], in_=sums[:, :]), reads=['imp'], writes=['imp'])
                    S.op('dve', lambda e: e.tensor_tensor(out=Ec[:, :, 0:nb], in0=Ec[:, :, 0:nb], in1=sums[:, :].unsqueeze(2).to_broadcast([128, 4, nb]), op=ALU.mult),
                         reads=['Ec', 'imp'], writes=['Ec'])
                    S.op('dve', lambda e: e.tensor_reduce(out=imp4[:, 0:nb], in_=Ec[:, :, 0:nb].rearrange("p g n -> p n g"), axis=AX.X, op=ALU.add), reads=['Ec'], writes=['imp'])
                    S.op('dve', lambda e: e.memset(imps[:, :], 0.0), reads=['imp'], writes=['imp'])
                    i2 = imp4[:, 0:nb].rearrange("p (m two) -> p m two", two=2)
                    S.op('dve', lambda e, i2=i2: e.tensor_tensor(out=imps[:, 0:nb // 2], in0=i2[:, :, 0], in1=i2[:, :, 1], op=ALU.add), reads=['imp'], writes=['imp'])
                    S.op('dve', lambda e, pos=pos: e.tensor_tensor(out=score[:, :], in0=imps[:, :], in1=Acst[:, pos, :], op=ALU.mult), reads=['imp', 'Acst'], writes=['imp'])
                    S.op('dve', lambda e, pos=pos: e.tensor_tensor(out=score[:, :], in0=score[:, :], in1=Bcst[:, pos, :], op=ALU.add), reads=['imp', 'Bcst'], writes=['imp'])
                    S.op('dve', lambda e: e.max(out=m1[:, :], in_=score[:, :]), reads=['imp'], writes=['imp'])
                    S.op('dve', lambda e: e.match_replace(out=work[:, :], in_to_replace=m1[:, :], in_values=score[:, :], imm_value=-3e4), reads=['imp'], writes=['imp'])
                    S.op('dve', lambda e: e.max(out=m2[:, :], in_=work[:, :]), reads=['imp'], writes=['imp'])
                    S.op('dve', lambda e: e.tensor_scalar(out=selm[:, :], in0=score[:, :], scalar1=m2[:, 7:8], scalar2=None, op0=ALU.is_ge), reads=['imp'], writes=['imp'])
                    S.op('dve', lambda e: e.tensor_scalar(out=MBt[:, 64:96], in0=selm[:, :], scalar1=-1.0, scalar2=-NEGB, op0=ALU.add, op1=ALU.mult), reads=['imp'], writes=['MBt'])
                    S.op('pe', lambda e: e.transpose(out=PT[:, 0:128], in_=MBt[:, :], identity=ident[:, :]), reads=['MBt', 'ident'], writes=['PT'])
                    S.op('dve', lambda e, j=j: e.tensor_copy(out=Qx[64:96, :, j * 128:(j + 1) * 128], in_=PT[64:96, 0:128].unsqueeze(1).to_broadcast([32, 4, 128])),
                         reads=['PT'], writes=[qxk])
                def do_head(g):
                    h = kv * 4 + g
                    stages = []
                    ec = EcT[g % 2]; eck = ('EcT', g % 2)

                    def cmpA():
                        pb, pk = bank(4)
                        S.op('pe', lambda e, pb=pb: e.matmul(pb[0:nb, :], lhsT=KcT[:, kv, 0:nb], rhs=Qx[0:64, g, :], start=True, stop=True), reads=[qxk, 'KcT'], writes=[pk])
                        S.op('act', lambda e, pb=pb: e.activation(out=ec[0:nb, :], in_=pb[0:nb, :], func=AF.Exp), reads=[pk], writes=[eck])
                        S.op('dve', lambda e: e.tensor_tensor(out=ec[0:nb, :], in0=ec[0:nb, :], in1=McT[0:nb, i, :], op=ALU.mult), reads=[eck, 'McT'], writes=[eck])

                    def cmpB():
                        for j in range(4):
                            S.op('pe', lambda e, j=j: e.matmul(ACCC[0][:, j * 128:j * 128 + 65], lhsT=ec[0:nb, j * 128:(j + 1) * 128], rhs=Vc[0:nb, kv, 0:65], start=True, stop=True),
                                 reads=[eck, 'Vc'], writes=[ACCC[1]], inc=(j == 3))
                    stages.append((cmpA, cmpB))
                    rr = 0
                    for kc in range(4 * i + 4):
                        md = kc - 4 * i
                        c0 = 128 * md if md > 0 else 0
                        ptk = rr % 3; rr += 1

                        def selA(kc=kc, md=md, c0=c0, ptk=ptk):
                            pb, pk = bank(4)
                            pt = PTs[ptk]
                            S.op('pe', lambda e, pb=pb: e.matmul(pb[:, c0:T], lhsT=KsT[0:96, kv, kc * 128:(kc + 1) * 128], rhs=Qx[0:96, g, c0:T], start=True, stop=(md < 0)),
                                 reads=[qxk, 'KsT', 'KsTE'], writes=[pk], inc=(md < 0))
                            if md >= 0:
                                S.op('pe', lambda e, pb=pb: e.matmul(pb[:, c0:c0 + 128], lhsT=ident[:, :], rhs=TriGEb[:, :], start=False, stop=True),
                                     reads=['ident', 'TriGEb'], writes=[pk])
                            S.op('act', lambda e, pb=pb, pt=pt: e.activation(out=pt[:, c0:T], in_=pb[:, c0:T], func=AF.Exp), reads=[pk], writes=[('PTs', ptk)])

                        def selB(kc=kc, md=md, ptk=ptk):
                            pt = PTs[ptk]
                            for j in range(max(md, 0), 4):
                                S.op('pe', lambda e, j=j, pt=pt: e.matmul(ACCS[0][:, j * 128:j * 128 + 65], lhsT=pt[:, j * 128:(j + 1) * 128], rhs=Vs[:, kc, kv, 0:65],
                                                                        start=(kc == 0 and j == 0), stop=(kc == 4 * i + 3 and j == 3)),
                                     reads=[('PTs', ptk), 'Vs'], writes=[ACCS[1]], inc=(j == 3))
                        stages.append((selA, selB))
                    for m in range(0 if i > 0 else 4, 8):
                        if m < 4:
                            kcol = pslot * T + m * 128; vch = pslot * 4 + m
                            ca, cb = 0, 128 * (m + 1); mcol = 128 * m; js = range(0, m + 1)
                        else:
                            kcol = slot * T + (m - 4) * 128; vch = slot * 4 + (m - 4)
                            ca, cb = 128 * (m - 4), T; mcol = 128 * (m - 4); js = range(m - 4, 4)
                        trib = TriLTb if m < 4 else TriGEb
                        ptk = rr % 3; rr += 1

                        def winA(kcol=kcol, ca=ca, cb=cb, mcol=mcol, trib=trib, ptk=ptk):
                            pb, pk = bank(4)
                            pt = PTs[ptk]
                            S.op('pe', lambda e, pb=pb: e.matmul(pb[:, ca:cb], lhsT=KwT[0:64, kv, kcol:kcol + 128], rhs=Qx[0:64, g, ca:cb], start=True, stop=False),
                                 reads=[qxk, 'KwT'], writes=[pk], inc=False)
                            S.op('pe', lambda e, pb=pb: e.matmul(pb[:, mcol:mcol + 128], lhsT=ident[:, :], rhs=trib[:, :], start=False, stop=True),
                                 reads=['ident', 'TriGEb', 'TriLTb'], writes=[pk])
                            S.op('act', lambda e, pb=pb, pt=pt: e.activation(out=pt[:, ca:cb], in_=pb[:, ca:cb], func=AF.Exp), reads=[pk], writes=[('PTs', ptk)])

                        def winB(m=m, js=js, vch=vch, ptk=ptk):
                            pt = PTs[ptk]
                            for j in js:
                                S.op('pe', lambda e, j=j, pt=pt: e.matmul(ACCW[0][:, j * 128:j * 128 + 65], lhsT=pt[:, j * 128:(j + 1) * 128], rhs=Vw[:, vch, kv, 0:65],
                                                                        start=(m == (0 if i > 0 else 4) and j == 0), stop=(m == 7 and j == 3)),
                                     reads=[('PTs', ptk), 'Vw'], writes=[ACCW[1]], inc=(j == js[-1]))
                        stages.append((winA, winB))
                    LA = 2
                    for n in range(len(stages) + LA):
                        if n < len(stages):
                            stages[n][0]()
                        if n >= LA:
                            stages[n - LA][1]()
                    tmps = (tmpA, tmpB, tmpC)
                    for br, (acc, acck) in enumerate((ACCC, ACCS, ACCW)):
                        a3 = acc[:, :].rearrange("p (j d) -> p j d", j=4)
                        S.op('dve', lambda e, br=br, a3=a3: e.tensor_scalar(out=rsf[:, br, :], in0=a3[:, :, 64], scalar1=1e-30, scalar2=None, op0=ALU.max), reads=[acck], writes=['rsf'])
                        S.op('dve', lambda e, br=br: e.reciprocal(out=rsf[:, br, :], in_=rsf[:, br, :]), reads=['rsf'], writes=['rsf'])
                        S.op('dve', lambda e, br=br: e.tensor_tensor(out=rsf[:, br, :], in0=rsf[:, br, :], in1=G[:, :, br * 16 + h], op=ALU.mult), reads=['rsf', 'G'], writes=['rsf'])
                        S.op('dve', lambda e, br=br, a3=a3: e.tensor_tensor(out=tmps[br][:, :, :], in0=a3[:, :, 0:64], in1=rsf[:, br, :].unsqueeze(2).to_broadcast([128, 4, 64]), op=ALU.mult),
                             reads=[acck, 'rsf'], writes=['t123'])
                    S.op('dve', lambda e: e.tensor_tensor(out=tmpA[:, :, :], in0=tmpA[:, :, :], in1=tmpB[:, :, :], op=ALU.add), reads=['t123'], writes=['t123'])
                    S.op('dve', lambda e, h=h: e.tensor_tensor(out=Ob[:, :, h * 64:(h + 1) * 64], in0=tmpA[:, :, :], in1=tmpC[:, :, :], op=ALU.add), reads=['t123'], writes=['Ob'])
                return do_head
            heads = {}
            for kv in range(5):
                if kv < 4:
                    heads[kv] = do_kv(kv)
                if kv >= 1 and heads[kv - 1] is not None:
                    for g in range(4 if dbg_stage >= 5 else 0):
                        heads[kv - 1](g)
            for j in range(4):
                for c in range(8):
                    S.op('pe', lambda e, c=c, j=j: e.transpose(out=PT[:, c * 128:(c + 1) * 128], in_=Ob[:, j, c * 128:(c + 1) * 128], identity=ident[:, :]),
                         reads=['Ob', 'ident'], writes=['PT'], inc=(c == 7))
                S.op('dve', lambda e, j=j: e.tensor_copy(out=hT[:, :, j * 128:(j + 1) * 128], in_=PT[:, :].rearrange("p (c n) -> p c n", c=8)), reads=['PT'], writes=['srcT'])
            nouts = Stream([lambda: load_b(s_nout.rearrange("p c n -> p (c n)"), 8 * D, 's_nout')], 0)

            def wfun(hh):
                ap, k = nouts.get(0)
                return ap[:, 0:8 * D].rearrange("p (c n) -> p c n", c=8)[:, :, hh * 512:(hh + 1) * 512], k
            out_proj_add(hT, wfun, 4, 128, 1.0)

        o_scr = dscr("o_scr", [3, NB, 4, 4, DT, 64], F32)

        def nsa_sample(row0, outs, win_out):
            SALL = ['pe', 'act', 'dve', 'pool', 'sp']
            norm_hT(1, 16, 3)
            nsa_rows(1, 16, row0, outs, win_out)
            S.barrier(SALL, list(S.bufs.keys()))
            S.op('pool', lambda e: e.memset(Gsumf[:], 0.0), writes=['Gsum'])
            for k in range(-3, 4):
                S.op('pool', lambda e, k=k: e.affine_select(out=Gsumf[:], in_=Gsumf[:], pattern=[[-1, 16]], compare_op=ALU.not_equal, fill=1.0,
                                                            base=-4 * k, channel_multiplier=1), reads=['Gsum'], writes=['Gsum'])
            S.op('dve', lambda e: e.tensor_copy(out=Gsum[:], in_=Gsumf[:]), reads=['Gsum'], writes=['Gsumb'])
            S.op('pool', lambda e: e.memset(CM4[:], 0.0), writes=['CM4'])
            S.op('pool', lambda e: e.affine_select(out=CM4[:], in_=CM4[:], pattern=[[-1, 4]], compare_op=ALU.is_ge, fill=NEGB, base=0, channel_multiplier=1),
                 reads=['CM4'], writes=['CM4'])
            S.op('pool', lambda e: e.memset(WM4[:], 0.0), writes=['WM4'])
            S.op('pool', lambda e: e.affine_select(out=WM4[:, 0:4], in_=WM4[:, 0:4], pattern=[[1, 4]], compare_op=ALU.is_ge, fill=NEGB, base=-1, channel_multiplier=-1),
                 reads=['WM4'], writes=['WM4'])
            pb, pk = bank(4)
            S.op('pe', lambda e, pb=pb: e.matmul(pb[0:16, 0:4], lhsT=Gsum[0:4, :], rhs=CM4[:, :], start=True, stop=True), reads=['Gsumb', 'CM4'], writes=[pk])
            S.op('dve', lambda e, pb=pb: e.tensor_copy(out=CM16[:, :], in_=pb[0:16, 0:4]), reads=[pk], writes=['CM16'])
            pb, pk = bank(4)
            S.op('pe', lambda e, pb=pb: e.matmul(pb[0:16, :], lhsT=Gsum[0:4, :], rhs=WM4[:, :], start=True, stop=True), reads=['Gsumb', 'WM4'], writes=[pk])
            S.op('dve', lambda e, pb=pb: e.tensor_copy(out=WM16[:, :], in_=pb[0:16, :]), reads=[pk], writes=['WM16'])
            for k in range(2):
                S.op('pool', lambda e, k=k: e.memset(W1bd[k][:, :, :], 0.0), writes=[('W1bd', k)])
                src = (w_k1, w_v1)[k].rearrange("l d e -> d l e")
                S.op('pool', lambda e, k=k, src=src: e.dma_start(out=W1bd[k][0:64, :, 0:64], in_=src), reads=[('W1bd', k)], writes=[('W1bd', k)], dma='cw')
                S.op('pool', lambda e, k=k, src=src: e.dma_start(out=W1bd[k][64:128, :, 64:128], in_=src), reads=[('W1bd', k)], writes=[('W1bd', k)], dma='cw')
            S.op('pool', lambda e: e.memset(W2sel[:], 0.0), writes=['W2sel'])
            for k in range(2):
                src = (w_k2, w_v2)[k]
                S.op('pool', lambda e, k=k, src=src: e.dma_start(out=W2sel[0:64, k, 0, :], in_=src), reads=['W2sel'], writes=['W2sel'], dma='cw')
                S.op('pool', lambda e, k=k, src=src: e.dma_start(out=W2sel[64:128, k, 1, :], in_=src), reads=['W2sel'], writes=['W2sel'], dma='cw')
            S.op('sp', lambda e: e.dma_start(out=peT2[0:64, :], in_=peT[:, :]), reads=['peT'], writes=['peT2'], dma='const')
            S.op('sp', lambda e: e.dma_start(out=peT2[64:128, :], in_=peT[:, :]), reads=['peT'], writes=['peT2'], dma='const')
            S.op('pool', lambda e: e.dma_start(out=pt_i[:, :], in_=ptab.rearrange("b n -> (b n)").partition_broadcast(128)), writes=['pt_i'], dma='cw')
            S.op('pool', lambda e: e.iota(iot_i[:], pattern=[[0, 1]], base=0, channel_multiplier=1), writes=['iot'])
            S.op('dve', lambda e: e.tensor_copy(out=iot_f[:], in_=iot_i[:]), reads=['iot'], writes=['iotf'])
            S.op('dve', lambda e: e.tensor_copy(out=PTF[:, :, :].rearrange("p b n -> p (b n)"), in_=pt_i[:, :]), reads=['pt_i'], writes=['PTF'])
            S.op('dve', lambda e: e.tensor_scalar(out=PTF[:, :, :], in0=PTF[:, :, :], scalar1=128.0, scalar2=iot_f[:, 0:1], op0=ALU.mult, op1=ALU.add),
                 reads=['PTF', 'iotf'], writes=['PTF'])
            S.op('dve', lambda e: e.tensor_copy(out=IDX[:, :, :], in_=PTF[:, :, :]), reads=['PTF'], writes=['IDX'])
            gw, gk = load_g(s_ng.rearrange("p c n -> p (c n)"), 48, 's_ng')
            pb, pk = bank(4)
            for c in range(8):
                S.op('pe', lambda e, c=c, pb=pb: e.matmul(pb[0:16, 0:48], lhsT=hT[:, c, 0:16], rhs=gw[:, c, :], start=(c == 0), stop=(c == 7)), reads=[gk, 'hT'], writes=[pk], inc=(c == 7))
            S.op('act', lambda e, pb=pb: e.activation(out=G[0:16, 0, :], in_=pb[0:16, 0:48], func=AF.Sigmoid), reads=[pk], writes=['G'])
            for (m, dstT, key) in ((2, KsN, 'KsN'), (4, KwN, 'KwN')):
                w, wk = load_g(s_nk[m].rearrange("p c j -> p (c j)"), 256, 's_nk')
                for kv in range(4):
                    pb, pk = bank(4)
                    for c in range(8):
                        S.op('pe', lambda e, c=c, kv=kv, pb=pb, w=w: e.matmul(pb[0:64, 0:16], lhsT=w[:, c, kv * 64:(kv + 1) * 64], rhs=hT[:, c, 0:16], start=(c == 0), stop=(c == 7)),
                             reads=[wk, 'hT'], writes=[pk], inc=(c == 7))
                    S.op('dve', lambda e, kv=kv, pb=pb, dstT=dstT: e.tensor_copy(out=dstT[:, kv, :], in_=pb[0:64, 0:16]), reads=[pk], writes=[key])
            for kv in range(4):
                qw, qk = load_g(s_nq[kv].rearrange("p c j -> p (c j)"), 256, 's_nq')
                for g in range(4):
                    pb, pk = bank(4)
                    for c in range(8):
                        S.op('pe', lambda e, c=c, g=g, pb=pb, qw=qw: e.matmul(pb[0:64, 0:16], lhsT=qw[:, c, g * 64:(g + 1) * 64], rhs=hT[:, c, 0:16], start=(c == 0), stop=(c == 7)),
                             reads=[qk, 'hT'], writes=[pk], inc=(c == 7))
                    S.op('act', lambda e, g=g, kv=kv, pb=pb: e.activation(out=Qs[:, kv * 4 + g, :], in_=pb[0:64, 0:16], func=AF.Copy, scale=0.125), reads=[pk], writes=['Qs'])
            wap, wk3 = load_b(s_nkv.rearrange("p c n -> p (c n)"), 8 * 1536, 's_nkv')
            w3 = wap[:, 0:8 * 1536].rearrange("p (c n) -> p c n", c=8)
            for b in range(NB):
                for (br, dstV, key) in ((1, VnS, 'VnS'), (2, VnW, 'VnW')):
                    pb, pk = bank(4)
                    for c in range(8):
                        S.op('pe', lambda e, c=c, b=b, br=br, pb=pb: e.matmul(pb[0:4, 0:256], lhsT=hT[:, c, 4 * b:4 * b + 4], rhs=w3[:, c, br * 512 + 256:br * 512 + 512],
                                                                           start=(c == 0), stop=(c == 7)), reads=[wk3, 'hT'], writes=[pk], inc=(c == 7))
                    S.op('dve', lambda e, b=b, pb=pb, dstV=dstV: e.tensor_copy(out=dstV[:, b, :, :], in_=pb[0:4, 0:256].rearrange("p (k d) -> p k d", k=4)), reads=[pk], writes=[key])

            def transposeP(src16, ncols, dst, skey='Pb', dkey='PsT'):
                nk = ncols // 128
                for k4 in range(nk):
                    S.op('pe', lambda e, k4=k4: e.transpose(out=PT[:, k4 * 16:(k4 + 1) * 16], in_=src16[0:16, k4 * 128:(k4 + 1) * 128], identity=ident[0:16, 0:16]),
                         reads=[skey, 'ident'], writes=['PT'], inc=(k4 == nk - 1))
                S.op('dve', lambda e: e.tensor_copy(out=dst[:, 0:nk, :], in_=PT[:, 0:nk * 16].rearrange("p (k q) -> p k q", q=16)), reads=['PT'], writes=[dkey])

            sg1b = sgs[1][:, :].bitcast(BF16)
            SMs = [SM, sgs[0][0:16, :]]
            Pbs = [Pb, sg1b[0:16, 0:512]]
            PsTs = [PsT, sg1b[:, 512:576].rearrange("p (k q) -> p k q", q=16)]
            SMk = ['SM', ('sg', 0)]; Pbk = ['Pb', ('sg', 1)]; PsTk = ['PsT', 'PsT1']

            ACC = [(PB[4], ('PB', 4)), (PB[5], ('PB', 5)), (PB[6], ('PB', 6))]
            ACCS4 = [PB[3], PB[4], PB[5], PB[6]]
            SHq = [ident[0:64, :], SH[0:64, :]]

            def gather_group(cache, b, grp, npg=8, slot0=0):
                for k in range(npg):
                    p = grp * npg + k
                    S.op('pool', lambda e, k=k, p=p: e.indirect_dma_start(out=CG[:, slot0 + k, :], out_offset=None, in_=cache[:, :],
                                                                          in_offset=bass.IndirectOffsetOnAxis(ap=IDX[:, b, p:p + 1], axis=0)),
                         reads=['IDX'], writes=[('CG', slot0 + k)], dma='pg')

            def transpose_group(ccs, add_pe, npair=4, slot0=0, xoff=0, xkey='XT'):
                for k2 in range(npair):
                    for ci, cc in enumerate(ccs):
                        for pg2 in range(2):
                            k = slot0 + 2 * k2 + pg2
                            S.op('pe', lambda e, k=k, cc=cc, ci=ci, pg2=pg2: e.transpose(out=PT[:, (ci * 2 + pg2) * 128:(ci * 2 + pg2 + 1) * 128], in_=CG[:, k, cc * 128:(cc + 1) * 128], identity=ident[:, :]),
                                 reads=[('CG', k), 'ident'], writes=['PT'], inc=(ci == len(ccs) - 1 and pg2 == 1))
                    src = PT[:, 0:len(ccs) * 256].rearrange("p (c r) -> p c r", c=len(ccs))
                    dst = XT[:, 0:len(ccs), xoff + k2 * 256:xoff + (k2 + 1) * 256]
                    if add_pe:
                        S.op('dve', lambda e, src=src, dst=dst: e.tensor_tensor(out=dst.rearrange("p c (n l) -> p c n l", l=32), in0=src.rearrange("p c (n l) -> p c n l", l=32),
                                                                                in1=peT2[:, :].unsqueeze(1).unsqueeze(1).to_broadcast([128, len(ccs), 8, 32]), op=ALU.add),
                             reads=['PT', 'peT2'], writes=[xkey])
                    else:
                        S.op('dve', lambda e, src=src, dst=dst: e.tensor_copy(out=dst, in_=src), reads=['PT'], writes=[xkey])

            for b in range(NB):
                for grp in range(16):
                    gather_group(cache_cmp, b, grp)
                    transpose_group([0, 1, 2, 3], True)
                    for cc in range(4):
                        pb, pk = bank(4)
                        x3 = XT[:, cc, :].rearrange("p (n l) -> p n l", l=32)
                        for l in range(32):
                            S.op('pe', lambda e, l=l, cc=cc, pb=pb, x3=x3: e.matmul(pb[:, 0:32], lhsT=W1bd[cc // 2][:, l, :], rhs=x3[:, :, l], start=(l == 0), stop=(l == 31)),
                                 reads=['XT', ('W1bd', 0), ('W1bd', 1)], writes=[pk], inc=(l == 31))
                        S.op('act', lambda e, pb=pb: e.activation(out=Ec16_g[:, 0:32], in_=pb[:, 0:32], func=AF.Copy), reads=[pk], writes=['gel'])
                        S.op('dve', lambda e: e.tensor_tensor(out=Ec16_g[:, 64:96], in0=Ec16_g[:, 0:32], in1=Ec16_g[:, 0:32], op=ALU.mult), reads=['gel'], writes=['gel'])
                        S.op('dve', lambda e: e.tensor_scalar(out=Ec16_g[:, 64:96], in0=Ec16_g[:, 64:96], scalar1=0.044715, scalar2=1.0, op0=ALU.mult, op1=ALU.add), reads=['gel'], writes=['gel'])
                        S.op('dve', lambda e: e.tensor_tensor(out=Ec16_g[:, 64:96], in0=Ec16_g[:, 64:96], in1=Ec16_g[:, 0:32], op=ALU.mult), reads=['gel'], writes=['gel'])
                        S.op('act', lambda e: e.activation(out=Ec16_g[:, 128:160], in_=Ec16_g[:, 64:96], func=AF.Sigmoid, scale=1.5957691216), reads=['gel'], writes=['gel'])
                        S.op('dve', lambda e, cc=cc, grp=grp: e.tensor_tensor(out=hidS[:, cc, grp * 32:(grp + 1) * 32], in0=Ec16_g[:, 128:160], in1=Ec16_g[:, 0:32], op=ALU.mult),
                             reads=['gel'], writes=['hidS'])
                for kv in range(4):
                    pb, pk = bank(4)
                    S.op('pe', lambda e, kv=kv, pb=pb: e.matmul(pb[0:64, :], lhsT=W2sel[:, 0, kv % 2, :], rhs=hidS[:, kv // 2, :], start=True, stop=True), reads=['hidS', 'W2sel'], writes=[pk])
                    S.op('dve', lambda e, kv=kv, pb=pb: e.tensor_copy(out=KcS[:, kv, :], in_=pb[0:64, :]), reads=[pk], writes=['KcS'])
                    pb, pk = bank(4)
                    for q4 in range(4):
                        S.op('pe', lambda e, kv=kv, q4=q4, pb=pb: e.matmul(pb[:, q4 * 64:(q4 + 1) * 64], lhsT=hidS[:, 2 + kv // 2, q4 * 128:(q4 + 1) * 128], rhs=W2sel[:, 1, kv % 2, :], start=True, stop=True),
                             reads=['hidS', 'W2sel'], writes=[pk], inc=(q4 == 3))
                    S.op('dve', lambda e, kv=kv, pb=pb: e.tensor_copy(out=VcS[:, :, kv, :], in_=pb[:, 0:256].rearrange("p (q d) -> p q d", q=4)), reads=[pk], writes=['VcS'])
                for q4 in range(4):
                    S.op('pool', lambda e, q4=q4, b=b: e.dma_start(out=CGw[:, q4, :], in_=cache_win[b, q4 * 128:(q4 + 1) * 128, :]), writes=['CGw'], dma='pg')
                for kv in range(4):
                    S.op('dve', lambda e, kv=kv, b=b: e.tensor_copy(out=QB[:, kv, :].rearrange("p (g t) -> p g t", g=4), in_=Qs[:, kv * 4:(kv + 1) * 4, 4 * b:4 * b + 4]), reads=['Qs'], writes=['QB'])
                    pb, pk = bank(4)
                    S.op('pe', lambda e, kv=kv, pb=pb: e.matmul(pb[:, 0:16], lhsT=SHq[kv % 2], rhs=QB[:, kv, :], start=True, stop=True), reads=['QB', 'SH', 'ident'], writes=[pk])
                    S.op('dve', lambda e, kv=kv, pb=pb: e.tensor_copy(out=QBp[:, kv, :], in_=pb[:, 0:16]), reads=[pk], writes=['QBp'])
                    pb, pk = bank(4)
                    S.op('pe', lambda e, kv=kv, pb=pb: e.matmul(pb[0:16, :], lhsT=QB[:, kv, :], rhs=KcS[:, kv, :], start=True, stop=True), reads=['QB', 'KcS'], writes=[pk])
                    S.op('act', lambda e, pb=pb: e.activation(out=Ec16[:, :], in_=pb[0:16, :], func=AF.Exp, accum_out=rsS[:, 0:1]), reads=[pk], writes=['Ec16', 'rsS'])
                    S.op('dve', lambda e: e.tensor_copy(out=Pb[:, :], in_=Ec16[:, :]), reads=['Ec16'], writes=['Pb'])
                    S.op('dve', lambda e: e.reciprocal(out=rsS[:, 0:1], in_=rsS[:, 0:1]), reads=['rsS'], writes=['rsS'])
                    S.op('dve', lambda e: e.tensor_scalar(out=Ec16[:, :], in0=Ec16[:, :], scalar1=rsS[:, 0:1], scalar2=None, op0=ALU.mult), reads=['Ec16', 'rsS'], writes=['Ec16'])
                    pb, pk = bank(4)
                    S.op('pe', lambda e, pb=pb: e.matmul(pb[0:16, :], lhsT=Gsumf[:, :], rhs=Ec16[:, :], start=True, stop=True), reads=['Ec16', 'Gsum'], writes=[pk])
                    p3 = pb[0:16, :].rearrange("p (m two) -> p m two", two=2)
                    S.op('dve', lambda e, p3=p3: e.tensor_copy(out=imp16[:, :].rearrange("p (m two) -> p m two", two=2), in_=p3), reads=[pk], writes=['imp16'])
                    i3 = imp16[:, :].rearrange("p (m two) -> p m two", two=2)
                    S.op('dve', lambda e, i3=i3: e.tensor_tensor(out=imps16[:, :], in0=i3[:, :, 0], in1=i3[:, :, 1], op=ALU.add), reads=['imp16'], writes=['imps16'])
                    S.op('dve', lambda e: e.memset(imps16[:, 0:1], 1e4), reads=['imps16'], writes=['imps16'])
                    S.op('dve', lambda e: e.memset(imps16[:, 255:256], 1e4), reads=['imps16'], writes=['imps16'])
                    S.op('dve', lambda e: e.max(out=m1s[:, :], in_=imps16[:, :]), reads=['imps16'], writes=['m1s'])
                    S.op('dve', lambda e: e.match_replace(out=work16[:, :], in_to_replace=m1s[:, :], in_values=imps16[:, :], imm_value=-3e4), reads=['imps16', 'm1s'], writes=['work16'])
                    S.op('dve', lambda e: e.max(out=m2s[:, :], in_=work16[:, :]), reads=['work16'], writes=['m2s'])
                    S.op('dve', lambda e: e.tensor_scalar(out=mb16[:, :], in0=imps16[:, :], scalar1=m2s[:, 6:7], scalar2=None, op0=ALU.is_ge), reads=['imps16', 'm2s'], writes=['mb16'])
                    S.op('dve', lambda e: e.tensor_scalar(out=mb16[:, :], in0=mb16[:, :], scalar1=-1.0, scalar2=-NEGB, op0=ALU.add, op1=ALU.mult), reads=['mb16'], writes=['mb16'])
                    transposeP(Pb, 512, PsT)
                    for q4 in range(4):
                        S.op('pe', lambda e, kv=kv, q4=q4: e.matmul(ACC[0][0][0:16, 0:64], lhsT=PsT[:, q4, :], rhs=VcS[:, q4, kv, :], start=(q4 == 0), stop=(q4 == 3)),
                             reads=['PsT', 'VcS'], writes=[ACC[0][1]], inc=(q4 == 3))
                    S.op('dve', lambda e: e.tensor_scalar(out=OB3[:, 0, :], in0=ACC[0][0][0:16, 0:64], scalar1=rsS[:, 0:1], scalar2=None, op0=ALU.mult), reads=[ACC[0][1], 'rsS'], writes=['OB3'])
                    if kv == 0:
                        for q4 in range(4):
                            for c2 in range(2):
                                S.op('pe', lambda e, q4=q4, c2=c2: e.transpose(out=PT[:, (q4 * 2 + c2) * 128:(q4 * 2 + c2 + 1) * 128], in_=CGw[:, q4, c2 * 128:(c2 + 1) * 128], identity=ident[:, :]),
                                     reads=['CGw', 'ident'], writes=['PT'], inc=(q4 == 3 and c2 == 1))
                        S.op('dve', lambda e: e.tensor_copy(out=XTw[:, :, :].rearrange("p c (q r) -> p q c r", q=4), in_=PT[:, :].rearrange("p (q c r) -> p q c r", q=4, c=2)), reads=['PT'], writes=['XTw'])
                    ncol = 0
                    pb, pk = bank(4)
                    S.op('pe', lambda e, kv=kv, pb=pb: e.matmul(pb[0:16, :], lhsT=QBp[:, kv, :], rhs=XTw[:, kv // 2, :], start=True, stop=True), reads=['QBp', 'XTw'], writes=[pk])
                    S.op('dve', lambda e, pb=pb: e.tensor_tensor(out=SM[:, :], in0=pb[0:16, :], in1=WM16[:, :], op=ALU.add), reads=[pk, 'WM16'], writes=['SM'])
                    S.op('act', lambda e: e.activation(out=Pb[:, :], in_=SM[:, :], func=AF.Exp, accum_out=ssum[:, 0:1]), reads=['SM'], writes=['Pb', 'ssum'])
                    transposeP(Pb, 512, PsT)
                    for q4 in range(4):
                        S.op('pe', lambda e, kv=kv, q4=q4: e.matmul(ACC[2][0][0:16, 0:64], lhsT=PsT[:, q4, :], rhs=CGw[:, q4, 256 + kv * 64:256 + (kv + 1) * 64], start=(q4 == 0), stop=False),
                             reads=['PsT', 'CGw'], writes=[ACC[2][1]], inc=(q4 == 3))
                    pb, pk = bank(4)
                    S.op('pe', lambda e, kv=kv, b=b, pb=pb: e.matmul(pb[0:16, 0:4], lhsT=QB[:, kv, :], rhs=KwN[:, kv, 4 * b:4 * b + 4], start=True, stop=True), reads=['QB', 'KwN'], writes=[pk])
                    S.op('dve', lambda e, pb=pb: e.tensor_tensor(out=SM[:, 0:4], in0=pb[0:16, 0:4], in1=CM16[:, :], op=ALU.add), reads=[pk, 'CM16'], writes=['SM'])
                    S.op('act', lambda e: e.activation(out=Pb[:, 0:4], in_=SM[:, 0:4], func=AF.Exp, accum_out=ssum[:, 1:2]), reads=['SM'], writes=['Pb', 'ssum'])
                    S.op('pe', lambda e: e.transpose(out=PT[0:4, 0:16], in_=Pb[0:16, 0:4], identity=ident[0:16, 0:16]), reads=['Pb', 'ident'], writes=['PT'])
                    S.op('dve', lambda e: e.tensor_copy(out=PsT[0:4, 0, :], in_=PT[0:4, 0:16]), reads=['PT'], writes=['PsT'])
                    S.op('pe', lambda e, kv=kv, b=b: e.matmul(ACC[2][0][0:16, 0:64], lhsT=PsT[0:4, 0, :], rhs=VnW[:, b, kv, :], start=False, stop=True), reads=['PsT', 'VnW'], writes=[ACC[2][1]])
                    S.op('dve', lambda e: e.tensor_tensor(out=rsS[:, 2:3], in0=ssum[:, 0:1], in1=ssum[:, 1:2], op=ALU.add), reads=['ssum'], writes=['rsS'])
                    S.op('dve', lambda e: e.reciprocal(out=rsS[:, 2:3], in_=rsS[:, 2:3]), reads=['rsS'], writes=['rsS'])
                    S.op('dve', lambda e: e.tensor_scalar(out=OB3[:, 2, :], in0=ACC[2][0][0:16, 0:64], scalar1=rsS[:, 2:3], scalar2=None, op0=ALU.mult), reads=[ACC[2][1], 'rsS'], writes=['OB3'])
                    S.op('dve', lambda e, kv=kv: e.tensor_copy(out=mbAll[:, kv, :], in_=mb16[:, :]), reads=['mb16'], writes=['mbAll'])
                    S.op('dve', lambda e, kv=kv: e.tensor_copy(out=OBall[:, kv, 0, :], in_=OB3[:, 0, :]), reads=['OB3'], writes=['OBall'])
                    S.op('dve', lambda e, kv=kv: e.tensor_copy(out=OBall[:, kv, 2, :], in_=OB3[:, 2, :]), reads=['OB3'], writes=['OBall'])
                ncols = {kv: 0 for kv in range(4)}
                sel_stages = []
                nst = 0
                for grp in range(32):
                    for kv in range(4):
                        kb = nst % 2
                        nst += 1

                        def selA(grp=grp, kv=kv, kb=kb, b=b):
                            hf = grp % 2
                            xk = ('XTs', hf)
                            if kv == 0:
                                gather_group(cache_sel, b, grp, npg=4, slot0=4 * hf)
                                transpose_group([0, 1], False, npair=2, slot0=4 * hf, xoff=512 * hf, xkey=xk)
                            pb, pk = bank(3)
                            S.op('pe', lambda e, pb=pb: e.matmul(pb[0:16, :], lhsT=QBp[:, kv, :], rhs=XT[:, kv // 2, hf * 512:(hf + 1) * 512], start=True, stop=True),
                                 reads=['QBp', xk], writes=[pk])
                            blk0 = grp * 8
                            S.op('dve', lambda e, pb=pb: e.tensor_tensor(out=SMs[kb][:, :].rearrange("p (n l) -> p n l", l=64), in0=pb[0:16, :].rearrange("p (n l) -> p n l", l=64),
                                                                          in1=mbAll[:, kv, blk0:blk0 + 8].unsqueeze(2).to_broadcast([16, 8, 64]), op=ALU.add),
                                 reads=[pk, 'mbAll'], writes=[SMk[kb]])
                            S.op('act', lambda e: e.activation(out=Pbs[kb][:, :], in_=SMs[kb][:, :], func=AF.Exp, accum_out=ssumS[:, kv, grp:grp + 1]), reads=[SMk[kb]], writes=[Pbk[kb], 'ssumS'])

                        def selB(grp=grp, kv=kv, kb=kb):
                            hf = grp % 2
                            transposeP(Pbs[kb], 512, PsTs[kb], skey=Pbk[kb], dkey=PsTk[kb])
                            for q4 in range(4):
                                first = (grp == 0 and q4 == 0)
                                S.op('pe', lambda e, q4=q4, first=first: e.matmul(ACCS4[kv][0:16, 0:64], lhsT=PsTs[kb][:, q4, :], rhs=CG[:, 4 * hf + q4, 256 + kv * 64:256 + (kv + 1) * 64],
                                                                             start=first, stop=False),
                                     reads=[PsTk[kb], ('CG', 4 * hf + q4)], writes=[('PB', 3 + kv)], inc=(q4 == 3))
                        sel_stages.append((selA, selB))
                for n in range(len(sel_stages) + 1):
                    if n < len(sel_stages):
                        sel_stages[n][0]()
                    if n >= 1:
                        sel_stages[n - 1][1]()
                for kv in range(4):
                    S.op('dve', lambda e, kv=kv, b=b: e.tensor_copy(out=QB[:, kv, :].rearrange("p (g t) -> p g t", g=4), in_=Qs[:, kv * 4:(kv + 1) * 4, 4 * b:4 * b + 4]), reads=['Qs'], writes=['QB'])
                    pb, pk = bank(3)
                    S.op('pe', lambda e, kv=kv, b=b, pb=pb: e.matmul(pb[0:16, 0:4], lhsT=QB[:, kv, :], rhs=KsN[:, kv, 4 * b:4 * b + 4], start=True, stop=True), reads=['QB', 'KsN'], writes=[pk])
                    S.op('dve', lambda e, pb=pb: e.tensor_tensor(out=SM[:, 0:4], in0=pb[0:16, 0:4], in1=CM16[:, :], op=ALU.add), reads=[pk, 'CM16'], writes=['SM'])
                    S.op('act', lambda e, kv=kv: e.activation(out=Pb[:, 0:4], in_=SM[:, 0:4], func=AF.Exp, accum_out=ssumS[:, kv, 32:33]), reads=['SM'], writes=['Pb', 'ssumS'])
                    S.op('pe', lambda e: e.transpose(out=PT[0:4, 0:16], in_=Pb[0:16, 0:4], identity=ident[0:16, 0:16]), reads=['Pb', 'ident'], writes=['PT'])
                    S.op('dve', lambda e: e.tensor_copy(out=PsT[0:4, 0, :], in_=PT[0:4, 0:16]), reads=['PT'], writes=['PsT'])
                    S.op('pe', lambda e, kv=kv, b=b: e.matmul(ACCS4[kv][0:16, 0:64], lhsT=PsT[0:4, 0, :], rhs=VnS[:, b, kv, :], start=False, stop=True), reads=['PsT', 'VnS'], writes=[('PB', 3 + kv)])
                    S.op('dve', lambda e, kv=kv: e.tensor_reduce(out=rsS[:, 1:2], in_=ssumS[:, kv, 0:33], axis=AX.X, op=ALU.add), reads=['ssumS'], writes=['rsS'])
                    S.op('dve', lambda e: e.reciprocal(out=rsS[:, 1:2], in_=rsS[:, 1:2]), reads=['rsS'], writes=['rsS'])
                    S.op('dve', lambda e, kv=kv: e.tensor_scalar(out=OBall[:, kv, 1, :], in0=ACCS4[kv][0:16, 0:64], scalar1=rsS[:, 1:2], scalar2=None, op0=ALU.mult), reads=[('PB', 3 + kv), 'rsS'], writes=['OBall'])
                for kv in range(4):
                    for br in range(3):
                        S.op('sp', lambda e, kv=kv, br=br, b=b: e.dma_start(out=o_scr[br, b, kv].rearrange("g t d -> (g t) d"), in_=OBall[:, kv, br, :]), reads=['OBall'], writes=['o_scr'], dma='st')
            S.barrier(SALL, [('CG', k) for k in range(8)] + ['XT', ('XTs', 0), ('XTs', 1), 'hidS', ('W1bd', 0), ('W1bd', 1), 'CGw', 'XTw'])
            for br in range(3):
                for b in range(NB):
                    for t in range(DT):
                        S.op('sp', lambda e, br=br, b=b, t=t: e.dma_start(out=OTk[4 * b + t:4 * b + t + 1, br, :, :], in_=o_scr[br, b, :, :, t, :].rearrange("k g d -> (k g) d").unsqueeze(0)),
                             reads=['o_scr'], writes=['OTk'], dma='ldx')
            for br in range(3):
                S.op('dve', lambda e, br=br: e.tensor_tensor(out=OTk[:, br, :, :], in0=OTk[:, br, :, :], in1=G[0:16, 0, br * 16:(br + 1) * 16].unsqueeze(2).to_broadcast([16, 16, 64]), op=ALU.mult),
                     reads=['OTk', 'G'], writes=['OTk'])
            S.op('dve', lambda e: e.tensor_tensor(out=OTk[:, 0, :, :], in0=OTk[:, 0, :, :], in1=OTk[:, 1, :, :], op=ALU.add), reads=['OTk'], writes=['OTk'])
            S.op('dve', lambda e: e.tensor_tensor(out=ObS[:, :].rearrange("p (h d) -> p h d", h=16), in0=OTk[:, 0, :, :], in1=OTk[:, 2, :, :], op=ALU.add), reads=['OTk'], writes=['ObS'])
            for c in range(8):
                S.op('pe', lambda e, c=c: e.transpose(out=PT[:, c * 128:c * 128 + 16], in_=ObS[0:16, c * 128:(c + 1) * 128], identity=ident[0:16, 0:16]), reads=['ObS', 'ident'], writes=['PT'], inc=(c == 7))
            S.op('dve', lambda e: e.tensor_copy(out=hT[:, :, 0:16], in_=PT[:, :].rearrange("p (c n) -> p c n", c=8)[:, :, 0:16]), reads=['PT'], writes=['srcT'])
            nouts = Stream([lambda: load_b(s_nout.rearrange("p c n -> p (c n)"), 8 * D, 's_nout')], 0)

            def wfun(hh):
                ap, k = nouts.get(0)
                return ap[:, 0:8 * D].rearrange("p (c n) -> p c n", c=8)[:, :, hh * 512:(hh + 1) * 512], k
            out_proj_add(hT, wfun, 1, 16, 1.0)
            S.barrier(SALL, [('sg', 0), ('sg', 1), 'PsT1'])

        def final_norm(nsub, npart, out_ap):
            for j in range(nsub):
                S.op('act', lambda e, j=j: e.activation(out=junk[:npart, :], in_=X[:npart, j, :], func=AF.Square, scale=1.0 / 32.0,
                                                         accum_out=ss[:npart, j:j + 1]), reads=['X'], writes=['junk', 'ss'])
            S.op('dve', lambda e: e.tensor_scalar(out=rstd[:npart, :nsub], in0=ss[:npart, :nsub], scalar1=EPS, scalar2=None, op0=ALU.add),
                 reads=['ss'], writes=['rstd'])
            S.op('act', lambda e: e.activation(out=rstd[:npart, :nsub], in_=rstd[:npart, :nsub], func=AF.Sqrt), reads=['rstd'], writes=['rstd'])
            S.op('dve', lambda e: e.reciprocal(out=rstd[:npart, :nsub], in_=rstd[:npart, :nsub]), reads=['rstd'], writes=['rstd'])
            for j in range(nsub):
                S.op('dve', lambda e, j=j: e.scalar_tensor_tensor(out=X[:npart, j, :], in0=X[:npart, j, :], scalar=rstd[:npart, j:j + 1], in1=gfin[:npart, :],
                                                                  op0=ALU.mult, op1=ALU.mult), reads=['X', 'rstd', 'gfin'], writes=['X'])
            S.op('pool', lambda e: e.dma_start(out=out_ap, in_=X[:npart, 0:nsub, :]), reads=['X'], writes=['yout'], dma='st')

        ARENA_KEYS = ['srcT', 'U', 'cgs', 'cacc']
        ALLENG = ['pe', 'act', 'dve', 'pool', 'sp']

        def run_tile(nsub, npart, x_ap, y_ap, first, sample, cs_out, row0, kv_outs, win_out, tile_i=None):
            S.op('sp', lambda e: e.dma_start(out=X[:npart, 0:nsub, :], in_=x_ap), writes=['X'], dma='ldx')
            ffn(0, 'a', nsub, npart)
            S.barrier(ALLENG, ARENA_KEYS)
            conv_layer(nsub, npart, first, sample, cs_out)
            S.barrier(ALLENG, ARENA_KEYS)
            ffn(0, 'b', nsub, npart)
            ffn(1, 'a', nsub, npart)
            S.barrier(ALLENG, ARENA_KEYS)
            if sample:
                nsa_sample(row0, kv_outs, win_out)
            else:
                nsa_prompt(tile_i, row0, kv_outs, win_out)
            S.barrier(ALLENG, ARENA_KEYS + NSA_KEYS)
            ffn(1, 'b', nsub, npart)
            final_norm(nsub, npart, y_ap)

        for s in range(nseq):
            for i in range(seqlen // T):
                r0 = s * seqlen + i * T
                last = (i == seqlen // T - 1)
                run_tile(4, 128, xp[r0:r0 + T, :].rearrange("(j p) d -> p j d", p=128),
                         yp[r0:r0 + T, :].rearrange("(j p) d -> p j d", p=128),
                         first=(i == 0), sample=False, cs_out=(csp[s] if last else None), row0=r0, kv_outs=(cmp_p, sel_p), tile_i=i,
                         win_out=((lambda j, st, s=s: [(win_p[s, j * 128:(j + 1) * 128, :], st[:, :])]) if last else None))
        for b in range(NB if with_sample else 0):
            for t2 in range(2):
                S.op('sp', lambda e, b=b, t2=t2: e.dma_start(out=ucarS[:, b, :, t2], in_=stc[b, t2].rearrange("(c p) -> p c", p=128), allow_slow_non_contiguous=True),
                     writes=['ucar'], dma='const')
        if with_sample:
          run_tile(1, NB * DT, xs.rearrange("(j p) d -> p j d", j=1), ys.rearrange("(j p) d -> p j d", j=1), first=False,
                 sample=True, cs_out=css, row0=0, kv_outs=(cmp_s, sel_s),
                 win_out=(lambda j, st: [(win_s[b, WB - DT:WB, :], st[DT * b:DT * (b + 1), :]) for b in range(NB)]))
        for b in range(NB if with_sample else 0):
            S.op('sp', lambda e, b=b: e.dma_start(out=win_s[b, 0:WB - DT, :], in_=cache_win[b, DT:WB, :]), writes=['wins'], dma='st')
        S.finish()
        S.emit()
    return nc


_NC_CACHE = {}


def kernel(**inp):
    f32 = lambda a: np.ascontiguousarray(np.asarray(a, dtype=np.float32))
    x_prompt = f32(inp["x_prompt"]); x_sample = f32(inp["x_sample"])
    state_conv = f32(inp["state_conv"])
    cache_cmp = f32(inp["cache_cmp_kv"]).reshape(5120 * 128, 512)
    cache_sel = f32(inp["cache_sel_kv"]).reshape(5120 * 128, 512)
    cache_win = f32(inp["cache_win_kv"]).reshape(32, WB, 512)
    page_table = np.ascontiguousarray(np.asarray(inp["page_table"], dtype=np.int32))
    if 'nc' not in _NC_CACHE:
        _NC_CACHE['nc'] = build_nc()
    nc = _NC_CACHE['nc']
    shared = {
        "cache_cmp": cache_cmp, "cache_sel": cache_sel,
        "norm_ffa": f32(inp["norm_ffa"]), "norm_mix": f32(inp["norm_mix"]), "norm_ffb": f32(inp["norm_ffb"]),
        "w_ffa_gu": f32(inp["w_ffa_gu"]), "w_ffa_down": f32(inp["w_ffa_down"]),
        "w_ffb_gu": f32(inp["w_ffb_gu"]), "w_ffb_down": f32(inp["w_ffb_down"]),
        "w_conv_in": f32(inp["w_conv_in"])[0], "w_conv": f32(inp["w_conv"])[0], "w_conv_out": f32(inp["w_conv_out"])[0],
        "w_nsa_in": f32(inp["w_nsa_in"])[0], "pe_cmp": f32(inp["pe_cmp"])[0],
        "w_k1": f32(inp["w_cmp_k1"])[0], "w_k2": f32(inp["w_cmp_k2"])[0], "w_v1": f32(inp["w_cmp_v1"])[0], "w_v2": f32(inp["w_cmp_v2"])[0],
        "w_nsa_out": f32(inp["w_nsa_out"])[0], "norm_final": f32(inp["norm_final"]),
    }
    in_maps = []
    for c in range(NCORE):
        m = dict(shared)
        m["xp"] = x_prompt[NSEQ * c:NSEQ * (c + 1)].reshape(NSEQ * SEQ, D)
        m["xs"] = x_sample[NB * c:NB * (c + 1)].reshape(NB * DT, D)
        m["stc"] = np.ascontiguousarray(state_conv[0, NB * c:NB * (c + 1)])
        m["cache_win"] = np.ascontiguousarray(cache_win[NB * c:NB * (c + 1)])
        m["ptab"] = np.ascontiguousarray(page_table[NB * c:NB * (c + 1)])
        in_maps.append(m)
    res = run_bass_kernel_spmd(nc, in_maps, core_ids=list(range(NCORE)))
    R = res.results
    cat = lambda k: np.concatenate([np.asarray(r[k]) for r in R], axis=0)
    y_prompt = cat("yp").reshape(16, SEQ, D)
    y_sample = cat("ys").reshape(32, DT, D)
    conv_p = cat("csp").reshape(1, 16, 2, D)
    conv_s = cat("css").reshape(1, 32, 2, D)
    cmp_p = cat("cmp_p").reshape(1, 16, SEQ, 2, 4, 64)
    cmp_s = cat("cmp_s").reshape(1, 32, DT, 2, 4, 64)
    sel_p = cat("sel_p").reshape(1, 16, SEQ, 2, 4, 64)
    sel_s = cat("sel_s").reshape(1, 32, DT, 2, 4, 64)
    win_p = cat("win_p").reshape(1, 16, WB, 2, 4, 64)
    win_s = cat("win_s").reshape(1, 32, WB, 2, 4, 64)
    return (y_prompt, y_sample, conv_p, conv_s, cmp_p, cmp_s, sel_p, sel_s, win_p, win_s)
```
